# Optimizing a Trainium2 kernel written in Bass

```python
import math
import jax, jax.numpy as jnp
from jax import lax
import numpy as np

D_MODEL = 2048
BATCH = 8
SEQ = 2048
DEPTH = 4

HEAD_DIM = 128
N_MIX_HEADS = D_MODEL // HEAD_DIM
N_MIXERS = 4
GROUP_HEADS = N_MIX_HEADS // N_MIXERS
GROUP_W = GROUP_HEADS * HEAD_DIM
MIX_W = N_MIXERS * GROUP_W
BLOCK_Q = 128
ROPE_THETA = 500000.0
FOX_HEADS = GROUP_HEADS
MLA_HEADS = GROUP_HEADS
MLA_Q_RANK = 384
MLA_KV_RANK = 256
MLA_NOPE = 128
MLA_ROPE = 64
MLA_V = 128
SGU_GROUPS = GROUP_HEADS
SGU_CH = HEAD_DIM
SGU_CHUNK = 128
DIFF_HEADS = GROUP_HEADS
DIFF_D = HEAD_DIM // 2
DIFF_ROT = DIFF_D // 4
MEM_LEN = 256
CROSS_HEADS = 4
CROSS_HEAD_DIM = 128
CROSS_W = CROSS_HEADS * CROSS_HEAD_DIM
FFN_HIDDEN = ((8 * D_MODEL // 3 + 255) // 256) * 256
EPS = 1e-6

IN_SIZES = (GROUP_W, GROUP_W, GROUP_W, FOX_HEADS,
            MLA_Q_RANK, MLA_KV_RANK, MLA_ROPE,
            2 * GROUP_W,
            GROUP_W, GROUP_W, GROUP_W)
N_IN = 3 * GROUP_W + FOX_HEADS + MLA_Q_RANK + MLA_KV_RANK + MLA_ROPE + 2 * GROUP_W + 3 * GROUP_W

kernel_name = 'hybrid_parallel_head_group_decoder'

F32 = jnp.float32


def rmsnorm(x, g):
    xf = x.astype(F32)
    y = xf * lax.rsqrt(jnp.mean(xf * xf, axis=-1, keepdims=True) + EPS)
    return (y * g.astype(F32)).astype(x.dtype)


def layernorm(x, g, b):
    xf = x.astype(F32)
    mu = jnp.mean(xf, axis=-1, keepdims=True)
    xc = xf - mu
    y = xc * lax.rsqrt(jnp.mean(xc * xc, axis=-1, keepdims=True) + EPS)
    return (y * g.astype(F32) + b.astype(F32)).astype(x.dtype)


def rope_tables(positions, rot_dim):
    freqs = ROPE_THETA ** (-jnp.arange(0, rot_dim, 2, dtype=F32) / rot_dim)
    ang = positions.astype(F32)[..., None] * freqs
    return jnp.cos(ang), jnp.sin(ang)


def apply_rope(x, cos, sin):
    half = cos.shape[-1]
    r = 2 * half
    c = cos[:, :, None, :].astype(x.dtype)
    s = sin[:, :, None, :].astype(x.dtype)
    x1, x2, xp = x[..., :half], x[..., half:r], x[..., r:]
    return jnp.concatenate([x1 * c - x2 * s, x1 * s + x2 * c, xp], axis=-1)


def _heads(a, h):
    b, s, _ = a.shape
    return a.reshape(b, s, h, -1)


def _bhsd(a):
    return a.transpose(0, 2, 1, 3)


def _merge(o):
    b, h, s, d = o.shape
    return o.transpose(0, 2, 1, 3).reshape(b, s, h * d)


def _to_blocks(a):
    b, h, s = a.shape[:3]
    a = a.reshape((b, h, s // BLOCK_Q, BLOCK_Q) + a.shape[3:])
    return jnp.moveaxis(a, 2, 0)


def _from_blocks(o):
    nb, b, h, blk, d = o.shape
    return jnp.moveaxis(o, 0, 2).reshape(b, h, nb * blk, d)


def _masked_softmax(scores, blk_idx):
    s = scores.shape[-1]
    q_pos = blk_idx * BLOCK_Q + jnp.arange(BLOCK_Q)
    mask = jnp.arange(s)[None, :] <= q_pos[:, None]
    return jax.nn.softmax(jnp.where(mask, scores, -jnp.inf), axis=-1)


def causal_attention(q, k, v, scale, log_decay=None):
    nb = q.shape[2] // BLOCK_Q
    xs = (jnp.arange(nb), _to_blocks(q))
    if log_decay is not None:
        xs = xs + (_to_blocks(log_decay),)

    def body(args):
        i, qb = args[0], args[1]
        sc = jnp.einsum('bhqd,bhkd->bhqk', qb, k).astype(F32) * scale
        if log_decay is not None:
            sc = sc + args[2].astype(F32)[..., None] - log_decay.astype(F32)[:, :, None, :]
        p = _masked_softmax(sc, i)
        return jnp.einsum('bhqk,bhkd->bhqd', p.astype(v.dtype), v)

    return _from_blocks(lax.map(body, xs))


def differential_attention(q1, q2, k1, k2, v, scale, lam):
    nb = q1.shape[2] // BLOCK_Q

    def body(args):
        i, q1b, q2b = args
        p1 = _masked_softmax(jnp.einsum('bhqd,bhkd->bhqk', q1b, k1).astype(F32) * scale, i)
        p2 = _masked_softmax(jnp.einsum('bhqd,bhkd->bhqk', q2b, k2).astype(F32) * scale, i)
        p = p1 - lam * p2
        return jnp.einsum('bhqk,bhkd->bhqd', p.astype(v.dtype), v)

    return _from_blocks(lax.map(body, (jnp.arange(nb), _to_blocks(q1), _to_blocks(q2))))


def fox_mixer(q, k, v, f_logit, b_f):
    log_f = jax.nn.log_sigmoid((f_logit + b_f).astype(F32))
    cum = jnp.cumsum(log_f, axis=1).transpose(0, 2, 1)
    o = causal_attention(_bhsd(_heads(q, FOX_HEADS)), _bhsd(_heads(k, FOX_HEADS)),
                         _bhsd(_heads(v, FOX_HEADS)), HEAD_DIM ** -0.5, cum)
    return _merge(o)


def mla_mixer(c_q, c_kv, k_rope, g_cq, g_ckv, w_uq, w_ukv, cos, sin):
    q = _heads(rmsnorm(c_q, g_cq) @ w_uq, MLA_HEADS)
    kv = _heads(rmsnorm(c_kv, g_ckv) @ w_ukv, MLA_HEADS)
    q = jnp.concatenate([q[..., :MLA_NOPE], apply_rope(q[..., MLA_NOPE:], cos, sin)], axis=-1)
    kr = apply_rope(k_rope[:, :, None, :], cos, sin)
    k = jnp.concatenate([kv[..., :MLA_NOPE],
                         jnp.broadcast_to(kr, kv.shape[:3] + (MLA_ROPE,))], axis=-1)
    v = kv[..., MLA_NOPE:]
    o = causal_attention(_bhsd(q), _bhsd(k), _bhsd(v), (MLA_NOPE + MLA_ROPE) ** -0.5)
    return _merge(o)


def sgu_mixer(uv, ln_g, ln_b, w_s, b_s):
    z = jax.nn.gelu(uv)
    u, v = z[..., :GROUP_W], z[..., GROUP_W:]
    v = layernorm(v, ln_g, ln_b)
    b, s, _ = v.shape
    v = v.reshape(b, s // SGU_CHUNK, SGU_CHUNK, SGU_GROUPS, SGU_CH)
    w = jnp.tril(w_s)
    mixed = jnp.einsum('gts,bnsgc->bntgc', w, v) + b_s.T[None, None, :, :, None]
    return u * mixed.reshape(b, s, GROUP_W)


def diff_mixer(q, k, v, lq1, lk1, lq2, lk2, g_diff, lam_init, cos, sin):
    q = _heads(q, DIFF_HEADS)
    k = _heads(k, DIFF_HEADS)
    v = _heads(v, DIFF_HEADS)
    q1 = apply_rope(q[..., :DIFF_D], cos, sin)
    q2 = apply_rope(q[..., DIFF_D:], cos, sin)
    k1 = apply_rope(k[..., :DIFF_D], cos, sin)
    k2 = apply_rope(k[..., DIFF_D:], cos, sin)
    lam = (jnp.exp(jnp.sum(lq1.astype(F32) * lk1.astype(F32)))
           - jnp.exp(jnp.sum(lq2.astype(F32) * lk2.astype(F32))) + lam_init)
    o = differential_attention(_bhsd(q1), _bhsd(q2), _bhsd(k1), _bhsd(k2), _bhsd(v),
                               DIFF_D ** -0.5, lam)
    o = rmsnorm(o, g_diff) * (1.0 - lam_init)
    return _merge(o)


def memory_cross_attention(xn, mem_n, w_q, w_k, w_v, w_o):
    q = _heads(xn @ w_q, CROSS_HEADS)
    k = _heads(mem_n @ w_k, CROSS_HEADS)
    v = _heads(mem_n @ w_v, CROSS_HEADS)
    sc = jnp.einsum('bshd,bmhd->bhsm', q, k).astype(F32) * CROSS_HEAD_DIM ** -0.5
    p = jax.nn.softmax(sc, axis=-1)
    o = jnp.einsum('bhsm,bmhd->bshd', p.astype(v.dtype), v)
    b, s = o.shape[:2]
    return o.reshape(b, s, CROSS_W) @ w_o


def swiglu(xn, w_gate, w_up, w_down):
    return (jax.nn.silu(xn @ w_gate) * (xn @ w_up)) @ w_down


def _split_points():
    pts, acc = [], 0
    for sz in IN_SIZES[:-1]:
        acc += sz
        pts.append(acc)
    return pts


def setup_inputs(seed: int = 0) -> dict:
    key = jax.random.key(seed)
    ks = jax.random.split(key, 32)

    def nrm(k, shape, scale):
        return jax.random.normal(k, shape, F32) * scale

    def gain(k, shape):
        return 1.0 + 0.02 * jax.random.normal(k, shape, F32)

    L, D = DEPTH, D_MODEL
    return {
        'x': nrm(ks[0], (BATCH, SEQ, D), 1.0),
        'mem': nrm(ks[1], (BATCH, MEM_LEN, D), 1.0),
        'positions': jnp.broadcast_to(jnp.arange(SEQ, dtype=jnp.int32)[None, :], (BATCH, SEQ)),
        'g_mix': gain(ks[2], (L, D)),
        'w_in': nrm(ks[3], (L, D, N_IN), D ** -0.5),
        'b_f': 3.0 + 0.5 * jax.random.normal(ks[4], (L, FOX_HEADS), F32),
        'g_cq': gain(ks[5], (L, MLA_Q_RANK)),
        'g_ckv': gain(ks[6], (L, MLA_KV_RANK)),
        'w_uq': nrm(ks[7], (L, MLA_Q_RANK, MLA_HEADS * (MLA_NOPE + MLA_ROPE)), MLA_Q_RANK ** -0.5),
        'w_ukv': nrm(ks[8], (L, MLA_KV_RANK, MLA_HEADS * (MLA_NOPE + MLA_V)), MLA_KV_RANK ** -0.5),
        'sgu_ln_g': gain(ks[9], (L, GROUP_W)),
        'sgu_ln_b': nrm(ks[10], (L, GROUP_W), 0.02),
        'w_s': nrm(ks[11], (L, SGU_GROUPS, SGU_CHUNK, SGU_CHUNK), SGU_CHUNK ** -0.5),
        'b_s': 1.0 + 0.1 * jax.random.normal(ks[12], (L, SGU_GROUPS, SGU_CHUNK), F32),
        'lam_q1': nrm(ks[13], (L, DIFF_D), 0.1),
        'lam_k1': nrm(ks[14], (L, DIFF_D), 0.1),
        'lam_q2': nrm(ks[15], (L, DIFF_D), 0.1),
        'lam_k2': nrm(ks[16], (L, DIFF_D), 0.1),
        'g_diff': gain(ks[17], (L, 2 * DIFF_D)),
        'w_o': nrm(ks[18], (L, MIX_W, D), MIX_W ** -0.5),
        'g_mem': gain(ks[19], (D,)),
        'g_cross': gain(ks[20], (L, D)),
        'w_cq': nrm(ks[21], (L, D, CROSS_W), D ** -0.5),
        'w_ck': nrm(ks[22], (L, D, CROSS_W), D ** -0.5),
        'w_cv': nrm(ks[23], (L, D, CROSS_W), D ** -0.5),
        'w_co': nrm(ks[24], (L, CROSS_W, D), CROSS_W ** -0.5),
        'g_ffn': gain(ks[25], (L, D)),
        'w_gate': nrm(ks[26], (L, D, FFN_HIDDEN), D ** -0.5),
        'w_up': nrm(ks[27], (L, D, FFN_HIDDEN), D ** -0.5),
        'w_down': nrm(ks[28], (L, FFN_HIDDEN, D), FFN_HIDDEN ** -0.5),
        'g_final': gain(ks[29], (D,)),
    }


def reference(x, mem, positions, g_mix, w_in, b_f, g_cq, g_ckv, w_uq, w_ukv, sgu_ln_g, sgu_ln_b,
              w_s, b_s, lam_q1, lam_k1, lam_q2, lam_k2, g_diff, w_o, g_mem, g_cross, w_cq, w_ck,
              w_cv, w_co, g_ffn, w_gate, w_up, w_down, g_final):
    cos_mla, sin_mla = rope_tables(positions, MLA_ROPE)
    cos_d, sin_d = rope_tables(positions, DIFF_ROT)
    mem_n = rmsnorm(mem, g_mem)
    pts = _split_points()
    for l in range(DEPTH):
        h = rmsnorm(x, g_mix[l])
        (fq, fk, fv, ff, cq, ckv, kr, uv, dq, dk, dv) = jnp.split(h @ w_in[l], pts, axis=-1)
        lam_init = 0.8 - 0.6 * math.exp(-0.3 * l)
        y_a = fox_mixer(fq, fk, fv, ff, b_f[l])
        y_b = mla_mixer(cq, ckv, kr, g_cq[l], g_ckv[l], w_uq[l], w_ukv[l], cos_mla, sin_mla)
        y_c = sgu_mixer(uv, sgu_ln_g[l], sgu_ln_b[l], w_s[l], b_s[l])
        y_d = diff_mixer(dq, dk, dv, lam_q1[l], lam_k1[l], lam_q2[l], lam_k2[l], g_diff[l],
                         lam_init, cos_d, sin_d)
        x = x + jnp.concatenate([y_a, y_b, y_c, y_d], axis=-1) @ w_o[l]
        x = x + memory_cross_attention(rmsnorm(x, g_cross[l]), mem_n, w_cq[l], w_ck[l], w_cv[l], w_co[l])
        x = x + swiglu(rmsnorm(x, g_ffn[l]), w_gate[l], w_up[l], w_down[l])
    return rmsnorm(x, g_final)
```

```python
import math
import numpy as np
import concourse.bass as bass
import concourse.mybir as mybir
from concourse.bass_utils import run_bass_kernel_spmd

F32 = mybir.dt.float32
BF16 = mybir.dt.bfloat16
I32 = mybir.dt.int32
AF = mybir.ActivationFunctionType
ALU = mybir.AluOpType

D = 2048
NCH = 16
MEM = 256
EPS = 1e-6
N_IN = 4804
COMPUTE = ("pe", "act", "dve", "pool")


ALL_RES = []


class Res:
    __slots__ = ("name", "wtok", "rtoks", "ld_sem", "ld_n", "st_sem", "st_n")

    def __init__(self, name):
        ALL_RES.append(self)
        self.name = name
        self.wtok = None
        self.rtoks = {}
        self.ld_sem = None
        self.ld_n = 0
        self.st_sem = None
        self.st_n = 0


class V:
    __slots__ = ("ap", "res")

    def __init__(self, ap, res):
        self.ap = ap
        self.res = res if isinstance(res, list) else [res]

    def __getitem__(self, idx):
        return self.ap[idx]


def _rl(vs):
    out = []
    for v in vs:
        if isinstance(v, V):
            out.extend(v.res)
        else:
            out.append(v)
    return out


class Prog:
    def __init__(self, nc):
        self.nc = nc
        self.streams = {e: [] for e in ("pe", "act", "dve", "pool", "sp")}
        self.esem = {e: nc.alloc_semaphore(f"es_{e}") for e in COMPUTE}
        self.ecnt = {e: 0 for e in COMPUTE}
        self.waited = {e: {} for e in self.streams}
        self.nsem = 0
        self.ninstr = 0
        self.nmm = 0
        self.phases = []

    def new_sem(self, name):
        self.nsem += 1
        return self.nc.alloc_semaphore(f"s{self.nsem}_{name}")

    def _need(self, e, tok):
        if tok is None:
            return
        sem, val, _ = tok
        w = self.waited[e]
        if w.get(sem.num, 0) >= val:
            return
        w[sem.num] = val
        self.streams[e].append(("wait", sem, val))

    def _deps(self, e, reads, writes, pe_acc):
        for r in reads:
            self._need(e, r.wtok)
        for r in writes:
            if not (pe_acc and r.wtok is not None and r.wtok[2] == "pe"):
                self._need(e, r.wtok)
            for tok in r.rtoks.values():
                self._need(e, tok)

    @staticmethod
    def _mark(tok, reads, writes):
        for r in reads:
            r.rtoks[tok[0].num] = tok
        for r in writes:
            r.wtok = tok
            r.rtoks = {}

    def op(self, e, fn, reads=(), writes=(), pe_acc=False, nmm=0):
        self.nmm += nmm
        reads, writes = _rl(reads), _rl(writes)
        self._deps(e, reads, writes, pe_acc)
        self.ecnt[e] += 1
        tok = (self.esem[e], self.ecnt[e], e)
        self.streams[e].append(("op", fn, self.esem[e]))
        self._mark(tok, reads, writes)
        self.ninstr += 1

    def dma(self, q, fns, reads=(), writes=(), owner=None, kind="ld"):
        reads, writes = _rl(reads), _rl(writes)
        self._deps(q, reads, writes, False)
        o = owner.res[0]
        if kind == "ld":
            if o.ld_sem is None:
                o.ld_sem = self.new_sem("ld")
            o.ld_n += len(fns)
            sem, n = o.ld_sem, o.ld_n
        else:
            if o.st_sem is None:
                o.st_sem = self.new_sem("st")
            o.st_n += len(fns)
            sem, n = o.st_sem, o.st_n
        tok = (sem, 16 * n, "dma")
        for fn in fns:
            self.streams[q].append(("dma", fn, sem))
        self._mark(tok, reads, writes)
        self.ninstr += len(fns)

    def wait_all(self, e, vs):
        for r in _rl(vs):
            self._need(e, r.wtok)
            for tok in r.rtoks.values():
                self._need(e, tok)

    def emit(self):
        nc = self.nc
        streams = self.streams

        def run(name):
            def body(engine):
                for item in streams[name]:
                    if item[0] == "wait":
                        engine.wait_ge(item[1], item[2])
                    elif item[0] == "op":
                        item[1](engine).then_inc(item[2], 1)
                    else:
                        item[1](engine).then_inc(item[2], 16)
            return body

        with nc.Block() as block:
            block.tensor(run("pe"))
            block.scalar(run("act"))
            block.vector(run("dve"))
            block.gpsimd(run("pool"))
            block.sync(run("sp"))


def small_layout(L):
    off = {}
    c = 0
    for l in range(L):
        for nm, w in (("g_mix", 16), ("g_cross", 16), ("g_ffn", 16), ("g_cq", 3), ("g_ckv", 2),
                      ("g_diff", 1), ("b_f", 1)):
            off[(nm, l)] = c
            c += w
    for nm in ("g_mem", "g_final"):
        off[(nm, 0)] = c
        c += 16
    return off, c


CONST_COLS = {"ident": 0, "tril": 128, "freq_mla": 256, "freq_diff": 257, "mask": 258}
NCST_SB = 258
NCONST = 258 + 4 * 512 + 128


def make_consts():
    c = np.zeros((128, NCONST), np.float32)
    c[:, 0:128] = np.eye(128, dtype=np.float32)
    p = np.arange(128)[:, None]
    f = np.arange(128)[None, :]
    c[:, 128:256] = (f <= p).astype(np.float32)
    ff = np.arange(512)[None, :]
    for j in range(4):
        c[:, 258 + j * 512:258 + (j + 1) * 512] = (ff - p - 128 * j >= 0).astype(np.float32)
    theta = 500000.0
    fm = theta ** (-np.arange(0, 64, 2, dtype=np.float32) / 64.0)
    c[0:64, CONST_COLS["freq_mla"]] = np.concatenate([fm, fm])
    fd = theta ** (-np.arange(0, 16, 2, dtype=np.float32) / 16.0)
    one = np.zeros(64, np.float32)
    one[0:8] = fd
    one[8:16] = fd
    c[:, CONST_COLS["freq_diff"]] = np.concatenate([one, one])
    R = np.zeros((128, 128), np.float32)
    for base in (0, 64):
        for i in range(8):
            R[base + 8 + i, base + i] = -1.0
            R[base + i, base + 8 + i] = 1.0
    c[:, 258 + 2048:258 + 2048 + 128] = R
    return c


def build_program(S, L, FF, dbg=None):
    del ALL_RES[:]
    HALVES = 2
    Th = S // HALVES
    NTB = Th // 512
    NKT = S // 128
    nc = bass.Bass("TRN2", target_bir_lowering=False)
    P = Prog(nc)

    def din(name, shape, dt=F32):
        return nc.dram_tensor(name, list(shape), dt, kind="ExternalInput").ap()

    x_d = din("x", [S, D])
    mem_d = din("mem", [MEM, D])
    pos_d = din("positions", [1, S], I32)
    w_in = din("w_in", [L, D, N_IN])
    w_uq = din("w_uq", [L, 384, 768])
    w_ukv = din("w_ukv", [L, 256, 1024])
    sgu_ln_g = din("sgu_ln_g", [L, 512])
    sgu_ln_b = din("sgu_ln_b", [L, 512])
    w_s = din("w_s", [L, 4, 128, 128])
    b_s = din("b_s", [L, 512])
    lam_d = {k: din(k, [L, 64]) for k in ("lam_q1", "lam_k1", "lam_q2", "lam_k2")}
    w_o = din("w_o", [L, D, D])
    w_cq = din("w_cq", [L, D, 512])
    w_ck = din("w_ck", [L, D, 512])
    w_cv = din("w_cv", [L, D, 512])
    w_co = din("w_co", [L, 512, D])
    w_gate = din("w_gate", [L, D, FF])
    w_up = din("w_up", [L, D, FF])
    w_down = din("w_down", [L, FF, D])
    soff, NS = small_layout(L)
    small_d = din("small", [128, NS])
    consts_d = din("consts", [128, NCONST])
    out_d = nc.dram_tensor("out", [S, D], F32, kind="ExternalOutput").ap()

    def scratch(name, shape, dt=BF16):
        return nc.dram_tensor("scr_" + name, list(shape), dt, kind="ExternalOutput").ap()

    SC = {}
    SCR = {}
    for l in range(L):
        for nm, shp, dt in (("fq", [512, S], BF16), ("fk", [512, S], BF16), ("fv", [S, 512], BF16),
                            ("fFp", [4, 3, S], BF16), ("fFn", [4, 3, S], BF16),
                            ("mqn", [512, S], BF16), ("mqr", [256, S], BF16), ("mkn", [512, S], BF16),
                            ("mkr", [64, S], BF16), ("mv", [S, 512], BF16),
                            ("dq", [512, S], BF16), ("dk", [512, S], BF16), ("dv", [S, 512], BF16)):
            SC[(nm, l)] = scratch(f"{nm}{l}", shp, dt)
            SCR[(nm, l)] = Res(f"{nm}{l}")
    for nm, shp in (("rCm", [64, S]), ("rSm", [64, S]), ("rCd", [128, S]), ("rSd", [128, S])):
        SC[nm] = scratch(nm, shp, F32)
        SCR[nm] = Res(nm)
    SC["memT"] = scratch("memT", [D, MEM], BF16)
    SCR["memT"] = Res("memT")

    def sb(name, shape, dt):
        return nc.alloc_sbuf_tensor("sb_" + name, list(shape), dt).ap()

    xs_ap = sb("xs", [128, NCH, Th], F32)
    xs = [[V(xs_ap[:, c, tb * 512:(tb + 1) * 512], Res(f"xs{c}_{tb}")) for tb in range(NTB)] for c in range(NCH)]
    hT_ap = sb("hT", [128, NCH, Th], BF16)
    hT = [[V(hT_ap[:, c, tb * 512:(tb + 1) * 512], Res(f"hT{c}_{tb}")) for tb in range(NTB)] for c in range(NCH)]
    NW = 4
    wslots = [V(sb(f"w{i}", [128, 4096], BF16), Res(f"w{i}")) for i in range(NW)]
    yT_ap = sb("yT", [128, 4, Th], BF16)
    yT = [[V(yT_ap[:, k, tb * 512:(tb + 1) * 512], Res(f"yT{k}_{tb}")) for tb in range(NTB)] for k in range(4)]
    NTMP = 6
    tmps = [V(sb(f"tmp{i}", [128, 512], F32), Res(f"tmp{i}")) for i in range(NTMP)]
    NBT = 4
    bts = [V(sb(f"bt{i}", [128, 512], BF16), Res(f"bt{i}")) for i in range(NBT)]
    small = V(sb("small", [128, NS], F32), Res("small"))
    cst = V(sb("cst", [128, NCST_SB], F32), Res("cst"))
    ident_f = cst.ap[:, 0:128]
    misc = V(sb("misc", [128, 16], F32), Res("misc"))
    cb = V(sb("cb", [128, 128 + 128 + 128 + 4 * 512 + 128], BF16), Res("cb"))
    ident_b = cb.ap[:, 0:128]
    ones_b = cb.ap[:, 128:256]
    tril_b = cb.ap[:, 256:384]
    mask_b = [cb.ap[:, 384 + j * 512:384 + (j + 1) * 512] for j in range(4)]
    Rd_b = cb.ap[:, 384 + 2048:384 + 2048 + 128]
    NBLK = 12
    blk_ap = sb("blk", [128, NBLK * 2048], BF16)
    blk_res = [Res(f"blk{i}") for i in range(NBLK)]

    def bview(b0, nb, dt=BF16, shape=None):
        ap = blk_ap[:, b0 * 2048:(b0 + nb) * 2048]
        if dt != BF16:
            ap = ap.bitcast(dt)
        return V(ap, blk_res[b0:b0 + nb])

    psb = [V(nc.alloc_psum_tensor(f"ps{i}", [128, 512], F32).ap(), Res(f"ps{i}")) for i in range(8)]
    rot = {"w": 0, "tmp": 0, "bt": 0, "S": 0, "O": 0, "Z": 0, "ev": 0}

    def nxt(kind, lst):
        i = rot[kind]
        rot[kind] = i + 1
        return lst[i % len(lst)]

    def ps_s():
        return nxt("S", psb[0:4])

    def ps_o():
        return nxt("O", psb[4:6])

    def ps_z():
        return nxt("Z", psb[6:8])

    def tmp():
        return nxt("tmp", tmps)

    def bt():
        return nxt("bt", bts)

    def evac_engine():
        rot["ev"] += 1
        return "act" if rot["ev"] % 2 else "dve"

    def sml(nm, l, j=0, n=128):
        c = soff[(nm, l)] + j
        return small.ap[0:n, c:c + 1]

    def load_w(src, a, b):
        slot = nxt("w", wslots)
        npart = src.shape[0]
        dst = slot.ap[0:npart, 0:a * b].rearrange("p (a b) -> p a b", a=a)
        if b > 512:
            bb = max(d_ for d_ in range(1, 513) if b % d_ == 0)
            d2 = dst.rearrange("p a (b2 b) -> p a b2 b", b=bb)
            s2 = src.rearrange("p a (b2 b) -> p a b2 b", b=bb)
        else:
            d2, s2 = dst, src
        P.dma("pool", [lambda e: e.dma_start(out=d2, in_=s2)], writes=[slot], owner=slot)
        return V(dst, slot.res)

    def mm(out_v, mms, reads):
        def fn(e):
            ins = None
            for (o, lt, r, st, sp) in mms:
                ins = e.matmul(o, lhsT=lt, rhs=r, start=st, stop=sp)
            return ins
        P.op("pe", fn, reads=reads, writes=[out_v], pe_acc=True, nmm=len(mms))

    def act(out_v, out_ap, in_v, in_ap, func, scale=1.0, bias=None, extra_reads=()):
        kw = {}
        if bias is not None:
            kw["bias"] = bias
        P.op("act", lambda e: e.activation(out=out_ap, in_=in_ap, func=func, scale=scale, **kw),
             reads=[in_v] + list(extra_reads), writes=[out_v])

    def copy_any(out_v, out_ap, in_v, in_ap, scale=None):
        eng = evac_engine()
        if eng == "act":
            if scale is None:
                P.op("act", lambda e: e.activation(out=out_ap, in_=in_ap, func=AF.Identity), reads=[in_v], writes=[out_v])
            else:
                P.op("act", lambda e: e.activation(out=out_ap, in_=in_ap, func=AF.Identity, scale=scale), reads=[in_v], writes=[out_v])
        else:
            if scale is None:
                P.op("dve", lambda e: e.tensor_copy(out=out_ap, in_=in_ap), reads=[in_v], writes=[out_v])
            else:
                P.op("dve", lambda e: e.tensor_scalar(out=out_ap, in0=in_ap, scalar1=scale, scalar2=None, op0=ALU.mult),
                     reads=[in_v], writes=[out_v])

    def dve_tt(out_v, out_ap, a_v, a_ap, b_v, b_ap, op):
        P.op("dve", lambda e: e.tensor_tensor(out=out_ap, in0=a_ap, in1=b_ap, op=op), reads=[a_v, b_v], writes=[out_v])

    def dve_ts(out_v, out_ap, a_v, a_ap, s1, s2, op0, op1=None, extra=()):
        if op1 is None:
            P.op("dve", lambda e: e.tensor_scalar(out=out_ap, in0=a_ap, scalar1=s1, scalar2=None, op0=op0),
                 reads=[a_v] + list(extra), writes=[out_v])
        else:
            P.op("dve", lambda e: e.tensor_scalar(out=out_ap, in0=a_ap, scalar1=s1, scalar2=s2, op0=op0, op1=op1),
                 reads=[a_v] + list(extra), writes=[out_v])

    def dve_stt(out_v, out_ap, a_v, a_ap, sc, b_v, b_ap, op0, op1, extra=()):
        P.op("dve", lambda e: e.scalar_tensor_tensor(out=out_ap, in0=a_ap, scalar=sc, in1=b_ap, op0=op0, op1=op1),
             reads=[a_v, b_v] + list(extra), writes=[out_v])

    def ld(dst_v, dst_ap, src_ap, src_res):
        P.dma("sp", [lambda e: e.dma_start(out=dst_ap, in_=src_ap)], reads=[src_res], writes=[dst_v], owner=dst_v)

    def st(dst_ap, dst_res, src_v, src_ap):
        P.dma("sp", [lambda e: e.dma_start(out=dst_ap, in_=src_ap)], reads=[src_v], writes=[dst_res], owner=src_v, kind="st")

    def rstd_from_ps(ps, npart, dim):
        t = tmp()
        act(t, t.ap[0:npart, :], ps, ps.ap[0:npart, :], AF.Sqrt, scale=1.0 / dim, bias=misc.ap[0:npart, 0:1], extra_reads=[misc])
        P.op("dve", lambda e: e.reciprocal(out=t.ap[0:npart, :], in_=t.ap[0:npart, :]), reads=[t], writes=[t])
        return t

    def sumsq_ps(srcs, npart):
        ps = ps_s()
        n = len(srcs)
        for i, (sv, sap) in enumerate(srcs):
            q = bt()
            act(q, q.ap[0:npart, :], sv, sap, AF.Square)
            mm(ps, [(ps.ap[:, :], ones_b[0:npart, :], q.ap[0:npart, :], i == 0, i == n - 1)], [q, cb])
        return ps

    P.dma("sp", [lambda e: e.dma_start(out=small.ap, in_=small_d)], writes=[small], owner=small)
    P.dma("sp", [lambda e: e.dma_start(out=cst.ap, in_=consts_d[:, 0:NCST_SB])], writes=[cst], owner=cst)
    P.op("dve", lambda e: e.memset(misc.ap[:, 0:1], EPS), writes=[misc])
    P.op("dve", lambda e: e.memset(misc.ap[:, 1:2], math.pi), writes=[misc])
    P.op("dve", lambda e: e.memset(misc.ap[:, 2:3], 1.0), writes=[misc])
    P.op("dve", lambda e: e.tensor_copy(out=cb.ap[:, 0:128], in_=cst.ap[:, 0:128]), reads=[cst], writes=[cb])
    P.op("dve", lambda e: e.memset(cb.ap[:, 128:256], 1.0), writes=[cb])
    P.op("dve", lambda e: e.tensor_copy(out=cb.ap[:, 256:384], in_=cst.ap[:, 128:256]), reads=[cst], writes=[cb])
    import os
    if "maskdma" not in os.environ.get("KSKIP", ""):
        P.dma("pool", [lambda e: e.dma_start(out=cb.ap[:, 384:384 + 2048].rearrange("p (j b) -> p j b", b=512),
                                             in_=consts_d[:, 258:258 + 2048].rearrange("p (j b) -> p j b", b=512))], writes=[cb], owner=cb)
        P.dma("pool", [lambda e: e.dma_start(out=cb.ap[:, 384 + 2048:384 + 2048 + 128], in_=consts_d[:, 258 + 2048:258 + 2048 + 128])], writes=[cb], owner=cb)

    import os
    KSKIP = os.environ.get("KSKIP", "")
    for (fc, npart, cn, sn) in [] if "rope" in KSKIP else ((CONST_COLS["freq_mla"], 64, "rCm", "rSm"), (CONST_COLS["freq_diff"], 128, "rCd", "rSd")):
        for b0 in range(0, S, 512):
            pi_ = bview(0, 1, I32)
            ld(pi_, pi_.ap[0:npart, 0:512], pos_d[0, b0:b0 + 512].partition_broadcast(npart), Res("posd"))
            ang = tmp()
            P.op("dve", lambda e, ang=ang, pi_=pi_, npart=npart: e.tensor_copy(out=ang.ap[0:npart, :], in_=pi_.ap[0:npart, 0:512]),
                 reads=[pi_], writes=[ang])
            dve_ts(ang, ang.ap[0:npart, :], ang, ang.ap[0:npart, :], cst.ap[0:npart, fc:fc + 1], None, ALU.mult, extra=[cst])
            for (shift, nm) in ((0.0, sn), (math.pi / 2, cn)):
                a2 = tmp()
                dve_ts(a2, a2.ap[0:npart, :], ang, ang.ap[0:npart, :], shift, None, ALU.add)
                kf = tmp()
                dve_ts(kf, kf.ap[0:npart, :], a2, a2.ap[0:npart, :], 1.0 / (2 * math.pi), None, ALU.mult)
                ki = bview(1, 1, I32)
                P.op("dve", lambda e, ki=ki, kf=kf, npart=npart: e.tensor_copy(out=ki.ap[0:npart, 0:512], in_=kf.ap[0:npart, :]), reads=[kf], writes=[ki])
                P.op("dve", lambda e, ki=ki, kf=kf, npart=npart: e.tensor_copy(out=kf.ap[0:npart, :], in_=ki.ap[0:npart, 0:512]), reads=[ki], writes=[kf])
                r = tmp()
                dve_stt(r, r.ap[0:npart, :], kf, kf.ap[0:npart, :], -2 * math.pi, a2, a2.ap[0:npart, :], ALU.mult, ALU.add)
                dve_ts(kf, kf.ap[0:npart, :], r, r.ap[0:npart, :], math.pi, 2 * math.pi, ALU.is_gt, ALU.mult)
                dve_tt(r, r.ap[0:npart, :], r, r.ap[0:npart, :], kf, kf.ap[0:npart, :], ALU.subtract)
                dve_ts(kf, kf.ap[0:npart, :], r, r.ap[0:npart, :], -math.pi, 2 * math.pi, ALU.is_lt, ALU.mult)
                dve_tt(r, r.ap[0:npart, :], r, r.ap[0:npart, :], kf, kf.ap[0:npart, :], ALU.add)
                dve_ts(r, r.ap[0:npart, :], r, r.ap[0:npart, :], 3.141592, -3.141592, ALU.min, ALU.max)
                act(r, r.ap[0:npart, :], r, r.ap[0:npart, :], AF.Sin)
                st(SC[nm][0:npart, b0:b0 + 512], SCR[nm], r, r.ap[0:npart, :])

    for mt in range(0 if "mem" in KSKIP else MEM // 128):
        pcs = []
        for q4 in range(4):
            mtile = tmp()
            ld(mtile, mtile.ap, mem_d[mt * 128:(mt + 1) * 128, q4 * 512:(q4 + 1) * 512], Res("memd"))
            junk = bt()
            P.op("act", lambda e, mtile=mtile, junk=junk, q4=q4: e.activation(out=junk.ap, in_=mtile.ap, func=AF.Square, accum_out=misc.ap[:, 4 + q4:5 + q4]),
                 reads=[mtile], writes=[junk, misc])
            pcs.append(mtile)
        for q4 in range(1, 4):
            dve_tt(misc, misc.ap[:, 4:5], misc, misc.ap[:, 4:5], misc, misc.ap[:, 4 + q4:5 + q4], ALU.add)
        act(misc, misc.ap[:, 5:6], misc, misc.ap[:, 4:5], AF.Sqrt, scale=1.0 / D, bias=misc.ap[:, 0:1])
        P.op("dve", lambda e: e.reciprocal(out=misc.ap[:, 5:6], in_=misc.ap[:, 5:6]), reads=[misc], writes=[misc])
        for q4 in range(4):
            mtile = pcs[q4]
            dve_ts(mtile, mtile.ap, mtile, mtile.ap, misc.ap[:, 5:6], None, ALU.mult, extra=[misc])
            ps = ps_s()
            for j in range(4):
                P.op("pe", lambda e, ps=ps, j=j, mtile=mtile: e.matmul(ps.ap[:, j * 128:(j + 1) * 128], lhsT=mtile.ap[:, j * 128:(j + 1) * 128], rhs=ident_f, start=True, stop=True),
                     reads=[mtile, cst], writes=[ps], pe_acc=True, nmm=1)
            o = bt()
            c0 = q4 * 4
            for j in range(4):
                c = c0 + j
                dve_ts(o, o.ap[:, j * 128:(j + 1) * 128], ps, ps.ap[:, j * 128:(j + 1) * 128], sml("g_mem", 0, c), None, ALU.mult, extra=[small])
            st(SC["memT"][c0 * 128:(c0 + 4) * 128, mt * 128:(mt + 1) * 128].rearrange("(j p) t -> p j t", p=128), SCR["memT"],
               o, o.ap.rearrange("p (j t) -> p j t", j=4))

    def norm_to_hT(gname, l):
        for tb in range(NTB):
            ps = sumsq_ps([(xs[c][tb], xs[c][tb].ap) for c in range(NCH)], 128)
            r = rstd_from_ps(ps, 128, D)
            for c in range(NCH):
                dve_stt(hT[c][tb], hT[c][tb].ap, xs[c][tb], xs[c][tb].ap, sml(gname, l, c), r, r.ap, ALU.mult, ALU.mult, extra=[small])

    def proj_fm(wv, wsel, nk, rhs, tb, npart_out=128):
        ps = ps_s()
        mms = []
        for k in range(nk):
            rv, rap = rhs[k]
            mms.append((ps.ap[0:npart_out, :], wsel(k), rap, k == 0, k == nk - 1))
        mm(ps, mms, [wv] + [rv for rv, _ in rhs])
        return ps

    def hrhs(tb):
        return [(hT[c][tb], hT[c][tb].ap) for c in range(NCH)]

    def wo_accumulate(l, wd, row0, half):
        for cg in range(2):
            wv = load_w(wd[l, row0:row0 + 512, cg * 1024:(cg + 1) * 1024].rearrange("(k p) j -> p k j", p=128), 4, 1024)
            for dtl in range(8):
                dt_ = cg * 8 + dtl
                for tb in range(NTB):
                    ps = nxt("O", psb[4:8])
                    mm(ps, [(ps.ap, wv.ap[:, k, dtl * 128:(dtl + 1) * 128], yT[k][tb].ap, k == 0, k == 3) for k in range(4)],
                       [wv] + [yT[k][tb] for k in range(4)])
                    dve_tt(xs[dt_][tb], xs[dt_][tb].ap, ps, ps.ap, xs[dt_][tb], xs[dt_][tb].ap, ALU.add)

    def store_fm(ps, npart, dst_ap, dst_res, scale=None):
        o = bt()
        copy_any(o, o.ap[0:npart, :], ps, ps.ap[0:npart, :], scale)
        st(dst_ap, dst_res, o, o.ap[0:npart, :])

    def proj_tm_store(l, wsrc_cols, dst_key, half, rhs_sel=None):
        for u in range(2):
            c0 = wsrc_cols + u * 256
            wv = load_w(w_in[l, :, c0:c0 + 256].rearrange("(c p) j -> p c j", p=128), NCH, 256)
            for tt in range(Th // 128):
                tb, o_ = divmod(tt, 4)
                ps = ps_s()
                mm(ps, [(ps.ap[:, 0:256], hT[c][tb].ap[:, o_ * 128:(o_ + 1) * 128], wv.ap[:, c, :], c == 0, c == NCH - 1) for c in range(NCH)],
                   [wv] + [hT[c][tb] for c in range(NCH)])
                o = bt()
                copy_any(o, o.ap[:, 0:256], ps, ps.ap[:, 0:256])
                r0 = half * Th + tt * 128
                st(SC[(dst_key, l)][r0:r0 + 128, u * 256:(u + 1) * 256], SCR[(dst_key, l)], o, o.ap[:, 0:256])

    qbuf = [bview(0, 1), bview(1, 1)]
    kbuf = [bview(2, 1), bview(3, 1)]
    vbuf = [bview(4, 1), bview(5, 1)]
    qrbuf = [bview(6, 1), bview(7, 1)]
    krbuf = bview(8, 1)
    Abuf = bview(9, 1)
    Bbuf = bview(10, 1)
    rot["hb"] = 0

    def attention(nkeys_fn, score_mms, vsel, ytile, tb, extra_reads, post=None):
        t0 = tb_global(tb)
        nkt = (t0 + 512) // 128
        po, pz = ps_o(), ps_z()
        pend = None
        pss = {}

        def score(kt):
            ps = ps_s()
            mm(ps, score_mms(kt, ps), extra_reads)
            pss[kt] = ps
        score(0)
        for kt in range(nkt):
            if kt + 1 < nkt:
                score(kt + 1)
            ps = pss.pop(kt)
            pt = bt()
            j = kt - t0 // 128
            if j >= 0 and nkeys_fn == "premask":
                P.op("dve", lambda e, ps=ps, j=j: e.scalar_tensor_tensor(out=ps.ap, in0=mask_b[j], scalar=60000.0, in1=ps.ap, op0=ALU.mult, op1=ALU.min),
                     reads=[cb, ps], writes=[ps])
            act(pt, pt.ap, ps, ps.ap, AF.Exp)
            if j >= 0:
                dve_tt(pt, pt.ap, pt, pt.ap, cb, mask_b[j], ALU.mult)
            mm(po, [(po.ap, vsel(kt), pt.ap, kt == 0, kt == nkt - 1)], [pt] + extra_reads)
            mm(pz, [(pz.ap, ones_b, pt.ap, kt == 0, kt == nkt - 1)], [pt, cb])
        rec = tmp()
        P.op("dve", lambda e: e.reciprocal(out=rec.ap, in_=pz.ap), reads=[pz], writes=[rec])
        return po, rec

    cur = {"half": 0}

    def tb_global(tb):
        return cur["half"] * Th + tb * 512

    def load_kv(l, kkey, vkey, h, i, nkeys):
        kb, vb = kbuf[i], vbuf[i]
        ld(kb, kb.ap[:, 0:nkeys], SC[(kkey, l)][h * 128:(h + 1) * 128, 0:nkeys], SCR[(kkey, l)])
        vdst = vb.ap[:, 0:(nkeys // 128) * 128].rearrange("p (t d) -> p t d", d=128)
        ld(vb, vdst, SC[(vkey, l)][0:nkeys, h * 128:(h + 1) * 128].rearrange("(t p) d -> p t d", p=128), SCR[(vkey, l)])
        return kb, vb, vdst

    def load_q(l, qkey, h, i, half):
        qb = qbuf[i]
        ld(qb, qb.ap[:, 0:Th], SC[(qkey, l)][h * 128:(h + 1) * 128, half * Th:(half + 1) * Th], SCR[(qkey, l)])
        return qb

    def fox(l, half):
        isq = 128 ** -0.5
        for (c0, key, scale) in ((0, "fq", isq), (512, "fk", None)):
            for u in range(2):
                wv = load_w(w_in[l, :, c0 + u * 256:c0 + (u + 1) * 256].rearrange("(c p) j -> p c j", p=128), NCH, 256)
                for t in range(2):
                    for tb in range(NTB):
                        ps = proj_fm(wv, lambda k, t=t, wv=wv: wv.ap[:, k, t * 128:(t + 1) * 128], NCH, hrhs(tb), tb)
                        r0 = (u * 2 + t) * 128
                        g0 = tb_global(tb)
                        store_fm(ps, 128, SC[(key, l)][r0:r0 + 128, g0:g0 + 512], SCR[(key, l)], scale)
        proj_tm_store(l, 1024, "fv", half)
        wv = load_w(w_in[l, :, 1536:1540].rearrange("(c p) j -> p c j", p=128), NCH, 4)
        carry = V(misc.ap[0:4, 8 + l:9 + l], misc.res)
        if half == 0:
            P.op("dve", lambda e: e.memset(misc.ap[0:4, 8 + l:9 + l], 0.0), writes=[misc])
        for tb in range(NTB):
            ps = proj_fm(wv, lambda k, wv=wv: wv.ap[:, k, 0:4], NCH, hrhs(tb), tb, npart_out=4)
            nbf = V(misc.ap[0:4, 3:4], misc.res)
            dve_ts(misc, misc.ap[0:4, 3:4], small, sml("b_f", l, 0, 4), -1.0, None, ALU.mult)
            e1 = tmp()
            act(e1, e1.ap[0:4, :], ps, ps.ap[0:4, :], AF.Exp, scale=-1.0, bias=misc.ap[0:4, 3:4], extra_reads=[misc])
            act(e1, e1.ap[0:4, :], e1, e1.ap[0:4, :], AF.Ln, scale=1.0, bias=misc.ap[0:4, 2:3], extra_reads=[misc])
            dve_ts(e1, e1.ap[0:4, :], e1, e1.ap[0:4, :], -1.0, None, ALU.mult)
            onesf = tmp()
            P.op("dve", lambda e, onesf=onesf: e.memset(onesf.ap[0:4, :], 1.0), writes=[onesf])
            Ft = tmp()
            P.op("dve", lambda e, Ft=Ft, onesf=onesf, e1=e1: e.tensor_tensor_scan(
                out=Ft.ap[0:4, :], data0=onesf.ap[0:4, :], data1=e1.ap[0:4, :], initial=misc.ap[0:4, 8 + l:9 + l],
                op0=ALU.mult, op1=ALU.add), reads=[onesf, e1, misc], writes=[Ft])
            P.op("dve", lambda e, Ft=Ft: e.tensor_copy(out=misc.ap[0:4, 8 + l:9 + l], in_=Ft.ap[0:4, 511:512]), reads=[Ft], writes=[misc])
            g0 = tb_global(tb)
            resid = Ft
            for part in range(3):
                hp = bt()
                P.op("dve", lambda e, hp=hp, resid=resid: e.tensor_copy(out=hp.ap[0:4, :], in_=resid.ap[0:4, :]), reads=[resid], writes=[hp])
                hn = bt()
                dve_ts(hn, hn.ap[0:4, :], hp, hp.ap[0:4, :], -1.0, None, ALU.mult)
                st(SC[("fFp", l)][:, part, g0:g0 + 512], SCR[("fFp", l)], hp, hp.ap[0:4, :])
                st(SC[("fFn", l)][:, part, g0:g0 + 512], SCR[("fFn", l)], hn, hn.ap[0:4, :])
                if part < 2:
                    nr = tmp()
                    dve_tt(nr, nr.ap[0:4, :], resid, resid.ap[0:4, :], hp, hp.ap[0:4, :], ALU.subtract)
                    resid = nr
        nkeys = (half + 1) * Th
        P.op("dve", lambda e: e.memset(Abuf.ap[0:6, :], 1.0), writes=[Abuf])
        P.op("dve", lambda e: e.memset(Bbuf.ap[0:6, :], 1.0), writes=[Bbuf])
        for h in range(4):
            i = h % 2
            kb, vb, vdst = load_kv(l, "fk", "fv", h, i, nkeys)
            qb = load_q(l, "fq", h, i, half)
            ld(Abuf, Abuf.ap[3:6, 0:nkeys], SC[("fFn", l)][h, :, 0:nkeys], SCR[("fFn", l)])
            ld(Bbuf, Bbuf.ap[0:3, 0:Th], SC[("fFp", l)][h, :, half * Th:(half + 1) * Th], SCR[("fFp", l)])
            for tb in range(NTB):
                def smm(kt, ps, kb=kb, qb=qb, tb=tb):
                    return [(ps.ap, kb.ap[:, kt * 128:(kt + 1) * 128], qb.ap[:, tb * 512:(tb + 1) * 512], True, False),
                            (ps.ap, Abuf.ap[0:6, kt * 128:(kt + 1) * 128], Bbuf.ap[0:6, tb * 512:(tb + 1) * 512], False, True)]
                po, rec = attention("premask", smm, lambda kt, vdst=vdst: vdst[:, kt, :], None, tb, [kb, qb, vb, Abuf, Bbuf])
                dve_tt(yT[h][tb], yT[h][tb].ap, po, po.ap, rec, rec.ap, ALU.mult)
        wo_accumulate(l, w_o, 0, half)

    def rope_store(pa, pb, npart, Cv, Sv, tb, dst_ap, dst_res, scale):
        c_ap = Cv.ap[0:npart, tb * 512:(tb + 1) * 512]
        s_ap = Sv.ap[0:npart, tb * 512:(tb + 1) * 512]
        t1, t2 = tmp(), tmp()
        dve_tt(t1, t1.ap[0:npart, :], pa, pa.ap[0:npart, :], Cv, c_ap, ALU.mult)
        dve_tt(t2, t2.ap[0:npart, :], pb, pb.ap[0:npart, :], Sv, s_ap, ALU.mult)
        o = bt()
        if scale is None:
            dve_tt(o, o.ap[0:npart, :], t1, t1.ap[0:npart, :], t2, t2.ap[0:npart, :], ALU.add)
        else:
            dve_tt(t1, t1.ap[0:npart, :], t1, t1.ap[0:npart, :], t2, t2.ap[0:npart, :], ALU.add)
            act(o, o.ap[0:npart, :], t1, t1.ap[0:npart, :], AF.Identity, scale=scale)
        st(dst_ap, dst_res, o, o.ap[0:npart, :])

    def load_rope_tables(Cn, Sn, npart, Cv, Sv, half):
        ld(Cv, Cv.ap[0:npart, 0:Th], SC[Cn][0:npart, half * Th:(half + 1) * Th], SCR[Cn])
        ld(Sv, Sv.ap[0:npart, 0:Th], SC[Sn][0:npart, half * Th:(half + 1) * Th], SCR[Sn])

    def mla(l, half):
        isq = 192 ** -0.5
        lat = [V(bview(i, 1, F32).ap[:, 0:512], [blk_res[i]]) for i in range(5)]
        cqn = bview(5, 2)
        ckvn = bview(7, 1)
        krot = V(blk_ap[:, 8 * 2048:8 * 2048 + 1024].rearrange("p (c j) -> p c j", c=NCH), [blk_res[8]])
        rt = V(blk_ap[:, 9 * 2048 + 1024:9 * 2048 + 2048].bitcast(F32), [blk_res[9]])
        uqrot_ap = blk_ap[:, 9 * 2048:9 * 2048 + 768].rearrange("p (kh d) -> p kh d", d=64)
        uqrot = V(uqrot_ap, [blk_res[9]])
        Cm, Sm = bview(10, 1, F32), bview(11, 1, F32)
        load_rope_tables("rCm", "rSm", 64, Cm, Sm, half)
        wq = load_w(w_in[l, :, 1540:1796].rearrange("(c p) j -> p c j", p=128), NCH, 256)
        wq2 = load_w(w_in[l, :, 1796:2052].rearrange("(c p) j -> p c j", p=128), NCH, 256)
        wq3 = load_w(w_in[l, :, 2052:2244].rearrange("(c p) j -> p c j", p=128), NCH, 192)

        def wcol(j):
            u, o_ = divmod(j * 128, 256)
            return (wq, wq2, wq3)[u], o_
        krw = wq3.ap[:, :, 128:192]
        P.op("dve", lambda e: e.tensor_scalar(out=krot.ap[:, :, 0:32], in0=krw[:, :, 32:64], scalar1=-1.0, scalar2=None, op0=ALU.mult),
             reads=[wq3], writes=[krot])
        P.op("dve", lambda e: e.tensor_copy(out=krot.ap[:, :, 32:64], in_=krw[:, :, 0:32]), reads=[wq3], writes=[krot])
        for tb in range(NTB):
            g0 = tb_global(tb)
            for j in range(5):
                wv_, o_ = wcol(j)
                ps = proj_fm(wv_, lambda k, wv_=wv_, o_=o_: wv_.ap[:, k, o_:o_ + 128], NCH, hrhs(tb), tb)
                copy_any(lat[j], lat[j].ap, ps, ps.ap)
            for (idx, dim, gnm, dstv) in (((0, 1, 2), 384, "g_cq", cqn), ((3, 4), 256, "g_ckv", ckvn)):
                ps = sumsq_ps([(lat[i], lat[i].ap) for i in idx], 128)
                act(rt, rt.ap, ps, ps.ap, AF.Sqrt, scale=1.0 / dim, bias=misc.ap[:, 0:1], extra_reads=[misc])
                P.op("dve", lambda e: e.reciprocal(out=rt.ap, in_=rt.ap), reads=[rt], writes=[rt])
                for n_, i in enumerate(idx):
                    dve_stt(dstv, dstv.ap[:, n_ * Th + tb * 512:n_ * Th + (tb + 1) * 512], lat[i], lat[i].ap, sml(gnm, l, n_), rt, rt.ap,
                            ALU.mult, ALU.mult, extra=[small])
            pa = proj_fm(wq3, lambda k: wq3.ap[:, k, 128:192], NCH, hrhs(tb), tb, npart_out=64)
            pb = proj_fm(krot, lambda k: krot.ap[:, k, :], NCH, hrhs(tb), tb, npart_out=64)
            rope_store(pa, pb, 64, Cm, Sm, tb, SC[("mkr", l)][:, g0:g0 + 512], SCR[("mkr", l)], None)
        wuq = load_w(w_uq[l].rearrange("(k p) j -> p k j", p=128), 3, 768)
        wukv = load_w(w_ukv[l].rearrange("(k p) j -> p k j", p=128), 2, 1024)
        uq4 = wuq.ap.rearrange("p k (h d) -> p (k h) d", d=192)
        P.op("dve", lambda e: e.tensor_scalar(out=uqrot_ap[:, :, 0:32], in0=uq4[:, :, 160:192], scalar1=-1.0, scalar2=None, op0=ALU.mult),
             reads=[wuq], writes=[uqrot])
        P.op("dve", lambda e: e.tensor_copy(out=uqrot_ap[:, :, 32:64], in_=uq4[:, :, 128:160]), reads=[wuq], writes=[uqrot])
        wv4 = wukv.ap.rearrange("p k (h two d) -> p k h two d", two=2, d=128)
        for tb in range(NTB):
            g0 = tb_global(tb)
            cq_r = [(cqn, cqn.ap[:, k * Th + tb * 512:k * Th + (tb + 1) * 512]) for k in range(3)]
            ckv_r = [(ckvn, ckvn.ap[:, k * Th + tb * 512:k * Th + (tb + 1) * 512]) for k in range(2)]
            for h in range(4):
                ps = proj_fm(wuq, lambda k, h=h: wuq.ap[:, k, h * 192:h * 192 + 128], 3, cq_r, tb)
                store_fm(ps, 128, SC[("mqn", l)][h * 128:(h + 1) * 128, g0:g0 + 512], SCR[("mqn", l)], isq)
                pa = proj_fm(wuq, lambda k, h=h: wuq.ap[:, k, h * 192 + 128:h * 192 + 192], 3, cq_r, tb, npart_out=64)
                pb = proj_fm(uqrot, lambda k, h=h: uqrot_ap[:, k * 4 + h, :], 3, cq_r, tb, npart_out=64)
                rope_store(pa, pb, 64, Cm, Sm, tb, SC[("mqr", l)][h * 64:(h + 1) * 64, g0:g0 + 512], SCR[("mqr", l)], isq)
                ps = proj_fm(wukv, lambda k, h=h: wukv.ap[:, k, h * 256:h * 256 + 128], 2, ckv_r, tb)
                store_fm(ps, 128, SC[("mkn", l)][h * 128:(h + 1) * 128, g0:g0 + 512], SCR[("mkn", l)])
            for o_ in range(4):
                ps = ps_s()
                mm(ps, [(ps.ap.rearrange("p (h d) -> p h d", d=128),
                         ckvn.ap[:, k * Th + tb * 512 + o_ * 128:k * Th + tb * 512 + (o_ + 1) * 128], wv4[:, k, :, 1, :], k == 0, k == 1)
                        for k in range(2)], [wukv, ckvn])
                o = bt()
                copy_any(o, o.ap, ps, ps.ap)
                r0 = g0 + o_ * 128
                st(SC[("mv", l)][r0:r0 + 128, :], SCR[("mv", l)], o, o.ap)
        nkeys = (half + 1) * Th
        ld(krbuf, krbuf.ap[0:64, 0:nkeys], SC[("mkr", l)][:, 0:nkeys], SCR[("mkr", l)])
        for h in range(4):
            i = h % 2
            kb, vb, vdst = load_kv(l, "mkn", "mv", h, i, nkeys)
            qb = load_q(l, "mqn", h, i, half)
            qr = qrbuf[i]
            ld(qr, qr.ap[0:64, 0:Th], SC[("mqr", l)][h * 64:(h + 1) * 64, half * Th:(half + 1) * Th], SCR[("mqr", l)])
            for tb in range(NTB):
                def smm(kt, ps, kb=kb, qb=qb, qr=qr, tb=tb):
                    return [(ps.ap, kb.ap[:, kt * 128:(kt + 1) * 128], qb.ap[:, tb * 512:(tb + 1) * 512], True, False),
                            (ps.ap, krbuf.ap[0:64, kt * 128:(kt + 1) * 128], qr.ap[0:64, tb * 512:(tb + 1) * 512], False, True)]
                po, rec = attention(None, smm, lambda kt, vdst=vdst: vdst[:, kt, :], None, tb, [kb, qb, vb, qr, krbuf])
                dve_tt(yT[h][tb], yT[h][tb].ap, po, po.ap, rec, rec.ap, ALU.mult)
        wo_accumulate(l, w_o, 512, half)

    def gelu_inplace(t, ap):
        u = tmp()
        dve_tt(u, u.ap, t, ap, t, ap, ALU.mult)
        dve_ts(u, u.ap, u, u.ap, 0.044715, 1.0, ALU.mult, ALU.add)
        dve_tt(u, u.ap, u, u.ap, t, ap, ALU.mult)
        act(u, u.ap, u, u.ap, AF.Sigmoid, scale=2.0 * math.sqrt(2.0 / math.pi))
        dve_tt(t, ap, t, ap, u, u.ap, ALU.mult)

    def sgu(l, half):
        uT = bview(0, 1)
        bsb = bview(1, 1, F32)
        lng = bview(2, 1, F32)
        lnb = bview(3, 1, F32)
        wsT = bview(4, 1)
        wsn = bview(5, 1, F32)
        ld(bsb, bsb.ap[:, 0:512], b_s[l, :].partition_broadcast(128), Res("bsd"))
        ld(lng, lng.ap[:, 0:512], sgu_ln_g[l, :].partition_broadcast(128), Res("lngd"))
        ld(lnb, lnb.ap[:, 0:512], sgu_ln_b[l, :].partition_broadcast(128), Res("lnbd"))
        ld(wsn, wsn.ap[:, 0:512].rearrange("p (g s) -> p g s", g=4), w_s[l].rearrange("g t s -> t g s"), Res("wsd"))
        wsm = bt()
        for g in range(4):
            dve_tt(wsm, wsm.ap[:, g * 128:(g + 1) * 128], wsn, wsn.ap[:, g * 128:(g + 1) * 128], cst, cst.ap[:, 128:256], ALU.mult)
        pst = ps_s()
        for g in range(4):
            P.op("pe", lambda e, g=g: e.matmul(pst.ap[:, g * 128:(g + 1) * 128], lhsT=wsm.ap[:, g * 128:(g + 1) * 128], rhs=ident_b, start=True, stop=True),
                 reads=[wsm, cb], writes=[pst], pe_acc=True, nmm=1)
        P.op("dve", lambda e: e.tensor_copy(out=wsT.ap[:, 0:512], in_=pst.ap[:, 0:512]), reads=[pst], writes=[wsT])
        wu = [load_w(w_in[l, :, 2244 + u * 256:2244 + (u + 1) * 256].rearrange("(c p) j -> p c j", p=128), NCH, 256) for u in range(2)]
        wvv = [load_w(w_in[l, :, 2756 + u * 256:2756 + (u + 1) * 256].rearrange("(c p) j -> p c j", p=128), NCH, 256) for u in range(2)]
        for tb in range(NTB):
            for g in range(4):
                wv_ = wu[g // 2]
                ps = proj_fm(wv_, lambda k, wv_=wv_, g=g: wv_.ap[:, k, (g % 2) * 128:(g % 2) * 128 + 128], NCH, hrhs(tb), tb)
                t = tmp()
                copy_any(t, t.ap, ps, ps.ap)
                gelu_inplace(t, t.ap)
                P.op("dve", lambda e, t=t, g=g: e.tensor_copy(out=uT.ap[:, g * 512:(g + 1) * 512], in_=t.ap), reads=[t], writes=[uT])
            for o_ in range(4):
                t = tmp()
                for u in range(2):
                    ps = ps_s()
                    mm(ps, [(ps.ap[:, 0:256], hT[c][tb].ap[:, o_ * 128:(o_ + 1) * 128], wvv[u].ap[:, c, :], c == 0, c == NCH - 1) for c in range(NCH)],
                       [wvv[u]] + [hT[c][tb] for c in range(NCH)])
                    copy_any(t, t.ap[:, u * 256:(u + 1) * 256], ps, ps.ap[:, 0:256])
                gelu_inplace(t, t.ap)
                junk = tmp()
                P.op("act", lambda e, t=t, junk=junk: e.activation(out=junk.ap, in_=t.ap, func=AF.Identity, accum_out=misc.ap[:, 6:7]),
                     reads=[t], writes=[junk, misc])
                P.op("act", lambda e, t=t, junk=junk: e.activation(out=junk.ap, in_=t.ap, func=AF.Square, accum_out=misc.ap[:, 7:8]),
                     reads=[t], writes=[junk, misc])
                dve_ts(misc, misc.ap[:, 6:7], misc, misc.ap[:, 6:7], 1.0 / 512, None, ALU.mult)
                dve_tt(misc, misc.ap[:, 12:13], misc, misc.ap[:, 6:7], misc, misc.ap[:, 6:7], ALU.mult)
                dve_stt(misc, misc.ap[:, 7:8], misc, misc.ap[:, 7:8], 1.0 / 512, misc, misc.ap[:, 12:13], ALU.mult, ALU.subtract)
                act(misc, misc.ap[:, 7:8], misc, misc.ap[:, 7:8], AF.Sqrt, scale=1.0, bias=misc.ap[:, 0:1])
                P.op("dve", lambda e: e.reciprocal(out=misc.ap[:, 7:8], in_=misc.ap[:, 7:8]), reads=[misc], writes=[misc])
                dve_ts(t, t.ap, t, t.ap, misc.ap[:, 6:7], misc.ap[:, 7:8], ALU.subtract, ALU.mult, extra=[misc])
                dve_tt(t, t.ap, t, t.ap, lng, lng.ap[:, 0:512], ALU.mult)
                vn = bt()
                dve_tt(vn, vn.ap, t, t.ap, lnb, lnb.ap[:, 0:512], ALU.add)
                ps = ps_s()
                for g in range(4):
                    mm(ps, [(ps.ap[:, g * 128:(g + 1) * 128], vn.ap[:, g * 128:(g + 1) * 128], wsT.ap[:, g * 128:(g + 1) * 128], True, True)], [vn, wsT])
                t2 = tmp()
                dve_tt(t2, t2.ap, ps, ps.ap, bsb, bsb.ap[:, 0:512], ALU.add)
                for g in range(4):
                    dve_tt(yT[g][tb], yT[g][tb].ap[:, o_ * 128:(o_ + 1) * 128], t2, t2.ap[:, g * 128:(g + 1) * 128],
                           uT, uT.ap[:, g * 512 + o_ * 128:g * 512 + (o_ + 1) * 128], ALU.mult)
        wo_accumulate(l, w_o, 1024, half)

    def diff(l, half):
        lam_init = 0.8 - 0.6 * math.exp(-0.3 * l)
        isq = 64 ** -0.5
        la = tmp()
        for i, (a, b) in enumerate((("lam_q1", "lam_k1"), ("lam_q2", "lam_k2"))):
            ld(la, la.ap[:, 0:64], lam_d[a][l, :].partition_broadcast(128), Res("lamd"))
            ld(la, la.ap[:, 64:128], lam_d[b][l, :].partition_broadcast(128), Res("lamd"))
            dve_tt(la, la.ap[:, 128:192], la, la.ap[:, 0:64], la, la.ap[:, 64:128], ALU.mult)
            P.op("act", lambda e, i=i: e.activation(out=la.ap[:, 192:256], in_=la.ap[:, 128:192], func=AF.Identity, accum_out=misc.ap[:, 13 + i:14 + i]),
                 reads=[la], writes=[la, misc])
            act(misc, misc.ap[:, 13 + i:14 + i], misc, misc.ap[:, 13 + i:14 + i], AF.Exp)
        dve_tt(misc, misc.ap[:, 15:16], misc, misc.ap[:, 14:15], misc, misc.ap[:, 13:14], ALU.subtract)
        dve_ts(misc, misc.ap[:, 15:16], misc, misc.ap[:, 15:16], -lam_init, None, ALU.add)
        neglam = misc.ap[:, 15:16]
        Cd, Sd = bview(0, 1, F32), bview(1, 1, F32)
        load_rope_tables("rCd", "rSd", 128, Cd, Sd, half)
        pend = []

        def finish(item):
            pa, q16, key, r0, tb, scale = item
            g0 = tb_global(tb)
            pb = ps_s()
            mm(pb, [(pb.ap, Rd_b, q16.ap, True, True)], [q16, cb])
            rope_store(pa, pb, 128, Cd, Sd, tb, SC[(key, l)][r0:r0 + 128, g0:g0 + 512], SCR[(key, l)], scale)
        for (c0, key, scale) in ((3268, "dq", isq), (3780, "dk", None)):
            for u in range(2):
                wv = load_w(w_in[l, :, c0 + u * 256:c0 + (u + 1) * 256].rearrange("(c p) j -> p c j", p=128), NCH, 256)
                for t in range(2):
                    for tb in range(NTB):
                        pa = proj_fm(wv, lambda k, t=t, wv=wv: wv.ap[:, k, t * 128:(t + 1) * 128], NCH, hrhs(tb), tb)
                        q16 = bt()
                        act(q16, q16.ap, pa, pa.ap, AF.Identity)
                        pend.append((pa, q16, key, (u * 2 + t) * 128, tb, scale))
                        if len(pend) > 1:
                            finish(pend.pop(0))
        while pend:
            finish(pend.pop(0))
        proj_tm_store(l, 4292, "dv", half)
        nkeys = (half + 1) * Th
        for h in range(4):
            i = h % 2
            kb, vb, vdst = load_kv(l, "dk", "dv", h, i, nkeys)
            qb = load_q(l, "dq", h, i, half)
            for tb in range(NTB):
                o1 = None
                for m in range(2):
                    def smm(kt, ps, kb=kb, qb=qb, tb=tb, m=m):
                        return [(ps.ap, kb.ap[m * 64:(m + 1) * 64, kt * 128:(kt + 1) * 128], qb.ap[m * 64:(m + 1) * 64, tb * 512:(tb + 1) * 512], True, True)]
                    po, rec = attention(None, smm, lambda kt, vdst=vdst: vdst[:, kt, :], None, tb, [kb, qb, vb])
                    if m == 0:
                        o1 = tmp()
                        dve_tt(o1, o1.ap, po, po.ap, rec, rec.ap, ALU.mult)
                    else:
                        dve_tt(rec, rec.ap, po, po.ap, rec, rec.ap, ALU.mult)
                        dve_stt(o1, o1.ap, rec, rec.ap, neglam, o1, o1.ap, ALU.mult, ALU.add, extra=[misc])
                ps = sumsq_ps([(o1, o1.ap)], 128)
                r = rstd_from_ps(ps, 128, 128)
                dve_stt(o1, o1.ap, o1, o1.ap, sml("g_diff", l, 0), r, r.ap, ALU.mult, ALU.mult, extra=[small])
                dve_ts(yT[h][tb], yT[h][tb].ap, o1, o1.ap, 1.0 - lam_init, None, ALU.mult)
        wo_accumulate(l, w_o, 1536, half)

    def cross(l, half):
        isq = 128 ** -0.5
        norm_to_hT("g_cross", l)
        memT = bview(0, 2)
        ld(memT, memT.ap.rearrange("p (c t) -> p c t", c=NCH), SC["memT"].rearrange("(c p) t -> p c t", p=128), SCR["memT"])
        memv = memT.ap.rearrange("p (c t) -> p c t", c=NCH)
        qx = bview(2, 2)
        kx = bview(4, 1)
        vx = bview(5, 1)
        for u in range(2):
            wv = load_w(w_ck[l, :, u * 256:(u + 1) * 256].rearrange("(c p) j -> p c j", p=128), NCH, 256)
            for t in range(2):
                h = u * 2 + t
                ps = ps_s()
                mm(ps, [(ps.ap[:, 0:256], wv.ap[:, c, t * 128:(t + 1) * 128], memv[:, c, :], c == 0, c == NCH - 1) for c in range(NCH)], [wv, memT])
                copy_any(kx, kx.ap[:, h * 256:(h + 1) * 256], ps, ps.ap[:, 0:256])
        for u in range(2):
            wv = load_w(w_cv[l, :, u * 256:(u + 1) * 256].rearrange("(c p) j -> p c j", p=128), NCH, 256)
            for mt in range(2):
                ps = ps_s()
                mm(ps, [(ps.ap[:, 0:256], memv[:, c, mt * 128:(mt + 1) * 128], wv.ap[:, c, :], c == 0, c == NCH - 1) for c in range(NCH)], [wv, memT])
                copy_any(vx, vx.ap[:, mt * 512 + u * 256:mt * 512 + (u + 1) * 256], ps, ps.ap[:, 0:256])
        for u in range(2):
            wv = load_w(w_cq[l, :, u * 256:(u + 1) * 256].rearrange("(c p) j -> p c j", p=128), NCH, 256)
            for t in range(2):
                h = u * 2 + t
                for tb in range(NTB):
                    ps = proj_fm(wv, lambda k, t=t, wv=wv: wv.ap[:, k, t * 128:(t + 1) * 128], NCH, hrhs(tb), tb)
                    copy_any(qx, qx.ap[:, h * Th + tb * 512:h * Th + (tb + 1) * 512], ps, ps.ap, isq)
        for h in range(4):
            for tb in range(NTB):
                po, pz = ps_o(), ps_z()
                for mt in range(2):
                    ps = ps_s()
                    mm(ps, [(ps.ap, kx.ap[:, h * 256 + mt * 128:h * 256 + (mt + 1) * 128], qx.ap[:, h * Th + tb * 512:h * Th + (tb + 1) * 512], True, True)], [kx, qx])
                    pt = bt()
                    act(pt, pt.ap, ps, ps.ap, AF.Exp)
                    mm(po, [(po.ap, vx.ap[:, mt * 512 + h * 128:mt * 512 + (h + 1) * 128], pt.ap, mt == 0, mt == 1)], [pt, vx])
                    mm(pz, [(pz.ap, ones_b, pt.ap, mt == 0, mt == 1)], [pt, cb])
                rec = tmp()
                P.op("dve", lambda e, rec=rec, pz=pz: e.reciprocal(out=rec.ap, in_=pz.ap), reads=[pz], writes=[rec])
                dve_tt(yT[h][tb], yT[h][tb].ap, po, po.ap, rec, rec.ap, ALU.mult)
        wo_accumulate(l, w_co, 0, half)

    def ffn(l, half):
        norm_to_hT("g_ffn", l)
        abuf = [bview(0, 2), bview(2, 2)]
        nslots = 2 * NTB
        dper = NCH // nslots

        def down_groups(wd, ab, dts):
            for dt_ in dts:
                for tb in range(NTB):
                    ps = nxt("O", psb[4:8])
                    mm(ps, [(ps.ap, wd.ap[:, t, dt_ * 128:(dt_ + 1) * 128], ab.ap[:, t * Th + tb * 512:t * Th + (tb + 1) * 512], t == 0, t == 1) for t in range(2)],
                       [wd, ab])
                    dve_tt(xs[dt_][tb], xs[dt_][tb].ap, ps, ps.ap, xs[dt_][tb], xs[dt_][tb].ap, ALU.add)
        prev = None
        for j in range(FF // 256):
            wg = load_w(w_gate[l, :, j * 256:(j + 1) * 256].rearrange("(c p) j -> p c j", p=128), NCH, 256)
            wu = load_w(w_up[l, :, j * 256:(j + 1) * 256].rearrange("(c p) j -> p c j", p=128), NCH, 256)
            ab = abuf[j % 2]
            k = 0
            for t in range(2):
                for tb in range(NTB):
                    pg = proj_fm(wg, lambda k_, t=t, wg=wg: wg.ap[:, k_, t * 128:(t + 1) * 128], NCH, hrhs(tb), tb)
                    pu = proj_fm(wu, lambda k_, t=t, wu=wu: wu.ap[:, k_, t * 128:(t + 1) * 128], NCH, hrhs(tb), tb)
                    sg = tmp()
                    act(sg, sg.ap, pg, pg.ap, AF.Silu)
                    dve_tt(ab, ab.ap[:, t * Th + tb * 512:t * Th + (tb + 1) * 512], pu, pu.ap, sg, sg.ap, ALU.mult)
                    if prev is not None:
                        down_groups(prev[0], prev[1], range(k * dper, (k + 1) * dper))
                    k += 1
            wd = load_w(w_down[l, j * 256:(j + 1) * 256, :].rearrange("(k p) d -> p k d", p=128), 2, 2048)
            prev = (wd, ab)
        down_groups(prev[0], prev[1], range(NCH))

    for half in range(HALVES):
        cur["half"] = half
        for tt in range(0 if "xload" in KSKIP else Th // 128):
            tb, o_ = divmod(tt, 4)
            r0 = half * Th + tt * 128
            for c0 in range(0, NCH, 4):
                xin = tmp()
                ld(xin, xin.ap, x_d[r0:r0 + 128, c0 * 128:(c0 + 4) * 128], Res("xd"))
                if "xmm" in KSKIP:
                    continue
                ps = ps_s()
                for j in range(4):
                    P.op("pe", lambda e, ps=ps, j=j, xin=xin: e.matmul(ps.ap[:, j * 128:(j + 1) * 128], lhsT=xin.ap[:, j * 128:(j + 1) * 128], rhs=ident_f, start=True, stop=True),
                         reads=[xin, cst], writes=[ps], pe_acc=True, nmm=1)
                dst_ap = xs_ap[:, c0:c0 + 4, tb * 512 + o_ * 128:tb * 512 + (o_ + 1) * 128]
                if "xcp" in KSKIP:
                    continue
                dv_ = V(dst_ap, [xs[c0 + j][tb].res[0] for j in range(4)])
                src_ = ps.ap.rearrange("p (j t) -> p j t", j=4)
                P.op("dve", lambda e, dst_ap=dst_ap, src_=src_: e.tensor_copy(out=dst_ap, in_=src_), reads=[ps], writes=[dv_])
        import os
        stg = os.environ.get("KSTAGE", "fmsdcx")
        for l in range(L):
            P.phases.append((f"h{half}l{l}:norm", P.nmm))
            norm_to_hT("g_mix", l)
            P.phases.append((f"h{half}l{l}:fox", P.nmm))
            if "f" in stg:
                fox(l, half)
            P.phases.append((f"h{half}l{l}:mla", P.nmm))
            if "m" in stg:
                mla(l, half)
            P.phases.append((f"h{half}l{l}:sgu", P.nmm))
            if "s" in stg:
                sgu(l, half)
            P.phases.append((f"h{half}l{l}:diff", P.nmm))
            if "d" in stg:
                diff(l, half)
            P.phases.append((f"h{half}l{l}:cross", P.nmm))
            if "c" in stg:
                cross(l, half)
            P.phases.append((f"h{half}l{l}:ffn", P.nmm))
            if "x" in stg:
                ffn(l, half)
        P.phases.append((f"h{half}:final", P.nmm))
        for tb in range(0 if "final" in KSKIP else NTB):
            ps = sumsq_ps([(xs[c][tb], xs[c][tb].ap) for c in range(NCH)], 128)
            r0_ = rstd_from_ps(ps, 128, D)
            r = V(bview(11, 1, F32).ap[:, 0:512], [blk_res[11]])
            P.op("dve", lambda e, r=r, r0_=r0_: e.tensor_copy(out=r.ap, in_=r0_.ap), reads=[r0_], writes=[r])
            for o_ in range(4):
                osb = bview(2 * (o_ % 2), 2, F32)
                for c0 in range(0, NCH, 4):
                    pst = ps_s()
                    for j in range(4):
                        c = c0 + j
                        xn = tmp()
                        dve_stt(xn, xn.ap[:, 0:128], xs[c][tb], xs[c][tb].ap[:, o_ * 128:(o_ + 1) * 128], sml("g_final", 0, c),
                                r, r.ap[:, o_ * 128:(o_ + 1) * 128], ALU.mult, ALU.mult, extra=[small])
                        P.op("pe", lambda e, pst=pst, j=j, xn=xn: e.matmul(pst.ap[:, j * 128:(j + 1) * 128], lhsT=xn.ap[:, 0:128], rhs=ident_f, start=True, stop=True),
                             reads=[xn, cst], writes=[pst], pe_acc=True, nmm=1)
                    copy_any(osb, osb.ap[:, c0 * 128:(c0 + 4) * 128], pst, pst.ap)
                r0 = half * Th + tb * 512 + o_ * 128
                st(out_d[r0:r0 + 128, :], Res("outd"), osb, osb.ap)
    P.wait_all("sp", list(ALL_RES))
    for e in COMPUTE:
        if P.ecnt[e]:
            P._need("sp", (P.esem[e], P.ecnt[e], e))
    P.emit()
    return nc, P


def pack_small(inp, b, L):
    soff, NS = small_layout(L)
    sm = np.zeros((128, NS), np.float32)

    def fm(v):
        return np.ascontiguousarray(v.reshape(-1, 128).T)
    for l in range(L):
        for nm in ("g_mix", "g_cross", "g_ffn", "g_cq", "g_ckv", "g_diff"):
            a = fm(np.asarray(inp[nm][l], np.float32))
            sm[:, soff[(nm, l)]:soff[(nm, l)] + a.shape[1]] = a
        sm[0:4, soff[("b_f", l)]] = np.asarray(inp["b_f"][l], np.float32)
    for nm in ("g_mem", "g_final"):
        a = fm(np.asarray(inp[nm], np.float32))
        sm[:, soff[(nm, 0)]:soff[(nm, 0)] + 16] = a
    return sm


def run(inp, S, L, FF, ncores):
    nc, P = build_program(S, L, FF)
    consts = make_consts()
    small = pack_small(inp, 0, L)
    shared = {}
    for k in ("w_in", "w_uq", "w_ukv", "sgu_ln_g", "sgu_ln_b", "w_s", "lam_q1", "lam_k1", "lam_q2", "lam_k2",
              "w_o", "w_cq", "w_ck", "w_cv", "w_co", "w_gate", "w_up", "w_down"):
        shared[k] = np.ascontiguousarray(np.asarray(inp[k], np.float32))
    shared["b_s"] = np.ascontiguousarray(np.asarray(inp["b_s"], np.float32).reshape(L, 512))
    shared["small"] = small
    shared["consts"] = consts
    in_maps = []
    for b in range(ncores):
        m = dict(shared)
        m["x"] = np.ascontiguousarray(np.asarray(inp["x"][b], np.float32))
        m["mem"] = np.ascontiguousarray(np.asarray(inp["mem"][b], np.float32))
        m["positions"] = np.ascontiguousarray(np.asarray(inp["positions"][b], np.int32).reshape(1, S))
        in_maps.append(m)
    res = run_bass_kernel_spmd(nc, in_maps, core_ids=list(range(ncores)))
    return np.stack([r["out"] for r in res.results], axis=0)


def kernel(**inputs):
    return run(inputs, 2048, 4, 5632, 8).astype(np.float32)
```

```python
import math
import numpy as np
import concourse.bass as bass
import concourse.mybir as mybir
from concourse.bass_utils import run_bass_kernel_spmd

F32 = mybir.dt.float32
BF16 = mybir.dt.bfloat16
I32 = mybir.dt.int32
AF = mybir.ActivationFunctionType
ALU = mybir.AluOpType

D = 2048
NCH = 16
MEM = 256
EPS = 1e-6
N_IN = 4804
COMPUTE = ("pe", "act", "dve", "pool")


ALL_RES = []


class Res:
    __slots__ = ("name", "wtok", "rtoks", "ld_sem", "ld_n", "st_sem", "st_n", "psum")

    def __init__(self, name, psum=False):
        ALL_RES.append(self)
        self.name = name
        self.psum = psum
        self.wtok = None
        self.rtoks = {}
        self.ld_sem = None
        self.ld_n = 0
        self.st_sem = None
        self.st_n = 0


class V:
    __slots__ = ("ap", "res")

    def __init__(self, ap, res):
        self.ap = ap
        self.res = res if isinstance(res, list) else [res]

    def __getitem__(self, idx):
        return self.ap[idx]


def _rl(vs):
    out = []
    for v in vs:
        if isinstance(v, V):
            out.extend(v.res)
        else:
            out.append(v)
    return out


class Prog:
    def __init__(self, nc):
        self.nc = nc
        self.streams = {e: [] for e in ("pe", "act", "dve", "pool", "sp")}
        self.esem = {e: nc.alloc_semaphore(f"es_{e}") for e in COMPUTE}
        self.ecnt = {e: 0 for e in COMPUTE}
        self.waited = {e: {} for e in self.streams}
        self.nsem = 0
        self.ninstr = 0
        self.nmm = 0
        self.phases = []

    def new_sem(self, name):
        self.nsem += 1
        return self.nc.alloc_semaphore(f"s{self.nsem}_{name}")

    def _need(self, e, tok):
        if tok is None:
            return
        sem, val, _ = tok
        w = self.waited[e]
        if w.get(sem.num, 0) >= val:
            return
        w[sem.num] = val
        self.streams[e].append(("wait", sem, val))

    def _deps(self, e, reads, writes, pe_acc):
        for r in reads:
            self._need(e, r.wtok)
            if r.psum:
                for tok in r.rtoks.values():
                    if tok[2] != e:
                        self._need(e, tok)
        for r in writes:
            if not (pe_acc and r.wtok is not None and r.wtok[2] == "pe"):
                self._need(e, r.wtok)
            for tok in r.rtoks.values():
                self._need(e, tok)

    @staticmethod
    def _mark(tok, reads, writes):
        for r in reads:
            r.rtoks[tok[0].num] = tok
        for r in writes:
            r.wtok = tok
            r.rtoks = {}

    def op(self, e, fn, reads=(), writes=(), pe_acc=False, nmm=0):
        self.nmm += nmm
        reads, writes = _rl(reads), _rl(writes)
        self._deps(e, reads, writes, pe_acc)
        self.ecnt[e] += 1
        tok = (self.esem[e], self.ecnt[e], e)
        self.streams[e].append(("op", fn, self.esem[e]))
        self._mark(tok, reads, writes)
        self.ninstr += 1

    def dma(self, q, fns, reads=(), writes=(), owner=None, kind="ld"):
        reads, writes = _rl(reads), _rl(writes)
        self._deps(q, reads, writes, False)
        o = owner.res[0]
        if kind == "ld":
            if o.ld_sem is None:
                o.ld_sem = self.new_sem("ld")
            o.ld_n += len(fns)
            sem, n = o.ld_sem, o.ld_n
        else:
            if o.st_sem is None:
                o.st_sem = self.new_sem("st")
            o.st_n += len(fns)
            sem, n = o.st_sem, o.st_n
        tok = (sem, 16 * n, "dma")
        for fn in fns:
            self.streams[q].append(("dma", fn, sem))
        self._mark(tok, reads, writes)
        self.ninstr += len(fns)

    @staticmethod
    def handoff(srcs, dsts):
        toks = []
        for s_ in _rl(srcs):
            if s_.wtok is not None:
                toks.append(s_.wtok)
            toks.extend(s_.rtoks.values())
        for d in _rl(dsts):
            for t in toks:
                cur_ = d.rtoks.get(t[0].num)
                if cur_ is None or cur_[1] < t[1]:
                    d.rtoks[t[0].num] = t

    def wait_all(self, e, vs):
        for r in _rl(vs):
            self._need(e, r.wtok)
            for tok in r.rtoks.values():
                self._need(e, tok)

    def emit(self):
        nc = self.nc
        streams = self.streams

        def run(name):
            def body(engine):
                for item in streams[name]:
                    if item[0] == "wait":
                        engine.wait_ge(item[1], item[2])
                    elif item[0] == "op":
                        item[1](engine).then_inc(item[2], 1)
                    else:
                        item[1](engine).then_inc(item[2], 16)
            return body

        with nc.Block() as block:
            block.tensor(run("pe"))
            block.scalar(run("act"))
            block.vector(run("dve"))
            block.gpsimd(run("pool"))
            block.sync(run("sp"))


def small_layout(L):
    off = {}
    c = 0
    for l in range(L):
        for nm, w in (("g_mix", 16), ("g_cross", 16), ("g_ffn", 16), ("g_cq", 3), ("g_ckv", 2),
                      ("g_diff", 1), ("b_f", 1)):
            off[(nm, l)] = c
            c += w
    for nm in ("g_mem", "g_final"):
        off[(nm, 0)] = c
        c += 16
    return off, c


CONST_COLS = {"ident": 0, "tril": 128, "freq_mla": 256, "freq_diff": 257, "mask": 258}
NCST_SB = 258
NCONST = 258 + 4 * 512 + 128


def make_consts():
    c = np.zeros((128, NCONST), np.float32)
    c[:, 0:128] = np.eye(128, dtype=np.float32)
    p = np.arange(128)[:, None]
    f = np.arange(128)[None, :]
    c[:, 128:256] = (f <= p).astype(np.float32)
    ff = np.arange(512)[None, :]
    for j in range(4):
        c[:, 258 + j * 512:258 + (j + 1) * 512] = (ff - p - 128 * j >= 0).astype(np.float32)
    theta = 500000.0
    fm = theta ** (-np.arange(0, 64, 2, dtype=np.float32) / 64.0)
    c[0:64, CONST_COLS["freq_mla"]] = np.concatenate([fm, fm])
    fd = theta ** (-np.arange(0, 16, 2, dtype=np.float32) / 16.0)
    one = np.zeros(64, np.float32)
    one[0:8] = fd
    one[8:16] = fd
    c[:, CONST_COLS["freq_diff"]] = np.concatenate([one, one])
    R = np.zeros((128, 128), np.float32)
    for base in (0, 64):
        for i in range(8):
            R[base + 8 + i, base + i] = -1.0
            R[base + i, base + 8 + i] = 1.0
    c[:, 258 + 2048:258 + 2048 + 128] = R
    return c


def build_program(S, L, FF, dbg=None):
    del ALL_RES[:]
    HALVES = 2
    Th = S // HALVES
    NTB = Th // 512
    NKT = S // 128
    nc = bass.Bass("TRN2", target_bir_lowering=False)
    P = Prog(nc)

    def din(name, shape, dt=F32):
        return nc.dram_tensor(name, list(shape), dt, kind="ExternalInput").ap()

    x_d = din("x", [S, D])
    mem_d = din("mem", [MEM, D])
    pos_d = din("positions", [1, S], I32)
    w_in = din("w_in", [L, D, N_IN])
    w_uq = din("w_uq", [L, 384, 768])
    w_ukv = din("w_ukv", [L, 256, 1024])
    sgu_ln_g = din("sgu_ln_g", [L, 512])
    sgu_ln_b = din("sgu_ln_b", [L, 512])
    w_s = din("w_s", [L, 4, 128, 128])
    b_s = din("b_s", [L, 512])
    lam_d = {k: din(k, [L, 64]) for k in ("lam_q1", "lam_k1", "lam_q2", "lam_k2")}
    w_o = din("w_o", [L, D, D])
    w_cq = din("w_cq", [L, D, 512])
    w_ck = din("w_ck", [L, D, 512])
    w_cv = din("w_cv", [L, D, 512])
    w_co = din("w_co", [L, 512, D])
    w_gate = din("w_gate", [L, D, FF])
    w_up = din("w_up", [L, D, FF])
    w_down = din("w_down", [L, FF, D])
    soff, NS = small_layout(L)
    small_d = din("small", [128, NS])
    consts_d = din("consts", [128, NCONST])
    out_d = nc.dram_tensor("out", [S, D], F32, kind="ExternalOutput").ap()

    def scratch(name, shape, dt=BF16):
        return nc.dram_tensor("scr_" + name, list(shape), dt, kind="ExternalOutput").ap()

    SC = {}
    SCR = {}
    for l in range(L):
        for nm, shp, dt in (("fq", [512, S], BF16), ("fk", [512, S], BF16), ("fv", [S, 512], BF16),
                            ("fFp", [4, 3, S], BF16), ("fFn", [4, 3, S], BF16),
                            ("mqn", [512, S], BF16), ("mqr", [256, S], BF16), ("mkn", [512, S], BF16),
                            ("mkr", [64, S], BF16), ("mv", [S, 512], BF16),
                            ("dq", [512, S], BF16), ("dk", [512, S], BF16), ("dv", [S, 512], BF16)):
            SC[(nm, l)] = scratch(f"{nm}{l}", shp, dt)
            SCR[(nm, l)] = Res(f"{nm}{l}")
    for nm, shp in (("rCm", [64, S]), ("rSm", [64, S]), ("rCd", [128, S]), ("rSd", [128, S])):
        SC[nm] = scratch(nm, shp, F32)
        SCR[nm] = Res(nm)
    SC["memT"] = scratch("memT", [D, MEM], BF16)
    SCR["memT"] = Res("memT")

    def sb(name, shape, dt):
        return nc.alloc_sbuf_tensor("sb_" + name, list(shape), dt).ap()

    xs_ap = sb("xs", [128, NCH, Th], F32)
    xs = [[V(xs_ap[:, c, tb * 512:(tb + 1) * 512], Res(f"xs{c}_{tb}")) for tb in range(NTB)] for c in range(NCH)]
    hT_ap = sb("hT", [128, NCH, Th], BF16)
    hT = [[V(hT_ap[:, c, tb * 512:(tb + 1) * 512], Res(f"hT{c}_{tb}")) for tb in range(NTB)] for c in range(NCH)]
    NW = 4
    wslots = [V(sb(f"w{i}", [128, 4096], BF16), Res(f"w{i}")) for i in range(NW)]
    yT_ap = sb("yT", [128, 4, Th], BF16)
    yT = [[V(yT_ap[:, k, tb * 512:(tb + 1) * 512], Res(f"yT{k}_{tb}")) for tb in range(NTB)] for k in range(4)]
    NTMP = 6
    tmps = [V(sb(f"tmp{i}", [128, 512], F32), Res(f"tmp{i}")) for i in range(NTMP)]
    NBT = 4
    bts = [V(sb(f"bt{i}", [128, 512], BF16), Res(f"bt{i}")) for i in range(NBT)]
    small = V(sb("small", [128, NS], F32), Res("small"))
    cst = V(sb("cst", [128, NCST_SB], F32), Res("cst"))
    ident_f = cst.ap[:, 0:128]
    misc = V(sb("misc", [128, 16], F32), Res("misc"))
    cb = V(sb("cb", [128, 128 + 128 + 128 + 4 * 512 + 128], BF16), Res("cb"))
    ident_b = cb.ap[:, 0:128]
    ones_b = cb.ap[:, 128:256]
    tril_b = cb.ap[:, 256:384]
    mask_b = [cb.ap[:, 384 + j * 512:384 + (j + 1) * 512] for j in range(4)]
    Rd_b = cb.ap[:, 384 + 2048:384 + 2048 + 128]
    NBLK = 12
    blk_ap = sb("blk", [128, NBLK * 2048], BF16)
    blk_res = [Res(f"blk{i}") for i in range(NBLK)]

    def bview(b0, nb, dt=BF16, shape=None):
        ap = blk_ap[:, b0 * 2048:(b0 + nb) * 2048]
        if dt != BF16:
            ap = ap.bitcast(dt)
        return V(ap, blk_res[b0:b0 + nb])

    psb = [V(nc.alloc_psum_tensor(f"ps{i}", [128, 512], F32).ap(), Res(f"ps{i}", psum=True)) for i in range(8)]
    rot = {"w": 0, "wf": 0, "tmp": 0, "bt": 0, "S": 0, "O": 0, "Z": 0, "ev": 0}

    def nxt(kind, lst):
        i = rot[kind]
        rot[kind] = i + 1
        return lst[i % len(lst)]

    def ps_s():
        return nxt("S", psb[0:4])

    def ps_o():
        return nxt("O", psb[4:6])

    def ps_z():
        return nxt("Z", psb[6:8])

    def tmp():
        return nxt("tmp", tmps)

    def bt():
        return nxt("bt", bts)

    def evac_engine():
        rot["ev"] += 1
        return "act" if rot["ev"] % 2 else "dve"

    def sml(nm, l, j=0, n=128):
        c = soff[(nm, l)] + j
        return small.ap[0:n, c:c + 1]

    def load_w(src, a, b, pool=None):
        slot = nxt("w", wslots) if pool is None else nxt("wf", pool)
        npart = src.shape[0]
        dst = slot.ap[0:npart, 0:a * b].rearrange("p (a b) -> p a b", a=a)
        if b > 512:
            bb = max(d_ for d_ in range(1, 513) if b % d_ == 0)
            d2 = dst.rearrange("p a (b2 b) -> p a b2 b", b=bb)
            s2 = src.rearrange("p a (b2 b) -> p a b2 b", b=bb)
        else:
            d2, s2 = dst, src
        P.dma("pool", [lambda e: e.dma_start(out=d2, in_=s2)], writes=[slot], owner=slot)
        return V(dst, slot.res)

    def mm(out_v, mms, reads):
        def fn(e):
            ins = None
            for (o, lt, r, st, sp) in mms:
                ins = e.matmul(o, lhsT=lt, rhs=r, start=st, stop=sp)
            return ins
        P.op("pe", fn, reads=reads, writes=[out_v], pe_acc=True, nmm=len(mms))

    def act(out_v, out_ap, in_v, in_ap, func, scale=1.0, bias=None, extra_reads=()):
        kw = {}
        if bias is not None:
            kw["bias"] = bias
        P.op("act", lambda e: e.activation(out=out_ap, in_=in_ap, func=func, scale=scale, **kw),
             reads=[in_v] + list(extra_reads), writes=[out_v])

    def copy_any(out_v, out_ap, in_v, in_ap, scale=None):
        eng = evac_engine()
        if eng == "act":
            if scale is None:
                P.op("act", lambda e: e.activation(out=out_ap, in_=in_ap, func=AF.Identity), reads=[in_v], writes=[out_v])
            else:
                P.op("act", lambda e: e.activation(out=out_ap, in_=in_ap, func=AF.Identity, scale=scale), reads=[in_v], writes=[out_v])
        else:
            if scale is None:
                P.op("dve", lambda e: e.tensor_copy(out=out_ap, in_=in_ap), reads=[in_v], writes=[out_v])
            else:
                P.op("dve", lambda e: e.tensor_scalar(out=out_ap, in0=in_ap, scalar1=scale, scalar2=None, op0=ALU.mult),
                     reads=[in_v], writes=[out_v])

    def dve_tt(out_v, out_ap, a_v, a_ap, b_v, b_ap, op):
        P.op("dve", lambda e: e.tensor_tensor(out=out_ap, in0=a_ap, in1=b_ap, op=op), reads=[a_v, b_v], writes=[out_v])

    def dve_ts(out_v, out_ap, a_v, a_ap, s1, s2, op0, op1=None, extra=()):
        if op1 is None:
            P.op("dve", lambda e: e.tensor_scalar(out=out_ap, in0=a_ap, scalar1=s1, scalar2=None, op0=op0),
                 reads=[a_v] + list(extra), writes=[out_v])
        else:
            P.op("dve", lambda e: e.tensor_scalar(out=out_ap, in0=a_ap, scalar1=s1, scalar2=s2, op0=op0, op1=op1),
                 reads=[a_v] + list(extra), writes=[out_v])

    def dve_stt(out_v, out_ap, a_v, a_ap, sc, b_v, b_ap, op0, op1, extra=()):
        P.op("dve", lambda e: e.scalar_tensor_tensor(out=out_ap, in0=a_ap, scalar=sc, in1=b_ap, op0=op0, op1=op1),
             reads=[a_v, b_v] + list(extra), writes=[out_v])

    def ld(dst_v, dst_ap, src_ap, src_res):
        P.dma("sp", [lambda e: e.dma_start(out=dst_ap, in_=src_ap)], reads=[src_res], writes=[dst_v], owner=dst_v)

    def st(dst_ap, dst_res, src_v, src_ap):
        P.dma("sp", [lambda e: e.dma_start(out=dst_ap, in_=src_ap)], reads=[src_v], writes=[dst_res], owner=src_v, kind="st")

    def rstd_from_ps(ps, npart, dim):
        t = tmp()
        act(t, t.ap[0:npart, :], ps, ps.ap[0:npart, :], AF.Sqrt, scale=1.0 / dim, bias=misc.ap[0:npart, 0:1], extra_reads=[misc])
        P.op("dve", lambda e: e.reciprocal(out=t.ap[0:npart, :], in_=t.ap[0:npart, :]), reads=[t], writes=[t])
        return t

    def sumsq_ps(srcs, npart):
        ps = ps_s()
        n = len(srcs)
        for i, (sv, sap) in enumerate(srcs):
            q = bt()
            act(q, q.ap[0:npart, :], sv, sap, AF.Square)
            mm(ps, [(ps.ap[:, :], ones_b[0:npart, :], q.ap[0:npart, :], i == 0, i == n - 1)], [q, cb])
        return ps

    P.dma("sp", [lambda e: e.dma_start(out=small.ap, in_=small_d)], writes=[small], owner=small)
    P.dma("sp", [lambda e: e.dma_start(out=cst.ap, in_=consts_d[:, 0:NCST_SB])], writes=[cst], owner=cst)
    P.op("dve", lambda e: e.memset(misc.ap[:, 0:1], EPS), writes=[misc])
    P.op("dve", lambda e: e.memset(misc.ap[:, 1:2], math.pi), writes=[misc])
    P.op("dve", lambda e: e.memset(misc.ap[:, 2:3], 1.0), writes=[misc])
    P.op("dve", lambda e: e.tensor_copy(out=cb.ap[:, 0:128], in_=cst.ap[:, 0:128]), reads=[cst], writes=[cb])
    P.op("dve", lambda e: e.memset(cb.ap[:, 128:256], 1.0), writes=[cb])
    P.op("dve", lambda e: e.tensor_copy(out=cb.ap[:, 256:384], in_=cst.ap[:, 128:256]), reads=[cst], writes=[cb])
    import os
    if "maskdma" not in os.environ.get("KSKIP", ""):
        P.dma("pool", [lambda e: e.dma_start(out=cb.ap[:, 384:384 + 2048].rearrange("p (j b) -> p j b", b=512),
                                             in_=consts_d[:, 258:258 + 2048].rearrange("p (j b) -> p j b", b=512))], writes=[cb], owner=cb)
        P.dma("pool", [lambda e: e.dma_start(out=cb.ap[:, 384 + 2048:384 + 2048 + 128], in_=consts_d[:, 258 + 2048:258 + 2048 + 128])], writes=[cb], owner=cb)

    import os
    KSKIP = os.environ.get("KSKIP", "")
    for (fc, npart, cn, sn) in [] if "rope" in KSKIP else ((CONST_COLS["freq_mla"], 64, "rCm", "rSm"), (CONST_COLS["freq_diff"], 128, "rCd", "rSd")):
        for b0 in range(0, S, 512):
            pi_ = bview(0, 1, I32)
            ld(pi_, pi_.ap[0:npart, 0:512], pos_d[0, b0:b0 + 512].partition_broadcast(npart), Res("posd"))
            ang = tmp()
            P.op("dve", lambda e, ang=ang, pi_=pi_, npart=npart: e.tensor_copy(out=ang.ap[0:npart, :], in_=pi_.ap[0:npart, 0:512]),
                 reads=[pi_], writes=[ang])
            dve_ts(ang, ang.ap[0:npart, :], ang, ang.ap[0:npart, :], cst.ap[0:npart, fc:fc + 1], None, ALU.mult, extra=[cst])
            for (shift, nm) in ((0.0, sn), (math.pi / 2, cn)):
                a2 = tmp()
                dve_ts(a2, a2.ap[0:npart, :], ang, ang.ap[0:npart, :], shift, None, ALU.add)
                kf = tmp()
                dve_ts(kf, kf.ap[0:npart, :], a2, a2.ap[0:npart, :], 1.0 / (2 * math.pi), None, ALU.mult)
                ki = bview(1, 1, I32)
                P.op("dve", lambda e, ki=ki, kf=kf, npart=npart: e.tensor_copy(out=ki.ap[0:npart, 0:512], in_=kf.ap[0:npart, :]), reads=[kf], writes=[ki])
                P.op("dve", lambda e, ki=ki, kf=kf, npart=npart: e.tensor_copy(out=kf.ap[0:npart, :], in_=ki.ap[0:npart, 0:512]), reads=[ki], writes=[kf])
                r = tmp()
                dve_stt(r, r.ap[0:npart, :], kf, kf.ap[0:npart, :], -2 * math.pi, a2, a2.ap[0:npart, :], ALU.mult, ALU.add)
                dve_ts(kf, kf.ap[0:npart, :], r, r.ap[0:npart, :], math.pi, 2 * math.pi, ALU.is_gt, ALU.mult)
                dve_tt(r, r.ap[0:npart, :], r, r.ap[0:npart, :], kf, kf.ap[0:npart, :], ALU.subtract)
                dve_ts(kf, kf.ap[0:npart, :], r, r.ap[0:npart, :], -math.pi, 2 * math.pi, ALU.is_lt, ALU.mult)
                dve_tt(r, r.ap[0:npart, :], r, r.ap[0:npart, :], kf, kf.ap[0:npart, :], ALU.add)
                dve_ts(r, r.ap[0:npart, :], r, r.ap[0:npart, :], 3.141592, -3.141592, ALU.min, ALU.max)
                act(r, r.ap[0:npart, :], r, r.ap[0:npart, :], AF.Sin)
                st(SC[nm][0:npart, b0:b0 + 512], SCR[nm], r, r.ap[0:npart, :])

    for mt in range(0 if "mem" in KSKIP else MEM // 128):
        pcs = []
        for q4 in range(4):
            mtile = tmp()
            ld(mtile, mtile.ap, mem_d[mt * 128:(mt + 1) * 128, q4 * 512:(q4 + 1) * 512], Res("memd"))
            junk = bt()
            P.op("act", lambda e, mtile=mtile, junk=junk, q4=q4: e.activation(out=junk.ap, in_=mtile.ap, func=AF.Square, accum_out=misc.ap[:, 4 + q4:5 + q4]),
                 reads=[mtile], writes=[junk, misc])
            pcs.append(mtile)
        for q4 in range(1, 4):
            dve_tt(misc, misc.ap[:, 4:5], misc, misc.ap[:, 4:5], misc, misc.ap[:, 4 + q4:5 + q4], ALU.add)
        act(misc, misc.ap[:, 5:6], misc, misc.ap[:, 4:5], AF.Sqrt, scale=1.0 / D, bias=misc.ap[:, 0:1])
        P.op("dve", lambda e: e.reciprocal(out=misc.ap[:, 5:6], in_=misc.ap[:, 5:6]), reads=[misc], writes=[misc])
        for q4 in range(4):
            mtile = pcs[q4]
            dve_ts(mtile, mtile.ap, mtile, mtile.ap, misc.ap[:, 5:6], None, ALU.mult, extra=[misc])
            ps = ps_s()
            for j in range(4):
                P.op("pe", lambda e, ps=ps, j=j, mtile=mtile: e.matmul(ps.ap[:, j * 128:(j + 1) * 128], lhsT=mtile.ap[:, j * 128:(j + 1) * 128], rhs=ident_f, start=True, stop=True),
                     reads=[mtile, cst], writes=[ps], pe_acc=True, nmm=1)
            o = bt()
            c0 = q4 * 4
            for j in range(4):
                c = c0 + j
                dve_ts(o, o.ap[:, j * 128:(j + 1) * 128], ps, ps.ap[:, j * 128:(j + 1) * 128], sml("g_mem", 0, c), None, ALU.mult, extra=[small])
            st(SC["memT"][c0 * 128:(c0 + 4) * 128, mt * 128:(mt + 1) * 128].rearrange("(j p) t -> p j t", p=128), SCR["memT"],
               o, o.ap.rearrange("p (j t) -> p j t", j=4))

    def norm_to_hT(gname, l):
        for tb in range(NTB):
            ps = sumsq_ps([(xs[c][tb], xs[c][tb].ap) for c in range(NCH)], 128)
            r = rstd_from_ps(ps, 128, D)
            for c in range(NCH):
                dve_stt(hT[c][tb], hT[c][tb].ap, xs[c][tb], xs[c][tb].ap, sml(gname, l, c), r, r.ap, ALU.mult, ALU.mult, extra=[small])

    def proj_fm(wv, wsel, nk, rhs, tb, npart_out=128):
        ps = ps_s()
        mms = []
        for k in range(nk):
            rv, rap = rhs[k]
            mms.append((ps.ap[0:npart_out, :], wsel(k), rap, k == 0, k == nk - 1))
        mm(ps, mms, [wv] + [rv for rv, _ in rhs])
        return ps

    def hrhs(tb):
        return [(hT[c][tb], hT[c][tb].ap) for c in range(NCH)]

    def wo_accumulate(l, wd, row0, half):
        for cg in range(2):
            wv = load_w(wd[l, row0:row0 + 512, cg * 1024:(cg + 1) * 1024].rearrange("(k p) j -> p k j", p=128), 4, 1024)
            for dtl in range(8):
                dt_ = cg * 8 + dtl
                for tb in range(NTB):
                    ps = nxt("O", psb[4:8])
                    mm(ps, [(ps.ap, wv.ap[:, k, dtl * 128:(dtl + 1) * 128], yT[k][tb].ap, k == 0, k == 3) for k in range(4)],
                       [wv] + [yT[k][tb] for k in range(4)])
                    dve_tt(xs[dt_][tb], xs[dt_][tb].ap, ps, ps.ap, xs[dt_][tb], xs[dt_][tb].ap, ALU.add)

    def store_fm(ps, npart, dst_ap, dst_res, scale=None):
        o = bt()
        copy_any(o, o.ap[0:npart, :], ps, ps.ap[0:npart, :], scale)
        st(dst_ap, dst_res, o, o.ap[0:npart, :])

    def proj_tm_store(l, wsrc_cols, dst_key, half, rhs_sel=None):
        for u in range(2):
            c0 = wsrc_cols + u * 256
            wv = load_w(w_in[l, :, c0:c0 + 256].rearrange("(c p) j -> p c j", p=128), NCH, 256)
            for tt in range(Th // 128):
                tb, o_ = divmod(tt, 4)
                ps = ps_s()
                mm(ps, [(ps.ap[:, 0:256], hT[c][tb].ap[:, o_ * 128:(o_ + 1) * 128], wv.ap[:, c, :], c == 0, c == NCH - 1) for c in range(NCH)],
                   [wv] + [hT[c][tb] for c in range(NCH)])
                o = bt()
                copy_any(o, o.ap[:, 0:256], ps, ps.ap[:, 0:256])
                r0 = half * Th + tt * 128
                st(SC[(dst_key, l)][r0:r0 + 128, u * 256:(u + 1) * 256], SCR[(dst_key, l)], o, o.ap[:, 0:256])

    qbuf = [bview(0, 1), bview(1, 1)]
    kbuf = [bview(2, 1), bview(3, 1)]
    vbuf = [bview(4, 1), bview(5, 1)]
    qrbuf = [bview(6, 1), bview(7, 1)]
    krbuf = bview(8, 1)
    Abuf = bview(9, 1)
    Bbuf = bview(10, 1)
    rot["hb"] = 0

    def attention(nkeys_fn, score_mms, vsel, ytile, tb, extra_reads, post=None):
        t0 = tb_global(tb)
        nkt = (t0 + 512) // 128
        po, pz = ps_o(), ps_z()
        pss = {}
        tri = mask_b[0][:, 0:128]

        def c0_of(kt):
            return max(0, kt - t0 // 128) * 128

        def score(kt):
            ps = ps_s()
            mm(ps, score_mms(kt, ps, c0_of(kt)), extra_reads)
            pss[kt] = ps
        LOOK = 2
        for k in range(min(LOOK, nkt)):
            score(k)
        for kt in range(nkt):
            if kt + LOOK < nkt:
                score(kt + LOOK)
            ps = pss.pop(kt)
            pt = bt()
            j = kt - t0 // 128
            c0 = c0_of(kt)
            if j >= 0 and nkeys_fn == "premask":
                P.op("dve", lambda e, ps=ps, c0=c0: e.scalar_tensor_tensor(out=ps.ap[:, c0:c0 + 128], in0=tri, scalar=60000.0, in1=ps.ap[:, c0:c0 + 128],
                                                                    op0=ALU.mult, op1=ALU.min),
                     reads=[cb, ps], writes=[ps])
            act(pt, pt.ap[:, c0:512], ps, ps.ap[:, c0:512], AF.Exp)
            if j >= 0:
                dve_tt(pt, pt.ap[:, c0:c0 + 128], pt, pt.ap[:, c0:c0 + 128], cb, tri, ALU.mult)
            mm(po, [(po.ap[:, c0:512], vsel(kt), pt.ap[:, c0:512], kt == 0, kt == nkt - 1)], [pt] + extra_reads)
            mm(pz, [(pz.ap[:, c0:512], ones_b, pt.ap[:, c0:512], kt == 0, kt == nkt - 1)], [pt, cb])
        rec = tmp()
        P.op("dve", lambda e: e.reciprocal(out=rec.ap, in_=pz.ap), reads=[pz], writes=[rec])
        return po, rec

    cur = {"half": 0}

    def tb_global(tb):
        return cur["half"] * Th + tb * 512

    def load_kv(l, kkey, vkey, h, i, nkeys):
        kb, vb = kbuf[i], vbuf[i]
        ld(kb, kb.ap[:, 0:nkeys], SC[(kkey, l)][h * 128:(h + 1) * 128, 0:nkeys], SCR[(kkey, l)])
        vdst = vb.ap[:, 0:(nkeys // 128) * 128].rearrange("p (t d) -> p t d", d=128)
        ld(vb, vdst, SC[(vkey, l)][0:nkeys, h * 128:(h + 1) * 128].rearrange("(t p) d -> p t d", p=128), SCR[(vkey, l)])
        return kb, vb, vdst

    def load_q(l, qkey, h, i, half):
        qb = qbuf[i]
        ld(qb, qb.ap[:, 0:Th], SC[(qkey, l)][h * 128:(h + 1) * 128, half * Th:(half + 1) * Th], SCR[(qkey, l)])
        return qb

    def fox(l, half, part):
        isq = 128 ** -0.5
        if part == "proj":
            wv = load_w(w_in[l, :, 1536:1540].rearrange("(c p) j -> p c j", p=128), NCH, 4)
            carry = V(misc.ap[0:4, 8 + l:9 + l], misc.res)
            if half == 0:
                P.op("dve", lambda e: e.memset(misc.ap[0:4, 8 + l:9 + l], 0.0), writes=[misc])
            for tb in range(NTB):
                ps = proj_fm(wv, lambda k, wv=wv: wv.ap[:, k, 0:4], NCH, hrhs(tb), tb, npart_out=4)
                nbf = V(misc.ap[0:4, 3:4], misc.res)
                dve_ts(misc, misc.ap[0:4, 3:4], small, sml("b_f", l, 0, 4), -1.0, None, ALU.mult)
                e1 = tmp()
                act(e1, e1.ap[0:4, :], ps, ps.ap[0:4, :], AF.Exp, scale=-1.0, bias=misc.ap[0:4, 3:4], extra_reads=[misc])
                act(e1, e1.ap[0:4, :], e1, e1.ap[0:4, :], AF.Ln, scale=1.0, bias=misc.ap[0:4, 2:3], extra_reads=[misc])
                dve_ts(e1, e1.ap[0:4, :], e1, e1.ap[0:4, :], -1.0, None, ALU.mult)
                onesf = tmp()
                P.op("dve", lambda e, onesf=onesf: e.memset(onesf.ap[0:4, :], 1.0), writes=[onesf])
                Ft = tmp()
                P.op("dve", lambda e, Ft=Ft, onesf=onesf, e1=e1: e.tensor_tensor_scan(
                    out=Ft.ap[0:4, :], data0=onesf.ap[0:4, :], data1=e1.ap[0:4, :], initial=misc.ap[0:4, 8 + l:9 + l],
                    op0=ALU.mult, op1=ALU.add), reads=[onesf, e1, misc], writes=[Ft])
                P.op("dve", lambda e, Ft=Ft: e.tensor_copy(out=misc.ap[0:4, 8 + l:9 + l], in_=Ft.ap[0:4, 511:512]), reads=[Ft], writes=[misc])
                g0 = tb_global(tb)
                resid = Ft
                for part in range(3):
                    hp = bt()
                    P.op("dve", lambda e, hp=hp, resid=resid: e.tensor_copy(out=hp.ap[0:4, :], in_=resid.ap[0:4, :]), reads=[resid], writes=[hp])
                    hn = bt()
                    dve_ts(hn, hn.ap[0:4, :], hp, hp.ap[0:4, :], -1.0, None, ALU.mult)
                    st(SC[("fFp", l)][:, part, g0:g0 + 512], SCR[("fFp", l)], hp, hp.ap[0:4, :])
                    st(SC[("fFn", l)][:, part, g0:g0 + 512], SCR[("fFn", l)], hn, hn.ap[0:4, :])
                    if part < 2:
                        nr = tmp()
                        dve_tt(nr, nr.ap[0:4, :], resid, resid.ap[0:4, :], hp, hp.ap[0:4, :], ALU.subtract)
                        resid = nr
            for (c0, key, scale) in ((0, "fq", isq), (512, "fk", None)):
                for u in range(2):
                    wv = load_w(w_in[l, :, c0 + u * 256:c0 + (u + 1) * 256].rearrange("(c p) j -> p c j", p=128), NCH, 256)
                    for t in range(2):
                        for tb in range(NTB):
                            ps = proj_fm(wv, lambda k, t=t, wv=wv: wv.ap[:, k, t * 128:(t + 1) * 128], NCH, hrhs(tb), tb)
                            r0 = (u * 2 + t) * 128
                            g0 = tb_global(tb)
                            store_fm(ps, 128, SC[(key, l)][r0:r0 + 128, g0:g0 + 512], SCR[(key, l)], scale)
            proj_tm_store(l, 1024, "fv", half)

            return
        nkeys = (half + 1) * Th
        P.op("dve", lambda e: e.memset(Abuf.ap[0:6, :], 1.0), writes=[Abuf])
        P.op("dve", lambda e: e.memset(Bbuf.ap[0:6, :], 1.0), writes=[Bbuf])
        for h in range(4):
            i = h % 2
            kb, vb, vdst = load_kv(l, "fk", "fv", h, i, nkeys)
            qb = load_q(l, "fq", h, i, half)
            ld(Abuf, Abuf.ap[3:6, 0:nkeys], SC[("fFn", l)][h, :, 0:nkeys], SCR[("fFn", l)])
            ld(Bbuf, Bbuf.ap[0:3, 0:Th], SC[("fFp", l)][h, :, half * Th:(half + 1) * Th], SCR[("fFp", l)])
            for tb in range(NTB):
                def smm(kt, ps, c0, kb=kb, qb=qb, tb=tb):
                    return [(ps.ap[:, c0:512], kb.ap[:, kt * 128:(kt + 1) * 128], qb.ap[:, tb * 512 + c0:(tb + 1) * 512], True, False),
                            (ps.ap[:, c0:512], Abuf.ap[0:6, kt * 128:(kt + 1) * 128], Bbuf.ap[0:6, tb * 512 + c0:(tb + 1) * 512], False, True)]
                po, rec = attention("premask", smm, lambda kt, vdst=vdst: vdst[:, kt, :], None, tb, [kb, qb, vb, Abuf, Bbuf])
                dve_tt(yT[h][tb], yT[h][tb].ap, po, po.ap, rec, rec.ap, ALU.mult)
        wo_accumulate(l, w_o, 0, half)

    def rope_store(pa, pb, npart, Cv, Sv, tb, dst_ap, dst_res, scale):
        c_ap = Cv.ap[0:npart, tb * 512:(tb + 1) * 512]
        s_ap = Sv.ap[0:npart, tb * 512:(tb + 1) * 512]
        t1, t2 = tmp(), tmp()
        dve_tt(t1, t1.ap[0:npart, :], pa, pa.ap[0:npart, :], Cv, c_ap, ALU.mult)
        dve_tt(t2, t2.ap[0:npart, :], pb, pb.ap[0:npart, :], Sv, s_ap, ALU.mult)
        o = bt()
        if scale is None:
            dve_tt(o, o.ap[0:npart, :], t1, t1.ap[0:npart, :], t2, t2.ap[0:npart, :], ALU.add)
        else:
            dve_tt(t1, t1.ap[0:npart, :], t1, t1.ap[0:npart, :], t2, t2.ap[0:npart, :], ALU.add)
            act(o, o.ap[0:npart, :], t1, t1.ap[0:npart, :], AF.Identity, scale=scale)
        st(dst_ap, dst_res, o, o.ap[0:npart, :])

    def load_rope_tables(Cn, Sn, npart, Cv, Sv, half):
        ld(Cv, Cv.ap[0:npart, 0:Th], SC[Cn][0:npart, half * Th:(half + 1) * Th], SCR[Cn])
        ld(Sv, Sv.ap[0:npart, 0:Th], SC[Sn][0:npart, half * Th:(half + 1) * Th], SCR[Sn])

    def mla(l, half, part):
        isq = 192 ** -0.5
        if part == "proj":
            lat = [V(bview(i, 1, F32).ap[:, 0:512], [blk_res[i]]) for i in range(5)]
            cqn = bview(5, 2)
            ckvn = bview(7, 1)
            krot = V(blk_ap[:, 8 * 2048:8 * 2048 + 1024].rearrange("p (c j) -> p c j", c=NCH), [blk_res[8]])
            rt = V(blk_ap[:, 9 * 2048 + 1024:9 * 2048 + 2048].bitcast(F32), [blk_res[9]])
            uqrot_ap = blk_ap[:, 9 * 2048:9 * 2048 + 768].rearrange("p (kh d) -> p kh d", d=64)
            uqrot = V(uqrot_ap, [blk_res[9]])
            Cm, Sm = bview(10, 1, F32), bview(11, 1, F32)
            load_rope_tables("rCm", "rSm", 64, Cm, Sm, half)
            wq = load_w(w_in[l, :, 1540:1796].rearrange("(c p) j -> p c j", p=128), NCH, 256)
            wq2 = load_w(w_in[l, :, 1796:2052].rearrange("(c p) j -> p c j", p=128), NCH, 256)
            wq3 = load_w(w_in[l, :, 2052:2244].rearrange("(c p) j -> p c j", p=128), NCH, 192)

            def wcol(j):
                u, o_ = divmod(j * 128, 256)
                return (wq, wq2, wq3)[u], o_
            krw = wq3.ap[:, :, 128:192]
            P.op("dve", lambda e: e.tensor_scalar(out=krot.ap[:, :, 0:32], in0=krw[:, :, 32:64], scalar1=-1.0, scalar2=None, op0=ALU.mult),
                 reads=[wq3], writes=[krot])
            P.op("dve", lambda e: e.tensor_copy(out=krot.ap[:, :, 32:64], in_=krw[:, :, 0:32]), reads=[wq3], writes=[krot])
            for tb in range(NTB):
                g0 = tb_global(tb)
                for j in range(5):
                    wv_, o_ = wcol(j)
                    ps = proj_fm(wv_, lambda k, wv_=wv_, o_=o_: wv_.ap[:, k, o_:o_ + 128], NCH, hrhs(tb), tb)
                    copy_any(lat[j], lat[j].ap, ps, ps.ap)
                for (idx, dim, gnm, dstv) in (((0, 1, 2), 384, "g_cq", cqn), ((3, 4), 256, "g_ckv", ckvn)):
                    ps = sumsq_ps([(lat[i], lat[i].ap) for i in idx], 128)
                    act(rt, rt.ap, ps, ps.ap, AF.Sqrt, scale=1.0 / dim, bias=misc.ap[:, 0:1], extra_reads=[misc])
                    P.op("dve", lambda e: e.reciprocal(out=rt.ap, in_=rt.ap), reads=[rt], writes=[rt])
                    for n_, i in enumerate(idx):
                        dve_stt(dstv, dstv.ap[:, n_ * Th + tb * 512:n_ * Th + (tb + 1) * 512], lat[i], lat[i].ap, sml(gnm, l, n_), rt, rt.ap,
                                ALU.mult, ALU.mult, extra=[small])
                pa = proj_fm(wq3, lambda k: wq3.ap[:, k, 128:192], NCH, hrhs(tb), tb, npart_out=64)
                pb = proj_fm(krot, lambda k: krot.ap[:, k, :], NCH, hrhs(tb), tb, npart_out=64)
                rope_store(pa, pb, 64, Cm, Sm, tb, SC[("mkr", l)][:, g0:g0 + 512], SCR[("mkr", l)], None)
            wuq = load_w(w_uq[l].rearrange("(k p) j -> p k j", p=128), 3, 768)
            wukv = load_w(w_ukv[l].rearrange("(k p) j -> p k j", p=128), 2, 1024)
            uq4 = wuq.ap.rearrange("p k (h d) -> p (k h) d", d=192)
            P.op("dve", lambda e: e.tensor_scalar(out=uqrot_ap[:, :, 0:32], in0=uq4[:, :, 160:192], scalar1=-1.0, scalar2=None, op0=ALU.mult),
                 reads=[wuq], writes=[uqrot])
            P.op("dve", lambda e: e.tensor_copy(out=uqrot_ap[:, :, 32:64], in_=uq4[:, :, 128:160]), reads=[wuq], writes=[uqrot])
            wv4 = wukv.ap.rearrange("p k (h two d) -> p k h two d", two=2, d=128)
            for tb in range(NTB):
                g0 = tb_global(tb)
                cq_r = [(cqn, cqn.ap[:, k * Th + tb * 512:k * Th + (tb + 1) * 512]) for k in range(3)]
                ckv_r = [(ckvn, ckvn.ap[:, k * Th + tb * 512:k * Th + (tb + 1) * 512]) for k in range(2)]
                for h in range(4):
                    ps = proj_fm(wuq, lambda k, h=h: wuq.ap[:, k, h * 192:h * 192 + 128], 3, cq_r, tb)
                    store_fm(ps, 128, SC[("mqn", l)][h * 128:(h + 1) * 128, g0:g0 + 512], SCR[("mqn", l)], isq)
                    pa = proj_fm(wuq, lambda k, h=h: wuq.ap[:, k, h * 192 + 128:h * 192 + 192], 3, cq_r, tb, npart_out=64)
                    pb = proj_fm(uqrot, lambda k, h=h: uqrot_ap[:, k * 4 + h, :], 3, cq_r, tb, npart_out=64)
                    rope_store(pa, pb, 64, Cm, Sm, tb, SC[("mqr", l)][h * 64:(h + 1) * 64, g0:g0 + 512], SCR[("mqr", l)], isq)
                    ps = proj_fm(wukv, lambda k, h=h: wukv.ap[:, k, h * 256:h * 256 + 128], 2, ckv_r, tb)
                    store_fm(ps, 128, SC[("mkn", l)][h * 128:(h + 1) * 128, g0:g0 + 512], SCR[("mkn", l)])
                for o_ in range(4):
                    ps = ps_s()
                    mm(ps, [(ps.ap.rearrange("p (h d) -> p h d", d=128),
                             ckvn.ap[:, k * Th + tb * 512 + o_ * 128:k * Th + tb * 512 + (o_ + 1) * 128], wv4[:, k, :, 1, :], k == 0, k == 1)
                            for k in range(2)], [wukv, ckvn])
                    o = bt()
                    copy_any(o, o.ap, ps, ps.ap)
                    r0 = g0 + o_ * 128
                    st(SC[("mv", l)][r0:r0 + 128, :], SCR[("mv", l)], o, o.ap)

            return
        nkeys = (half + 1) * Th
        ld(krbuf, krbuf.ap[0:64, 0:nkeys], SC[("mkr", l)][:, 0:nkeys], SCR[("mkr", l)])
        for h in range(4):
            i = h % 2
            kb, vb, vdst = load_kv(l, "mkn", "mv", h, i, nkeys)
            qb = load_q(l, "mqn", h, i, half)
            qr = qrbuf[i]
            ld(qr, qr.ap[0:64, 0:Th], SC[("mqr", l)][h * 64:(h + 1) * 64, half * Th:(half + 1) * Th], SCR[("mqr", l)])
            for tb in range(NTB):
                def smm(kt, ps, c0, kb=kb, qb=qb, qr=qr, tb=tb):
                    return [(ps.ap[:, c0:512], kb.ap[:, kt * 128:(kt + 1) * 128], qb.ap[:, tb * 512 + c0:(tb + 1) * 512], True, False),
                            (ps.ap[:, c0:512], krbuf.ap[0:64, kt * 128:(kt + 1) * 128], qr.ap[0:64, tb * 512 + c0:(tb + 1) * 512], False, True)]
                po, rec = attention(None, smm, lambda kt, vdst=vdst: vdst[:, kt, :], None, tb, [kb, qb, vb, qr, krbuf])
                dve_tt(yT[h][tb], yT[h][tb].ap, po, po.ap, rec, rec.ap, ALU.mult)
        wo_accumulate(l, w_o, 512, half)

    sgst_ap = sb("sgst", [128, 8 * (Th // 128)], F32)
    sgst = [V(sgst_ap[:, i * 8:(i + 1) * 8], Res(f"sgst{i}")) for i in range(Th // 128)]

    def sgu(l, half):
        NT = Th // 128
        uT = bview(0, 2)
        bsb = V(bview(2, 1, F32).ap[:, 0:512], [blk_res[2]])
        lng = V(bview(3, 1, F32).ap[:, 0:512], [blk_res[3]])
        lnb = V(bview(4, 1, F32).ap[:, 0:512], [blk_res[4]])
        wsT = V(blk_ap[:, 5 * 2048:5 * 2048 + 512], [blk_res[5]])
        wsn = V(blk_ap[:, 5 * 2048 + 1024:5 * 2048 + 2048].bitcast(F32), [blk_res[5]])
        vt = [V(blk_ap[:, (6 + i // 2) * 2048 + (i % 2) * 1024:(6 + i // 2) * 2048 + (i % 2 + 1) * 1024].bitcast(F32), Res(f"vt{i}"))
              for i in range(NT)]
        vnb = [V(blk_ap[:, 10 * 2048 + i * 512:10 * 2048 + (i + 1) * 512], [Res(f"vnb{i}")]) for i in range(4)]
        P.handoff(blk_res[6:11], vt + vnb)
        ld(bsb, bsb.ap, b_s[l, :].partition_broadcast(128), Res("bsd"))
        ld(lng, lng.ap, sgu_ln_g[l, :].partition_broadcast(128), Res("lngd"))
        ld(lnb, lnb.ap, sgu_ln_b[l, :].partition_broadcast(128), Res("lnbd"))
        ld(wsn, wsn.ap.rearrange("p (g s) -> p g s", g=4), w_s[l].rearrange("g t s -> t g s"), Res("wsd"))
        wsm = bt()
        for g in range(4):
            dve_tt(wsm, wsm.ap[:, g * 128:(g + 1) * 128], wsn, wsn.ap[:, g * 128:(g + 1) * 128], cst, cst.ap[:, 128:256], ALU.mult)
        pst = ps_s()
        for g in range(4):
            P.op("pe", lambda e, g=g: e.matmul(pst.ap[:, g * 128:(g + 1) * 128], lhsT=wsm.ap[:, g * 128:(g + 1) * 128], rhs=ident_b, start=True, stop=True),
                 reads=[wsm, cb], writes=[pst], pe_acc=True, nmm=1)
        P.op("dve", lambda e: e.tensor_copy(out=wsT.ap, in_=pst.ap), reads=[pst], writes=[wsT])
        wu = [load_w(w_in[l, :, 2244 + u * 256:2244 + (u + 1) * 256].rearrange("(c p) j -> p c j", p=128), NCH, 256) for u in range(2)]
        wvv = [load_w(w_in[l, :, 2756 + u * 256:2756 + (u + 1) * 256].rearrange("(c p) j -> p c j", p=128), NCH, 256) for u in range(2)]
        pend = []

        def mix(item):
            tt, vn = item
            tb, o_ = divmod(tt, 4)
            ps = ps_s()
            for g in range(4):
                mm(ps, [(ps.ap[:, g * 128:(g + 1) * 128], vn.ap[:, g * 128:(g + 1) * 128], wsT.ap[:, g * 128:(g + 1) * 128], True, True)], [vn, wsT])
            t2 = tmp()
            dve_tt(t2, t2.ap, ps, ps.ap, bsb, bsb.ap, ALU.add)
            y3 = yT_ap[:, :, tb * 512 + o_ * 128:tb * 512 + (o_ + 1) * 128]
            u3 = uT.ap[:, 0:4 * Th].rearrange("p (g t) -> p g t", g=4)[:, :, tt * 128:(tt + 1) * 128]
            P.op("dve", lambda e, y3=y3, t2=t2, u3=u3: e.tensor_tensor(out=y3, in0=t2.ap.rearrange("p (g t) -> p g t", g=4), in1=u3, op=ALU.mult),
                 reads=[t2, uT], writes=[yT[g][tb] for g in range(4)])
        for tb in range(NTB):
            for g in range(4):
                wv_ = wu[g // 2]
                ps = proj_fm(wv_, lambda k, wv_=wv_, g=g: wv_.ap[:, k, (g % 2) * 128:(g % 2) * 128 + 128], NCH, hrhs(tb), tb)
                act(uT, uT.ap[:, g * Th + tb * 512:g * Th + (tb + 1) * 512], ps, ps.ap, AF.Gelu_apprx_tanh)
            for o_ in range(4):
                tt = tb * 4 + o_
                t, stt_ = vt[tt], sgst[tt]
                ps = ps_s()
                for u in range(2):
                    mm(ps, [(ps.ap[:, u * 256:(u + 1) * 256], hT[c][tb].ap[:, o_ * 128:(o_ + 1) * 128], wvv[u].ap[:, c, :], c == 0, c == NCH - 1) for c in range(NCH)],
                       [wvv[u]] + [hT[c][tb] for c in range(NCH)])
                P.op("act", lambda e, t=t, ps=ps, stt_=stt_: e.activation(out=t.ap, in_=ps.ap, func=AF.Gelu_apprx_tanh, accum_out=stt_.ap[:, 0:1]),
                     reads=[ps], writes=[t, stt_])
                junk = bt()
                P.op("act", lambda e, t=t, junk=junk, stt_=stt_: e.activation(out=junk.ap, in_=t.ap, func=AF.Square, accum_out=stt_.ap[:, 1:2]),
                     reads=[t], writes=[junk, stt_])
                dve_ts(stt_, stt_.ap[:, 2:4], stt_, stt_.ap[:, 0:2], 1.0 / 512, None, ALU.mult)
                dve_stt(stt_, stt_.ap[:, 4:5], stt_, stt_.ap[:, 2:3], stt_.ap[:, 2:3], stt_, stt_.ap[:, 3:4], ALU.mult, ALU.subtract)
                act(stt_, stt_.ap[:, 5:6], stt_, stt_.ap[:, 4:5], AF.Sqrt, scale=-1.0, bias=misc.ap[:, 0:1], extra_reads=[misc])
                P.op("dve", lambda e, stt_=stt_: e.reciprocal(out=stt_.ap[:, 5:6], in_=stt_.ap[:, 5:6]), reads=[stt_], writes=[stt_])
                dve_ts(t, t.ap, t, t.ap, stt_.ap[:, 2:3], stt_.ap[:, 5:6], ALU.subtract, ALU.mult, extra=[stt_])
                dve_tt(t, t.ap, t, t.ap, lng, lng.ap, ALU.mult)
                vn = vnb[tt % 4]
                dve_tt(vn, vn.ap, t, t.ap, lnb, lnb.ap, ALU.add)
                pend.append((tt, vn))
                if len(pend) > 2:
                    mix(pend.pop(0))
        while pend:
            mix(pend.pop(0))
        P.handoff(vt + vnb, blk_res[6:11])
        wo_accumulate(l, w_o, 1024, half)

    def diff(l, half, part):
        lam_init = 0.8 - 0.6 * math.exp(-0.3 * l)
        isq = 64 ** -0.5
        if part == "proj":
            Cd, Sd = bview(0, 1, F32), bview(1, 1, F32)
            load_rope_tables("rCd", "rSd", 128, Cd, Sd, half)
            pend = []

            def finish(item):
                pa, q16, key, r0, tb, scale = item
                g0 = tb_global(tb)
                pb = ps_s()
                mm(pb, [(pb.ap, Rd_b, q16.ap, True, True)], [q16, cb])
                rope_store(pa, pb, 128, Cd, Sd, tb, SC[(key, l)][r0:r0 + 128, g0:g0 + 512], SCR[(key, l)], scale)
            for (c0, key, scale) in ((3268, "dq", isq), (3780, "dk", None)):
                for u in range(2):
                    wv = load_w(w_in[l, :, c0 + u * 256:c0 + (u + 1) * 256].rearrange("(c p) j -> p c j", p=128), NCH, 256)
                    for t in range(2):
                        for tb in range(NTB):
                            pa = proj_fm(wv, lambda k, t=t, wv=wv: wv.ap[:, k, t * 128:(t + 1) * 128], NCH, hrhs(tb), tb)
                            q16 = bt()
                            act(q16, q16.ap, pa, pa.ap, AF.Identity)
                            pend.append((pa, q16, key, (u * 2 + t) * 128, tb, scale))
                            if len(pend) > 1:
                                finish(pend.pop(0))
            while pend:
                finish(pend.pop(0))
            proj_tm_store(l, 4292, "dv", half)

            return
        la = tmp()
        for i, (a, b) in enumerate((("lam_q1", "lam_k1"), ("lam_q2", "lam_k2"))):
            ld(la, la.ap[:, 0:64], lam_d[a][l, :].partition_broadcast(128), Res("lamd"))
            ld(la, la.ap[:, 64:128], lam_d[b][l, :].partition_broadcast(128), Res("lamd"))
            dve_tt(la, la.ap[:, 128:192], la, la.ap[:, 0:64], la, la.ap[:, 64:128], ALU.mult)
            P.op("act", lambda e, i=i: e.activation(out=la.ap[:, 192:256], in_=la.ap[:, 128:192], func=AF.Identity, accum_out=misc.ap[:, 13 + i:14 + i]),
                 reads=[la], writes=[la, misc])
            act(misc, misc.ap[:, 13 + i:14 + i], misc, misc.ap[:, 13 + i:14 + i], AF.Exp)
        dve_tt(misc, misc.ap[:, 15:16], misc, misc.ap[:, 14:15], misc, misc.ap[:, 13:14], ALU.subtract)
        dve_ts(misc, misc.ap[:, 15:16], misc, misc.ap[:, 15:16], -lam_init, None, ALU.add)
        neglam = misc.ap[:, 15:16]
        nkeys = (half + 1) * Th
        o1b = [V(blk_ap[:, 6 * 2048 + i_ * 1024:6 * 2048 + (i_ + 1) * 1024].bitcast(F32), [Res(f"o1b{i_}")]) for i_ in range(2)]
        P.handoff([blk_res[6]], o1b)
        pendn = []

        def post(item):
            o1, h, tb = item
            ps = sumsq_ps([(o1, o1.ap)], 128)
            r = rstd_from_ps(ps, 128, 128)
            dve_stt(o1, o1.ap, o1, o1.ap, sml("g_diff", l, 0), r, r.ap, ALU.mult, ALU.mult, extra=[small])
            dve_ts(yT[h][tb], yT[h][tb].ap, o1, o1.ap, 1.0 - lam_init, None, ALU.mult)
        nq = 0
        for h in range(4):
            i = h % 2
            kb, vb, vdst = load_kv(l, "dk", "dv", h, i, nkeys)
            qb = load_q(l, "dq", h, i, half)
            for tb in range(NTB):
                o1 = o1b[nq % 2]
                nq += 1
                for m in range(2):
                    def smm(kt, ps, c0, kb=kb, qb=qb, tb=tb, m=m):
                        return [(ps.ap[:, c0:512], kb.ap[m * 64:(m + 1) * 64, kt * 128:(kt + 1) * 128], qb.ap[m * 64:(m + 1) * 64, tb * 512 + c0:(tb + 1) * 512], True, True)]
                    po, rec = attention(None, smm, lambda kt, vdst=vdst: vdst[:, kt, :], None, tb, [kb, qb, vb])
                    if m == 0:
                        dve_tt(o1, o1.ap, po, po.ap, rec, rec.ap, ALU.mult)
                    else:
                        dve_tt(rec, rec.ap, po, po.ap, rec, rec.ap, ALU.mult)
                        dve_stt(o1, o1.ap, rec, rec.ap, neglam, o1, o1.ap, ALU.mult, ALU.add, extra=[misc])
                pendn.append((o1, h, tb))
                if len(pendn) > 1:
                    post(pendn.pop(0))
        while pendn:
            post(pendn.pop(0))
        P.handoff(o1b, [blk_res[6]])
        wo_accumulate(l, w_o, 1536, half)

    def cross(l, half):
        isq = 128 ** -0.5
        norm_to_hT("g_cross", l)
        memT = bview(0, 2)
        ld(memT, memT.ap.rearrange("p (c t) -> p c t", c=NCH), SC["memT"].rearrange("(c p) t -> p c t", p=128), SCR["memT"])
        memv = memT.ap.rearrange("p (c t) -> p c t", c=NCH)
        qx = bview(2, 2)
        kx = bview(4, 1)
        vx = bview(5, 1)
        for u in range(2):
            wv = load_w(w_ck[l, :, u * 256:(u + 1) * 256].rearrange("(c p) j -> p c j", p=128), NCH, 256)
            for t in range(2):
                h = u * 2 + t
                ps = ps_s()
                mm(ps, [(ps.ap[:, 0:256], wv.ap[:, c, t * 128:(t + 1) * 128], memv[:, c, :], c == 0, c == NCH - 1) for c in range(NCH)], [wv, memT])
                copy_any(kx, kx.ap[:, h * 256:(h + 1) * 256], ps, ps.ap[:, 0:256])
        for u in range(2):
            wv = load_w(w_cv[l, :, u * 256:(u + 1) * 256].rearrange("(c p) j -> p c j", p=128), NCH, 256)
            for mt in range(2):
                ps = ps_s()
                mm(ps, [(ps.ap[:, 0:256], memv[:, c, mt * 128:(mt + 1) * 128], wv.ap[:, c, :], c == 0, c == NCH - 1) for c in range(NCH)], [wv, memT])
                copy_any(vx, vx.ap[:, mt * 512 + u * 256:mt * 512 + (u + 1) * 256], ps, ps.ap[:, 0:256])
        for u in range(2):
            wv = load_w(w_cq[l, :, u * 256:(u + 1) * 256].rearrange("(c p) j -> p c j", p=128), NCH, 256)
            for t in range(2):
                h = u * 2 + t
                for tb in range(NTB):
                    ps = proj_fm(wv, lambda k, t=t, wv=wv: wv.ap[:, k, t * 128:(t + 1) * 128], NCH, hrhs(tb), tb)
                    copy_any(qx, qx.ap[:, h * Th + tb * 512:h * Th + (tb + 1) * 512], ps, ps.ap, isq)
        for h in range(4):
            for tb in range(NTB):
                po, pz = ps_o(), ps_z()
                for mt in range(2):
                    ps = ps_s()
                    mm(ps, [(ps.ap, kx.ap[:, h * 256 + mt * 128:h * 256 + (mt + 1) * 128], qx.ap[:, h * Th + tb * 512:h * Th + (tb + 1) * 512], True, True)], [kx, qx])
                    pt = bt()
                    act(pt, pt.ap, ps, ps.ap, AF.Exp)
                    mm(po, [(po.ap, vx.ap[:, mt * 512 + h * 128:mt * 512 + (h + 1) * 128], pt.ap, mt == 0, mt == 1)], [pt, vx])
                    mm(pz, [(pz.ap, ones_b, pt.ap, mt == 0, mt == 1)], [pt, cb])
                rec = tmp()
                P.op("dve", lambda e, rec=rec, pz=pz: e.reciprocal(out=rec.ap, in_=pz.ap), reads=[pz], writes=[rec])
                dve_tt(yT[h][tb], yT[h][tb].ap, po, po.ap, rec, rec.ap, ALU.mult)
        wo_accumulate(l, w_co, 0, half)

    def ffn(l, half):
        norm_to_hT("g_ffn", l)
        abuf = [bview(0, 1), bview(1, 1)]
        wpool = wslots + [bview(2, 2), bview(4, 2), bview(6, 2), bview(8, 2), bview(10, 2)]
        nslots = 2 * NTB
        dper = NCH // nslots

        def down_groups(wd, ab, dts):
            for dt_ in dts:
                for tb in range(NTB):
                    ps = nxt("O", psb[4:8])
                    mm(ps, [(ps.ap, wd.ap[:, t, dt_ * 128:(dt_ + 1) * 128], ab.ap[:, t * Th + tb * 512:t * Th + (tb + 1) * 512], t == 0, t == 1) for t in range(2)],
                       [wd, ab])
                    dve_tt(xs[dt_][tb], xs[dt_][tb].ap, ps, ps.ap, xs[dt_][tb], xs[dt_][tb].ap, ALU.add)
        prev = None
        for j in range(FF // 256):
            wg = load_w(w_gate[l, :, j * 256:(j + 1) * 256].rearrange("(c p) j -> p c j", p=128), NCH, 256, wpool)
            wu = load_w(w_up[l, :, j * 256:(j + 1) * 256].rearrange("(c p) j -> p c j", p=128), NCH, 256, wpool)
            ab = abuf[j % 2]
            k = 0
            for t in range(2):
                for tb in range(NTB):
                    pg = proj_fm(wg, lambda k_, t=t, wg=wg: wg.ap[:, k_, t * 128:(t + 1) * 128], NCH, hrhs(tb), tb)
                    pu = proj_fm(wu, lambda k_, t=t, wu=wu: wu.ap[:, k_, t * 128:(t + 1) * 128], NCH, hrhs(tb), tb)
                    sg = tmp()
                    act(sg, sg.ap, pg, pg.ap, AF.Silu)
                    dve_tt(ab, ab.ap[:, t * Th + tb * 512:t * Th + (tb + 1) * 512], pu, pu.ap, sg, sg.ap, ALU.mult)
                    if prev is not None:
                        down_groups(prev[0], prev[1], range(k * dper, (k + 1) * dper))
                    k += 1
            wd = load_w(w_down[l, j * 256:(j + 1) * 256, :].rearrange("(k p) d -> p k d", p=128), 2, 2048, wpool)
            prev = (wd, ab)
        down_groups(prev[0], prev[1], range(NCH))

    for half in range(HALVES):
        cur["half"] = half
        for tt in range(0 if "xload" in KSKIP else Th // 128):
            tb, o_ = divmod(tt, 4)
            r0 = half * Th + tt * 128
            for c0 in range(0, NCH, 4):
                xin = tmp()
                ld(xin, xin.ap, x_d[r0:r0 + 128, c0 * 128:(c0 + 4) * 128], Res("xd"))
                if "xmm" in KSKIP:
                    continue
                ps = ps_s()
                for j in range(4):
                    P.op("pe", lambda e, ps=ps, j=j, xin=xin: e.matmul(ps.ap[:, j * 128:(j + 1) * 128], lhsT=xin.ap[:, j * 128:(j + 1) * 128], rhs=ident_f, start=True, stop=True),
                         reads=[xin, cst], writes=[ps], pe_acc=True, nmm=1)
                dst_ap = xs_ap[:, c0:c0 + 4, tb * 512 + o_ * 128:tb * 512 + (o_ + 1) * 128]
                if "xcp" in KSKIP:
                    continue
                dv_ = V(dst_ap, [xs[c0 + j][tb].res[0] for j in range(4)])
                src_ = ps.ap.rearrange("p (j t) -> p j t", j=4)
                P.op("dve", lambda e, dst_ap=dst_ap, src_=src_: e.tensor_copy(out=dst_ap, in_=src_), reads=[ps], writes=[dv_])
        import os
        stg = os.environ.get("KSTAGE", "fmsdcx")
        for l in range(L):
            P.phases.append((f"h{half}l{l}:norm", P.nmm))
            norm_to_hT("g_mix", l)
            P.phases.append((f"h{half}l{l}:projs", P.nmm))
            fox(l, half, "proj")
            mla(l, half, "proj")
            diff(l, half, "proj")
            P.phases.append((f"h{half}l{l}:sgu", P.nmm))
            sgu(l, half)
            P.phases.append((f"h{half}l{l}:fox", P.nmm))
            fox(l, half, "attn")
            P.phases.append((f"h{half}l{l}:mla", P.nmm))
            mla(l, half, "attn")
            P.phases.append((f"h{half}l{l}:diff", P.nmm))
            diff(l, half, "attn")
            P.phases.append((f"h{half}l{l}:cross", P.nmm))
            cross(l, half)
            P.phases.append((f"h{half}l{l}:ffn", P.nmm))
            ffn(l, half)
        P.phases.append((f"h{half}:final", P.nmm))
        for tb in range(0 if "final" in KSKIP else NTB):
            ps = sumsq_ps([(xs[c][tb], xs[c][tb].ap) for c in range(NCH)], 128)
            r0_ = rstd_from_ps(ps, 128, D)
            r = V(bview(11, 1, F32).ap[:, 0:512], [blk_res[11]])
            P.op("dve", lambda e, r=r, r0_=r0_: e.tensor_copy(out=r.ap, in_=r0_.ap), reads=[r0_], writes=[r])
            for o_ in range(4):
                osb = bview(2 * (o_ % 2), 2, F32)
                for c0 in range(0, NCH, 4):
                    pst = ps_s()
                    for j in range(4):
                        c = c0 + j
                        xn = tmp()
                        dve_stt(xn, xn.ap[:, 0:128], xs[c][tb], xs[c][tb].ap[:, o_ * 128:(o_ + 1) * 128], sml("g_final", 0, c),
                                r, r.ap[:, o_ * 128:(o_ + 1) * 128], ALU.mult, ALU.mult, extra=[small])
                        P.op("pe", lambda e, pst=pst, j=j, xn=xn: e.matmul(pst.ap[:, j * 128:(j + 1) * 128], lhsT=xn.ap[:, 0:128], rhs=ident_f, start=True, stop=True),
                             reads=[xn, cst], writes=[pst], pe_acc=True, nmm=1)
                    copy_any(osb, osb.ap[:, c0 * 128:(c0 + 4) * 128], pst, pst.ap)
                r0 = half * Th + tb * 512 + o_ * 128
                st(out_d[r0:r0 + 128, :], Res("outd"), osb, osb.ap)
    P.wait_all("sp", list(ALL_RES))
    for e in COMPUTE:
        if P.ecnt[e]:
            P._need("sp", (P.esem[e], P.ecnt[e], e))
    P.emit()
    return nc, P


def pack_small(inp, b, L):
    soff, NS = small_layout(L)
    sm = np.zeros((128, NS), np.float32)

    def fm(v):
        return np.ascontiguousarray(v.reshape(-1, 128).T)
    for l in range(L):
        for nm in ("g_mix", "g_cross", "g_ffn", "g_cq", "g_ckv", "g_diff"):
            a = fm(np.asarray(inp[nm][l], np.float32))
            sm[:, soff[(nm, l)]:soff[(nm, l)] + a.shape[1]] = a
        sm[0:4, soff[("b_f", l)]] = np.asarray(inp["b_f"][l], np.float32)
    for nm in ("g_mem", "g_final"):
        a = fm(np.asarray(inp[nm], np.float32))
        sm[:, soff[(nm, 0)]:soff[(nm, 0)] + 16] = a
    return sm


def run(inp, S, L, FF, ncores):
    nc, P = build_program(S, L, FF)
    consts = make_consts()
    small = pack_small(inp, 0, L)
    shared = {}
    for k in ("w_in", "w_uq", "w_ukv", "sgu_ln_g", "sgu_ln_b", "w_s", "lam_q1", "lam_k1", "lam_q2", "lam_k2",
              "w_o", "w_cq", "w_ck", "w_cv", "w_co", "w_gate", "w_up", "w_down"):
        shared[k] = np.ascontiguousarray(np.asarray(inp[k], np.float32))
    shared["b_s"] = np.ascontiguousarray(np.asarray(inp["b_s"], np.float32).reshape(L, 512))
    shared["small"] = small
    shared["consts"] = consts
    in_maps = []
    for b in range(ncores):
        m = dict(shared)
        m["x"] = np.ascontiguousarray(np.asarray(inp["x"][b], np.float32))
        m["mem"] = np.ascontiguousarray(np.asarray(inp["mem"][b], np.float32))
        m["positions"] = np.ascontiguousarray(np.asarray(inp["positions"][b], np.int32).reshape(1, S))
        in_maps.append(m)
    res = run_bass_kernel_spmd(nc, in_maps, core_ids=list(range(ncores)))
    return np.stack([r["out"] for r in res.results], axis=0)


def kernel(**inputs):
    return run(inputs, 2048, 4, 5632, 8).astype(np.float32)
```

```python
import math
import numpy as np
import concourse.bass as bass
import concourse.mybir as mybir
from concourse.bass_utils import run_bass_kernel_spmd

F32 = mybir.dt.float32
BF16 = mybir.dt.bfloat16
I32 = mybir.dt.int32
AF = mybir.ActivationFunctionType
ALU = mybir.AluOpType

D = 2048
NCH = 16
MEM = 256
EPS = 1e-6
N_IN = 4804
COMPUTE = ("pe", "act", "dve", "pool")


ALL_RES = []


class Res:
    __slots__ = ("name", "wtok", "rtoks", "ld_sem", "ld_n", "st_sem", "st_n", "psum")

    def __init__(self, name, psum=False):
        ALL_RES.append(self)
        self.name = name
        self.psum = psum
        self.wtok = None
        self.rtoks = {}
        self.ld_sem = None
        self.ld_n = 0
        self.st_sem = None
        self.st_n = 0


class V:
    __slots__ = ("ap", "res")

    def __init__(self, ap, res):
        self.ap = ap
        self.res = res if isinstance(res, list) else [res]

    def __getitem__(self, idx):
        return self.ap[idx]


def _rl(vs):
    out = []
    for v in vs:
        if isinstance(v, V):
            out.extend(v.res)
        else:
            out.append(v)
    return out


class Prog:
    def __init__(self, nc):
        self.nc = nc
        self.streams = {e: [] for e in ("pe", "act", "dve", "pool", "sp")}
        self.esem = {e: nc.alloc_semaphore(f"es_{e}") for e in COMPUTE}
        self.ecnt = {e: 0 for e in COMPUTE}
        self.waited = {e: {} for e in self.streams}
        self.nsem = 0
        self.ninstr = 0
        self.nmm = 0
        self.phases = []

    def new_sem(self, name):
        self.nsem += 1
        return self.nc.alloc_semaphore(f"s{self.nsem}_{name}")

    def _need(self, e, tok):
        if tok is None:
            return
        sem, val, _ = tok
        w = self.waited[e]
        if w.get(sem.num, 0) >= val:
            return
        w[sem.num] = val
        self.streams[e].append(("wait", sem, val))

    def _deps(self, e, reads, writes, pe_acc):
        for r in reads:
            self._need(e, r.wtok)
            if r.psum:
                for tok in r.rtoks.values():
                    if tok[2] != e:
                        self._need(e, tok)
        for r in writes:
            if not (pe_acc and r.wtok is not None and r.wtok[2] == "pe"):
                self._need(e, r.wtok)
            for tok in r.rtoks.values():
                self._need(e, tok)

    @staticmethod
    def _mark(tok, reads, writes):
        for r in reads:
            r.rtoks[tok[0].num] = tok
        for r in writes:
            r.wtok = tok
            r.rtoks = {}

    def op(self, e, fn, reads=(), writes=(), pe_acc=False, nmm=0):
        self.nmm += nmm
        reads, writes = _rl(reads), _rl(writes)
        self._deps(e, reads, writes, pe_acc)
        self.ecnt[e] += 1
        tok = (self.esem[e], self.ecnt[e], e)
        self.streams[e].append(("op", fn, self.esem[e]))
        self._mark(tok, reads, writes)
        self.ninstr += 1

    def dma(self, q, fns, reads=(), writes=(), owner=None, kind="ld"):
        reads, writes = _rl(reads), _rl(writes)
        self._deps(q, reads, writes, False)
        o = owner.res[0]
        if kind == "ld":
            if o.ld_sem is None:
                o.ld_sem = self.new_sem("ld")
            o.ld_n += len(fns)
            sem, n = o.ld_sem, o.ld_n
        else:
            if o.st_sem is None:
                o.st_sem = self.new_sem("st")
            o.st_n += len(fns)
            sem, n = o.st_sem, o.st_n
        tok = (sem, 16 * n, "dma")
        for fn in fns:
            self.streams[q].append(("dma", fn, sem))
        self._mark(tok, reads, writes)
        self.ninstr += len(fns)

    @staticmethod
    def handoff(srcs, dsts):
        toks = []
        for s_ in _rl(srcs):
            if s_.wtok is not None:
                toks.append(s_.wtok)
            toks.extend(s_.rtoks.values())
        for d in _rl(dsts):
            for t in toks:
                cur_ = d.rtoks.get(t[0].num)
                if cur_ is None or cur_[1] < t[1]:
                    d.rtoks[t[0].num] = t

    def wait_all(self, e, vs):
        for r in _rl(vs):
            self._need(e, r.wtok)
            for tok in r.rtoks.values():
                self._need(e, tok)

    def emit(self):
        nc = self.nc
        streams = self.streams

        def run(name):
            def body(engine):
                for item in streams[name]:
                    if item[0] == "wait":
                        engine.wait_ge(item[1], item[2])
                    elif item[0] == "op":
                        item[1](engine).then_inc(item[2], 1)
                    else:
                        item[1](engine).then_inc(item[2], 16)
            return body

        with nc.Block() as block:
            block.tensor(run("pe"))
            block.scalar(run("act"))
            block.vector(run("dve"))
            block.gpsimd(run("pool"))
            block.sync(run("sp"))


def small_layout(L):
    off = {}
    c = 0
    for l in range(L):
        for nm, w in (("g_mix", 16), ("g_cross", 16), ("g_ffn", 16), ("g_cq", 3), ("g_ckv", 2),
                      ("g_diff", 1), ("b_f", 1)):
            off[(nm, l)] = c
            c += w
    for nm in ("g_mem", "g_final"):
        off[(nm, 0)] = c
        c += 16
    return off, c


CONST_COLS = {"ident": 0, "tril": 128, "freq_mla": 256, "freq_diff": 257, "mask": 258}
NCST_SB = 258
NCONST = 258 + 4 * 512 + 128


def make_consts():
    c = np.zeros((128, NCONST), np.float32)
    c[:, 0:128] = np.eye(128, dtype=np.float32)
    p = np.arange(128)[:, None]
    f = np.arange(128)[None, :]
    c[:, 128:256] = (f <= p).astype(np.float32)
    ff = np.arange(512)[None, :]
    for j in range(4):
        c[:, 258 + j * 512:258 + (j + 1) * 512] = (ff - p - 128 * j >= 0).astype(np.float32)
    theta = 500000.0
    fm = theta ** (-np.arange(0, 64, 2, dtype=np.float32) / 64.0)
    c[0:64, CONST_COLS["freq_mla"]] = np.concatenate([fm, fm])
    fd = theta ** (-np.arange(0, 16, 2, dtype=np.float32) / 16.0)
    one = np.zeros(64, np.float32)
    one[0:8] = fd
    one[8:16] = fd
    c[:, CONST_COLS["freq_diff"]] = np.concatenate([one, one])
    R = np.zeros((128, 128), np.float32)
    for base in (0, 64):
        for i in range(8):
            R[base + 8 + i, base + i] = -1.0
            R[base + i, base + 8 + i] = 1.0
    c[:, 258 + 2048:258 + 2048 + 128] = R
    return c


def build_program(S, L, FF, dbg=None):
    del ALL_RES[:]
    HALVES = 2
    Th = S // HALVES
    NTB = Th // 512
    NKT = S // 128
    nc = bass.Bass("TRN2", target_bir_lowering=False)
    P = Prog(nc)

    def din(name, shape, dt=F32):
        return nc.dram_tensor(name, list(shape), dt, kind="ExternalInput").ap()

    x_d = din("x", [S, D])
    mem_d = din("mem", [MEM, D])
    pos_d = din("positions", [1, S], I32)
    w_in = din("w_in", [L, D, N_IN])
    w_uq = din("w_uq", [L, 384, 768])
    w_ukv = din("w_ukv", [L, 256, 1024])
    sgu_ln_g = din("sgu_ln_g", [L, 512])
    sgu_ln_b = din("sgu_ln_b", [L, 512])
    w_s = din("w_s", [L, 4, 128, 128])
    b_s = din("b_s", [L, 512])
    lam_d = {k: din(k, [L, 64]) for k in ("lam_q1", "lam_k1", "lam_q2", "lam_k2")}
    w_o = din("w_o", [L, D, D])
    w_cq = din("w_cq", [L, D, 512])
    w_ck = din("w_ck", [L, D, 512])
    w_cv = din("w_cv", [L, D, 512])
    w_co = din("w_co", [L, 512, D])
    w_gate = din("w_gate", [L, D, FF])
    w_up = din("w_up", [L, D, FF])
    w_down = din("w_down", [L, FF, D])
    soff, NS = small_layout(L)
    small_d = din("small", [128, NS])
    consts_d = din("consts", [128, NCONST])
    out_d = nc.dram_tensor("out", [S, D], F32, kind="ExternalOutput").ap()

    def scratch(name, shape, dt=BF16):
        return nc.dram_tensor("scr_" + name, list(shape), dt, kind="ExternalOutput").ap()

    SC = {}
    SCR = {}
    for l in range(L):
        for nm, shp, dt in (("fq", [512, S], BF16), ("fk", [512, S], BF16), ("fv", [S, 512], BF16),
                            ("fFp", [4, 3, S], BF16), ("fFn", [4, 3, S], BF16),
                            ("mqn", [512, S], BF16), ("mqr", [256, S], BF16), ("mkn", [512, S], BF16),
                            ("mkr", [64, S], BF16), ("mv", [S, 512], BF16),
                            ("dq", [512, S], BF16), ("dk", [512, S], BF16), ("dv", [S, 512], BF16)):
            SC[(nm, l)] = scratch(f"{nm}{l}", shp, dt)
            SCR[(nm, l)] = Res(f"{nm}{l}")
    for nm, shp in (("rCm", [64, S]), ("rSm", [64, S]), ("rCd", [128, S]), ("rSd", [128, S])):
        SC[nm] = scratch(nm, shp, F32)
        SCR[nm] = Res(nm)
    SC["memT"] = scratch("memT", [D, MEM], BF16)
    SCR["memT"] = Res("memT")

    def sb(name, shape, dt):
        return nc.alloc_sbuf_tensor("sb_" + name, list(shape), dt).ap()

    xs_ap = sb("xs", [128, NCH, Th], F32)
    xs = [[V(xs_ap[:, c, tb * 512:(tb + 1) * 512], Res(f"xs{c}_{tb}")) for tb in range(NTB)] for c in range(NCH)]
    hT_ap = sb("hT", [128, NCH, Th], BF16)
    hT = [[V(hT_ap[:, c, tb * 512:(tb + 1) * 512], Res(f"hT{c}_{tb}")) for tb in range(NTB)] for c in range(NCH)]
    NW = 4
    wslots = [V(sb(f"w{i}", [128, 4096], BF16), Res(f"w{i}")) for i in range(NW)]
    yT_ap = sb("yT", [128, 4, Th], BF16)
    yT = [[V(yT_ap[:, k, tb * 512:(tb + 1) * 512], Res(f"yT{k}_{tb}")) for tb in range(NTB)] for k in range(4)]
    NTMP = 6
    tmps = [V(sb(f"tmp{i}", [128, 512], F32), Res(f"tmp{i}")) for i in range(NTMP)]
    NBT = 4
    bts = [V(sb(f"bt{i}", [128, 512], BF16), Res(f"bt{i}")) for i in range(NBT)]
    small = V(sb("small", [128, NS], F32), Res("small"))
    cst = V(sb("cst", [128, NCST_SB], F32), Res("cst"))
    ident_f = cst.ap[:, 0:128]
    misc = V(sb("misc", [128, 16], F32), Res("misc"))
    cb = V(sb("cb", [128, 128 + 128 + 128 + 4 * 512 + 128], BF16), Res("cb"))
    ident_b = cb.ap[:, 0:128]
    ones_b = cb.ap[:, 128:256]
    tril_b = cb.ap[:, 256:384]
    mask_b = [cb.ap[:, 384 + j * 512:384 + (j + 1) * 512] for j in range(4)]
    Rd_b = cb.ap[:, 384 + 2048:384 + 2048 + 128]
    NBLK = 12
    blk_ap = sb("blk", [128, NBLK * 2048], BF16)
    blk_res = [Res(f"blk{i}") for i in range(NBLK)]

    def bview(b0, nb, dt=BF16, shape=None):
        ap = blk_ap[:, b0 * 2048:(b0 + nb) * 2048]
        if dt != BF16:
            ap = ap.bitcast(dt)
        return V(ap, blk_res[b0:b0 + nb])

    psb = [V(nc.alloc_psum_tensor(f"ps{i}", [128, 512], F32).ap(), Res(f"ps{i}", psum=True)) for i in range(8)]
    rot = {"w": 0, "wf": 0, "tmp": 0, "bt": 0, "S": 0, "O": 0, "Z": 0, "ev": 0}

    def nxt(kind, lst):
        i = rot[kind]
        rot[kind] = i + 1
        return lst[i % len(lst)]

    def ps_s():
        return nxt("S", psb[0:4])

    def ps_o():
        return nxt("O", psb[4:6])

    def ps_z():
        return nxt("Z", psb[6:8])

    def tmp():
        return nxt("tmp", tmps)

    def bt():
        return nxt("bt", bts)

    def evac_engine():
        rot["ev"] += 1
        return "act" if rot["ev"] % 2 else "dve"

    def sml(nm, l, j=0, n=128):
        c = soff[(nm, l)] + j
        return small.ap[0:n, c:c + 1]

    def load_w(src, a, b, pool=None):
        slot = nxt("w", wslots) if pool is None else nxt("wf", pool)
        npart = src.shape[0]
        dst = slot.ap[0:npart, 0:a * b].rearrange("p (a b) -> p a b", a=a)
        if b > 512:
            bb = max(d_ for d_ in range(1, 513) if b % d_ == 0)
            d2 = dst.rearrange("p a (b2 b) -> p a b2 b", b=bb)
            s2 = src.rearrange("p a (b2 b) -> p a b2 b", b=bb)
        else:
            d2, s2 = dst, src
        P.dma("pool", [lambda e: e.dma_start(out=d2, in_=s2)], writes=[slot], owner=slot)
        return V(dst, slot.res)

    def mm(out_v, mms, reads):
        def fn(e):
            ins = None
            for (o, lt, r, st, sp) in mms:
                ins = e.matmul(o, lhsT=lt, rhs=r, start=st, stop=sp)
            return ins
        P.op("pe", fn, reads=reads, writes=[out_v], pe_acc=True, nmm=len(mms))

    def act(out_v, out_ap, in_v, in_ap, func, scale=1.0, bias=None, extra_reads=()):
        kw = {}
        if bias is not None:
            kw["bias"] = bias
        P.op("act", lambda e: e.activation(out=out_ap, in_=in_ap, func=func, scale=scale, **kw),
             reads=[in_v] + list(extra_reads), writes=[out_v])

    def copy_any(out_v, out_ap, in_v, in_ap, scale=None):
        eng = evac_engine()
        if eng == "act":
            if scale is None:
                P.op("act", lambda e: e.activation(out=out_ap, in_=in_ap, func=AF.Identity), reads=[in_v], writes=[out_v])
            else:
                P.op("act", lambda e: e.activation(out=out_ap, in_=in_ap, func=AF.Identity, scale=scale), reads=[in_v], writes=[out_v])
        else:
            if scale is None:
                P.op("dve", lambda e: e.tensor_copy(out=out_ap, in_=in_ap), reads=[in_v], writes=[out_v])
            else:
                P.op("dve", lambda e: e.tensor_scalar(out=out_ap, in0=in_ap, scalar1=scale, scalar2=None, op0=ALU.mult),
                     reads=[in_v], writes=[out_v])

    def dve_tt(out_v, out_ap, a_v, a_ap, b_v, b_ap, op):
        P.op("dve", lambda e: e.tensor_tensor(out=out_ap, in0=a_ap, in1=b_ap, op=op), reads=[a_v, b_v], writes=[out_v])

    def dve_ts(out_v, out_ap, a_v, a_ap, s1, s2, op0, op1=None, extra=()):
        if op1 is None:
            P.op("dve", lambda e: e.tensor_scalar(out=out_ap, in0=a_ap, scalar1=s1, scalar2=None, op0=op0),
                 reads=[a_v] + list(extra), writes=[out_v])
        else:
            P.op("dve", lambda e: e.tensor_scalar(out=out_ap, in0=a_ap, scalar1=s1, scalar2=s2, op0=op0, op1=op1),
                 reads=[a_v] + list(extra), writes=[out_v])

    def dve_stt(out_v, out_ap, a_v, a_ap, sc, b_v, b_ap, op0, op1, extra=()):
        P.op("dve", lambda e: e.scalar_tensor_tensor(out=out_ap, in0=a_ap, scalar=sc, in1=b_ap, op0=op0, op1=op1),
             reads=[a_v, b_v] + list(extra), writes=[out_v])

    def ld(dst_v, dst_ap, src_ap, src_res):
        P.dma("sp", [lambda e: e.dma_start(out=dst_ap, in_=src_ap)], reads=[src_res], writes=[dst_v], owner=dst_v)

    def st(dst_ap, dst_res, src_v, src_ap):
        P.dma("sp", [lambda e: e.dma_start(out=dst_ap, in_=src_ap)], reads=[src_v], writes=[dst_res], owner=src_v, kind="st")

    def rstd_from_ps(ps, npart, dim):
        t = tmp()
        act(t, t.ap[0:npart, :], ps, ps.ap[0:npart, :], AF.Ln, scale=1.0 / dim, bias=misc.ap[0:npart, 0:1], extra_reads=[misc])
        act(t, t.ap[0:npart, :], t, t.ap[0:npart, :], AF.Exp, scale=-0.5)
        return t

    def sumsq_ps(srcs, npart):
        ps = ps_s()
        n = len(srcs)
        for i, (sv, sap) in enumerate(srcs):
            q = bt()
            act(q, q.ap[0:npart, :], sv, sap, AF.Square)
            mm(ps, [(ps.ap[:, :], ones_b[0:npart, :], q.ap[0:npart, :], i == 0, i == n - 1)], [q, cb])
        return ps

    P.dma("sp", [lambda e: e.dma_start(out=small.ap, in_=small_d)], writes=[small], owner=small)
    P.dma("sp", [lambda e: e.dma_start(out=cst.ap, in_=consts_d[:, 0:NCST_SB])], writes=[cst], owner=cst)
    P.op("dve", lambda e: e.memset(misc.ap[:, 0:1], EPS), writes=[misc])
    P.op("dve", lambda e: e.memset(misc.ap[:, 1:2], math.pi), writes=[misc])
    P.op("dve", lambda e: e.memset(misc.ap[:, 2:3], 1.0), writes=[misc])
    P.op("dve", lambda e: e.tensor_copy(out=cb.ap[:, 0:128], in_=cst.ap[:, 0:128]), reads=[cst], writes=[cb])
    P.op("dve", lambda e: e.memset(cb.ap[:, 128:256], 1.0), writes=[cb])
    P.op("dve", lambda e: e.tensor_copy(out=cb.ap[:, 256:384], in_=cst.ap[:, 128:256]), reads=[cst], writes=[cb])
    import os
    if "maskdma" not in os.environ.get("KSKIP", ""):
        P.dma("pool", [lambda e: e.dma_start(out=cb.ap[:, 384:384 + 2048].rearrange("p (j b) -> p j b", b=512),
                                             in_=consts_d[:, 258:258 + 2048].rearrange("p (j b) -> p j b", b=512))], writes=[cb], owner=cb)
        P.dma("pool", [lambda e: e.dma_start(out=cb.ap[:, 384 + 2048:384 + 2048 + 128], in_=consts_d[:, 258 + 2048:258 + 2048 + 128])], writes=[cb], owner=cb)

    import os
    KSKIP = os.environ.get("KSKIP", "")
    for (fc, npart, cn, sn) in [] if "rope" in KSKIP else ((CONST_COLS["freq_mla"], 64, "rCm", "rSm"), (CONST_COLS["freq_diff"], 128, "rCd", "rSd")):
        for b0 in range(0, S, 512):
            pi_ = bview(0, 1, I32)
            ld(pi_, pi_.ap[0:npart, 0:512], pos_d[0, b0:b0 + 512].partition_broadcast(npart), Res("posd"))
            ang = tmp()
            P.op("dve", lambda e, ang=ang, pi_=pi_, npart=npart: e.tensor_copy(out=ang.ap[0:npart, :], in_=pi_.ap[0:npart, 0:512]),
                 reads=[pi_], writes=[ang])
            dve_ts(ang, ang.ap[0:npart, :], ang, ang.ap[0:npart, :], cst.ap[0:npart, fc:fc + 1], None, ALU.mult, extra=[cst])
            for (shift, nm) in ((0.0, sn), (math.pi / 2, cn)):
                a2 = tmp()
                dve_ts(a2, a2.ap[0:npart, :], ang, ang.ap[0:npart, :], shift, None, ALU.add)
                kf = tmp()
                dve_ts(kf, kf.ap[0:npart, :], a2, a2.ap[0:npart, :], 1.0 / (2 * math.pi), None, ALU.mult)
                ki = bview(1, 1, I32)
                P.op("dve", lambda e, ki=ki, kf=kf, npart=npart: e.tensor_copy(out=ki.ap[0:npart, 0:512], in_=kf.ap[0:npart, :]), reads=[kf], writes=[ki])
                P.op("dve", lambda e, ki=ki, kf=kf, npart=npart: e.tensor_copy(out=kf.ap[0:npart, :], in_=ki.ap[0:npart, 0:512]), reads=[ki], writes=[kf])
                r = tmp()
                dve_stt(r, r.ap[0:npart, :], kf, kf.ap[0:npart, :], -2 * math.pi, a2, a2.ap[0:npart, :], ALU.mult, ALU.add)
                dve_ts(kf, kf.ap[0:npart, :], r, r.ap[0:npart, :], math.pi, 2 * math.pi, ALU.is_gt, ALU.mult)
                dve_tt(r, r.ap[0:npart, :], r, r.ap[0:npart, :], kf, kf.ap[0:npart, :], ALU.subtract)
                dve_ts(kf, kf.ap[0:npart, :], r, r.ap[0:npart, :], -math.pi, 2 * math.pi, ALU.is_lt, ALU.mult)
                dve_tt(r, r.ap[0:npart, :], r, r.ap[0:npart, :], kf, kf.ap[0:npart, :], ALU.add)
                dve_ts(r, r.ap[0:npart, :], r, r.ap[0:npart, :], 3.141592, -3.141592, ALU.min, ALU.max)
                act(r, r.ap[0:npart, :], r, r.ap[0:npart, :], AF.Sin)
                st(SC[nm][0:npart, b0:b0 + 512], SCR[nm], r, r.ap[0:npart, :])

    for mt in range(0 if "mem" in KSKIP else MEM // 128):
        pcs = []
        for q4 in range(4):
            mtile = tmp()
            ld(mtile, mtile.ap, mem_d[mt * 128:(mt + 1) * 128, q4 * 512:(q4 + 1) * 512], Res("memd"))
            junk = bt()
            P.op("act", lambda e, mtile=mtile, junk=junk, q4=q4: e.activation(out=junk.ap, in_=mtile.ap, func=AF.Square, accum_out=misc.ap[:, 4 + q4:5 + q4]),
                 reads=[mtile], writes=[junk, misc])
            pcs.append(mtile)
        for q4 in range(1, 4):
            dve_tt(misc, misc.ap[:, 4:5], misc, misc.ap[:, 4:5], misc, misc.ap[:, 4 + q4:5 + q4], ALU.add)
        act(misc, misc.ap[:, 5:6], misc, misc.ap[:, 4:5], AF.Sqrt, scale=1.0 / D, bias=misc.ap[:, 0:1])
        P.op("dve", lambda e: e.reciprocal(out=misc.ap[:, 5:6], in_=misc.ap[:, 5:6]), reads=[misc], writes=[misc])
        for q4 in range(4):
            mtile = pcs[q4]
            dve_ts(mtile, mtile.ap, mtile, mtile.ap, misc.ap[:, 5:6], None, ALU.mult, extra=[misc])
            ps = ps_s()
            for j in range(4):
                P.op("pe", lambda e, ps=ps, j=j, mtile=mtile: e.matmul(ps.ap[:, j * 128:(j + 1) * 128], lhsT=mtile.ap[:, j * 128:(j + 1) * 128], rhs=ident_f, start=True, stop=True),
                     reads=[mtile, cst], writes=[ps], pe_acc=True, nmm=1)
            o = bt()
            c0 = q4 * 4
            for j in range(4):
                c = c0 + j
                dve_ts(o, o.ap[:, j * 128:(j + 1) * 128], ps, ps.ap[:, j * 128:(j + 1) * 128], sml("g_mem", 0, c), None, ALU.mult, extra=[small])
            st(SC["memT"][c0 * 128:(c0 + 4) * 128, mt * 128:(mt + 1) * 128].rearrange("(j p) t -> p j t", p=128), SCR["memT"],
               o, o.ap.rearrange("p (j t) -> p j t", j=4))

    def norm_to_hT(gname, l):
        for tb in range(NTB):
            ps = sumsq_ps([(xs[c][tb], xs[c][tb].ap) for c in range(NCH)], 128)
            r = rstd_from_ps(ps, 128, D)
            for c in range(NCH):
                dve_stt(hT[c][tb], hT[c][tb].ap, xs[c][tb], xs[c][tb].ap, sml(gname, l, c), r, r.ap, ALU.mult, ALU.mult, extra=[small])

    def proj_fm(wv, wsel, nk, rhs, tb, npart_out=128):
        ps = ps_s()
        mms = []
        for k in range(nk):
            rv, rap = rhs[k]
            mms.append((ps.ap[0:npart_out, :], wsel(k), rap, k == 0, k == nk - 1))
        mm(ps, mms, [wv] + [rv for rv, _ in rhs])
        return ps

    def hrhs(tb):
        return [(hT[c][tb], hT[c][tb].ap) for c in range(NCH)]

    def wo_accumulate(l, wd, row0, half):
        for cg in range(2):
            wv = load_w(wd[l, row0:row0 + 512, cg * 1024:(cg + 1) * 1024].rearrange("(k p) j -> p k j", p=128), 4, 1024)
            for dtl in range(8):
                dt_ = cg * 8 + dtl
                for tb in range(NTB):
                    ps = nxt("O", psb[4:8])
                    mm(ps, [(ps.ap, wv.ap[:, k, dtl * 128:(dtl + 1) * 128], yT[k][tb].ap, k == 0, k == 3) for k in range(4)],
                       [wv] + [yT[k][tb] for k in range(4)])
                    dve_tt(xs[dt_][tb], xs[dt_][tb].ap, ps, ps.ap, xs[dt_][tb], xs[dt_][tb].ap, ALU.add)

    def store_fm(ps, npart, dst_ap, dst_res, scale=None):
        o = bt()
        copy_any(o, o.ap[0:npart, :], ps, ps.ap[0:npart, :], scale)
        st(dst_ap, dst_res, o, o.ap[0:npart, :])

    def proj_tm_store(l, wsrc_cols, dst_key, half, rhs_sel=None):
        for u in range(2):
            c0 = wsrc_cols + u * 256
            wv = load_w(w_in[l, :, c0:c0 + 256].rearrange("(c p) j -> p c j", p=128), NCH, 256)
            for tt in range(Th // 128):
                tb, o_ = divmod(tt, 4)
                ps = ps_s()
                mm(ps, [(ps.ap[:, 0:256], hT[c][tb].ap[:, o_ * 128:(o_ + 1) * 128], wv.ap[:, c, :], c == 0, c == NCH - 1) for c in range(NCH)],
                   [wv] + [hT[c][tb] for c in range(NCH)])
                o = bt()
                copy_any(o, o.ap[:, 0:256], ps, ps.ap[:, 0:256])
                r0 = half * Th + tt * 128
                st(SC[(dst_key, l)][r0:r0 + 128, u * 256:(u + 1) * 256], SCR[(dst_key, l)], o, o.ap[:, 0:256])

    qbuf = [bview(0, 1), bview(1, 1)]
    kbuf = [bview(2, 1), bview(3, 1)]
    vbuf = [bview(4, 1), bview(5, 1)]
    qrbuf = [bview(6, 1), bview(7, 1)]
    krbuf = bview(8, 1)
    Abuf = bview(9, 1)
    Bbuf = bview(10, 1)
    rot["hb"] = 0

    def attention(nkeys_fn, score_mms, vsel, ytile, tb, extra_reads, post=None):
        t0 = tb_global(tb)
        nkt = (t0 + 512) // 128
        po, pz = ps_o(), ps_z()
        pss = {}
        tri = mask_b[0][:, 0:128]

        def c0_of(kt):
            return max(0, kt - t0 // 128) * 128

        def score(kt):
            ps = ps_s()
            mm(ps, score_mms(kt, ps, c0_of(kt)), extra_reads)
            pss[kt] = ps
        LOOK = 2
        for k in range(min(LOOK, nkt)):
            score(k)
        for kt in range(nkt):
            if kt + LOOK < nkt:
                score(kt + LOOK)
            ps = pss.pop(kt)
            pt = bt()
            j = kt - t0 // 128
            c0 = c0_of(kt)
            if j >= 0 and nkeys_fn == "premask":
                P.op("dve", lambda e, ps=ps, c0=c0: e.scalar_tensor_tensor(out=ps.ap[:, c0:c0 + 128], in0=tri, scalar=60000.0, in1=ps.ap[:, c0:c0 + 128],
                                                                    op0=ALU.mult, op1=ALU.min),
                     reads=[cb, ps], writes=[ps])
            act(pt, pt.ap[:, c0:512], ps, ps.ap[:, c0:512], AF.Exp)
            if j >= 0:
                dve_tt(pt, pt.ap[:, c0:c0 + 128], pt, pt.ap[:, c0:c0 + 128], cb, tri, ALU.mult)
            mm(po, [(po.ap[:, c0:512], vsel(kt), pt.ap[:, c0:512], kt == 0, kt == nkt - 1)], [pt] + extra_reads)
            mm(pz, [(pz.ap[:, c0:512], ones_b, pt.ap[:, c0:512], kt == 0, kt == nkt - 1)], [pt, cb])
        rec = tmp()
        act(rec, rec.ap, pz, pz.ap, AF.Ln)
        act(rec, rec.ap, rec, rec.ap, AF.Exp, scale=-1.0)
        return po, rec

    cur = {"half": 0}

    def tb_global(tb):
        return cur["half"] * Th + tb * 512

    def load_kv(l, kkey, vkey, h, i, nkeys):
        kb, vb = kbuf[i], vbuf[i]
        ld(kb, kb.ap[:, 0:nkeys], SC[(kkey, l)][h * 128:(h + 1) * 128, 0:nkeys], SCR[(kkey, l)])
        vdst = vb.ap[:, 0:(nkeys // 128) * 128].rearrange("p (t d) -> p t d", d=128)
        ld(vb, vdst, SC[(vkey, l)][0:nkeys, h * 128:(h + 1) * 128].rearrange("(t p) d -> p t d", p=128), SCR[(vkey, l)])
        return kb, vb, vdst

    def load_q(l, qkey, h, i, half):
        qb = qbuf[i]
        ld(qb, qb.ap[:, 0:Th], SC[(qkey, l)][h * 128:(h + 1) * 128, half * Th:(half + 1) * Th], SCR[(qkey, l)])
        return qb

    def fox(l, half, part):
        isq = 128 ** -0.5
        if part == "proj":
            wv = load_w(w_in[l, :, 1536:1540].rearrange("(c p) j -> p c j", p=128), NCH, 4)
            carry = V(misc.ap[0:4, 8 + l:9 + l], misc.res)
            if half == 0:
                P.op("dve", lambda e: e.memset(misc.ap[0:4, 8 + l:9 + l], 0.0), writes=[misc])
            for tb in range(NTB):
                ps = proj_fm(wv, lambda k, wv=wv: wv.ap[:, k, 0:4], NCH, hrhs(tb), tb, npart_out=4)
                nbf = V(misc.ap[0:4, 3:4], misc.res)
                dve_ts(misc, misc.ap[0:4, 3:4], small, sml("b_f", l, 0, 4), -1.0, None, ALU.mult)
                e1 = tmp()
                act(e1, e1.ap[0:4, :], ps, ps.ap[0:4, :], AF.Exp, scale=-1.0, bias=misc.ap[0:4, 3:4], extra_reads=[misc])
                act(e1, e1.ap[0:4, :], e1, e1.ap[0:4, :], AF.Ln, scale=1.0, bias=misc.ap[0:4, 2:3], extra_reads=[misc])
                dve_ts(e1, e1.ap[0:4, :], e1, e1.ap[0:4, :], -1.0, None, ALU.mult)
                onesf = tmp()
                P.op("dve", lambda e, onesf=onesf: e.memset(onesf.ap[0:4, :], 1.0), writes=[onesf])
                Ft = tmp()
                P.op("dve", lambda e, Ft=Ft, onesf=onesf, e1=e1: e.tensor_tensor_scan(
                    out=Ft.ap[0:4, :], data0=onesf.ap[0:4, :], data1=e1.ap[0:4, :], initial=misc.ap[0:4, 8 + l:9 + l],
                    op0=ALU.mult, op1=ALU.add), reads=[onesf, e1, misc], writes=[Ft])
                P.op("dve", lambda e, Ft=Ft: e.tensor_copy(out=misc.ap[0:4, 8 + l:9 + l], in_=Ft.ap[0:4, 511:512]), reads=[Ft], writes=[misc])
                g0 = tb_global(tb)
                resid = Ft
                for part in range(3):
                    hp = bt()
                    P.op("dve", lambda e, hp=hp, resid=resid: e.tensor_copy(out=hp.ap[0:4, :], in_=resid.ap[0:4, :]), reads=[resid], writes=[hp])
                    hn = bt()
                    dve_ts(hn, hn.ap[0:4, :], hp, hp.ap[0:4, :], -1.0, None, ALU.mult)
                    st(SC[("fFp", l)][:, part, g0:g0 + 512], SCR[("fFp", l)], hp, hp.ap[0:4, :])
                    st(SC[("fFn", l)][:, part, g0:g0 + 512], SCR[("fFn", l)], hn, hn.ap[0:4, :])
                    if part < 2:
                        nr = tmp()
                        dve_tt(nr, nr.ap[0:4, :], resid, resid.ap[0:4, :], hp, hp.ap[0:4, :], ALU.subtract)
                        resid = nr
            for (c0, key, scale) in ((0, "fq", isq), (512, "fk", None)):
                for u in range(2):
                    wv = load_w(w_in[l, :, c0 + u * 256:c0 + (u + 1) * 256].rearrange("(c p) j -> p c j", p=128), NCH, 256)
                    for t in range(2):
                        for tb in range(NTB):
                            ps = proj_fm(wv, lambda k, t=t, wv=wv: wv.ap[:, k, t * 128:(t + 1) * 128], NCH, hrhs(tb), tb)
                            r0 = (u * 2 + t) * 128
                            g0 = tb_global(tb)
                            store_fm(ps, 128, SC[(key, l)][r0:r0 + 128, g0:g0 + 512], SCR[(key, l)], scale)
            proj_tm_store(l, 1024, "fv", half)

            return
        nkeys = (half + 1) * Th
        P.op("dve", lambda e: e.memset(Abuf.ap[:, :], 0.0), writes=[Abuf])
        P.op("dve", lambda e: e.memset(Bbuf.ap[:, :], 0.0), writes=[Bbuf])
        P.op("dve", lambda e: e.memset(Abuf.ap[0:6, :], 1.0), writes=[Abuf])
        P.op("dve", lambda e: e.memset(Bbuf.ap[0:6, :], 1.0), writes=[Bbuf])
        for h in range(4):
            i = h % 2
            kb, vb, vdst = load_kv(l, "fk", "fv", h, i, nkeys)
            qb = load_q(l, "fq", h, i, half)
            ld(Abuf, Abuf.ap[3:6, 0:nkeys], SC[("fFn", l)][h, :, 0:nkeys], SCR[("fFn", l)])
            ld(Bbuf, Bbuf.ap[0:3, 0:Th], SC[("fFp", l)][h, :, half * Th:(half + 1) * Th], SCR[("fFp", l)])
            for tb in range(NTB):
                def smm(kt, ps, c0, kb=kb, qb=qb, tb=tb):
                    return [(ps.ap[:, c0:512], kb.ap[:, kt * 128:(kt + 1) * 128], qb.ap[:, tb * 512 + c0:(tb + 1) * 512], True, False),
                            (ps.ap[:, c0:512], Abuf.ap[:, kt * 128:(kt + 1) * 128], Bbuf.ap[:, tb * 512 + c0:(tb + 1) * 512], False, True)]
                po, rec = attention("premask", smm, lambda kt, vdst=vdst: vdst[:, kt, :], None, tb, [kb, qb, vb, Abuf, Bbuf])
                dve_tt(yT[h][tb], yT[h][tb].ap, po, po.ap, rec, rec.ap, ALU.mult)
        wo_accumulate(l, w_o, 0, half)

    def rope_store(pa, pb, npart, Cv, Sv, tb, dst_ap, dst_res, scale):
        c_ap = Cv.ap[0:npart, tb * 512:(tb + 1) * 512]
        s_ap = Sv.ap[0:npart, tb * 512:(tb + 1) * 512]
        t1, t2 = tmp(), tmp()
        dve_tt(t1, t1.ap[0:npart, :], pa, pa.ap[0:npart, :], Cv, c_ap, ALU.mult)
        dve_tt(t2, t2.ap[0:npart, :], pb, pb.ap[0:npart, :], Sv, s_ap, ALU.mult)
        o = bt()
        if scale is None:
            dve_tt(o, o.ap[0:npart, :], t1, t1.ap[0:npart, :], t2, t2.ap[0:npart, :], ALU.add)
        else:
            dve_tt(t1, t1.ap[0:npart, :], t1, t1.ap[0:npart, :], t2, t2.ap[0:npart, :], ALU.add)
            act(o, o.ap[0:npart, :], t1, t1.ap[0:npart, :], AF.Identity, scale=scale)
        st(dst_ap, dst_res, o, o.ap[0:npart, :])

    def load_rope_tables(Cn, Sn, npart, Cv, Sv, half):
        ld(Cv, Cv.ap[0:npart, 0:Th], SC[Cn][0:npart, half * Th:(half + 1) * Th], SCR[Cn])
        ld(Sv, Sv.ap[0:npart, 0:Th], SC[Sn][0:npart, half * Th:(half + 1) * Th], SCR[Sn])

    def mla(l, half, part):
        isq = 192 ** -0.5
        if part == "proj":
            lat = [V(bview(i, 1, F32).ap[:, 0:512], [blk_res[i]]) for i in range(5)]
            cqn = bview(5, 2)
            ckvn = bview(7, 1)
            krot = V(blk_ap[:, 8 * 2048:8 * 2048 + 1024].rearrange("p (c j) -> p c j", c=NCH), [blk_res[8]])
            rt = V(blk_ap[:, 9 * 2048 + 1024:9 * 2048 + 2048].bitcast(F32), [blk_res[9]])
            uqrot_ap = blk_ap[:, 9 * 2048:9 * 2048 + 768].rearrange("p (kh d) -> p kh d", d=64)
            uqrot = V(uqrot_ap, [blk_res[9]])
            Cm, Sm = bview(10, 1, F32), bview(11, 1, F32)
            load_rope_tables("rCm", "rSm", 64, Cm, Sm, half)
            wq = load_w(w_in[l, :, 1540:1796].rearrange("(c p) j -> p c j", p=128), NCH, 256)
            wq2 = load_w(w_in[l, :, 1796:2052].rearrange("(c p) j -> p c j", p=128), NCH, 256)
            wq3 = load_w(w_in[l, :, 2052:2244].rearrange("(c p) j -> p c j", p=128), NCH, 192)

            def wcol(j):
                u, o_ = divmod(j * 128, 256)
                return (wq, wq2, wq3)[u], o_
            krw = wq3.ap[:, :, 128:192]
            P.op("dve", lambda e: e.tensor_scalar(out=krot.ap[:, :, 0:32], in0=krw[:, :, 32:64], scalar1=-1.0, scalar2=None, op0=ALU.mult),
                 reads=[wq3], writes=[krot])
            P.op("dve", lambda e: e.tensor_copy(out=krot.ap[:, :, 32:64], in_=krw[:, :, 0:32]), reads=[wq3], writes=[krot])
            for tb in range(NTB):
                g0 = tb_global(tb)
                for j in range(5):
                    wv_, o_ = wcol(j)
                    ps = proj_fm(wv_, lambda k, wv_=wv_, o_=o_: wv_.ap[:, k, o_:o_ + 128], NCH, hrhs(tb), tb)
                    copy_any(lat[j], lat[j].ap, ps, ps.ap)
                for (idx, dim, gnm, dstv) in (((0, 1, 2), 384, "g_cq", cqn), ((3, 4), 256, "g_ckv", ckvn)):
                    ps = sumsq_ps([(lat[i], lat[i].ap) for i in idx], 128)
                    act(rt, rt.ap, ps, ps.ap, AF.Ln, scale=1.0 / dim, bias=misc.ap[:, 0:1], extra_reads=[misc])
                    act(rt, rt.ap, rt, rt.ap, AF.Exp, scale=-0.5)
                    for n_, i in enumerate(idx):
                        dve_stt(dstv, dstv.ap[:, n_ * Th + tb * 512:n_ * Th + (tb + 1) * 512], lat[i], lat[i].ap, sml(gnm, l, n_), rt, rt.ap,
                                ALU.mult, ALU.mult, extra=[small])
                pa = proj_fm(wq3, lambda k: wq3.ap[:, k, 128:192], NCH, hrhs(tb), tb, npart_out=64)
                pb = proj_fm(krot, lambda k: krot.ap[:, k, :], NCH, hrhs(tb), tb, npart_out=64)
                rope_store(pa, pb, 64, Cm, Sm, tb, SC[("mkr", l)][:, g0:g0 + 512], SCR[("mkr", l)], None)
            wuq = load_w(w_uq[l].rearrange("(k p) j -> p k j", p=128), 3, 768)
            wukv = load_w(w_ukv[l].rearrange("(k p) j -> p k j", p=128), 2, 1024)
            uq4 = wuq.ap.rearrange("p k (h d) -> p (k h) d", d=192)
            P.op("dve", lambda e: e.tensor_scalar(out=uqrot_ap[:, :, 0:32], in0=uq4[:, :, 160:192], scalar1=-1.0, scalar2=None, op0=ALU.mult),
                 reads=[wuq], writes=[uqrot])
            P.op("dve", lambda e: e.tensor_copy(out=uqrot_ap[:, :, 32:64], in_=uq4[:, :, 128:160]), reads=[wuq], writes=[uqrot])
            wv4 = wukv.ap.rearrange("p k (h two d) -> p k h two d", two=2, d=128)
            for tb in range(NTB):
                g0 = tb_global(tb)
                cq_r = [(cqn, cqn.ap[:, k * Th + tb * 512:k * Th + (tb + 1) * 512]) for k in range(3)]
                ckv_r = [(ckvn, ckvn.ap[:, k * Th + tb * 512:k * Th + (tb + 1) * 512]) for k in range(2)]
                for h in range(4):
                    ps = proj_fm(wuq, lambda k, h=h: wuq.ap[:, k, h * 192:h * 192 + 128], 3, cq_r, tb)
                    store_fm(ps, 128, SC[("mqn", l)][h * 128:(h + 1) * 128, g0:g0 + 512], SCR[("mqn", l)], isq)
                    pa = proj_fm(wuq, lambda k, h=h: wuq.ap[:, k, h * 192 + 128:h * 192 + 192], 3, cq_r, tb, npart_out=64)
                    pb = proj_fm(uqrot, lambda k, h=h: uqrot_ap[:, k * 4 + h, :], 3, cq_r, tb, npart_out=64)
                    rope_store(pa, pb, 64, Cm, Sm, tb, SC[("mqr", l)][h * 64:(h + 1) * 64, g0:g0 + 512], SCR[("mqr", l)], isq)
                    ps = proj_fm(wukv, lambda k, h=h: wukv.ap[:, k, h * 256:h * 256 + 128], 2, ckv_r, tb)
                    store_fm(ps, 128, SC[("mkn", l)][h * 128:(h + 1) * 128, g0:g0 + 512], SCR[("mkn", l)])
                for o_ in range(4):
                    ps = ps_s()
                    mm(ps, [(ps.ap.rearrange("p (h d) -> p h d", d=128),
                             ckvn.ap[:, k * Th + tb * 512 + o_ * 128:k * Th + tb * 512 + (o_ + 1) * 128], wv4[:, k, :, 1, :], k == 0, k == 1)
                            for k in range(2)], [wukv, ckvn])
                    o = bt()
                    copy_any(o, o.ap, ps, ps.ap)
                    r0 = g0 + o_ * 128
                    st(SC[("mv", l)][r0:r0 + 128, :], SCR[("mv", l)], o, o.ap)

            return
        nkeys = (half + 1) * Th
        P.op("dve", lambda e: e.memset(krbuf.ap[64:128, :], 0.0), writes=[krbuf])
        for qr_ in qrbuf:
            P.op("dve", lambda e, qr_=qr_: e.memset(qr_.ap[64:128, :], 0.0), writes=[qr_])
        ld(krbuf, krbuf.ap[0:64, 0:nkeys], SC[("mkr", l)][:, 0:nkeys], SCR[("mkr", l)])
        for h in range(4):
            i = h % 2
            kb, vb, vdst = load_kv(l, "mkn", "mv", h, i, nkeys)
            qb = load_q(l, "mqn", h, i, half)
            qr = qrbuf[i]
            ld(qr, qr.ap[0:64, 0:Th], SC[("mqr", l)][h * 64:(h + 1) * 64, half * Th:(half + 1) * Th], SCR[("mqr", l)])
            for tb in range(NTB):
                def smm(kt, ps, c0, kb=kb, qb=qb, qr=qr, tb=tb):
                    return [(ps.ap[:, c0:512], kb.ap[:, kt * 128:(kt + 1) * 128], qb.ap[:, tb * 512 + c0:(tb + 1) * 512], True, False),
                            (ps.ap[:, c0:512], krbuf.ap[:, kt * 128:(kt + 1) * 128], qr.ap[:, tb * 512 + c0:(tb + 1) * 512], False, True)]
                po, rec = attention(None, smm, lambda kt, vdst=vdst: vdst[:, kt, :], None, tb, [kb, qb, vb, qr, krbuf])
                dve_tt(yT[h][tb], yT[h][tb].ap, po, po.ap, rec, rec.ap, ALU.mult)
        wo_accumulate(l, w_o, 512, half)

    sgst_ap = sb("sgst", [128, 8 * (Th // 128)], F32)
    sgst = [V(sgst_ap[:, i * 8:(i + 1) * 8], Res(f"sgst{i}")) for i in range(Th // 128)]

    def sgu(l, half):
        NT = Th // 128
        uT = bview(0, 2)
        bsb = V(bview(2, 1, F32).ap[:, 0:512], [blk_res[2]])
        lng = V(bview(3, 1, F32).ap[:, 0:512], [blk_res[3]])
        lnb = V(bview(4, 1, F32).ap[:, 0:512], [blk_res[4]])
        wsT = V(blk_ap[:, 5 * 2048:5 * 2048 + 512], [blk_res[5]])
        wsn = V(blk_ap[:, 5 * 2048 + 1024:5 * 2048 + 2048].bitcast(F32), [blk_res[5]])
        vt = [V(blk_ap[:, (6 + i // 2) * 2048 + (i % 2) * 1024:(6 + i // 2) * 2048 + (i % 2 + 1) * 1024].bitcast(F32), Res(f"vt{i}"))
              for i in range(NT)]
        vnb = [V(blk_ap[:, 10 * 2048 + i * 512:10 * 2048 + (i + 1) * 512], [Res(f"vnb{i}")]) for i in range(4)]
        P.handoff(blk_res[6:11], vt + vnb)
        ld(bsb, bsb.ap, b_s[l, :].partition_broadcast(128), Res("bsd"))
        ld(lng, lng.ap, sgu_ln_g[l, :].partition_broadcast(128), Res("lngd"))
        ld(lnb, lnb.ap, sgu_ln_b[l, :].partition_broadcast(128), Res("lnbd"))
        ld(wsn, wsn.ap.rearrange("p (g s) -> p g s", g=4), w_s[l].rearrange("g t s -> t g s"), Res("wsd"))
        wsm = bt()
        for g in range(4):
            dve_tt(wsm, wsm.ap[:, g * 128:(g + 1) * 128], wsn, wsn.ap[:, g * 128:(g + 1) * 128], cst, cst.ap[:, 128:256], ALU.mult)
        pst = ps_s()
        for g in range(4):
            P.op("pe", lambda e, g=g: e.matmul(pst.ap[:, g * 128:(g + 1) * 128], lhsT=wsm.ap[:, g * 128:(g + 1) * 128], rhs=ident_b, start=True, stop=True),
                 reads=[wsm, cb], writes=[pst], pe_acc=True, nmm=1)
        P.op("dve", lambda e: e.tensor_copy(out=wsT.ap, in_=pst.ap), reads=[pst], writes=[wsT])
        wu = [load_w(w_in[l, :, 2244 + u * 256:2244 + (u + 1) * 256].rearrange("(c p) j -> p c j", p=128), NCH, 256) for u in range(2)]
        wvv = [load_w(w_in[l, :, 2756 + u * 256:2756 + (u + 1) * 256].rearrange("(c p) j -> p c j", p=128), NCH, 256) for u in range(2)]
        pend = []

        def mix(item):
            tt, vn = item
            tb, o_ = divmod(tt, 4)
            ps = ps_s()
            for g in range(4):
                mm(ps, [(ps.ap[:, g * 128:(g + 1) * 128], vn.ap[:, g * 128:(g + 1) * 128], wsT.ap[:, g * 128:(g + 1) * 128], True, True)], [vn, wsT])
            t2 = tmp()
            dve_tt(t2, t2.ap, ps, ps.ap, bsb, bsb.ap, ALU.add)
            y3 = yT_ap[:, :, tb * 512 + o_ * 128:tb * 512 + (o_ + 1) * 128]
            u3 = uT.ap[:, 0:4 * Th].rearrange("p (g t) -> p g t", g=4)[:, :, tt * 128:(tt + 1) * 128]
            P.op("dve", lambda e, y3=y3, t2=t2, u3=u3: e.tensor_tensor(out=y3, in0=t2.ap.rearrange("p (g t) -> p g t", g=4), in1=u3, op=ALU.mult),
                 reads=[t2, uT], writes=[yT[g][tb] for g in range(4)])
        for tb in range(NTB):
            for g in range(4):
                wv_ = wu[g // 2]
                ps = proj_fm(wv_, lambda k, wv_=wv_, g=g: wv_.ap[:, k, (g % 2) * 128:(g % 2) * 128 + 128], NCH, hrhs(tb), tb)
                act(uT, uT.ap[:, g * Th + tb * 512:g * Th + (tb + 1) * 512], ps, ps.ap, AF.Gelu_apprx_tanh)
            for o_ in range(4):
                tt = tb * 4 + o_
                t, stt_ = vt[tt], sgst[tt]
                ps = ps_s()
                for u in range(2):
                    mm(ps, [(ps.ap[:, u * 256:(u + 1) * 256], hT[c][tb].ap[:, o_ * 128:(o_ + 1) * 128], wvv[u].ap[:, c, :], c == 0, c == NCH - 1) for c in range(NCH)],
                       [wvv[u]] + [hT[c][tb] for c in range(NCH)])
                P.op("act", lambda e, t=t, ps=ps, stt_=stt_: e.activation(out=t.ap, in_=ps.ap, func=AF.Gelu_apprx_tanh, accum_out=stt_.ap[:, 0:1]),
                     reads=[ps], writes=[t, stt_])
                junk = bt()
                P.op("act", lambda e, t=t, junk=junk, stt_=stt_: e.activation(out=junk.ap, in_=t.ap, func=AF.Square, accum_out=stt_.ap[:, 1:2]),
                     reads=[t], writes=[junk, stt_])
                dve_ts(stt_, stt_.ap[:, 2:4], stt_, stt_.ap[:, 0:2], 1.0 / 512, None, ALU.mult)
                dve_stt(stt_, stt_.ap[:, 4:5], stt_, stt_.ap[:, 2:3], stt_.ap[:, 2:3], stt_, stt_.ap[:, 3:4], ALU.mult, ALU.subtract)
                act(stt_, stt_.ap[:, 5:6], stt_, stt_.ap[:, 4:5], AF.Sqrt, scale=-1.0, bias=misc.ap[:, 0:1], extra_reads=[misc])
                P.op("dve", lambda e, stt_=stt_: e.reciprocal(out=stt_.ap[:, 5:6], in_=stt_.ap[:, 5:6]), reads=[stt_], writes=[stt_])
                dve_ts(t, t.ap, t, t.ap, stt_.ap[:, 2:3], stt_.ap[:, 5:6], ALU.subtract, ALU.mult, extra=[stt_])
                dve_tt(t, t.ap, t, t.ap, lng, lng.ap, ALU.mult)
                vn = vnb[tt % 4]
                dve_tt(vn, vn.ap, t, t.ap, lnb, lnb.ap, ALU.add)
                pend.append((tt, vn))
                if len(pend) > 2:
                    mix(pend.pop(0))
        while pend:
            mix(pend.pop(0))
        P.handoff(vt + vnb, blk_res[6:11])
        wo_accumulate(l, w_o, 1024, half)

    def diff(l, half, part):
        lam_init = 0.8 - 0.6 * math.exp(-0.3 * l)
        isq = 64 ** -0.5
        if part == "proj":
            Cd, Sd = bview(0, 1, F32), bview(1, 1, F32)
            load_rope_tables("rCd", "rSd", 128, Cd, Sd, half)
            pend = []

            def finish(item):
                pa, q16, key, r0, tb, scale = item
                g0 = tb_global(tb)
                pb = ps_s()
                mm(pb, [(pb.ap, Rd_b, q16.ap, True, True)], [q16, cb])
                rope_store(pa, pb, 128, Cd, Sd, tb, SC[(key, l)][r0:r0 + 128, g0:g0 + 512], SCR[(key, l)], scale)
            for (c0, key, scale) in ((3268, "dq", isq), (3780, "dk", None)):
                for u in range(2):
                    wv = load_w(w_in[l, :, c0 + u * 256:c0 + (u + 1) * 256].rearrange("(c p) j -> p c j", p=128), NCH, 256)
                    for t in range(2):
                        for tb in range(NTB):
                            pa = proj_fm(wv, lambda k, t=t, wv=wv: wv.ap[:, k, t * 128:(t + 1) * 128], NCH, hrhs(tb), tb)
                            q16 = bt()
                            act(q16, q16.ap, pa, pa.ap, AF.Identity)
                            pend.append((pa, q16, key, (u * 2 + t) * 128, tb, scale))
                            if len(pend) > 1:
                                finish(pend.pop(0))
            while pend:
                finish(pend.pop(0))
            proj_tm_store(l, 4292, "dv", half)

            return
        la = tmp()
        for i, (a, b) in enumerate((("lam_q1", "lam_k1"), ("lam_q2", "lam_k2"))):
            ld(la, la.ap[:, 0:64], lam_d[a][l, :].partition_broadcast(128), Res("lamd"))
            ld(la, la.ap[:, 64:128], lam_d[b][l, :].partition_broadcast(128), Res("lamd"))
            dve_tt(la, la.ap[:, 128:192], la, la.ap[:, 0:64], la, la.ap[:, 64:128], ALU.mult)
            P.op("act", lambda e, i=i: e.activation(out=la.ap[:, 192:256], in_=la.ap[:, 128:192], func=AF.Identity, accum_out=misc.ap[:, 13 + i:14 + i]),
                 reads=[la], writes=[la, misc])
            act(misc, misc.ap[:, 13 + i:14 + i], misc, misc.ap[:, 13 + i:14 + i], AF.Exp)
        dve_tt(misc, misc.ap[:, 15:16], misc, misc.ap[:, 14:15], misc, misc.ap[:, 13:14], ALU.subtract)
        dve_ts(misc, misc.ap[:, 15:16], misc, misc.ap[:, 15:16], -lam_init, None, ALU.add)
        neglam = misc.ap[:, 15:16]
        nkeys = (half + 1) * Th
        o1b = [V(blk_ap[:, 6 * 2048 + i_ * 1024:6 * 2048 + (i_ + 1) * 1024].bitcast(F32), [Res(f"o1b{i_}")]) for i_ in range(2)]
        P.handoff([blk_res[6]], o1b)
        pendn = []

        def post(item):
            o1, h, tb = item
            ps = sumsq_ps([(o1, o1.ap)], 128)
            r = rstd_from_ps(ps, 128, 128)
            dve_stt(o1, o1.ap, o1, o1.ap, sml("g_diff", l, 0), r, r.ap, ALU.mult, ALU.mult, extra=[small])
            dve_ts(yT[h][tb], yT[h][tb].ap, o1, o1.ap, 1.0 - lam_init, None, ALU.mult)
        nq = 0
        qm1 = [bview(7, 1), bview(8, 1)]
        for i_ in range(2):
            P.op("dve", lambda e, i_=i_: e.memset(qbuf[i_].ap[64:128, :], 0.0), writes=[qbuf[i_]])
            P.op("dve", lambda e, i_=i_: e.memset(qm1[i_].ap[0:64, :], 0.0), writes=[qm1[i_]])
        for h in range(4):
            i = h % 2
            kb, vb, vdst = load_kv(l, "dk", "dv", h, i, nkeys)
            qms = (qbuf[i], qm1[i])
            ld(qms[0], qms[0].ap[0:64, 0:Th], SC[("dq", l)][h * 128:h * 128 + 64, half * Th:(half + 1) * Th], SCR[("dq", l)])
            ld(qms[1], qms[1].ap[64:128, 0:Th], SC[("dq", l)][h * 128 + 64:(h + 1) * 128, half * Th:(half + 1) * Th], SCR[("dq", l)])
            for tb in range(NTB):
                o1 = o1b[nq % 2]
                nq += 1
                for m in range(2):
                    def smm(kt, ps, c0, kb=kb, qb=qms[m], tb=tb):
                        return [(ps.ap[:, c0:512], kb.ap[:, kt * 128:(kt + 1) * 128], qb.ap[:, tb * 512 + c0:(tb + 1) * 512], True, True)]
                    po, rec = attention(None, smm, lambda kt, vdst=vdst: vdst[:, kt, :], None, tb, [kb, qms[m], vb])
                    if m == 0:
                        dve_tt(o1, o1.ap, po, po.ap, rec, rec.ap, ALU.mult)
                    else:
                        dve_tt(rec, rec.ap, po, po.ap, rec, rec.ap, ALU.mult)
                        dve_stt(o1, o1.ap, rec, rec.ap, neglam, o1, o1.ap, ALU.mult, ALU.add, extra=[misc])
                pendn.append((o1, h, tb))
                if len(pendn) > 1:
                    post(pendn.pop(0))
        while pendn:
            post(pendn.pop(0))
        P.handoff(o1b, [blk_res[6]])
        wo_accumulate(l, w_o, 1536, half)

    def cross(l, half):
        isq = 128 ** -0.5
        norm_to_hT("g_cross", l)
        memT = bview(0, 2)
        ld(memT, memT.ap.rearrange("p (c t) -> p c t", c=NCH), SC["memT"].rearrange("(c p) t -> p c t", p=128), SCR["memT"])
        memv = memT.ap.rearrange("p (c t) -> p c t", c=NCH)
        qx = bview(2, 2)
        kx = bview(4, 1)
        vx = bview(5, 1)
        for u in range(2):
            wv = load_w(w_ck[l, :, u * 256:(u + 1) * 256].rearrange("(c p) j -> p c j", p=128), NCH, 256)
            for t in range(2):
                h = u * 2 + t
                ps = ps_s()
                mm(ps, [(ps.ap[:, 0:256], wv.ap[:, c, t * 128:(t + 1) * 128], memv[:, c, :], c == 0, c == NCH - 1) for c in range(NCH)], [wv, memT])
                copy_any(kx, kx.ap[:, h * 256:(h + 1) * 256], ps, ps.ap[:, 0:256])
        for u in range(2):
            wv = load_w(w_cv[l, :, u * 256:(u + 1) * 256].rearrange("(c p) j -> p c j", p=128), NCH, 256)
            for mt in range(2):
                ps = ps_s()
                mm(ps, [(ps.ap[:, 0:256], memv[:, c, mt * 128:(mt + 1) * 128], wv.ap[:, c, :], c == 0, c == NCH - 1) for c in range(NCH)], [wv, memT])
                copy_any(vx, vx.ap[:, mt * 512 + u * 256:mt * 512 + (u + 1) * 256], ps, ps.ap[:, 0:256])
        for u in range(2):
            wv = load_w(w_cq[l, :, u * 256:(u + 1) * 256].rearrange("(c p) j -> p c j", p=128), NCH, 256)
            for t in range(2):
                h = u * 2 + t
                for tb in range(NTB):
                    ps = proj_fm(wv, lambda k, t=t, wv=wv: wv.ap[:, k, t * 128:(t + 1) * 128], NCH, hrhs(tb), tb)
                    copy_any(qx, qx.ap[:, h * Th + tb * 512:h * Th + (tb + 1) * 512], ps, ps.ap, isq)
        for h in range(4):
            for tb in range(NTB):
                po, pz = ps_o(), ps_z()
                for mt in range(2):
                    ps = ps_s()
                    mm(ps, [(ps.ap, kx.ap[:, h * 256 + mt * 128:h * 256 + (mt + 1) * 128], qx.ap[:, h * Th + tb * 512:h * Th + (tb + 1) * 512], True, True)], [kx, qx])
                    pt = bt()
                    act(pt, pt.ap, ps, ps.ap, AF.Exp)
                    mm(po, [(po.ap, vx.ap[:, mt * 512 + h * 128:mt * 512 + (h + 1) * 128], pt.ap, mt == 0, mt == 1)], [pt, vx])
                    mm(pz, [(pz.ap, ones_b, pt.ap, mt == 0, mt == 1)], [pt, cb])
                rec = tmp()
                act(rec, rec.ap, pz, pz.ap, AF.Ln)
                act(rec, rec.ap, rec, rec.ap, AF.Exp, scale=-1.0)
                dve_tt(yT[h][tb], yT[h][tb].ap, po, po.ap, rec, rec.ap, ALU.mult)
        wo_accumulate(l, w_co, 0, half)

    def ffn(l, half):
        norm_to_hT("g_ffn", l)
        abuf = [bview(0, 1), bview(1, 1)]
        wpool = wslots + [bview(2, 2), bview(4, 2), bview(6, 2), bview(8, 2), bview(10, 2)]
        nslots = 2 * NTB
        dper = NCH // nslots

        def down_groups(wd, ab, dts):
            for dt_ in dts:
                for tb in range(NTB):
                    ps = nxt("O", psb[4:8])
                    mm(ps, [(ps.ap, wd.ap[:, t, dt_ * 128:(dt_ + 1) * 128], ab.ap[:, t * Th + tb * 512:t * Th + (tb + 1) * 512], t == 0, t == 1) for t in range(2)],
                       [wd, ab])
                    dve_tt(xs[dt_][tb], xs[dt_][tb].ap, ps, ps.ap, xs[dt_][tb], xs[dt_][tb].ap, ALU.add)
        prev = None
        for j in range(FF // 256):
            wg = load_w(w_gate[l, :, j * 256:(j + 1) * 256].rearrange("(c p) j -> p c j", p=128), NCH, 256, wpool)
            wu = load_w(w_up[l, :, j * 256:(j + 1) * 256].rearrange("(c p) j -> p c j", p=128), NCH, 256, wpool)
            ab = abuf[j % 2]
            k = 0
            for t in range(2):
                for tb in range(NTB):
                    pg = proj_fm(wg, lambda k_, t=t, wg=wg: wg.ap[:, k_, t * 128:(t + 1) * 128], NCH, hrhs(tb), tb)
                    pu = proj_fm(wu, lambda k_, t=t, wu=wu: wu.ap[:, k_, t * 128:(t + 1) * 128], NCH, hrhs(tb), tb)
                    sg = tmp()
                    act(sg, sg.ap, pg, pg.ap, AF.Silu)
                    dve_tt(ab, ab.ap[:, t * Th + tb * 512:t * Th + (tb + 1) * 512], pu, pu.ap, sg, sg.ap, ALU.mult)
                    if prev is not None:
                        down_groups(prev[0], prev[1], range(k * dper, (k + 1) * dper))
                    k += 1
            wd = load_w(w_down[l, j * 256:(j + 1) * 256, :].rearrange("(k p) d -> p k d", p=128), 2, 2048, wpool)
            prev = (wd, ab)
        down_groups(prev[0], prev[1], range(NCH))

    for half in range(HALVES):
        cur["half"] = half
        for tt in range(0 if "xload" in KSKIP else Th // 128):
            tb, o_ = divmod(tt, 4)
            r0 = half * Th + tt * 128
            for c0 in range(0, NCH, 4):
                xin = tmp()
                ld(xin, xin.ap, x_d[r0:r0 + 128, c0 * 128:(c0 + 4) * 128], Res("xd"))
                if "xmm" in KSKIP:
                    continue
                ps = ps_s()
                for j in range(4):
                    P.op("pe", lambda e, ps=ps, j=j, xin=xin: e.matmul(ps.ap[:, j * 128:(j + 1) * 128], lhsT=xin.ap[:, j * 128:(j + 1) * 128], rhs=ident_f, start=True, stop=True),
                         reads=[xin, cst], writes=[ps], pe_acc=True, nmm=1)
                dst_ap = xs_ap[:, c0:c0 + 4, tb * 512 + o_ * 128:tb * 512 + (o_ + 1) * 128]
                if "xcp" in KSKIP:
                    continue
                dv_ = V(dst_ap, [xs[c0 + j][tb].res[0] for j in range(4)])
                src_ = ps.ap.rearrange("p (j t) -> p j t", j=4)
                P.op("dve", lambda e, dst_ap=dst_ap, src_=src_: e.tensor_copy(out=dst_ap, in_=src_), reads=[ps], writes=[dv_])
        import os
        stg = os.environ.get("KSTAGE", "fmsdcx")
        for l in range(L):
            P.phases.append((f"h{half}l{l}:norm", P.nmm))
            norm_to_hT("g_mix", l)
            P.phases.append((f"h{half}l{l}:projs", P.nmm))
            fox(l, half, "proj")
            mla(l, half, "proj")
            diff(l, half, "proj")
            P.phases.append((f"h{half}l{l}:sgu", P.nmm))
            sgu(l, half)
            P.phases.append((f"h{half}l{l}:fox", P.nmm))
            fox(l, half, "attn")
            P.phases.append((f"h{half}l{l}:mla", P.nmm))
            mla(l, half, "attn")
            P.phases.append((f"h{half}l{l}:diff", P.nmm))
            diff(l, half, "attn")
            P.phases.append((f"h{half}l{l}:cross", P.nmm))
            cross(l, half)
            P.phases.append((f"h{half}l{l}:ffn", P.nmm))
            ffn(l, half)
        P.phases.append((f"h{half}:final", P.nmm))
        for tb in range(0 if "final" in KSKIP else NTB):
            ps = sumsq_ps([(xs[c][tb], xs[c][tb].ap) for c in range(NCH)], 128)
            r0_ = rstd_from_ps(ps, 128, D)
            r = V(bview(11, 1, F32).ap[:, 0:512], [blk_res[11]])
            P.op("dve", lambda e, r=r, r0_=r0_: e.tensor_copy(out=r.ap, in_=r0_.ap), reads=[r0_], writes=[r])
            for o_ in range(4):
                osb = bview(2 * (o_ % 2), 2, F32)
                for c0 in range(0, NCH, 4):
                    pst = ps_s()
                    for j in range(4):
                        c = c0 + j
                        xn = tmp()
                        dve_stt(xn, xn.ap[:, 0:128], xs[c][tb], xs[c][tb].ap[:, o_ * 128:(o_ + 1) * 128], sml("g_final", 0, c),
                                r, r.ap[:, o_ * 128:(o_ + 1) * 128], ALU.mult, ALU.mult, extra=[small])
                        P.op("pe", lambda e, pst=pst, j=j, xn=xn: e.matmul(pst.ap[:, j * 128:(j + 1) * 128], lhsT=xn.ap[:, 0:128], rhs=ident_f, start=True, stop=True),
                             reads=[xn, cst], writes=[pst], pe_acc=True, nmm=1)
                    copy_any(osb, osb.ap[:, c0 * 128:(c0 + 4) * 128], pst, pst.ap)
                r0 = half * Th + tb * 512 + o_ * 128
                st(out_d[r0:r0 + 128, :], Res("outd"), osb, osb.ap)
    P.wait_all("sp", list(ALL_RES))
    for e in COMPUTE:
        if P.ecnt[e]:
            P._need("sp", (P.esem[e], P.ecnt[e], e))
    P.emit()
    return nc, P


def pack_small(inp, b, L):
    soff, NS = small_layout(L)
    sm = np.zeros((128, NS), np.float32)

    def fm(v):
        return np.ascontiguousarray(v.reshape(-1, 128).T)
    for l in range(L):
        for nm in ("g_mix", "g_cross", "g_ffn", "g_cq", "g_ckv", "g_diff"):
            a = fm(np.asarray(inp[nm][l], np.float32))
            sm[:, soff[(nm, l)]:soff[(nm, l)] + a.shape[1]] = a
        sm[0:4, soff[("b_f", l)]] = np.asarray(inp["b_f"][l], np.float32)
    for nm in ("g_mem", "g_final"):
        a = fm(np.asarray(inp[nm], np.float32))
        sm[:, soff[(nm, 0)]:soff[(nm, 0)] + 16] = a
    return sm


def run(inp, S, L, FF, ncores):
    nc, P = build_program(S, L, FF)
    consts = make_consts()
    small = pack_small(inp, 0, L)
    shared = {}
    for k in ("w_in", "w_uq", "w_ukv", "sgu_ln_g", "sgu_ln_b", "w_s", "lam_q1", "lam_k1", "lam_q2", "lam_k2",
              "w_o", "w_cq", "w_ck", "w_cv", "w_co", "w_gate", "w_up", "w_down"):
        shared[k] = np.ascontiguousarray(np.asarray(inp[k], np.float32))
    shared["b_s"] = np.ascontiguousarray(np.asarray(inp["b_s"], np.float32).reshape(L, 512))
    shared["small"] = small
    shared["consts"] = consts
    in_maps = []
    for b in range(ncores):
        m = dict(shared)
        m["x"] = np.ascontiguousarray(np.asarray(inp["x"][b], np.float32))
        m["mem"] = np.ascontiguousarray(np.asarray(inp["mem"][b], np.float32))
        m["positions"] = np.ascontiguousarray(np.asarray(inp["positions"][b], np.int32).reshape(1, S))
        in_maps.append(m)
    res = run_bass_kernel_spmd(nc, in_maps, core_ids=list(range(ncores)))
    return np.stack([r["out"] for r in res.results], axis=0)


def kernel(**inputs):
    return run(inputs, 2048, 4, 5632, 8).astype(np.float32)
```

```python
import math
import numpy as np
import concourse.bass as bass
import concourse.mybir as mybir
from concourse.bass_utils import run_bass_kernel_spmd

F32 = mybir.dt.float32
BF16 = mybir.dt.bfloat16
I32 = mybir.dt.int32
AF = mybir.ActivationFunctionType
ALU = mybir.AluOpType

D = 2048
NCH = 16
MEM = 256
EPS = 1e-6
N_IN = 4804
COMPUTE = ("pe", "act", "dve", "pool")


ALL_RES = []


class Res:
    __slots__ = ("name", "wtok", "rtoks", "ld_sem", "ld_n", "st_sem", "st_n", "psum")

    def __init__(self, name, psum=False):
        ALL_RES.append(self)
        self.name = name
        self.psum = psum
        self.wtok = None
        self.rtoks = {}
        self.ld_sem = None
        self.ld_n = 0
        self.st_sem = None
        self.st_n = 0


class V:
    __slots__ = ("ap", "res")

    def __init__(self, ap, res):
        self.ap = ap
        self.res = res if isinstance(res, list) else [res]

    def __getitem__(self, idx):
        return self.ap[idx]


def _rl(vs):
    out = []
    for v in vs:
        if isinstance(v, V):
            out.extend(v.res)
        else:
            out.append(v)
    return out


class Prog:
    def __init__(self, nc):
        self.nc = nc
        self.streams = {e: [] for e in ("pe", "act", "dve", "pool", "sp")}
        self.esem = {e: nc.alloc_semaphore(f"es_{e}") for e in COMPUTE}
        self.ecnt = {e: 0 for e in COMPUTE}
        self.waited = {e: {} for e in self.streams}
        self.nsem = 0
        self.ninstr = 0
        self.nmm = 0
        self.phases = []

    def new_sem(self, name):
        self.nsem += 1
        return self.nc.alloc_semaphore(f"s{self.nsem}_{name}")

    def _need(self, e, tok):
        if tok is None:
            return
        sem, val, _ = tok
        w = self.waited[e]
        if w.get(sem.num, 0) >= val:
            return
        w[sem.num] = val
        self.streams[e].append(("wait", sem, val))

    def _deps(self, e, reads, writes, pe_acc):
        for r in reads:
            self._need(e, r.wtok)
            if r.psum:
                for tok in r.rtoks.values():
                    if tok[2] != e:
                        self._need(e, tok)
        for r in writes:
            if not (pe_acc and r.wtok is not None and r.wtok[2] == "pe"):
                self._need(e, r.wtok)
            for tok in r.rtoks.values():
                self._need(e, tok)

    @staticmethod
    def _mark(tok, reads, writes):
        for r in reads:
            r.rtoks[tok[0].num] = tok
        for r in writes:
            r.wtok = tok
            r.rtoks = {}

    def op(self, e, fn, reads=(), writes=(), pe_acc=False, nmm=0):
        self.nmm += nmm
        reads, writes = _rl(reads), _rl(writes)
        self._deps(e, reads, writes, pe_acc)
        self.ecnt[e] += 1
        tok = (self.esem[e], self.ecnt[e], e)
        self.streams[e].append(("op", fn, self.esem[e]))
        self._mark(tok, reads, writes)
        self.ninstr += 1

    def dma(self, q, fns, reads=(), writes=(), owner=None, kind="ld"):
        reads, writes = _rl(reads), _rl(writes)
        self._deps(q, reads, writes, False)
        o = owner.res[0]
        if kind == "ld":
            if o.ld_sem is None:
                o.ld_sem = self.new_sem("ld")
            o.ld_n += len(fns)
            sem, n = o.ld_sem, o.ld_n
        else:
            if o.st_sem is None:
                o.st_sem = self.new_sem("st")
            o.st_n += len(fns)
            sem, n = o.st_sem, o.st_n
        tok = (sem, 16 * n, "dma")
        for fn in fns:
            self.streams[q].append(("dma", fn, sem))
        self._mark(tok, reads, writes)
        self.ninstr += len(fns)

    @staticmethod
    def handoff(srcs, dsts):
        toks = []
        for s_ in _rl(srcs):
            if s_.wtok is not None:
                toks.append(s_.wtok)
            toks.extend(s_.rtoks.values())
        for d in _rl(dsts):
            for t in toks:
                cur_ = d.rtoks.get(t[0].num)
                if cur_ is None or cur_[1] < t[1]:
                    d.rtoks[t[0].num] = t

    def wait_all(self, e, vs):
        for r in _rl(vs):
            self._need(e, r.wtok)
            for tok in r.rtoks.values():
                self._need(e, tok)

    def emit(self):
        nc = self.nc
        streams = self.streams

        def run(name):
            def body(engine):
                for item in streams[name]:
                    if item[0] == "wait":
                        engine.wait_ge(item[1], item[2])
                    elif item[0] == "op":
                        item[1](engine).then_inc(item[2], 1)
                    else:
                        item[1](engine).then_inc(item[2], 16)
            return body

        with nc.Block() as block:
            block.tensor(run("pe"))
            block.scalar(run("act"))
            block.vector(run("dve"))
            block.gpsimd(run("pool"))
            block.sync(run("sp"))


def small_layout(L):
    off = {}
    c = 0
    for l in range(L):
        for nm, w in (("g_mix", 16), ("g_cross", 16), ("g_ffn", 16), ("g_cq", 3), ("g_ckv", 2),
                      ("g_diff", 1), ("b_f", 1)):
            off[(nm, l)] = c
            c += w
    for nm in ("g_mem", "g_final"):
        off[(nm, 0)] = c
        c += 16
    return off, c


CONST_COLS = {"ident": 0, "tril": 128, "freq_mla": 256, "freq_diff": 257, "mask": 258}
NCST_SB = 258
NCONST = 258 + 4 * 512 + 128


def make_consts():
    c = np.zeros((128, NCONST), np.float32)
    c[:, 0:128] = np.eye(128, dtype=np.float32)
    p = np.arange(128)[:, None]
    f = np.arange(128)[None, :]
    c[:, 128:256] = (f <= p).astype(np.float32)
    ff = np.arange(512)[None, :]
    for j in range(4):
        c[:, 258 + j * 512:258 + (j + 1) * 512] = (ff - p - 128 * j >= 0).astype(np.float32)
    theta = 500000.0
    fm = theta ** (-np.arange(0, 64, 2, dtype=np.float32) / 64.0)
    c[0:64, CONST_COLS["freq_mla"]] = np.concatenate([fm, fm])
    fd = theta ** (-np.arange(0, 16, 2, dtype=np.float32) / 16.0)
    one = np.zeros(64, np.float32)
    one[0:8] = fd
    one[8:16] = fd
    c[:, CONST_COLS["freq_diff"]] = np.concatenate([one, one])
    R = np.zeros((128, 128), np.float32)
    for base in (0, 64):
        for i in range(8):
            R[base + 8 + i, base + i] = -1.0
            R[base + i, base + 8 + i] = 1.0
    c[:, 258 + 2048:258 + 2048 + 128] = R
    return c


def build_program(S, L, FF, dbg=None):
    del ALL_RES[:]
    import os as _os
    KOPT = _os.environ.get("KOPT", "FCD")
    HALVES = 2
    Th = S // HALVES
    NTB = Th // 512
    NKT = S // 128
    nc = bass.Bass("TRN2", target_bir_lowering=False)
    P = Prog(nc)

    def din(name, shape, dt=F32):
        return nc.dram_tensor(name, list(shape), dt, kind="ExternalInput").ap()

    x_d = din("x", [S, D])
    mem_d = din("mem", [MEM, D])
    pos_d = din("positions", [1, S], I32)
    w_in = din("w_in", [L, D, N_IN])
    w_uq = din("w_uq", [L, 384, 768])
    w_ukv = din("w_ukv", [L, 256, 1024])
    sgu_ln_g = din("sgu_ln_g", [L, 512])
    sgu_ln_b = din("sgu_ln_b", [L, 512])
    w_s = din("w_s", [L, 4, 128, 128])
    b_s = din("b_s", [L, 512])
    lam_d = {k: din(k, [L, 64]) for k in ("lam_q1", "lam_k1", "lam_q2", "lam_k2")}
    w_o = din("w_o", [L, D, D])
    w_cq = din("w_cq", [L, D, 512])
    w_ck = din("w_ck", [L, D, 512])
    w_cv = din("w_cv", [L, D, 512])
    w_co = din("w_co", [L, 512, D])
    w_gate = din("w_gate", [L, D, FF])
    w_up = din("w_up", [L, D, FF])
    w_down = din("w_down", [L, FF, D])
    soff, NS = small_layout(L)
    small_d = din("small", [128, NS])
    consts_d = din("consts", [128, NCONST])
    out_d = nc.dram_tensor("out", [S, D], F32, kind="ExternalOutput").ap()

    def scratch(name, shape, dt=BF16):
        return nc.dram_tensor("scr_" + name, list(shape), dt, kind="ExternalOutput").ap()

    SC = {}
    SCR = {}
    for l in range(L):
        for nm, shp, dt in (("fq", [512, S], BF16), ("fk", [512, S], BF16), ("fv", [S, 512], BF16),
                            ("fFp", [4, 3, S], BF16), ("fFn", [4, 3, S], BF16),
                            ("mqn", [512, S], BF16), ("mqr", [256, S], BF16), ("mkn", [512, S], BF16),
                            ("mkr", [64, S], BF16), ("mv", [S, 512], BF16),
                            ("dq", [512, S], BF16), ("dk", [512, S], BF16), ("dv", [S, 512], BF16)):
            SC[(nm, l)] = scratch(f"{nm}{l}", shp, dt)
            SCR[(nm, l)] = Res(f"{nm}{l}")
    for nm, shp in (("rCm", [64, S]), ("rSm", [64, S]), ("rCd", [128, S]), ("rSd", [128, S])):
        SC[nm] = scratch(nm, shp, F32)
        SCR[nm] = Res(nm)
    SC["memT"] = scratch("memT", [D, MEM], BF16)
    SCR["memT"] = Res("memT")

    def sb(name, shape, dt):
        return nc.alloc_sbuf_tensor("sb_" + name, list(shape), dt).ap()

    xs_ap = sb("xs", [128, NCH, Th], F32)
    xs = [[V(xs_ap[:, c, tb * 512:(tb + 1) * 512], Res(f"xs{c}_{tb}")) for tb in range(NTB)] for c in range(NCH)]
    hT_ap = sb("hT", [128, NCH, Th], BF16)
    hT = [[V(hT_ap[:, c, tb * 512:(tb + 1) * 512], Res(f"hT{c}_{tb}")) for tb in range(NTB)] for c in range(NCH)]
    NW = 4
    wslots = [V(sb(f"w{i}", [128, 4096], BF16), Res(f"w{i}")) for i in range(NW)]
    yT_ap = sb("yT", [128, 4, Th], BF16)
    yT = [[V(yT_ap[:, k, tb * 512:(tb + 1) * 512], Res(f"yT{k}_{tb}")) for tb in range(NTB)] for k in range(4)]
    NTMP = 6
    tmps = [V(sb(f"tmp{i}", [128, 512], F32), Res(f"tmp{i}")) for i in range(NTMP)]
    NBT = 4
    bts = [V(sb(f"bt{i}", [128, 512], BF16), Res(f"bt{i}")) for i in range(NBT)]
    small = V(sb("small", [128, NS], F32), Res("small"))
    cst = V(sb("cst", [128, NCST_SB], F32), Res("cst"))
    ident_f = cst.ap[:, 0:128]
    misc = V(sb("misc", [128, 16], F32), Res("misc"))
    cb = V(sb("cb", [128, 128 + 128 + 128 + 4 * 512 + 128], BF16), Res("cb"))
    ident_b = cb.ap[:, 0:128]
    ones_b = cb.ap[:, 128:256]
    tril_b = cb.ap[:, 256:384]
    mask_b = [cb.ap[:, 384 + j * 512:384 + (j + 1) * 512] for j in range(4)]
    Rd_b = cb.ap[:, 384 + 2048:384 + 2048 + 128]
    NBLK = 12
    blk_ap = sb("blk", [128, NBLK * 2048], BF16)
    blk_res = [Res(f"blk{i}") for i in range(NBLK)]

    def bview(b0, nb, dt=BF16, shape=None):
        ap = blk_ap[:, b0 * 2048:(b0 + nb) * 2048]
        if dt != BF16:
            ap = ap.bitcast(dt)
        return V(ap, blk_res[b0:b0 + nb])

    psb = [V(nc.alloc_psum_tensor(f"ps{i}", [128, 512], F32).ap(), Res(f"ps{i}", psum=True)) for i in range(8)]
    rot = {"w": 0, "wf": 0, "tmp": 0, "bt": 0, "S": 0, "O": 0, "Z": 0, "ev": 0}

    def nxt(kind, lst):
        i = rot[kind]
        rot[kind] = i + 1
        return lst[i % len(lst)]

    def ps_s():
        return nxt("S", psb[0:4])

    def ps_o():
        return nxt("O", psb[4:6])

    def ps_z():
        return nxt("Z", psb[6:8])

    def tmp():
        return nxt("tmp", tmps)

    def bt():
        return nxt("bt", bts)

    def evac_engine():
        rot["ev"] += 1
        return "act" if rot["ev"] % 2 else "dve"

    def sml(nm, l, j=0, n=128):
        c = soff[(nm, l)] + j
        return small.ap[0:n, c:c + 1]

    def load_w(src, a, b, pool=None):
        slot = nxt("w", wslots) if pool is None else nxt("wf", pool)
        npart = src.shape[0]
        dst = slot.ap[0:npart, 0:a * b].rearrange("p (a b) -> p a b", a=a)
        if b > 512:
            bb = max(d_ for d_ in range(1, 513) if b % d_ == 0)
            d2 = dst.rearrange("p a (b2 b) -> p a b2 b", b=bb)
            s2 = src.rearrange("p a (b2 b) -> p a b2 b", b=bb)
        else:
            d2, s2 = dst, src
        P.dma("pool", [lambda e: e.dma_start(out=d2, in_=s2)], writes=[slot], owner=slot)
        return V(dst, slot.res)

    def mm(out_v, mms, reads):
        def fn(e):
            ins = None
            for (o, lt, r, st, sp) in mms:
                ins = e.matmul(o, lhsT=lt, rhs=r, start=st, stop=sp)
            return ins
        P.op("pe", fn, reads=reads, writes=[out_v], pe_acc=True, nmm=len(mms))

    def act(out_v, out_ap, in_v, in_ap, func, scale=1.0, bias=None, extra_reads=()):
        kw = {}
        if bias is not None:
            kw["bias"] = bias
        P.op("act", lambda e: e.activation(out=out_ap, in_=in_ap, func=func, scale=scale, **kw),
             reads=[in_v] + list(extra_reads), writes=[out_v])

    def copy_any(out_v, out_ap, in_v, in_ap, scale=None):
        eng = evac_engine()
        if eng == "act":
            if scale is None:
                P.op("act", lambda e: e.activation(out=out_ap, in_=in_ap, func=AF.Identity), reads=[in_v], writes=[out_v])
            else:
                P.op("act", lambda e: e.activation(out=out_ap, in_=in_ap, func=AF.Identity, scale=scale), reads=[in_v], writes=[out_v])
        else:
            if scale is None:
                P.op("dve", lambda e: e.tensor_copy(out=out_ap, in_=in_ap), reads=[in_v], writes=[out_v])
            else:
                P.op("dve", lambda e: e.tensor_scalar(out=out_ap, in0=in_ap, scalar1=scale, scalar2=None, op0=ALU.mult),
                     reads=[in_v], writes=[out_v])

    def dve_tt(out_v, out_ap, a_v, a_ap, b_v, b_ap, op):
        P.op("dve", lambda e: e.tensor_tensor(out=out_ap, in0=a_ap, in1=b_ap, op=op), reads=[a_v, b_v], writes=[out_v])

    def dve_ts(out_v, out_ap, a_v, a_ap, s1, s2, op0, op1=None, extra=()):
        if op1 is None:
            P.op("dve", lambda e: e.tensor_scalar(out=out_ap, in0=a_ap, scalar1=s1, scalar2=None, op0=op0),
                 reads=[a_v] + list(extra), writes=[out_v])
        else:
            P.op("dve", lambda e: e.tensor_scalar(out=out_ap, in0=a_ap, scalar1=s1, scalar2=s2, op0=op0, op1=op1),
                 reads=[a_v] + list(extra), writes=[out_v])

    def dve_stt(out_v, out_ap, a_v, a_ap, sc, b_v, b_ap, op0, op1, extra=()):
        P.op("dve", lambda e: e.scalar_tensor_tensor(out=out_ap, in0=a_ap, scalar=sc, in1=b_ap, op0=op0, op1=op1),
             reads=[a_v, b_v] + list(extra), writes=[out_v])

    def ld(dst_v, dst_ap, src_ap, src_res):
        P.dma("sp", [lambda e: e.dma_start(out=dst_ap, in_=src_ap)], reads=[src_res], writes=[dst_v], owner=dst_v)

    def st(dst_ap, dst_res, src_v, src_ap):
        P.dma("sp", [lambda e: e.dma_start(out=dst_ap, in_=src_ap)], reads=[src_v], writes=[dst_res], owner=src_v, kind="st")

    def rstd_from_ps(ps, npart, dim):
        t = tmp()
        act(t, t.ap[0:npart, :], ps, ps.ap[0:npart, :], AF.Ln, scale=1.0 / dim, bias=misc.ap[0:npart, 0:1], extra_reads=[misc])
        act(t, t.ap[0:npart, :], t, t.ap[0:npart, :], AF.Exp, scale=-0.5)
        return t

    def sumsq_ps(srcs, npart):
        ps = ps_s()
        n = len(srcs)
        for i, (sv, sap) in enumerate(srcs):
            q = bt()
            act(q, q.ap[0:npart, :], sv, sap, AF.Square)
            mm(ps, [(ps.ap[:, :], ones_b[0:npart, :], q.ap[0:npart, :], i == 0, i == n - 1)], [q, cb])
        return ps

    P.dma("sp", [lambda e: e.dma_start(out=small.ap, in_=small_d)], writes=[small], owner=small)
    P.dma("sp", [lambda e: e.dma_start(out=cst.ap, in_=consts_d[:, 0:NCST_SB])], writes=[cst], owner=cst)
    P.op("dve", lambda e: e.memset(misc.ap[:, 0:1], EPS), writes=[misc])
    P.op("dve", lambda e: e.memset(misc.ap[:, 1:2], math.pi), writes=[misc])
    P.op("dve", lambda e: e.memset(misc.ap[:, 2:3], 1.0), writes=[misc])
    P.op("dve", lambda e: e.tensor_copy(out=cb.ap[:, 0:128], in_=cst.ap[:, 0:128]), reads=[cst], writes=[cb])
    P.op("dve", lambda e: e.memset(cb.ap[:, 128:256], 1.0), writes=[cb])
    P.op("dve", lambda e: e.tensor_copy(out=cb.ap[:, 256:384], in_=cst.ap[:, 128:256]), reads=[cst], writes=[cb])
    import os
    if "maskdma" not in os.environ.get("KSKIP", ""):
        P.dma("pool", [lambda e: e.dma_start(out=cb.ap[:, 384:384 + 2048].rearrange("p (j b) -> p j b", b=512),
                                             in_=consts_d[:, 258:258 + 2048].rearrange("p (j b) -> p j b", b=512))], writes=[cb], owner=cb)
        P.dma("pool", [lambda e: e.dma_start(out=cb.ap[:, 384 + 2048:384 + 2048 + 128], in_=consts_d[:, 258 + 2048:258 + 2048 + 128])], writes=[cb], owner=cb)

    import os
    KSKIP = os.environ.get("KSKIP", "")
    for (fc, npart, cn, sn) in [] if "rope" in KSKIP else ((CONST_COLS["freq_mla"], 64, "rCm", "rSm"), (CONST_COLS["freq_diff"], 128, "rCd", "rSd")):
        for b0 in range(0, S, 512):
            pi_ = bview(0, 1, I32)
            ld(pi_, pi_.ap[0:npart, 0:512], pos_d[0, b0:b0 + 512].partition_broadcast(npart), Res("posd"))
            ang = tmp()
            P.op("dve", lambda e, ang=ang, pi_=pi_, npart=npart: e.tensor_copy(out=ang.ap[0:npart, :], in_=pi_.ap[0:npart, 0:512]),
                 reads=[pi_], writes=[ang])
            dve_ts(ang, ang.ap[0:npart, :], ang, ang.ap[0:npart, :], cst.ap[0:npart, fc:fc + 1], None, ALU.mult, extra=[cst])
            for (shift, nm) in ((0.0, sn), (math.pi / 2, cn)):
                a2 = tmp()
                dve_ts(a2, a2.ap[0:npart, :], ang, ang.ap[0:npart, :], shift, None, ALU.add)
                kf = tmp()
                dve_ts(kf, kf.ap[0:npart, :], a2, a2.ap[0:npart, :], 1.0 / (2 * math.pi), None, ALU.mult)
                ki = bview(1, 1, I32)
                P.op("dve", lambda e, ki=ki, kf=kf, npart=npart: e.tensor_copy(out=ki.ap[0:npart, 0:512], in_=kf.ap[0:npart, :]), reads=[kf], writes=[ki])
                P.op("dve", lambda e, ki=ki, kf=kf, npart=npart: e.tensor_copy(out=kf.ap[0:npart, :], in_=ki.ap[0:npart, 0:512]), reads=[ki], writes=[kf])
                r = tmp()
                dve_stt(r, r.ap[0:npart, :], kf, kf.ap[0:npart, :], -2 * math.pi, a2, a2.ap[0:npart, :], ALU.mult, ALU.add)
                dve_ts(kf, kf.ap[0:npart, :], r, r.ap[0:npart, :], math.pi, 2 * math.pi, ALU.is_gt, ALU.mult)
                dve_tt(r, r.ap[0:npart, :], r, r.ap[0:npart, :], kf, kf.ap[0:npart, :], ALU.subtract)
                dve_ts(kf, kf.ap[0:npart, :], r, r.ap[0:npart, :], -math.pi, 2 * math.pi, ALU.is_lt, ALU.mult)
                dve_tt(r, r.ap[0:npart, :], r, r.ap[0:npart, :], kf, kf.ap[0:npart, :], ALU.add)
                dve_ts(r, r.ap[0:npart, :], r, r.ap[0:npart, :], 3.141592, -3.141592, ALU.min, ALU.max)
                act(r, r.ap[0:npart, :], r, r.ap[0:npart, :], AF.Sin)
                st(SC[nm][0:npart, b0:b0 + 512], SCR[nm], r, r.ap[0:npart, :])

    for mt in range(0 if "mem" in KSKIP else MEM // 128):
        pcs = []
        for q4 in range(4):
            mtile = tmp()
            ld(mtile, mtile.ap, mem_d[mt * 128:(mt + 1) * 128, q4 * 512:(q4 + 1) * 512], Res("memd"))
            junk = bt()
            P.op("act", lambda e, mtile=mtile, junk=junk, q4=q4: e.activation(out=junk.ap, in_=mtile.ap, func=AF.Square, accum_out=misc.ap[:, 4 + q4:5 + q4]),
                 reads=[mtile], writes=[junk, misc])
            pcs.append(mtile)
        for q4 in range(1, 4):
            dve_tt(misc, misc.ap[:, 4:5], misc, misc.ap[:, 4:5], misc, misc.ap[:, 4 + q4:5 + q4], ALU.add)
        act(misc, misc.ap[:, 5:6], misc, misc.ap[:, 4:5], AF.Sqrt, scale=1.0 / D, bias=misc.ap[:, 0:1])
        P.op("dve", lambda e: e.reciprocal(out=misc.ap[:, 5:6], in_=misc.ap[:, 5:6]), reads=[misc], writes=[misc])
        for q4 in range(4):
            mtile = pcs[q4]
            dve_ts(mtile, mtile.ap, mtile, mtile.ap, misc.ap[:, 5:6], None, ALU.mult, extra=[misc])
            ps = ps_s()
            for j in range(4):
                P.op("pe", lambda e, ps=ps, j=j, mtile=mtile: e.matmul(ps.ap[:, j * 128:(j + 1) * 128], lhsT=mtile.ap[:, j * 128:(j + 1) * 128], rhs=ident_f, start=True, stop=True),
                     reads=[mtile, cst], writes=[ps], pe_acc=True, nmm=1)
            o = bt()
            c0 = q4 * 4
            for j in range(4):
                c = c0 + j
                dve_ts(o, o.ap[:, j * 128:(j + 1) * 128], ps, ps.ap[:, j * 128:(j + 1) * 128], sml("g_mem", 0, c), None, ALU.mult, extra=[small])
            st(SC["memT"][c0 * 128:(c0 + 4) * 128, mt * 128:(mt + 1) * 128].rearrange("(j p) t -> p j t", p=128), SCR["memT"],
               o, o.ap.rearrange("p (j t) -> p j t", j=4))

    def norm_to_hT(gname, l):
        for tb in range(NTB):
            ps = sumsq_ps([(xs[c][tb], xs[c][tb].ap) for c in range(NCH)], 128)
            r = rstd_from_ps(ps, 128, D)
            for c in range(NCH):
                dve_stt(hT[c][tb], hT[c][tb].ap, xs[c][tb], xs[c][tb].ap, sml(gname, l, c), r, r.ap, ALU.mult, ALU.mult, extra=[small])

    def proj_fm(wv, wsel, nk, rhs, tb, npart_out=128):
        ps = ps_s()
        mms = []
        for k in range(nk):
            rv, rap = rhs[k]
            mms.append((ps.ap[0:npart_out, :], wsel(k), rap, k == 0, k == nk - 1))
        mm(ps, mms, [wv] + [rv for rv, _ in rhs])
        return ps

    def hrhs(tb):
        return [(hT[c][tb], hT[c][tb].ap) for c in range(NCH)]

    def wo_accumulate(l, wd, row0, half):
        for cg in range(2):
            wv = load_w(wd[l, row0:row0 + 512, cg * 1024:(cg + 1) * 1024].rearrange("(k p) j -> p k j", p=128), 4, 1024)
            for dtl in range(8):
                dt_ = cg * 8 + dtl
                for tb in range(NTB):
                    ps = nxt("O", psb[4:8])
                    mm(ps, [(ps.ap, wv.ap[:, k, dtl * 128:(dtl + 1) * 128], yT[k][tb].ap, k == 0, k == 3) for k in range(4)],
                       [wv] + [yT[k][tb] for k in range(4)])
                    dve_tt(xs[dt_][tb], xs[dt_][tb].ap, ps, ps.ap, xs[dt_][tb], xs[dt_][tb].ap, ALU.add)

    def store_fm(ps, npart, dst_ap, dst_res, scale=None):
        o = bt()
        copy_any(o, o.ap[0:npart, :], ps, ps.ap[0:npart, :], scale)
        st(dst_ap, dst_res, o, o.ap[0:npart, :])

    def proj_tm_store(l, wsrc_cols, dst_key, half, rhs_sel=None):
        for u in range(2):
            c0 = wsrc_cols + u * 256
            wv = load_w(w_in[l, :, c0:c0 + 256].rearrange("(c p) j -> p c j", p=128), NCH, 256)
            for tt in range(Th // 128):
                tb, o_ = divmod(tt, 4)
                ps = ps_s()
                mm(ps, [(ps.ap[:, 0:256], hT[c][tb].ap[:, o_ * 128:(o_ + 1) * 128], wv.ap[:, c, :], c == 0, c == NCH - 1) for c in range(NCH)],
                   [wv] + [hT[c][tb] for c in range(NCH)])
                o = bt()
                copy_any(o, o.ap[:, 0:256], ps, ps.ap[:, 0:256])
                r0 = half * Th + tt * 128
                st(SC[(dst_key, l)][r0:r0 + 128, u * 256:(u + 1) * 256], SCR[(dst_key, l)], o, o.ap[:, 0:256])

    qbuf = [bview(0, 1), bview(1, 1)]
    kbuf = [bview(2, 1), bview(3, 1)]
    vbuf = [bview(4, 1), bview(5, 1)]
    qrbuf = [bview(6, 1), bview(7, 1)]
    krbuf = bview(8, 1)
    Abuf = bview(9, 1)
    Bbuf = bview(10, 1)
    rot["hb"] = 0

    fin = {"st": None}

    def fin_stage1():
        f = fin["st"]
        if f is None or f.get("rec") is not None:
            return
        rec = tmp()
        act(rec, rec.ap, f["pz"], f["pz"].ap, AF.Ln)
        f["rec"] = rec

    def fin_stage2():
        f = fin["st"]
        if f is None:
            return
        fin_stage1()
        rec = f["rec"]
        act(rec, rec.ap, rec, rec.ap, AF.Exp, scale=-1.0)
        fin["st"] = None
        f["consume"](f["po"], rec)

    def attention(nkeys_fn, score_mms, vsel, consume, tb, extra_reads, post=None):
        t0 = tb_global(tb)
        nkt = (t0 + 512) // 128
        po, pz = ps_o(), ps_z()
        pss = {}
        tri = mask_b[0][:, 0:128]

        def c0_of(kt):
            return max(0, kt - t0 // 128) * 128

        def score(kt):
            ps = ps_s()
            mm(ps, score_mms(kt, ps, c0_of(kt)), extra_reads)
            pss[kt] = ps
        LOOK = 2
        for k in range(min(LOOK, nkt)):
            score(k)
        for kt in range(nkt):
            if kt + LOOK < nkt:
                score(kt + LOOK)
            ps = pss.pop(kt)
            pt = bt()
            j = kt - t0 // 128
            c0 = c0_of(kt)
            if j >= 0 and nkeys_fn == "premask":
                P.op("dve", lambda e, ps=ps, c0=c0: e.scalar_tensor_tensor(out=ps.ap[:, c0:c0 + 128], in0=tri, scalar=60000.0, in1=ps.ap[:, c0:c0 + 128],
                                                                    op0=ALU.mult, op1=ALU.min),
                     reads=[cb, ps], writes=[ps])
            act(pt, pt.ap[:, c0:512], ps, ps.ap[:, c0:512], AF.Exp)
            if kt == 1:
                fin_stage1()
            if j >= 0:
                dve_tt(pt, pt.ap[:, c0:c0 + 128], pt, pt.ap[:, c0:c0 + 128], cb, tri, ALU.mult)
            mm(po, [(po.ap[:, c0:512], vsel(kt), pt.ap[:, c0:512], kt == 0, kt == nkt - 1)], [pt] + extra_reads)
            mm(pz, [(pz.ap[:, c0:512], ones_b, pt.ap[:, c0:512], kt == 0, kt == nkt - 1)], [pt, cb])
            if kt == 2:
                fin_stage2()
        assert fin["st"] is None
        fin["st"] = {"po": po, "pz": pz, "consume": consume}
        if "F" not in KOPT:
            fin_stage2()

    cur = {"half": 0}

    def tb_global(tb):
        return cur["half"] * Th + tb * 512

    def load_kv(l, kkey, vkey, h, i, nkeys):
        kb, vb = kbuf[i], vbuf[i]
        ld(kb, kb.ap[:, 0:nkeys], SC[(kkey, l)][h * 128:(h + 1) * 128, 0:nkeys], SCR[(kkey, l)])
        vdst = vb.ap[:, 0:(nkeys // 128) * 128].rearrange("p (t d) -> p t d", d=128)
        ld(vb, vdst, SC[(vkey, l)][0:nkeys, h * 128:(h + 1) * 128].rearrange("(t p) d -> p t d", p=128), SCR[(vkey, l)])
        return kb, vb, vdst

    def load_q(l, qkey, h, i, half):
        qb = qbuf[i]
        ld(qb, qb.ap[:, 0:Th], SC[(qkey, l)][h * 128:(h + 1) * 128, half * Th:(half + 1) * Th], SCR[(qkey, l)])
        return qb

    def fox(l, half, part):
        isq = 128 ** -0.5
        if part == "proj":
            wv = load_w(w_in[l, :, 1536:1540].rearrange("(c p) j -> p c j", p=128), NCH, 4)
            carry = V(misc.ap[0:4, 8 + l:9 + l], misc.res)
            if half == 0:
                P.op("dve", lambda e: e.memset(misc.ap[0:4, 8 + l:9 + l], 0.0), writes=[misc])
            for tb in range(NTB):
                ps = proj_fm(wv, lambda k, wv=wv: wv.ap[:, k, 0:4], NCH, hrhs(tb), tb, npart_out=4)
                nbf = V(misc.ap[0:4, 3:4], misc.res)
                dve_ts(misc, misc.ap[0:4, 3:4], small, sml("b_f", l, 0, 4), -1.0, None, ALU.mult)
                e1 = tmp()
                act(e1, e1.ap[0:4, :], ps, ps.ap[0:4, :], AF.Exp, scale=-1.0, bias=misc.ap[0:4, 3:4], extra_reads=[misc])
                act(e1, e1.ap[0:4, :], e1, e1.ap[0:4, :], AF.Ln, scale=1.0, bias=misc.ap[0:4, 2:3], extra_reads=[misc])
                dve_ts(e1, e1.ap[0:4, :], e1, e1.ap[0:4, :], -1.0, None, ALU.mult)
                onesf = tmp()
                P.op("dve", lambda e, onesf=onesf: e.memset(onesf.ap[0:4, :], 1.0), writes=[onesf])
                Ft = tmp()
                P.op("dve", lambda e, Ft=Ft, onesf=onesf, e1=e1: e.tensor_tensor_scan(
                    out=Ft.ap[0:4, :], data0=onesf.ap[0:4, :], data1=e1.ap[0:4, :], initial=misc.ap[0:4, 8 + l:9 + l],
                    op0=ALU.mult, op1=ALU.add), reads=[onesf, e1, misc], writes=[Ft])
                P.op("dve", lambda e, Ft=Ft: e.tensor_copy(out=misc.ap[0:4, 8 + l:9 + l], in_=Ft.ap[0:4, 511:512]), reads=[Ft], writes=[misc])
                g0 = tb_global(tb)
                resid = Ft
                for part in range(3):
                    hp = bt()
                    P.op("dve", lambda e, hp=hp, resid=resid: e.tensor_copy(out=hp.ap[0:4, :], in_=resid.ap[0:4, :]), reads=[resid], writes=[hp])
                    hn = bt()
                    dve_ts(hn, hn.ap[0:4, :], hp, hp.ap[0:4, :], -1.0, None, ALU.mult)
                    st(SC[("fFp", l)][:, part, g0:g0 + 512], SCR[("fFp", l)], hp, hp.ap[0:4, :])
                    st(SC[("fFn", l)][:, part, g0:g0 + 512], SCR[("fFn", l)], hn, hn.ap[0:4, :])
                    if part < 2:
                        nr = tmp()
                        dve_tt(nr, nr.ap[0:4, :], resid, resid.ap[0:4, :], hp, hp.ap[0:4, :], ALU.subtract)
                        resid = nr
            for (c0, key, scale) in ((0, "fq", isq), (512, "fk", None)):
                for u in range(2):
                    wv = load_w(w_in[l, :, c0 + u * 256:c0 + (u + 1) * 256].rearrange("(c p) j -> p c j", p=128), NCH, 256)
                    for t in range(2):
                        for tb in range(NTB):
                            ps = proj_fm(wv, lambda k, t=t, wv=wv: wv.ap[:, k, t * 128:(t + 1) * 128], NCH, hrhs(tb), tb)
                            r0 = (u * 2 + t) * 128
                            g0 = tb_global(tb)
                            store_fm(ps, 128, SC[(key, l)][r0:r0 + 128, g0:g0 + 512], SCR[(key, l)], scale)
            proj_tm_store(l, 1024, "fv", half)

            return
        nkeys = (half + 1) * Th
        P.op("dve", lambda e: e.memset(Abuf.ap[:, :], 0.0), writes=[Abuf])
        P.op("dve", lambda e: e.memset(Bbuf.ap[:, :], 0.0), writes=[Bbuf])
        P.op("dve", lambda e: e.memset(Abuf.ap[0:6, :], 1.0), writes=[Abuf])
        P.op("dve", lambda e: e.memset(Bbuf.ap[0:6, :], 1.0), writes=[Bbuf])
        for h in range(4):
            i = h % 2
            kb, vb, vdst = load_kv(l, "fk", "fv", h, i, nkeys)
            qb = load_q(l, "fq", h, i, half)
            ld(Abuf, Abuf.ap[3:6, 0:nkeys], SC[("fFn", l)][h, :, 0:nkeys], SCR[("fFn", l)])
            ld(Bbuf, Bbuf.ap[0:3, 0:Th], SC[("fFp", l)][h, :, half * Th:(half + 1) * Th], SCR[("fFp", l)])
            for tb in range(NTB):
                def smm(kt, ps, c0, kb=kb, qb=qb, tb=tb):
                    return [(ps.ap[:, c0:512], kb.ap[:, kt * 128:(kt + 1) * 128], qb.ap[:, tb * 512 + c0:(tb + 1) * 512], True, False),
                            (ps.ap[:, c0:512], Abuf.ap[:, kt * 128:(kt + 1) * 128], Bbuf.ap[:, tb * 512 + c0:(tb + 1) * 512], False, True)]
                attention("premask", smm, lambda kt, vdst=vdst: vdst[:, kt, :],
                          lambda po, rec, h=h, tb=tb: dve_tt(yT[h][tb], yT[h][tb].ap, po, po.ap, rec, rec.ap, ALU.mult),
                          tb, [kb, qb, vb, Abuf, Bbuf])
        fin_stage2()
        wo_accumulate(l, w_o, 0, half)

    def rope_store(pa, pb, npart, Cv, Sv, tb, dst_ap, dst_res, scale):
        c_ap = Cv.ap[0:npart, tb * 512:(tb + 1) * 512]
        s_ap = Sv.ap[0:npart, tb * 512:(tb + 1) * 512]
        t1, t2 = tmp(), tmp()
        dve_tt(t1, t1.ap[0:npart, :], pa, pa.ap[0:npart, :], Cv, c_ap, ALU.mult)
        dve_tt(t2, t2.ap[0:npart, :], pb, pb.ap[0:npart, :], Sv, s_ap, ALU.mult)
        o = bt()
        if scale is None:
            dve_tt(o, o.ap[0:npart, :], t1, t1.ap[0:npart, :], t2, t2.ap[0:npart, :], ALU.add)
        else:
            dve_tt(t1, t1.ap[0:npart, :], t1, t1.ap[0:npart, :], t2, t2.ap[0:npart, :], ALU.add)
            act(o, o.ap[0:npart, :], t1, t1.ap[0:npart, :], AF.Identity, scale=scale)
        st(dst_ap, dst_res, o, o.ap[0:npart, :])

    def load_rope_tables(Cn, Sn, npart, Cv, Sv, half):
        ld(Cv, Cv.ap[0:npart, 0:Th], SC[Cn][0:npart, half * Th:(half + 1) * Th], SCR[Cn])
        ld(Sv, Sv.ap[0:npart, 0:Th], SC[Sn][0:npart, half * Th:(half + 1) * Th], SCR[Sn])

    def mla(l, half, part):
        isq = 192 ** -0.5
        if part == "proj":
            lat = [V(bview(i, 1, F32).ap[:, 0:512], [blk_res[i]]) for i in range(5)]
            cqn = bview(5, 2)
            ckvn = bview(7, 1)
            krot = V(blk_ap[:, 8 * 2048:8 * 2048 + 1024].rearrange("p (c j) -> p c j", c=NCH), [blk_res[8]])
            rt = V(blk_ap[:, 9 * 2048 + 1024:9 * 2048 + 2048].bitcast(F32), [blk_res[9]])
            uqrot_ap = blk_ap[:, 9 * 2048:9 * 2048 + 768].rearrange("p (kh d) -> p kh d", d=64)
            uqrot = V(uqrot_ap, [blk_res[9]])
            Cm, Sm = bview(10, 1, F32), bview(11, 1, F32)
            load_rope_tables("rCm", "rSm", 64, Cm, Sm, half)
            wq = load_w(w_in[l, :, 1540:1796].rearrange("(c p) j -> p c j", p=128), NCH, 256)
            wq2 = load_w(w_in[l, :, 1796:2052].rearrange("(c p) j -> p c j", p=128), NCH, 256)
            wq3 = load_w(w_in[l, :, 2052:2244].rearrange("(c p) j -> p c j", p=128), NCH, 192)

            def wcol(j):
                u, o_ = divmod(j * 128, 256)
                return (wq, wq2, wq3)[u], o_
            krw = wq3.ap[:, :, 128:192]
            P.op("dve", lambda e: e.tensor_scalar(out=krot.ap[:, :, 0:32], in0=krw[:, :, 32:64], scalar1=-1.0, scalar2=None, op0=ALU.mult),
                 reads=[wq3], writes=[krot])
            P.op("dve", lambda e: e.tensor_copy(out=krot.ap[:, :, 32:64], in_=krw[:, :, 0:32]), reads=[wq3], writes=[krot])
            for tb in range(NTB):
                g0 = tb_global(tb)
                for j in range(5):
                    wv_, o_ = wcol(j)
                    ps = proj_fm(wv_, lambda k, wv_=wv_, o_=o_: wv_.ap[:, k, o_:o_ + 128], NCH, hrhs(tb), tb)
                    copy_any(lat[j], lat[j].ap, ps, ps.ap)
                for (idx, dim, gnm, dstv) in (((0, 1, 2), 384, "g_cq", cqn), ((3, 4), 256, "g_ckv", ckvn)):
                    ps = sumsq_ps([(lat[i], lat[i].ap) for i in idx], 128)
                    act(rt, rt.ap, ps, ps.ap, AF.Ln, scale=1.0 / dim, bias=misc.ap[:, 0:1], extra_reads=[misc])
                    act(rt, rt.ap, rt, rt.ap, AF.Exp, scale=-0.5)
                    for n_, i in enumerate(idx):
                        dve_stt(dstv, dstv.ap[:, n_ * Th + tb * 512:n_ * Th + (tb + 1) * 512], lat[i], lat[i].ap, sml(gnm, l, n_), rt, rt.ap,
                                ALU.mult, ALU.mult, extra=[small])
                pa = proj_fm(wq3, lambda k: wq3.ap[:, k, 128:192], NCH, hrhs(tb), tb, npart_out=64)
                pb = proj_fm(krot, lambda k: krot.ap[:, k, :], NCH, hrhs(tb), tb, npart_out=64)
                rope_store(pa, pb, 64, Cm, Sm, tb, SC[("mkr", l)][:, g0:g0 + 512], SCR[("mkr", l)], None)
            wuq = load_w(w_uq[l].rearrange("(k p) j -> p k j", p=128), 3, 768)
            wukv = load_w(w_ukv[l].rearrange("(k p) j -> p k j", p=128), 2, 1024)
            uq4 = wuq.ap.rearrange("p k (h d) -> p (k h) d", d=192)
            P.op("dve", lambda e: e.tensor_scalar(out=uqrot_ap[:, :, 0:32], in0=uq4[:, :, 160:192], scalar1=-1.0, scalar2=None, op0=ALU.mult),
                 reads=[wuq], writes=[uqrot])
            P.op("dve", lambda e: e.tensor_copy(out=uqrot_ap[:, :, 32:64], in_=uq4[:, :, 128:160]), reads=[wuq], writes=[uqrot])
            wv4 = wukv.ap.rearrange("p k (h two d) -> p k h two d", two=2, d=128)
            for tb in range(NTB):
                g0 = tb_global(tb)
                cq_r = [(cqn, cqn.ap[:, k * Th + tb * 512:k * Th + (tb + 1) * 512]) for k in range(3)]
                ckv_r = [(ckvn, ckvn.ap[:, k * Th + tb * 512:k * Th + (tb + 1) * 512]) for k in range(2)]
                for h in range(4):
                    ps = proj_fm(wuq, lambda k, h=h: wuq.ap[:, k, h * 192:h * 192 + 128], 3, cq_r, tb)
                    store_fm(ps, 128, SC[("mqn", l)][h * 128:(h + 1) * 128, g0:g0 + 512], SCR[("mqn", l)], isq)
                    pa = proj_fm(wuq, lambda k, h=h: wuq.ap[:, k, h * 192 + 128:h * 192 + 192], 3, cq_r, tb, npart_out=64)
                    pb = proj_fm(uqrot, lambda k, h=h: uqrot_ap[:, k * 4 + h, :], 3, cq_r, tb, npart_out=64)
                    rope_store(pa, pb, 64, Cm, Sm, tb, SC[("mqr", l)][h * 64:(h + 1) * 64, g0:g0 + 512], SCR[("mqr", l)], isq)
                    ps = proj_fm(wukv, lambda k, h=h: wukv.ap[:, k, h * 256:h * 256 + 128], 2, ckv_r, tb)
                    store_fm(ps, 128, SC[("mkn", l)][h * 128:(h + 1) * 128, g0:g0 + 512], SCR[("mkn", l)])
                for o_ in range(4):
                    ps = ps_s()
                    mm(ps, [(ps.ap.rearrange("p (h d) -> p h d", d=128),
                             ckvn.ap[:, k * Th + tb * 512 + o_ * 128:k * Th + tb * 512 + (o_ + 1) * 128], wv4[:, k, :, 1, :], k == 0, k == 1)
                            for k in range(2)], [wukv, ckvn])
                    o = bt()
                    copy_any(o, o.ap, ps, ps.ap)
                    r0 = g0 + o_ * 128
                    st(SC[("mv", l)][r0:r0 + 128, :], SCR[("mv", l)], o, o.ap)

            return
        nkeys = (half + 1) * Th
        P.op("dve", lambda e: e.memset(krbuf.ap[64:128, :], 0.0), writes=[krbuf])
        for qr_ in qrbuf:
            P.op("dve", lambda e, qr_=qr_: e.memset(qr_.ap[64:128, :], 0.0), writes=[qr_])
        ld(krbuf, krbuf.ap[0:64, 0:nkeys], SC[("mkr", l)][:, 0:nkeys], SCR[("mkr", l)])
        for h in range(4):
            i = h % 2
            kb, vb, vdst = load_kv(l, "mkn", "mv", h, i, nkeys)
            qb = load_q(l, "mqn", h, i, half)
            qr = qrbuf[i]
            ld(qr, qr.ap[0:64, 0:Th], SC[("mqr", l)][h * 64:(h + 1) * 64, half * Th:(half + 1) * Th], SCR[("mqr", l)])
            for tb in range(NTB):
                def smm(kt, ps, c0, kb=kb, qb=qb, qr=qr, tb=tb):
                    return [(ps.ap[:, c0:512], kb.ap[:, kt * 128:(kt + 1) * 128], qb.ap[:, tb * 512 + c0:(tb + 1) * 512], True, False),
                            (ps.ap[:, c0:512], krbuf.ap[:, kt * 128:(kt + 1) * 128], qr.ap[:, tb * 512 + c0:(tb + 1) * 512], False, True)]
                attention(None, smm, lambda kt, vdst=vdst: vdst[:, kt, :],
                          lambda po, rec, h=h, tb=tb: dve_tt(yT[h][tb], yT[h][tb].ap, po, po.ap, rec, rec.ap, ALU.mult),
                          tb, [kb, qb, vb, qr, krbuf])
        fin_stage2()
        wo_accumulate(l, w_o, 512, half)

    sgst_ap = sb("sgst", [128, 8 * (Th // 128)], F32)
    sgst = [V(sgst_ap[:, i * 8:(i + 1) * 8], Res(f"sgst{i}")) for i in range(Th // 128)]

    def sgu(l, half):
        NT = Th // 128
        uT = bview(0, 2)
        bsb = V(bview(2, 1, F32).ap[:, 0:512], [blk_res[2]])
        lng = V(bview(3, 1, F32).ap[:, 0:512], [blk_res[3]])
        lnb = V(bview(4, 1, F32).ap[:, 0:512], [blk_res[4]])
        wsT = V(blk_ap[:, 5 * 2048:5 * 2048 + 512], [blk_res[5]])
        wsn = V(blk_ap[:, 5 * 2048 + 1024:5 * 2048 + 2048].bitcast(F32), [blk_res[5]])
        vt = [V(blk_ap[:, (6 + i // 2) * 2048 + (i % 2) * 1024:(6 + i // 2) * 2048 + (i % 2 + 1) * 1024].bitcast(F32), Res(f"vt{i}"))
              for i in range(NT)]
        vnb = [V(blk_ap[:, 10 * 2048 + i * 512:10 * 2048 + (i + 1) * 512], [Res(f"vnb{i}")]) for i in range(4)]
        P.handoff(blk_res[6:11], vt + vnb)
        ld(bsb, bsb.ap, b_s[l, :].partition_broadcast(128), Res("bsd"))
        ld(lng, lng.ap, sgu_ln_g[l, :].partition_broadcast(128), Res("lngd"))
        ld(lnb, lnb.ap, sgu_ln_b[l, :].partition_broadcast(128), Res("lnbd"))
        ld(wsn, wsn.ap.rearrange("p (g s) -> p g s", g=4), w_s[l].rearrange("g t s -> t g s"), Res("wsd"))
        built = [False]

        def build_wsT():
            if built[0]:
                return
            built[0] = True
            wsm = bt()
            for g in range(4):
                dve_tt(wsm, wsm.ap[:, g * 128:(g + 1) * 128], wsn, wsn.ap[:, g * 128:(g + 1) * 128], cst, cst.ap[:, 128:256], ALU.mult)
            pst = ps_s()
            for g in range(4):
                P.op("pe", lambda e, g=g: e.matmul(pst.ap[:, g * 128:(g + 1) * 128], lhsT=wsm.ap[:, g * 128:(g + 1) * 128], rhs=ident_b, start=True, stop=True),
                     reads=[wsm, cb], writes=[pst], pe_acc=True, nmm=1)
            P.op("dve", lambda e: e.tensor_copy(out=wsT.ap, in_=pst.ap), reads=[pst], writes=[wsT])
        wu = [load_w(w_in[l, :, 2244 + u * 256:2244 + (u + 1) * 256].rearrange("(c p) j -> p c j", p=128), NCH, 256) for u in range(2)]
        wvv = [load_w(w_in[l, :, 2756 + u * 256:2756 + (u + 1) * 256].rearrange("(c p) j -> p c j", p=128), NCH, 256) for u in range(2)]
        pend = []

        def mix(item):
            tt, vn = item
            tb, o_ = divmod(tt, 4)
            build_wsT()
            ps = ps_s()
            for g in range(4):
                mm(ps, [(ps.ap[:, g * 128:(g + 1) * 128], vn.ap[:, g * 128:(g + 1) * 128], wsT.ap[:, g * 128:(g + 1) * 128], True, True)], [vn, wsT])
            t2 = tmp()
            dve_tt(t2, t2.ap, ps, ps.ap, bsb, bsb.ap, ALU.add)
            y3 = yT_ap[:, :, tb * 512 + o_ * 128:tb * 512 + (o_ + 1) * 128]
            u3 = uT.ap[:, 0:4 * Th].rearrange("p (g t) -> p g t", g=4)[:, :, tt * 128:(tt + 1) * 128]
            P.op("dve", lambda e, y3=y3, t2=t2, u3=u3: e.tensor_tensor(out=y3, in0=t2.ap.rearrange("p (g t) -> p g t", g=4), in1=u3, op=ALU.mult),
                 reads=[t2, uT], writes=[yT[g][tb] for g in range(4)])
        for tb in range(NTB):
            for g in range(4):
                wv_ = wu[g // 2]
                ps = proj_fm(wv_, lambda k, wv_=wv_, g=g: wv_.ap[:, k, (g % 2) * 128:(g % 2) * 128 + 128], NCH, hrhs(tb), tb)
                act(uT, uT.ap[:, g * Th + tb * 512:g * Th + (tb + 1) * 512], ps, ps.ap, AF.Gelu_apprx_tanh)
            for o_ in range(4):
                tt = tb * 4 + o_
                t, stt_ = vt[tt], sgst[tt]
                ps = ps_s()
                for u in range(2):
                    mm(ps, [(ps.ap[:, u * 256:(u + 1) * 256], hT[c][tb].ap[:, o_ * 128:(o_ + 1) * 128], wvv[u].ap[:, c, :], c == 0, c == NCH - 1) for c in range(NCH)],
                       [wvv[u]] + [hT[c][tb] for c in range(NCH)])
                P.op("act", lambda e, t=t, ps=ps, stt_=stt_: e.activation(out=t.ap, in_=ps.ap, func=AF.Gelu_apprx_tanh, accum_out=stt_.ap[:, 0:1]),
                     reads=[ps], writes=[t, stt_])
                junk = bt()
                P.op("act", lambda e, t=t, junk=junk, stt_=stt_: e.activation(out=junk.ap, in_=t.ap, func=AF.Square, accum_out=stt_.ap[:, 1:2]),
                     reads=[t], writes=[junk, stt_])
                dve_ts(stt_, stt_.ap[:, 2:4], stt_, stt_.ap[:, 0:2], 1.0 / 512, None, ALU.mult)
                dve_stt(stt_, stt_.ap[:, 4:5], stt_, stt_.ap[:, 2:3], stt_.ap[:, 2:3], stt_, stt_.ap[:, 3:4], ALU.mult, ALU.subtract)
                act(stt_, stt_.ap[:, 5:6], stt_, stt_.ap[:, 4:5], AF.Sqrt, scale=-1.0, bias=misc.ap[:, 0:1], extra_reads=[misc])
                P.op("dve", lambda e, stt_=stt_: e.reciprocal(out=stt_.ap[:, 5:6], in_=stt_.ap[:, 5:6]), reads=[stt_], writes=[stt_])
                dve_ts(t, t.ap, t, t.ap, stt_.ap[:, 2:3], stt_.ap[:, 5:6], ALU.subtract, ALU.mult, extra=[stt_])
                dve_tt(t, t.ap, t, t.ap, lng, lng.ap, ALU.mult)
                vn = vnb[tt % 4]
                dve_tt(vn, vn.ap, t, t.ap, lnb, lnb.ap, ALU.add)
                pend.append((tt, vn))
                if len(pend) > 2:
                    mix(pend.pop(0))
        while pend:
            mix(pend.pop(0))
        P.handoff(vt + vnb, blk_res[6:11])
        wo_accumulate(l, w_o, 1024, half)

    def diff(l, half, part):
        lam_init = 0.8 - 0.6 * math.exp(-0.3 * l)
        isq = 64 ** -0.5
        if part == "proj":
            Cd, Sd = bview(0, 1, F32), bview(1, 1, F32)
            load_rope_tables("rCd", "rSd", 128, Cd, Sd, half)
            pend = []

            def finish(item):
                pa, q16, key, r0, tb, scale = item
                g0 = tb_global(tb)
                pb = ps_s()
                mm(pb, [(pb.ap, Rd_b, q16.ap, True, True)], [q16, cb])
                rope_store(pa, pb, 128, Cd, Sd, tb, SC[(key, l)][r0:r0 + 128, g0:g0 + 512], SCR[(key, l)], scale)
            for (c0, key, scale) in ((3268, "dq", isq), (3780, "dk", None)):
                for u in range(2):
                    wv = load_w(w_in[l, :, c0 + u * 256:c0 + (u + 1) * 256].rearrange("(c p) j -> p c j", p=128), NCH, 256)
                    for t in range(2):
                        for tb in range(NTB):
                            pa = proj_fm(wv, lambda k, t=t, wv=wv: wv.ap[:, k, t * 128:(t + 1) * 128], NCH, hrhs(tb), tb)
                            q16 = bt()
                            act(q16, q16.ap, pa, pa.ap, AF.Identity)
                            pend.append((pa, q16, key, (u * 2 + t) * 128, tb, scale))
                            if len(pend) > 1:
                                finish(pend.pop(0))
            while pend:
                finish(pend.pop(0))
            proj_tm_store(l, 4292, "dv", half)

            return
        la = tmp()
        for i, (a, b) in enumerate((("lam_q1", "lam_k1"), ("lam_q2", "lam_k2"))):
            ld(la, la.ap[:, 0:64], lam_d[a][l, :].partition_broadcast(128), Res("lamd"))
            ld(la, la.ap[:, 64:128], lam_d[b][l, :].partition_broadcast(128), Res("lamd"))
            dve_tt(la, la.ap[:, 128:192], la, la.ap[:, 0:64], la, la.ap[:, 64:128], ALU.mult)
            P.op("act", lambda e, i=i: e.activation(out=la.ap[:, 192:256], in_=la.ap[:, 128:192], func=AF.Identity, accum_out=misc.ap[:, 13 + i:14 + i]),
                 reads=[la], writes=[la, misc])
            act(misc, misc.ap[:, 13 + i:14 + i], misc, misc.ap[:, 13 + i:14 + i], AF.Exp)
        dve_tt(misc, misc.ap[:, 15:16], misc, misc.ap[:, 14:15], misc, misc.ap[:, 13:14], ALU.subtract)
        dve_ts(misc, misc.ap[:, 15:16], misc, misc.ap[:, 15:16], -lam_init, None, ALU.add)
        neglam = misc.ap[:, 15:16]
        nkeys = (half + 1) * Th
        o1b = [V(blk_ap[:, 6 * 2048 + i_ * 1024:6 * 2048 + (i_ + 1) * 1024].bitcast(F32), [Res(f"o1b{i_}")]) for i_ in range(2)]
        P.handoff([blk_res[6]], o1b)
        pendn = []

        def post(item):
            o1, h, tb = item
            ps = sumsq_ps([(o1, o1.ap)], 128)
            r = rstd_from_ps(ps, 128, 128)
            dve_stt(o1, o1.ap, o1, o1.ap, sml("g_diff", l, 0), r, r.ap, ALU.mult, ALU.mult, extra=[small])
            dve_ts(yT[h][tb], yT[h][tb].ap, o1, o1.ap, 1.0 - lam_init, None, ALU.mult)
        nq = 0
        qm1 = [bview(7, 1), bview(8, 1)]
        for i_ in range(2):
            P.op("dve", lambda e, i_=i_: e.memset(qbuf[i_].ap[64:128, :], 0.0), writes=[qbuf[i_]])
            P.op("dve", lambda e, i_=i_: e.memset(qm1[i_].ap[0:64, :], 0.0), writes=[qm1[i_]])
        for h in range(4):
            i = h % 2
            kb, vb, vdst = load_kv(l, "dk", "dv", h, i, nkeys)
            qms = (qbuf[i], qm1[i])
            ld(qms[0], qms[0].ap[0:64, 0:Th], SC[("dq", l)][h * 128:h * 128 + 64, half * Th:(half + 1) * Th], SCR[("dq", l)])
            ld(qms[1], qms[1].ap[64:128, 0:Th], SC[("dq", l)][h * 128 + 64:(h + 1) * 128, half * Th:(half + 1) * Th], SCR[("dq", l)])
            for tb in range(NTB):
                o1 = o1b[nq % 2]
                nq += 1
                for m in range(2):
                    def smm(kt, ps, c0, kb=kb, qb=qms[m], tb=tb):
                        return [(ps.ap[:, c0:512], kb.ap[:, kt * 128:(kt + 1) * 128], qb.ap[:, tb * 512 + c0:(tb + 1) * 512], True, True)]
                    if m == 0:
                        def cons(po, rec, o1=o1):
                            dve_tt(o1, o1.ap, po, po.ap, rec, rec.ap, ALU.mult)
                    else:
                        def cons(po, rec, o1=o1, h=h, tb=tb):
                            dve_tt(rec, rec.ap, po, po.ap, rec, rec.ap, ALU.mult)
                            dve_stt(o1, o1.ap, rec, rec.ap, neglam, o1, o1.ap, ALU.mult, ALU.add, extra=[misc])
                            pendn.append((o1, h, tb))
                            if len(pendn) > 1:
                                post(pendn.pop(0))
                    attention(None, smm, lambda kt, vdst=vdst: vdst[:, kt, :], cons, tb, [kb, qms[m], vb])
        fin_stage2()
        while pendn:
            post(pendn.pop(0))
        P.handoff(o1b, [blk_res[6]])
        wo_accumulate(l, w_o, 1536, half)

    def cross(l, half):
        isq = 128 ** -0.5
        memT = bview(0, 2)
        ld(memT, memT.ap.rearrange("p (c t) -> p c t", c=NCH), SC["memT"].rearrange("(c p) t -> p c t", p=128), SCR["memT"])
        memv = memT.ap.rearrange("p (c t) -> p c t", c=NCH)
        qx = bview(2, 2)
        kx = bview(4, 1)
        vx = bview(5, 1)
        for u in range(2):
            wv = load_w(w_ck[l, :, u * 256:(u + 1) * 256].rearrange("(c p) j -> p c j", p=128), NCH, 256)
            for t in range(2):
                h = u * 2 + t
                ps = ps_s()
                mm(ps, [(ps.ap[:, 0:256], wv.ap[:, c, t * 128:(t + 1) * 128], memv[:, c, :], c == 0, c == NCH - 1) for c in range(NCH)], [wv, memT])
                copy_any(kx, kx.ap[:, h * 256:(h + 1) * 256], ps, ps.ap[:, 0:256])
        for u in range(2):
            wv = load_w(w_cv[l, :, u * 256:(u + 1) * 256].rearrange("(c p) j -> p c j", p=128), NCH, 256)
            for mt in range(2):
                ps = ps_s()
                mm(ps, [(ps.ap[:, 0:256], memv[:, c, mt * 128:(mt + 1) * 128], wv.ap[:, c, :], c == 0, c == NCH - 1) for c in range(NCH)], [wv, memT])
                copy_any(vx, vx.ap[:, mt * 512 + u * 256:mt * 512 + (u + 1) * 256], ps, ps.ap[:, 0:256])
        norm_to_hT("g_cross", l)
        for u in range(2):
            wv = load_w(w_cq[l, :, u * 256:(u + 1) * 256].rearrange("(c p) j -> p c j", p=128), NCH, 256)
            for t in range(2):
                h = u * 2 + t
                for tb in range(NTB):
                    ps = proj_fm(wv, lambda k, t=t, wv=wv: wv.ap[:, k, t * 128:(t + 1) * 128], NCH, hrhs(tb), tb)
                    copy_any(qx, qx.ap[:, h * Th + tb * 512:h * Th + (tb + 1) * 512], ps, ps.ap, isq)
        for h in range(4):
            for tb in range(NTB):
                po, pz = ps_o(), ps_z()
                for mt in range(2):
                    ps = ps_s()
                    mm(ps, [(ps.ap, kx.ap[:, h * 256 + mt * 128:h * 256 + (mt + 1) * 128], qx.ap[:, h * Th + tb * 512:h * Th + (tb + 1) * 512], True, True)], [kx, qx])
                    pt = bt()
                    act(pt, pt.ap, ps, ps.ap, AF.Exp)
                    mm(po, [(po.ap, vx.ap[:, mt * 512 + h * 128:mt * 512 + (h + 1) * 128], pt.ap, mt == 0, mt == 1)], [pt, vx])
                    mm(pz, [(pz.ap, ones_b, pt.ap, mt == 0, mt == 1)], [pt, cb])
                rec = tmp()
                act(rec, rec.ap, pz, pz.ap, AF.Ln)
                act(rec, rec.ap, rec, rec.ap, AF.Exp, scale=-1.0)
                dve_tt(yT[h][tb], yT[h][tb].ap, po, po.ap, rec, rec.ap, ALU.mult)
        wo_accumulate(l, w_co, 0, half)

    def ffn(l, half):
        norm_to_hT("g_ffn", l)
        abuf = [bview(0, 1), bview(1, 1)]
        wpool = wslots + [bview(2, 2), bview(4, 2), bview(6, 2), bview(8, 2), bview(10, 2)]
        nslots = 2 * NTB
        dper = NCH // nslots

        def down_groups(wd, ab, dts):
            for dt_ in dts:
                for tb in range(NTB):
                    ps = nxt("O", psb[4:8])
                    mm(ps, [(ps.ap, wd.ap[:, t, dt_ * 128:(dt_ + 1) * 128], ab.ap[:, t * Th + tb * 512:t * Th + (tb + 1) * 512], t == 0, t == 1) for t in range(2)],
                       [wd, ab])
                    dve_tt(xs[dt_][tb], xs[dt_][tb].ap, ps, ps.ap, xs[dt_][tb], xs[dt_][tb].ap, ALU.add)
        prev = None
        for j in range(FF // 256):
            wg = load_w(w_gate[l, :, j * 256:(j + 1) * 256].rearrange("(c p) j -> p c j", p=128), NCH, 256, wpool)
            wu = load_w(w_up[l, :, j * 256:(j + 1) * 256].rearrange("(c p) j -> p c j", p=128), NCH, 256, wpool)
            ab = abuf[j % 2]
            k = 0
            for t in range(2):
                for tb in range(NTB):
                    pg = proj_fm(wg, lambda k_, t=t, wg=wg: wg.ap[:, k_, t * 128:(t + 1) * 128], NCH, hrhs(tb), tb)
                    sg = tmp()
                    act(sg, sg.ap, pg, pg.ap, AF.Silu)
                    if prev is not None and "B" in KOPT:
                        down_groups(prev[0], prev[1], range(k * dper, k * dper + dper // 2))
                    pu = proj_fm(wu, lambda k_, t=t, wu=wu: wu.ap[:, k_, t * 128:(t + 1) * 128], NCH, hrhs(tb), tb)
                    dve_tt(ab, ab.ap[:, t * Th + tb * 512:t * Th + (tb + 1) * 512], pu, pu.ap, sg, sg.ap, ALU.mult)
                    if prev is not None:
                        down_groups(prev[0], prev[1], range(k * dper + dper // 2, (k + 1) * dper) if "B" in KOPT else range(k * dper, (k + 1) * dper))
                    k += 1
            wd = load_w(w_down[l, j * 256:(j + 1) * 256, :].rearrange("(k p) d -> p k d", p=128), 2, 2048, wpool)
            prev = (wd, ab)
        down_groups(prev[0], prev[1], range(NCH))

    for half in range(HALVES):
        cur["half"] = half
        for tt in range(0 if "xload" in KSKIP else Th // 128):
            tb, o_ = divmod(tt, 4)
            r0 = half * Th + tt * 128
            for c0 in range(0, NCH, 4):
                xin = tmp()
                ld(xin, xin.ap, x_d[r0:r0 + 128, c0 * 128:(c0 + 4) * 128], Res("xd"))
                if "xmm" in KSKIP:
                    continue
                ps = ps_s()
                for j in range(4):
                    P.op("pe", lambda e, ps=ps, j=j, xin=xin: e.matmul(ps.ap[:, j * 128:(j + 1) * 128], lhsT=xin.ap[:, j * 128:(j + 1) * 128], rhs=ident_f, start=True, stop=True),
                         reads=[xin, cst], writes=[ps], pe_acc=True, nmm=1)
                dst_ap = xs_ap[:, c0:c0 + 4, tb * 512 + o_ * 128:tb * 512 + (o_ + 1) * 128]
                if "xcp" in KSKIP:
                    continue
                dv_ = V(dst_ap, [xs[c0 + j][tb].res[0] for j in range(4)])
                src_ = ps.ap.rearrange("p (j t) -> p j t", j=4)
                P.op("dve", lambda e, dst_ap=dst_ap, src_=src_: e.tensor_copy(out=dst_ap, in_=src_), reads=[ps], writes=[dv_])
        import os
        stg = os.environ.get("KSTAGE", "fmsdcx")
        for l in range(L):
            P.phases.append((f"h{half}l{l}:norm", P.nmm))
            norm_to_hT("g_mix", l)
            P.phases.append((f"h{half}l{l}:projs", P.nmm))
            fox(l, half, "proj")
            mla(l, half, "proj")
            diff(l, half, "proj")
            P.phases.append((f"h{half}l{l}:sgu", P.nmm))
            sgu(l, half)
            P.phases.append((f"h{half}l{l}:fox", P.nmm))
            fox(l, half, "attn")
            P.phases.append((f"h{half}l{l}:mla", P.nmm))
            mla(l, half, "attn")
            P.phases.append((f"h{half}l{l}:diff", P.nmm))
            diff(l, half, "attn")
            P.phases.append((f"h{half}l{l}:cross", P.nmm))
            cross(l, half)
            P.phases.append((f"h{half}l{l}:ffn", P.nmm))
            ffn(l, half)
        P.phases.append((f"h{half}:final", P.nmm))
        for tb in range(0 if "final" in KSKIP else NTB):
            ps = sumsq_ps([(xs[c][tb], xs[c][tb].ap) for c in range(NCH)], 128)
            r0_ = rstd_from_ps(ps, 128, D)
            r = V(bview(11, 1, F32).ap[:, 0:512], [blk_res[11]])
            P.op("dve", lambda e, r=r, r0_=r0_: e.tensor_copy(out=r.ap, in_=r0_.ap), reads=[r0_], writes=[r])
            for o_ in range(4):
                osb = bview(2 * (o_ % 2), 2, F32)
                for c0 in range(0, NCH, 4):
                    pst = ps_s()
                    for j in range(4):
                        c = c0 + j
                        xn = tmp()
                        dve_stt(xn, xn.ap[:, 0:128], xs[c][tb], xs[c][tb].ap[:, o_ * 128:(o_ + 1) * 128], sml("g_final", 0, c),
                                r, r.ap[:, o_ * 128:(o_ + 1) * 128], ALU.mult, ALU.mult, extra=[small])
                        P.op("pe", lambda e, pst=pst, j=j, xn=xn: e.matmul(pst.ap[:, j * 128:(j + 1) * 128], lhsT=xn.ap[:, 0:128], rhs=ident_f, start=True, stop=True),
                             reads=[xn, cst], writes=[pst], pe_acc=True, nmm=1)
                    copy_any(osb, osb.ap[:, c0 * 128:(c0 + 4) * 128], pst, pst.ap)
                r0 = half * Th + tb * 512 + o_ * 128
                st(out_d[r0:r0 + 128, :], Res("outd"), osb, osb.ap)
    P.wait_all("sp", list(ALL_RES))
    for e in COMPUTE:
        if P.ecnt[e]:
            P._need("sp", (P.esem[e], P.ecnt[e], e))
    P.emit()
    return nc, P


def pack_small(inp, b, L):
    soff, NS = small_layout(L)
    sm = np.zeros((128, NS), np.float32)

    def fm(v):
        return np.ascontiguousarray(v.reshape(-1, 128).T)
    for l in range(L):
        for nm in ("g_mix", "g_cross", "g_ffn", "g_cq", "g_ckv", "g_diff"):
            a = fm(np.asarray(inp[nm][l], np.float32))
            sm[:, soff[(nm, l)]:soff[(nm, l)] + a.shape[1]] = a
        sm[0:4, soff[("b_f", l)]] = np.asarray(inp["b_f"][l], np.float32)
    for nm in ("g_mem", "g_final"):
        a = fm(np.asarray(inp[nm], np.float32))
        sm[:, soff[(nm, 0)]:soff[(nm, 0)] + 16] = a
    return sm


def run(inp, S, L, FF, ncores):
    nc, P = build_program(S, L, FF)
    consts = make_consts()
    small = pack_small(inp, 0, L)
    shared = {}
    for k in ("w_in", "w_uq", "w_ukv", "sgu_ln_g", "sgu_ln_b", "w_s", "lam_q1", "lam_k1", "lam_q2", "lam_k2",
              "w_o", "w_cq", "w_ck", "w_cv", "w_co", "w_gate", "w_up", "w_down"):
        shared[k] = np.ascontiguousarray(np.asarray(inp[k], np.float32))
    shared["b_s"] = np.ascontiguousarray(np.asarray(inp["b_s"], np.float32).reshape(L, 512))
    shared["small"] = small
    shared["consts"] = consts
    in_maps = []
    for b in range(ncores):
        m = dict(shared)
        m["x"] = np.ascontiguousarray(np.asarray(inp["x"][b], np.float32))
        m["mem"] = np.ascontiguousarray(np.asarray(inp["mem"][b], np.float32))
        m["positions"] = np.ascontiguousarray(np.asarray(inp["positions"][b], np.int32).reshape(1, S))
        in_maps.append(m)
    res = run_bass_kernel_spmd(nc, in_maps, core_ids=list(range(ncores)))
    return np.stack([r["out"] for r in res.results], axis=0)


def kernel(**inputs):
    return run(inputs, 2048, 4, 5632, 8).astype(np.float32)
```

```python
import math
import numpy as np
import concourse.bass as bass
import concourse.mybir as mybir
from concourse.bass_utils import run_bass_kernel_spmd

F32 = mybir.dt.float32
BF16 = mybir.dt.bfloat16
I32 = mybir.dt.int32
AF = mybir.ActivationFunctionType
ALU = mybir.AluOpType

D = 2048
NCH = 16
MEM = 256
EPS = 1e-6
N_IN = 4804
COMPUTE = ("pe", "act", "dve", "pool")


ALL_RES = []


class Res:
    __slots__ = ("name", "wtok", "rtoks", "ld_sem", "ld_n", "st_sem", "st_n", "psum")

    def __init__(self, name, psum=False):
        ALL_RES.append(self)
        self.name = name
        self.psum = psum
        self.wtok = None
        self.rtoks = {}
        self.ld_sem = None
        self.ld_n = 0
        self.st_sem = None
        self.st_n = 0


class V:
    __slots__ = ("ap", "res")

    def __init__(self, ap, res):
        self.ap = ap
        self.res = res if isinstance(res, list) else [res]

    def __getitem__(self, idx):
        return self.ap[idx]


def _rl(vs):
    out = []
    for v in vs:
        if isinstance(v, V):
            out.extend(v.res)
        else:
            out.append(v)
    return out


class Prog:
    def __init__(self, nc):
        self.nc = nc
        self.streams = {e: [] for e in ("pe", "act", "dve", "pool", "sp")}
        self.esem = {e: nc.alloc_semaphore(f"es_{e}") for e in COMPUTE}
        self.ecnt = {e: 0 for e in COMPUTE}
        self.waited = {e: {} for e in self.streams}
        self.nsem = 0
        self.ninstr = 0
        self.nmm = 0
        self.phases = []

    def new_sem(self, name):
        self.nsem += 1
        return self.nc.alloc_semaphore(f"s{self.nsem}_{name}")

    def _need(self, e, tok):
        if tok is None:
            return
        sem, val, _ = tok
        w = self.waited[e]
        if w.get(sem.num, 0) >= val:
            return
        w[sem.num] = val
        self.streams[e].append(("wait", sem, val))

    def _deps(self, e, reads, writes, pe_acc):
        for r in reads:
            self._need(e, r.wtok)
            if r.psum:
                for tok in r.rtoks.values():
                    if tok[2] != e:
                        self._need(e, tok)
        for r in writes:
            if not (pe_acc and r.wtok is not None and r.wtok[2] == "pe"):
                self._need(e, r.wtok)
            for tok in r.rtoks.values():
                self._need(e, tok)

    @staticmethod
    def _mark(tok, reads, writes):
        for r in reads:
            r.rtoks[tok[0].num] = tok
        for r in writes:
            r.wtok = tok
            r.rtoks = {}

    def op(self, e, fn, reads=(), writes=(), pe_acc=False, nmm=0):
        self.nmm += nmm
        reads, writes = _rl(reads), _rl(writes)
        self._deps(e, reads, writes, pe_acc)
        self.ecnt[e] += 1
        tok = (self.esem[e], self.ecnt[e], e)
        self.streams[e].append(("op", fn, self.esem[e]))
        self._mark(tok, reads, writes)
        self.ninstr += 1

    def dma(self, q, fns, reads=(), writes=(), owner=None, kind="ld"):
        reads, writes = _rl(reads), _rl(writes)
        self._deps(q, reads, writes, False)
        o = owner.res[0]
        if kind == "ld":
            if o.ld_sem is None:
                o.ld_sem = self.new_sem("ld")
            o.ld_n += len(fns)
            sem, n = o.ld_sem, o.ld_n
        else:
            if o.st_sem is None:
                o.st_sem = self.new_sem("st")
            o.st_n += len(fns)
            sem, n = o.st_sem, o.st_n
        tok = (sem, 16 * n, "dma")
        for fn in fns:
            self.streams[q].append(("dma", fn, sem))
        self._mark(tok, reads, writes)
        self.ninstr += len(fns)

    @staticmethod
    def handoff(srcs, dsts):
        toks = []
        for s_ in _rl(srcs):
            if s_.wtok is not None:
                toks.append(s_.wtok)
            toks.extend(s_.rtoks.values())
        for d in _rl(dsts):
            for t in toks:
                cur_ = d.rtoks.get(t[0].num)
                if cur_ is None or cur_[1] < t[1]:
                    d.rtoks[t[0].num] = t

    def wait_all(self, e, vs):
        for r in _rl(vs):
            self._need(e, r.wtok)
            for tok in r.rtoks.values():
                self._need(e, tok)

    def emit(self):
        nc = self.nc
        streams = self.streams

        def run(name):
            def body(engine):
                for item in streams[name]:
                    if item[0] == "wait":
                        engine.wait_ge(item[1], item[2])
                    elif item[0] == "op":
                        item[1](engine).then_inc(item[2], 1)
                    else:
                        item[1](engine).then_inc(item[2], 16)
            return body

        with nc.Block() as block:
            block.tensor(run("pe"))
            block.scalar(run("act"))
            block.vector(run("dve"))
            block.gpsimd(run("pool"))
            block.sync(run("sp"))


def small_layout(L):
    off = {}
    c = 0
    for l in range(L):
        for nm, w in (("g_mix", 16), ("g_cross", 16), ("g_ffn", 16), ("g_cq", 3), ("g_ckv", 2),
                      ("g_diff", 1), ("b_f", 1)):
            off[(nm, l)] = c
            c += w
    for nm in ("g_mem", "g_final"):
        off[(nm, 0)] = c
        c += 16
    return off, c


CONST_COLS = {"ident": 0, "tril": 128, "freq_mla": 256, "freq_diff": 257, "mask": 258}
NCST_SB = 258
NCONST = 258 + 4 * 512 + 128


def make_consts():
    c = np.zeros((128, NCONST), np.float32)
    c[:, 0:128] = np.eye(128, dtype=np.float32)
    p = np.arange(128)[:, None]
    f = np.arange(128)[None, :]
    c[:, 128:256] = (f <= p).astype(np.float32)
    ff = np.arange(512)[None, :]
    for j in range(4):
        c[:, 258 + j * 512:258 + (j + 1) * 512] = (ff - p - 128 * j >= 0).astype(np.float32)
    theta = 500000.0
    fm = theta ** (-np.arange(0, 64, 2, dtype=np.float32) / 64.0)
    c[0:64, CONST_COLS["freq_mla"]] = np.concatenate([fm, fm])
    fd = theta ** (-np.arange(0, 16, 2, dtype=np.float32) / 16.0)
    one = np.zeros(64, np.float32)
    one[0:8] = fd
    one[8:16] = fd
    c[:, CONST_COLS["freq_diff"]] = np.concatenate([one, one])
    R = np.zeros((128, 128), np.float32)
    for base in (0, 64):
        for i in range(8):
            R[base + 8 + i, base + i] = -1.0
            R[base + i, base + 8 + i] = 1.0
    c[:, 258 + 2048:258 + 2048 + 128] = R
    return c


def build_program(S, L, FF, dbg=None):
    del ALL_RES[:]
    import os as _os
    KOPT = _os.environ.get("KOPT", "FCD")
    HALVES = 2
    Th = S // HALVES
    NTB = Th // 512
    NKT = S // 128
    nc = bass.Bass("TRN2", target_bir_lowering=False)
    P = Prog(nc)

    def din(name, shape, dt=F32):
        return nc.dram_tensor(name, list(shape), dt, kind="ExternalInput").ap()

    x_d = din("x", [S, D])
    mem_d = din("mem", [MEM, D])
    pos_d = din("positions", [1, S], I32)
    w_in = din("w_in", [L, D, N_IN])
    w_uq = din("w_uq", [L, 384, 768])
    w_ukv = din("w_ukv", [L, 256, 1024])
    sgu_ln_g = din("sgu_ln_g", [L, 512])
    sgu_ln_b = din("sgu_ln_b", [L, 512])
    w_s = din("w_s", [L, 4, 128, 128])
    b_s = din("b_s", [L, 512])
    lam_d = {k: din(k, [L, 64]) for k in ("lam_q1", "lam_k1", "lam_q2", "lam_k2")}
    w_o = din("w_o", [L, D, D])
    w_cq = din("w_cq", [L, D, 512])
    w_ck = din("w_ck", [L, D, 512])
    w_cv = din("w_cv", [L, D, 512])
    w_co = din("w_co", [L, 512, D])
    w_gate = din("w_gate", [L, D, FF])
    w_up = din("w_up", [L, D, FF])
    w_down = din("w_down", [L, FF, D])
    soff, NS = small_layout(L)
    small_d = din("small", [128, NS])
    consts_d = din("consts", [128, NCONST])
    out_d = nc.dram_tensor("out", [S, D], F32, kind="ExternalOutput").ap()

    def scratch(name, shape, dt=BF16):
        return nc.dram_tensor("scr_" + name, list(shape), dt, kind="ExternalOutput").ap()

    SC = {}
    SCR = {}
    for l in range(L):
        for nm, shp, dt in (("fq", [512, S], BF16), ("fk", [512, S], BF16), ("fv", [S, 512], BF16),
                            ("fFp", [4, 3, S], BF16), ("fFn", [4, 3, S], BF16),
                            ("mqn", [512, S], BF16), ("mqr", [256, S], BF16), ("mkn", [512, S], BF16),
                            ("mkr", [64, S], BF16), ("mv", [S, 512], BF16),
                            ("dq", [512, S], BF16), ("dk", [512, S], BF16), ("dv", [S, 512], BF16)):
            SC[(nm, l)] = scratch(f"{nm}{l}", shp, dt)
            SCR[(nm, l)] = Res(f"{nm}{l}")
    for nm, shp in (("rCm", [64, S]), ("rSm", [64, S]), ("rCd", [128, S]), ("rSd", [128, S])):
        SC[nm] = scratch(nm, shp, F32)
        SCR[nm] = Res(nm)
    SC["memT"] = scratch("memT", [D, MEM], BF16)
    SCR["memT"] = Res("memT")

    def sb(name, shape, dt):
        return nc.alloc_sbuf_tensor("sb_" + name, list(shape), dt).ap()

    xs_ap = sb("xs", [128, NCH, Th], F32)
    xs = [[V(xs_ap[:, c, tb * 512:(tb + 1) * 512], Res(f"xs{c}_{tb}")) for tb in range(NTB)] for c in range(NCH)]
    hT_ap = sb("hT", [128, NCH, Th], BF16)
    hT = [[V(hT_ap[:, c, tb * 512:(tb + 1) * 512], Res(f"hT{c}_{tb}")) for tb in range(NTB)] for c in range(NCH)]
    NW = 4
    wslots = [V(sb(f"w{i}", [128, 4096], BF16), Res(f"w{i}")) for i in range(NW)]
    yT_ap = sb("yT", [128, 4, Th], BF16)
    yT = [[V(yT_ap[:, k, tb * 512:(tb + 1) * 512], Res(f"yT{k}_{tb}")) for tb in range(NTB)] for k in range(4)]
    NTMP = 6
    tmps = [V(sb(f"tmp{i}", [128, 512], F32), Res(f"tmp{i}")) for i in range(NTMP)]
    NBT = 4
    bts = [V(sb(f"bt{i}", [128, 512], BF16), Res(f"bt{i}")) for i in range(NBT)]
    small = V(sb("small", [128, NS], F32), Res("small"))
    cst = V(sb("cst", [128, NCST_SB], F32), Res("cst"))
    ident_f = cst.ap[:, 0:128]
    misc = V(sb("misc", [128, 16], F32), Res("misc"))
    cb = V(sb("cb", [128, 128 + 128 + 128 + 4 * 512 + 128], BF16), Res("cb"))
    ident_b = cb.ap[:, 0:128]
    ones_b = cb.ap[:, 128:256]
    tril_b = cb.ap[:, 256:384]
    mask_b = [cb.ap[:, 384 + j * 512:384 + (j + 1) * 512] for j in range(4)]
    Rd_b = cb.ap[:, 384 + 2048:384 + 2048 + 128]
    NBLK = 12
    blk_ap = sb("blk", [128, NBLK * 2048], BF16)
    blk_res = [Res(f"blk{i}") for i in range(NBLK)]

    def bview(b0, nb, dt=BF16, shape=None):
        ap = blk_ap[:, b0 * 2048:(b0 + nb) * 2048]
        if dt != BF16:
            ap = ap.bitcast(dt)
        return V(ap, blk_res[b0:b0 + nb])

    psb = [V(nc.alloc_psum_tensor(f"ps{i}", [128, 512], F32).ap(), Res(f"ps{i}", psum=True)) for i in range(8)]
    rot = {"w": 0, "wf": 0, "tmp": 0, "bt": 0, "S": 0, "O": 0, "Z": 0, "ev": 0}

    def nxt(kind, lst):
        i = rot[kind]
        rot[kind] = i + 1
        return lst[i % len(lst)]

    def ps_s():
        return nxt("S", psb[0:4])

    def ps_o():
        return nxt("O", psb[4:6])

    def ps_z():
        return nxt("Z", psb[6:8])

    def tmp():
        return nxt("tmp", tmps)

    def bt():
        return nxt("bt", bts)

    def evac_engine():
        rot["ev"] += 1
        return "act" if rot["ev"] % 2 else "dve"

    def sml(nm, l, j=0, n=128):
        c = soff[(nm, l)] + j
        return small.ap[0:n, c:c + 1]

    def load_w(src, a, b, pool=None):
        slot = nxt("w", wslots) if pool is None else nxt("wf", pool)
        npart = src.shape[0]
        dst = slot.ap[0:npart, 0:a * b].rearrange("p (a b) -> p a b", a=a)
        if b > 512:
            bb = max(d_ for d_ in range(1, 513) if b % d_ == 0)
            d2 = dst.rearrange("p a (b2 b) -> p a b2 b", b=bb)
            s2 = src.rearrange("p a (b2 b) -> p a b2 b", b=bb)
        else:
            d2, s2 = dst, src
        P.dma("pool", [lambda e: e.dma_start(out=d2, in_=s2)], writes=[slot], owner=slot)
        return V(dst, slot.res)

    def mm(out_v, mms, reads):
        def fn(e):
            ins = None
            for (o, lt, r, st, sp) in mms:
                ins = e.matmul(o, lhsT=lt, rhs=r, start=st, stop=sp)
            return ins
        P.op("pe", fn, reads=reads, writes=[out_v], pe_acc=True, nmm=len(mms))

    def act(out_v, out_ap, in_v, in_ap, func, scale=1.0, bias=None, extra_reads=()):
        kw = {}
        if bias is not None:
            kw["bias"] = bias
        P.op("act", lambda e: e.activation(out=out_ap, in_=in_ap, func=func, scale=scale, **kw),
             reads=[in_v] + list(extra_reads), writes=[out_v])

    def copy_any(out_v, out_ap, in_v, in_ap, scale=None):
        eng = evac_engine()
        if eng == "act":
            if scale is None:
                P.op("act", lambda e: e.activation(out=out_ap, in_=in_ap, func=AF.Identity), reads=[in_v], writes=[out_v])
            else:
                P.op("act", lambda e: e.activation(out=out_ap, in_=in_ap, func=AF.Identity, scale=scale), reads=[in_v], writes=[out_v])
        else:
            if scale is None:
                P.op("dve", lambda e: e.tensor_copy(out=out_ap, in_=in_ap), reads=[in_v], writes=[out_v])
            else:
                P.op("dve", lambda e: e.tensor_scalar(out=out_ap, in0=in_ap, scalar1=scale, scalar2=None, op0=ALU.mult),
                     reads=[in_v], writes=[out_v])

    def dve_tt(out_v, out_ap, a_v, a_ap, b_v, b_ap, op):
        P.op("dve", lambda e: e.tensor_tensor(out=out_ap, in0=a_ap, in1=b_ap, op=op), reads=[a_v, b_v], writes=[out_v])

    def dve_ts(out_v, out_ap, a_v, a_ap, s1, s2, op0, op1=None, extra=()):
        if op1 is None:
            P.op("dve", lambda e: e.tensor_scalar(out=out_ap, in0=a_ap, scalar1=s1, scalar2=None, op0=op0),
                 reads=[a_v] + list(extra), writes=[out_v])
        else:
            P.op("dve", lambda e: e.tensor_scalar(out=out_ap, in0=a_ap, scalar1=s1, scalar2=s2, op0=op0, op1=op1),
                 reads=[a_v] + list(extra), writes=[out_v])

    def dve_stt(out_v, out_ap, a_v, a_ap, sc, b_v, b_ap, op0, op1, extra=()):
        P.op("dve", lambda e: e.scalar_tensor_tensor(out=out_ap, in0=a_ap, scalar=sc, in1=b_ap, op0=op0, op1=op1),
             reads=[a_v, b_v] + list(extra), writes=[out_v])

    def ld(dst_v, dst_ap, src_ap, src_res):
        P.dma("sp", [lambda e: e.dma_start(out=dst_ap, in_=src_ap)], reads=[src_res], writes=[dst_v], owner=dst_v)

    def st(dst_ap, dst_res, src_v, src_ap):
        P.dma("sp", [lambda e: e.dma_start(out=dst_ap, in_=src_ap)], reads=[src_v], writes=[dst_res], owner=src_v, kind="st")

    def rstd_from_ps(ps, npart, dim):
        t = tmp()
        act(t, t.ap[0:npart, :], ps, ps.ap[0:npart, :], AF.Ln, scale=1.0 / dim, bias=misc.ap[0:npart, 0:1], extra_reads=[misc])
        act(t, t.ap[0:npart, :], t, t.ap[0:npart, :], AF.Exp, scale=-0.5)
        return t

    def sumsq_ps(srcs, npart):
        ps = ps_s()
        n = len(srcs)
        for i, (sv, sap) in enumerate(srcs):
            q = bt()
            act(q, q.ap[0:npart, :], sv, sap, AF.Square)
            mm(ps, [(ps.ap[:, :], ones_b[0:npart, :], q.ap[0:npart, :], i == 0, i == n - 1)], [q, cb])
        return ps

    P.dma("sp", [lambda e: e.dma_start(out=small.ap, in_=small_d)], writes=[small], owner=small)
    P.dma("sp", [lambda e: e.dma_start(out=cst.ap, in_=consts_d[:, 0:NCST_SB])], writes=[cst], owner=cst)
    P.op("dve", lambda e: e.memset(misc.ap[:, 0:1], EPS), writes=[misc])
    P.op("dve", lambda e: e.memset(misc.ap[:, 1:2], math.pi), writes=[misc])
    P.op("dve", lambda e: e.memset(misc.ap[:, 2:3], 1.0), writes=[misc])
    P.op("dve", lambda e: e.tensor_copy(out=cb.ap[:, 0:128], in_=cst.ap[:, 0:128]), reads=[cst], writes=[cb])
    P.op("dve", lambda e: e.memset(cb.ap[:, 128:256], 1.0), writes=[cb])
    P.op("dve", lambda e: e.tensor_copy(out=cb.ap[:, 256:384], in_=cst.ap[:, 128:256]), reads=[cst], writes=[cb])
    import os
    if "maskdma" not in os.environ.get("KSKIP", ""):
        P.dma("pool", [lambda e: e.dma_start(out=cb.ap[:, 384:384 + 2048].rearrange("p (j b) -> p j b", b=512),
                                             in_=consts_d[:, 258:258 + 2048].rearrange("p (j b) -> p j b", b=512))], writes=[cb], owner=cb)
        P.dma("pool", [lambda e: e.dma_start(out=cb.ap[:, 384 + 2048:384 + 2048 + 128], in_=consts_d[:, 258 + 2048:258 + 2048 + 128])], writes=[cb], owner=cb)

    lamt = V(sb("lamt", [128, 8], F32), Res("lamt"))
    for l_ in range(L):
        lam_init_ = 0.8 - 0.6 * math.exp(-0.3 * l_)
        la = tmp()
        P.dma("sp", [lambda e, la=la, nm=nm, i_=i_, l_=l_: e.dma_start(out=la.ap[:, i_ * 64:(i_ + 1) * 64], in_=lam_d[nm][l_, :].partition_broadcast(128))
                     for i_, nm in enumerate(("lam_q1", "lam_k1", "lam_q2", "lam_k2"))], writes=[la], owner=la)
        dve_tt(la, la.ap[:, 256:320], la, la.ap[:, 0:64], la, la.ap[:, 64:128], ALU.mult)
        dve_tt(la, la.ap[:, 320:384], la, la.ap[:, 128:192], la, la.ap[:, 192:256], ALU.mult)
        for i_ in range(2):
            P.op("act", lambda e, la=la, i_=i_: e.activation(out=la.ap[:, 384 + i_ * 64:448 + i_ * 64], in_=la.ap[:, 256 + i_ * 64:320 + i_ * 64], func=AF.Identity,
                                                            accum_out=misc.ap[:, 13 + i_:14 + i_]), reads=[la], writes=[la, misc])
        act(misc, misc.ap[:, 13:15], misc, misc.ap[:, 13:15], AF.Exp)
        dve_tt(lamt, lamt.ap[:, l_:l_ + 1], misc, misc.ap[:, 14:15], misc, misc.ap[:, 13:14], ALU.subtract)
        dve_ts(lamt, lamt.ap[:, l_:l_ + 1], lamt, lamt.ap[:, l_:l_ + 1], -lam_init_, None, ALU.add)

    import os
    KSKIP = os.environ.get("KSKIP", "")
    for (fc, npart, cn, sn) in [] if "rope" in KSKIP else ((CONST_COLS["freq_mla"], 64, "rCm", "rSm"), (CONST_COLS["freq_diff"], 128, "rCd", "rSd")):
        for b0 in range(0, S, 512):
            pi_ = bview(0, 1, I32)
            ld(pi_, pi_.ap[0:npart, 0:512], pos_d[0, b0:b0 + 512].partition_broadcast(npart), Res("posd"))
            ang = tmp()
            P.op("dve", lambda e, ang=ang, pi_=pi_, npart=npart: e.tensor_copy(out=ang.ap[0:npart, :], in_=pi_.ap[0:npart, 0:512]),
                 reads=[pi_], writes=[ang])
            dve_ts(ang, ang.ap[0:npart, :], ang, ang.ap[0:npart, :], cst.ap[0:npart, fc:fc + 1], None, ALU.mult, extra=[cst])
            for (shift, nm) in ((0.0, sn), (math.pi / 2, cn)):
                a2 = tmp()
                dve_ts(a2, a2.ap[0:npart, :], ang, ang.ap[0:npart, :], shift, None, ALU.add)
                kf = tmp()
                dve_ts(kf, kf.ap[0:npart, :], a2, a2.ap[0:npart, :], 1.0 / (2 * math.pi), None, ALU.mult)
                ki = bview(1, 1, I32)
                P.op("dve", lambda e, ki=ki, kf=kf, npart=npart: e.tensor_copy(out=ki.ap[0:npart, 0:512], in_=kf.ap[0:npart, :]), reads=[kf], writes=[ki])
                P.op("dve", lambda e, ki=ki, kf=kf, npart=npart: e.tensor_copy(out=kf.ap[0:npart, :], in_=ki.ap[0:npart, 0:512]), reads=[ki], writes=[kf])
                r = tmp()
                dve_stt(r, r.ap[0:npart, :], kf, kf.ap[0:npart, :], -2 * math.pi, a2, a2.ap[0:npart, :], ALU.mult, ALU.add)
                dve_ts(kf, kf.ap[0:npart, :], r, r.ap[0:npart, :], math.pi, 2 * math.pi, ALU.is_gt, ALU.mult)
                dve_tt(r, r.ap[0:npart, :], r, r.ap[0:npart, :], kf, kf.ap[0:npart, :], ALU.subtract)
                dve_ts(kf, kf.ap[0:npart, :], r, r.ap[0:npart, :], -math.pi, 2 * math.pi, ALU.is_lt, ALU.mult)
                dve_tt(r, r.ap[0:npart, :], r, r.ap[0:npart, :], kf, kf.ap[0:npart, :], ALU.add)
                dve_ts(r, r.ap[0:npart, :], r, r.ap[0:npart, :], 3.141592, -3.141592, ALU.min, ALU.max)
                act(r, r.ap[0:npart, :], r, r.ap[0:npart, :], AF.Sin)
                st(SC[nm][0:npart, b0:b0 + 512], SCR[nm], r, r.ap[0:npart, :])

    for mt in range(0 if "mem" in KSKIP else MEM // 128):
        pcs = []
        for q4 in range(4):
            mtile = tmp()
            ld(mtile, mtile.ap, mem_d[mt * 128:(mt + 1) * 128, q4 * 512:(q4 + 1) * 512], Res("memd"))
            junk = bt()
            P.op("act", lambda e, mtile=mtile, junk=junk, q4=q4: e.activation(out=junk.ap, in_=mtile.ap, func=AF.Square, accum_out=misc.ap[:, 4 + q4:5 + q4]),
                 reads=[mtile], writes=[junk, misc])
            pcs.append(mtile)
        for q4 in range(1, 4):
            dve_tt(misc, misc.ap[:, 4:5], misc, misc.ap[:, 4:5], misc, misc.ap[:, 4 + q4:5 + q4], ALU.add)
        act(misc, misc.ap[:, 5:6], misc, misc.ap[:, 4:5], AF.Sqrt, scale=1.0 / D, bias=misc.ap[:, 0:1])
        P.op("dve", lambda e: e.reciprocal(out=misc.ap[:, 5:6], in_=misc.ap[:, 5:6]), reads=[misc], writes=[misc])
        for q4 in range(4):
            mtile = pcs[q4]
            dve_ts(mtile, mtile.ap, mtile, mtile.ap, misc.ap[:, 5:6], None, ALU.mult, extra=[misc])
            ps = ps_s()
            for j in range(4):
                P.op("pe", lambda e, ps=ps, j=j, mtile=mtile: e.matmul(ps.ap[:, j * 128:(j + 1) * 128], lhsT=mtile.ap[:, j * 128:(j + 1) * 128], rhs=ident_f, start=True, stop=True),
                     reads=[mtile, cst], writes=[ps], pe_acc=True, nmm=1)
            o = bt()
            c0 = q4 * 4
            for j in range(4):
                c = c0 + j
                dve_ts(o, o.ap[:, j * 128:(j + 1) * 128], ps, ps.ap[:, j * 128:(j + 1) * 128], sml("g_mem", 0, c), None, ALU.mult, extra=[small])
            st(SC["memT"][c0 * 128:(c0 + 4) * 128, mt * 128:(mt + 1) * 128].rearrange("(j p) t -> p j t", p=128), SCR["memT"],
               o, o.ap.rearrange("p (j t) -> p j t", j=4))

    def norm_to_hT(gname, l):
        for tb in range(NTB):
            ps = sumsq_ps([(xs[c][tb], xs[c][tb].ap) for c in range(NCH)], 128)
            r = rstd_from_ps(ps, 128, D)
            for c in range(NCH):
                dve_stt(hT[c][tb], hT[c][tb].ap, xs[c][tb], xs[c][tb].ap, sml(gname, l, c), r, r.ap, ALU.mult, ALU.mult, extra=[small])

    def proj_fm(wv, wsel, nk, rhs, tb, npart_out=128):
        ps = ps_s()
        mms = []
        for k in range(nk):
            rv, rap = rhs[k]
            mms.append((ps.ap[0:npart_out, :], wsel(k), rap, k == 0, k == nk - 1))
        mm(ps, mms, [wv] + [rv for rv, _ in rhs])
        return ps

    def hrhs(tb):
        return [(hT[c][tb], hT[c][tb].ap) for c in range(NCH)]

    def wo_accumulate(l, wd, row0, half):
        for cg in range(2):
            wv = load_w(wd[l, row0:row0 + 512, cg * 1024:(cg + 1) * 1024].rearrange("(k p) j -> p k j", p=128), 4, 1024)
            for dtl in range(8):
                dt_ = cg * 8 + dtl
                for tb in range(NTB):
                    ps = nxt("O", psb[4:8])
                    mm(ps, [(ps.ap, wv.ap[:, k, dtl * 128:(dtl + 1) * 128], yT[k][tb].ap, k == 0, k == 3) for k in range(4)],
                       [wv] + [yT[k][tb] for k in range(4)])
                    dve_tt(xs[dt_][tb], xs[dt_][tb].ap, ps, ps.ap, xs[dt_][tb], xs[dt_][tb].ap, ALU.add)

    def store_fm(ps, npart, dst_ap, dst_res, scale=None):
        o = bt()
        copy_any(o, o.ap[0:npart, :], ps, ps.ap[0:npart, :], scale)
        st(dst_ap, dst_res, o, o.ap[0:npart, :])

    def proj_tm_store(l, wsrc_cols, dst_key, half, rhs_sel=None):
        for u in range(2):
            c0 = wsrc_cols + u * 256
            wv = load_w(w_in[l, :, c0:c0 + 256].rearrange("(c p) j -> p c j", p=128), NCH, 256)
            for tt in range(Th // 128):
                tb, o_ = divmod(tt, 4)
                ps = ps_s()
                mm(ps, [(ps.ap[:, 0:256], hT[c][tb].ap[:, o_ * 128:(o_ + 1) * 128], wv.ap[:, c, :], c == 0, c == NCH - 1) for c in range(NCH)],
                   [wv] + [hT[c][tb] for c in range(NCH)])
                o = bt()
                copy_any(o, o.ap[:, 0:256], ps, ps.ap[:, 0:256])
                r0 = half * Th + tt * 128
                st(SC[(dst_key, l)][r0:r0 + 128, u * 256:(u + 1) * 256], SCR[(dst_key, l)], o, o.ap[:, 0:256])

    qbuf = [bview(0, 1), bview(1, 1)]
    kbuf = [bview(2, 1), bview(3, 1)]
    vbuf = [bview(4, 1), bview(5, 1)]
    qrbuf = [bview(6, 1), bview(7, 1)]
    krbuf = bview(8, 1)
    Abuf = bview(9, 1)
    Bbuf = bview(10, 1)
    rot["hb"] = 0

    fin = {"st": None}

    def fin_stage1():
        f = fin["st"]
        if f is None or f.get("rec") is not None:
            return
        rec = tmp()
        act(rec, rec.ap, f["pz"], f["pz"].ap, AF.Ln)
        f["rec"] = rec

    def fin_stage2():
        f = fin["st"]
        if f is None:
            return
        fin_stage1()
        rec = f["rec"]
        act(rec, rec.ap, rec, rec.ap, AF.Exp, scale=-1.0)
        fin["st"] = None
        f["consume"](f["po"], rec)

    def attention(nkeys_fn, score_mms, vsel, consume, tb, extra_reads, post=None):
        t0 = tb_global(tb)
        nkt = (t0 + 512) // 128
        po, pz = ps_o(), ps_z()
        pss = {}
        tri = mask_b[0][:, 0:128]

        def c0_of(kt):
            return max(0, kt - t0 // 128) * 128

        def score(kt):
            ps = ps_s()
            mm(ps, score_mms(kt, ps, c0_of(kt)), extra_reads)
            pss[kt] = ps
        LOOK = 2
        for k in range(min(LOOK, nkt)):
            score(k)
        for kt in range(nkt):
            if kt + LOOK < nkt:
                score(kt + LOOK)
            ps = pss.pop(kt)
            pt = bt()
            j = kt - t0 // 128
            c0 = c0_of(kt)
            if j >= 0 and nkeys_fn == "premask":
                P.op("dve", lambda e, ps=ps, c0=c0: e.scalar_tensor_tensor(out=ps.ap[:, c0:c0 + 128], in0=tri, scalar=60000.0, in1=ps.ap[:, c0:c0 + 128],
                                                                    op0=ALU.mult, op1=ALU.min),
                     reads=[cb, ps], writes=[ps])
            act(pt, pt.ap[:, c0:512], ps, ps.ap[:, c0:512], AF.Exp)
            if kt == 1:
                fin_stage1()
            if j >= 0:
                dve_tt(pt, pt.ap[:, c0:c0 + 128], pt, pt.ap[:, c0:c0 + 128], cb, tri, ALU.mult)
            mm(po, [(po.ap[:, c0:512], vsel(kt), pt.ap[:, c0:512], kt == 0, kt == nkt - 1)], [pt] + extra_reads)
            mm(pz, [(pz.ap[:, c0:512], ones_b, pt.ap[:, c0:512], kt == 0, kt == nkt - 1)], [pt, cb])
            if kt == 2:
                fin_stage2()
        assert fin["st"] is None
        fin["st"] = {"po": po, "pz": pz, "consume": consume}
        if "F" not in KOPT:
            fin_stage2()

    cur = {"half": 0}

    def tb_global(tb):
        return cur["half"] * Th + tb * 512

    def load_kv(l, kkey, vkey, h, i, nkeys):
        kb, vb = kbuf[i], vbuf[i]
        ld(kb, kb.ap[:, 0:nkeys], SC[(kkey, l)][h * 128:(h + 1) * 128, 0:nkeys], SCR[(kkey, l)])
        vdst = vb.ap[:, 0:(nkeys // 128) * 128].rearrange("p (t d) -> p t d", d=128)
        ld(vb, vdst, SC[(vkey, l)][0:nkeys, h * 128:(h + 1) * 128].rearrange("(t p) d -> p t d", p=128), SCR[(vkey, l)])
        return kb, vb, vdst

    def load_q(l, qkey, h, i, half):
        qb = qbuf[i]
        ld(qb, qb.ap[:, 0:Th], SC[(qkey, l)][h * 128:(h + 1) * 128, half * Th:(half + 1) * Th], SCR[(qkey, l)])
        return qb

    def fox(l, half, part):
        isq = 128 ** -0.5
        if part == "proj":
            wv = load_w(w_in[l, :, 1536:1540].rearrange("(c p) j -> p c j", p=128), NCH, 4)
            carry = V(misc.ap[0:4, 8 + l:9 + l], misc.res)
            if half == 0:
                P.op("dve", lambda e: e.memset(misc.ap[0:4, 8 + l:9 + l], 0.0), writes=[misc])
            for tb in range(NTB):
                ps = proj_fm(wv, lambda k, wv=wv: wv.ap[:, k, 0:4], NCH, hrhs(tb), tb, npart_out=4)
                nbf = V(misc.ap[0:4, 3:4], misc.res)
                dve_ts(misc, misc.ap[0:4, 3:4], small, sml("b_f", l, 0, 4), -1.0, None, ALU.mult)
                e1 = tmp()
                act(e1, e1.ap[0:4, :], ps, ps.ap[0:4, :], AF.Exp, scale=-1.0, bias=misc.ap[0:4, 3:4], extra_reads=[misc])
                act(e1, e1.ap[0:4, :], e1, e1.ap[0:4, :], AF.Ln, scale=1.0, bias=misc.ap[0:4, 2:3], extra_reads=[misc])
                dve_ts(e1, e1.ap[0:4, :], e1, e1.ap[0:4, :], -1.0, None, ALU.mult)
                onesf = tmp()
                P.op("dve", lambda e, onesf=onesf: e.memset(onesf.ap[0:4, :], 1.0), writes=[onesf])
                Ft = tmp()
                P.op("dve", lambda e, Ft=Ft, onesf=onesf, e1=e1: e.tensor_tensor_scan(
                    out=Ft.ap[0:4, :], data0=onesf.ap[0:4, :], data1=e1.ap[0:4, :], initial=misc.ap[0:4, 8 + l:9 + l],
                    op0=ALU.mult, op1=ALU.add), reads=[onesf, e1, misc], writes=[Ft])
                P.op("dve", lambda e, Ft=Ft: e.tensor_copy(out=misc.ap[0:4, 8 + l:9 + l], in_=Ft.ap[0:4, 511:512]), reads=[Ft], writes=[misc])
                g0 = tb_global(tb)
                resid = Ft
                for part in range(3):
                    hp = bt()
                    P.op("dve", lambda e, hp=hp, resid=resid: e.tensor_copy(out=hp.ap[0:4, :], in_=resid.ap[0:4, :]), reads=[resid], writes=[hp])
                    hn = bt()
                    dve_ts(hn, hn.ap[0:4, :], hp, hp.ap[0:4, :], -1.0, None, ALU.mult)
                    st(SC[("fFp", l)][:, part, g0:g0 + 512], SCR[("fFp", l)], hp, hp.ap[0:4, :])
                    st(SC[("fFn", l)][:, part, g0:g0 + 512], SCR[("fFn", l)], hn, hn.ap[0:4, :])
                    if part < 2:
                        nr = tmp()
                        dve_tt(nr, nr.ap[0:4, :], resid, resid.ap[0:4, :], hp, hp.ap[0:4, :], ALU.subtract)
                        resid = nr
            for (c0, key, scale) in ((0, "fq", isq), (512, "fk", None)):
                for u in range(2):
                    wv = load_w(w_in[l, :, c0 + u * 256:c0 + (u + 1) * 256].rearrange("(c p) j -> p c j", p=128), NCH, 256)
                    for t in range(2):
                        for tb in range(NTB):
                            ps = proj_fm(wv, lambda k, t=t, wv=wv: wv.ap[:, k, t * 128:(t + 1) * 128], NCH, hrhs(tb), tb)
                            r0 = (u * 2 + t) * 128
                            g0 = tb_global(tb)
                            store_fm(ps, 128, SC[(key, l)][r0:r0 + 128, g0:g0 + 512], SCR[(key, l)], scale)
            proj_tm_store(l, 1024, "fv", half)

            return
        nkeys = (half + 1) * Th
        P.op("dve", lambda e: e.memset(Abuf.ap[:, :], 0.0), writes=[Abuf])
        P.op("dve", lambda e: e.memset(Bbuf.ap[:, :], 0.0), writes=[Bbuf])
        P.op("dve", lambda e: e.memset(Abuf.ap[0:6, :], 1.0), writes=[Abuf])
        P.op("dve", lambda e: e.memset(Bbuf.ap[0:6, :], 1.0), writes=[Bbuf])
        for h in range(4):
            i = h % 2
            kb, vb, vdst = load_kv(l, "fk", "fv", h, i, nkeys)
            qb = load_q(l, "fq", h, i, half)
            ld(Abuf, Abuf.ap[3:6, 0:nkeys], SC[("fFn", l)][h, :, 0:nkeys], SCR[("fFn", l)])
            ld(Bbuf, Bbuf.ap[0:3, 0:Th], SC[("fFp", l)][h, :, half * Th:(half + 1) * Th], SCR[("fFp", l)])
            for tb in range(NTB):
                def smm(kt, ps, c0, kb=kb, qb=qb, tb=tb):
                    return [(ps.ap[:, c0:512], kb.ap[:, kt * 128:(kt + 1) * 128], qb.ap[:, tb * 512 + c0:(tb + 1) * 512], True, False),
                            (ps.ap[:, c0:512], Abuf.ap[:, kt * 128:(kt + 1) * 128], Bbuf.ap[:, tb * 512 + c0:(tb + 1) * 512], False, True)]
                attention("premask", smm, lambda kt, vdst=vdst: vdst[:, kt, :],
                          lambda po, rec, h=h, tb=tb: dve_tt(yT[h][tb], yT[h][tb].ap, po, po.ap, rec, rec.ap, ALU.mult),
                          tb, [kb, qb, vb, Abuf, Bbuf])
        fin_stage2()
        wo_accumulate(l, w_o, 0, half)

    def rope_store(pa, pb, npart, Cv, Sv, tb, dst_ap, dst_res, scale):
        c_ap = Cv.ap[0:npart, tb * 512:(tb + 1) * 512]
        s_ap = Sv.ap[0:npart, tb * 512:(tb + 1) * 512]
        t1, t2 = tmp(), tmp()
        dve_tt(t1, t1.ap[0:npart, :], pa, pa.ap[0:npart, :], Cv, c_ap, ALU.mult)
        dve_tt(t2, t2.ap[0:npart, :], pb, pb.ap[0:npart, :], Sv, s_ap, ALU.mult)
        o = bt()
        if scale is None:
            dve_tt(o, o.ap[0:npart, :], t1, t1.ap[0:npart, :], t2, t2.ap[0:npart, :], ALU.add)
        else:
            dve_tt(t1, t1.ap[0:npart, :], t1, t1.ap[0:npart, :], t2, t2.ap[0:npart, :], ALU.add)
            act(o, o.ap[0:npart, :], t1, t1.ap[0:npart, :], AF.Identity, scale=scale)
        st(dst_ap, dst_res, o, o.ap[0:npart, :])

    def load_rope_tables(Cn, Sn, npart, Cv, Sv, half):
        ld(Cv, Cv.ap[0:npart, 0:Th], SC[Cn][0:npart, half * Th:(half + 1) * Th], SCR[Cn])
        ld(Sv, Sv.ap[0:npart, 0:Th], SC[Sn][0:npart, half * Th:(half + 1) * Th], SCR[Sn])

    def mla(l, half, part):
        isq = 192 ** -0.5
        if part == "proj":
            lat = [V(bview(i, 1, F32).ap[:, 0:512], [blk_res[i]]) for i in range(5)]
            cqn = bview(5, 2)
            ckvn = bview(7, 1)
            krot = V(blk_ap[:, 8 * 2048:8 * 2048 + 1024].rearrange("p (c j) -> p c j", c=NCH), [blk_res[8]])
            rt = V(blk_ap[:, 9 * 2048 + 1024:9 * 2048 + 2048].bitcast(F32), [blk_res[9]])
            uqrot_ap = blk_ap[:, 9 * 2048:9 * 2048 + 768].rearrange("p (kh d) -> p kh d", d=64)
            uqrot = V(uqrot_ap, [blk_res[9]])
            Cm, Sm = bview(10, 1, F32), bview(11, 1, F32)
            load_rope_tables("rCm", "rSm", 64, Cm, Sm, half)
            wq = load_w(w_in[l, :, 1540:1796].rearrange("(c p) j -> p c j", p=128), NCH, 256)
            wq2 = load_w(w_in[l, :, 1796:2052].rearrange("(c p) j -> p c j", p=128), NCH, 256)
            wq3 = load_w(w_in[l, :, 2052:2244].rearrange("(c p) j -> p c j", p=128), NCH, 192)

            def wcol(j):
                u, o_ = divmod(j * 128, 256)
                return (wq, wq2, wq3)[u], o_
            krw = wq3.ap[:, :, 128:192]
            P.op("dve", lambda e: e.tensor_scalar(out=krot.ap[:, :, 0:32], in0=krw[:, :, 32:64], scalar1=-1.0, scalar2=None, op0=ALU.mult),
                 reads=[wq3], writes=[krot])
            P.op("dve", lambda e: e.tensor_copy(out=krot.ap[:, :, 32:64], in_=krw[:, :, 0:32]), reads=[wq3], writes=[krot])
            for tb in range(NTB):
                g0 = tb_global(tb)
                for j in range(5):
                    wv_, o_ = wcol(j)
                    ps = proj_fm(wv_, lambda k, wv_=wv_, o_=o_: wv_.ap[:, k, o_:o_ + 128], NCH, hrhs(tb), tb)
                    copy_any(lat[j], lat[j].ap, ps, ps.ap)
                for (idx, dim, gnm, dstv) in (((0, 1, 2), 384, "g_cq", cqn), ((3, 4), 256, "g_ckv", ckvn)):
                    ps = sumsq_ps([(lat[i], lat[i].ap) for i in idx], 128)
                    act(rt, rt.ap, ps, ps.ap, AF.Ln, scale=1.0 / dim, bias=misc.ap[:, 0:1], extra_reads=[misc])
                    act(rt, rt.ap, rt, rt.ap, AF.Exp, scale=-0.5)
                    for n_, i in enumerate(idx):
                        dve_stt(dstv, dstv.ap[:, n_ * Th + tb * 512:n_ * Th + (tb + 1) * 512], lat[i], lat[i].ap, sml(gnm, l, n_), rt, rt.ap,
                                ALU.mult, ALU.mult, extra=[small])
                pa = proj_fm(wq3, lambda k: wq3.ap[:, k, 128:192], NCH, hrhs(tb), tb, npart_out=64)
                pb = proj_fm(krot, lambda k: krot.ap[:, k, :], NCH, hrhs(tb), tb, npart_out=64)
                rope_store(pa, pb, 64, Cm, Sm, tb, SC[("mkr", l)][:, g0:g0 + 512], SCR[("mkr", l)], None)
            wuq = load_w(w_uq[l].rearrange("(k p) j -> p k j", p=128), 3, 768)
            wukv = load_w(w_ukv[l].rearrange("(k p) j -> p k j", p=128), 2, 1024)
            uq4 = wuq.ap.rearrange("p k (h d) -> p (k h) d", d=192)
            P.op("dve", lambda e: e.tensor_scalar(out=uqrot_ap[:, :, 0:32], in0=uq4[:, :, 160:192], scalar1=-1.0, scalar2=None, op0=ALU.mult),
                 reads=[wuq], writes=[uqrot])
            P.op("dve", lambda e: e.tensor_copy(out=uqrot_ap[:, :, 32:64], in_=uq4[:, :, 128:160]), reads=[wuq], writes=[uqrot])
            wv4 = wukv.ap.rearrange("p k (h two d) -> p k h two d", two=2, d=128)
            for tb in range(NTB):
                g0 = tb_global(tb)
                cq_r = [(cqn, cqn.ap[:, k * Th + tb * 512:k * Th + (tb + 1) * 512]) for k in range(3)]
                ckv_r = [(ckvn, ckvn.ap[:, k * Th + tb * 512:k * Th + (tb + 1) * 512]) for k in range(2)]
                for h in range(4):
                    ps = proj_fm(wuq, lambda k, h=h: wuq.ap[:, k, h * 192:h * 192 + 128], 3, cq_r, tb)
                    store_fm(ps, 128, SC[("mqn", l)][h * 128:(h + 1) * 128, g0:g0 + 512], SCR[("mqn", l)], isq)
                    pa = proj_fm(wuq, lambda k, h=h: wuq.ap[:, k, h * 192 + 128:h * 192 + 192], 3, cq_r, tb, npart_out=64)
                    pb = proj_fm(uqrot, lambda k, h=h: uqrot_ap[:, k * 4 + h, :], 3, cq_r, tb, npart_out=64)
                    rope_store(pa, pb, 64, Cm, Sm, tb, SC[("mqr", l)][h * 64:(h + 1) * 64, g0:g0 + 512], SCR[("mqr", l)], isq)
                    ps = proj_fm(wukv, lambda k, h=h: wukv.ap[:, k, h * 256:h * 256 + 128], 2, ckv_r, tb)
                    store_fm(ps, 128, SC[("mkn", l)][h * 128:(h + 1) * 128, g0:g0 + 512], SCR[("mkn", l)])
                for o_ in range(4):
                    ps = ps_s()
                    mm(ps, [(ps.ap.rearrange("p (h d) -> p h d", d=128),
                             ckvn.ap[:, k * Th + tb * 512 + o_ * 128:k * Th + tb * 512 + (o_ + 1) * 128], wv4[:, k, :, 1, :], k == 0, k == 1)
                            for k in range(2)], [wukv, ckvn])
                    o = bt()
                    copy_any(o, o.ap, ps, ps.ap)
                    r0 = g0 + o_ * 128
                    st(SC[("mv", l)][r0:r0 + 128, :], SCR[("mv", l)], o, o.ap)

            return
        nkeys = (half + 1) * Th
        P.op("dve", lambda e: e.memset(krbuf.ap[64:128, :], 0.0), writes=[krbuf])
        for qr_ in qrbuf:
            P.op("dve", lambda e, qr_=qr_: e.memset(qr_.ap[64:128, :], 0.0), writes=[qr_])
        ld(krbuf, krbuf.ap[0:64, 0:nkeys], SC[("mkr", l)][:, 0:nkeys], SCR[("mkr", l)])
        for h in range(4):
            i = h % 2
            kb, vb, vdst = load_kv(l, "mkn", "mv", h, i, nkeys)
            qb = load_q(l, "mqn", h, i, half)
            qr = qrbuf[i]
            ld(qr, qr.ap[0:64, 0:Th], SC[("mqr", l)][h * 64:(h + 1) * 64, half * Th:(half + 1) * Th], SCR[("mqr", l)])
            for tb in range(NTB):
                def smm(kt, ps, c0, kb=kb, qb=qb, qr=qr, tb=tb):
                    return [(ps.ap[:, c0:512], kb.ap[:, kt * 128:(kt + 1) * 128], qb.ap[:, tb * 512 + c0:(tb + 1) * 512], True, False),
                            (ps.ap[:, c0:512], krbuf.ap[:, kt * 128:(kt + 1) * 128], qr.ap[:, tb * 512 + c0:(tb + 1) * 512], False, True)]
                attention(None, smm, lambda kt, vdst=vdst: vdst[:, kt, :],
                          lambda po, rec, h=h, tb=tb: dve_tt(yT[h][tb], yT[h][tb].ap, po, po.ap, rec, rec.ap, ALU.mult),
                          tb, [kb, qb, vb, qr, krbuf])
        fin_stage2()
        wo_accumulate(l, w_o, 512, half)

    sgst_ap = sb("sgst", [128, 8 * (Th // 128)], F32)
    sgst = [V(sgst_ap[:, i * 8:(i + 1) * 8], Res(f"sgst{i}")) for i in range(Th // 128)]

    def sgu(l, half):
        NT = Th // 128
        uT = bview(0, 2)
        bsb = V(bview(2, 1, F32).ap[:, 0:512], [blk_res[2]])
        lng = V(bview(3, 1, F32).ap[:, 0:512], [blk_res[3]])
        lnb = V(bview(4, 1, F32).ap[:, 0:512], [blk_res[4]])
        wsT = V(blk_ap[:, 5 * 2048:5 * 2048 + 512], [blk_res[5]])
        wsn = V(blk_ap[:, 5 * 2048 + 1024:5 * 2048 + 2048].bitcast(F32), [blk_res[5]])
        vt = [V(blk_ap[:, (6 + i // 2) * 2048 + (i % 2) * 1024:(6 + i // 2) * 2048 + (i % 2 + 1) * 1024].bitcast(F32), Res(f"vt{i}"))
              for i in range(NT)]
        vnb = [V(blk_ap[:, 10 * 2048 + i * 512:10 * 2048 + (i + 1) * 512], [Res(f"vnb{i}")]) for i in range(4)]
        P.handoff(blk_res[6:11], vt + vnb)
        ld(bsb, bsb.ap, b_s[l, :].partition_broadcast(128), Res("bsd"))
        ld(lng, lng.ap, sgu_ln_g[l, :].partition_broadcast(128), Res("lngd"))
        ld(lnb, lnb.ap, sgu_ln_b[l, :].partition_broadcast(128), Res("lnbd"))
        ld(wsn, wsn.ap.rearrange("p (g s) -> p g s", g=4), w_s[l].rearrange("g t s -> t g s"), Res("wsd"))
        built = [False]

        def build_wsT():
            if built[0]:
                return
            built[0] = True
            wsm = bt()
            for g in range(4):
                dve_tt(wsm, wsm.ap[:, g * 128:(g + 1) * 128], wsn, wsn.ap[:, g * 128:(g + 1) * 128], cst, cst.ap[:, 128:256], ALU.mult)
            pst = ps_s()
            for g in range(4):
                P.op("pe", lambda e, g=g: e.matmul(pst.ap[:, g * 128:(g + 1) * 128], lhsT=wsm.ap[:, g * 128:(g + 1) * 128], rhs=ident_b, start=True, stop=True),
                     reads=[wsm, cb], writes=[pst], pe_acc=True, nmm=1)
            P.op("dve", lambda e: e.tensor_copy(out=wsT.ap, in_=pst.ap), reads=[pst], writes=[wsT])
        wu = [load_w(w_in[l, :, 2244 + u * 256:2244 + (u + 1) * 256].rearrange("(c p) j -> p c j", p=128), NCH, 256) for u in range(2)]
        wvv = [load_w(w_in[l, :, 2756 + u * 256:2756 + (u + 1) * 256].rearrange("(c p) j -> p c j", p=128), NCH, 256) for u in range(2)]
        pend = []

        def mix(item):
            tt, vn = item
            tb, o_ = divmod(tt, 4)
            build_wsT()
            ps = ps_s()
            for g in range(4):
                mm(ps, [(ps.ap[:, g * 128:(g + 1) * 128], vn.ap[:, g * 128:(g + 1) * 128], wsT.ap[:, g * 128:(g + 1) * 128], True, True)], [vn, wsT])
            t2 = tmp()
            dve_tt(t2, t2.ap, ps, ps.ap, bsb, bsb.ap, ALU.add)
            y3 = yT_ap[:, :, tb * 512 + o_ * 128:tb * 512 + (o_ + 1) * 128]
            u3 = uT.ap[:, 0:4 * Th].rearrange("p (g t) -> p g t", g=4)[:, :, tt * 128:(tt + 1) * 128]
            P.op("dve", lambda e, y3=y3, t2=t2, u3=u3: e.tensor_tensor(out=y3, in0=t2.ap.rearrange("p (g t) -> p g t", g=4), in1=u3, op=ALU.mult),
                 reads=[t2, uT], writes=[yT[g][tb] for g in range(4)])
        for tb in range(NTB):
            for g in range(4):
                wv_ = wu[g // 2]
                ps = proj_fm(wv_, lambda k, wv_=wv_, g=g: wv_.ap[:, k, (g % 2) * 128:(g % 2) * 128 + 128], NCH, hrhs(tb), tb)
                act(uT, uT.ap[:, g * Th + tb * 512:g * Th + (tb + 1) * 512], ps, ps.ap, AF.Gelu_apprx_tanh)
            for o_ in range(4):
                tt = tb * 4 + o_
                t, stt_ = vt[tt], sgst[tt]
                ps = ps_s()
                for u in range(2):
                    mm(ps, [(ps.ap[:, u * 256:(u + 1) * 256], hT[c][tb].ap[:, o_ * 128:(o_ + 1) * 128], wvv[u].ap[:, c, :], c == 0, c == NCH - 1) for c in range(NCH)],
                       [wvv[u]] + [hT[c][tb] for c in range(NCH)])
                P.op("act", lambda e, t=t, ps=ps, stt_=stt_: e.activation(out=t.ap, in_=ps.ap, func=AF.Gelu_apprx_tanh, accum_out=stt_.ap[:, 0:1]),
                     reads=[ps], writes=[t, stt_])
                junk = bt()
                P.op("act", lambda e, t=t, junk=junk, stt_=stt_: e.activation(out=junk.ap, in_=t.ap, func=AF.Square, accum_out=stt_.ap[:, 1:2]),
                     reads=[t], writes=[junk, stt_])
                dve_ts(stt_, stt_.ap[:, 2:4], stt_, stt_.ap[:, 0:2], 1.0 / 512, None, ALU.mult)
                dve_stt(stt_, stt_.ap[:, 4:5], stt_, stt_.ap[:, 2:3], stt_.ap[:, 2:3], stt_, stt_.ap[:, 3:4], ALU.mult, ALU.subtract)
                act(stt_, stt_.ap[:, 5:6], stt_, stt_.ap[:, 4:5], AF.Sqrt, scale=-1.0, bias=misc.ap[:, 0:1], extra_reads=[misc])
                P.op("dve", lambda e, stt_=stt_: e.reciprocal(out=stt_.ap[:, 5:6], in_=stt_.ap[:, 5:6]), reads=[stt_], writes=[stt_])
                dve_ts(t, t.ap, t, t.ap, stt_.ap[:, 2:3], stt_.ap[:, 5:6], ALU.subtract, ALU.mult, extra=[stt_])
                dve_tt(t, t.ap, t, t.ap, lng, lng.ap, ALU.mult)
                vn = vnb[tt % 4]
                dve_tt(vn, vn.ap, t, t.ap, lnb, lnb.ap, ALU.add)
                pend.append((tt, vn))
                if len(pend) > 2:
                    mix(pend.pop(0))
        while pend:
            mix(pend.pop(0))
        P.handoff(vt + vnb, blk_res[6:11])
        wo_accumulate(l, w_o, 1024, half)

    def diff(l, half, part):
        lam_init = 0.8 - 0.6 * math.exp(-0.3 * l)
        isq = 64 ** -0.5
        if part == "proj":
            Cd, Sd = bview(0, 1, F32), bview(1, 1, F32)
            load_rope_tables("rCd", "rSd", 128, Cd, Sd, half)
            pend = []

            def finish(item):
                pa, q16, key, r0, tb, scale = item
                g0 = tb_global(tb)
                pb = ps_s()
                mm(pb, [(pb.ap, Rd_b, q16.ap, True, True)], [q16, cb])
                rope_store(pa, pb, 128, Cd, Sd, tb, SC[(key, l)][r0:r0 + 128, g0:g0 + 512], SCR[(key, l)], scale)
            for (c0, key, scale) in ((3268, "dq", isq), (3780, "dk", None)):
                for u in range(2):
                    wv = load_w(w_in[l, :, c0 + u * 256:c0 + (u + 1) * 256].rearrange("(c p) j -> p c j", p=128), NCH, 256)
                    for t in range(2):
                        for tb in range(NTB):
                            pa = proj_fm(wv, lambda k, t=t, wv=wv: wv.ap[:, k, t * 128:(t + 1) * 128], NCH, hrhs(tb), tb)
                            q16 = bt()
                            act(q16, q16.ap, pa, pa.ap, AF.Identity)
                            pend.append((pa, q16, key, (u * 2 + t) * 128, tb, scale))
                            if len(pend) > 1:
                                finish(pend.pop(0))
            while pend:
                finish(pend.pop(0))
            proj_tm_store(l, 4292, "dv", half)

            return
        neglam = lamt.ap[:, l:l + 1]
        nkeys = (half + 1) * Th
        o1b = [V(blk_ap[:, 6 * 2048 + i_ * 1024:6 * 2048 + (i_ + 1) * 1024].bitcast(F32), [Res(f"o1b{i_}")]) for i_ in range(2)]
        P.handoff([blk_res[6]], o1b)
        pendn = []

        def post(item):
            o1, h, tb = item
            ps = sumsq_ps([(o1, o1.ap)], 128)
            r = rstd_from_ps(ps, 128, 128)
            dve_stt(o1, o1.ap, o1, o1.ap, sml("g_diff", l, 0), r, r.ap, ALU.mult, ALU.mult, extra=[small])
            dve_ts(yT[h][tb], yT[h][tb].ap, o1, o1.ap, 1.0 - lam_init, None, ALU.mult)
        nq = 0
        qm1 = [bview(7, 1), bview(8, 1)]
        for i_ in range(2):
            P.op("dve", lambda e, i_=i_: e.memset(qbuf[i_].ap[64:128, :], 0.0), writes=[qbuf[i_]])
            P.op("dve", lambda e, i_=i_: e.memset(qm1[i_].ap[0:64, :], 0.0), writes=[qm1[i_]])
        for h in range(4):
            i = h % 2
            kb, vb, vdst = load_kv(l, "dk", "dv", h, i, nkeys)
            qms = (qbuf[i], qm1[i])
            ld(qms[0], qms[0].ap[0:64, 0:Th], SC[("dq", l)][h * 128:h * 128 + 64, half * Th:(half + 1) * Th], SCR[("dq", l)])
            ld(qms[1], qms[1].ap[64:128, 0:Th], SC[("dq", l)][h * 128 + 64:(h + 1) * 128, half * Th:(half + 1) * Th], SCR[("dq", l)])
            for tb in range(NTB):
                o1 = o1b[nq % 2]
                nq += 1
                for m in range(2):
                    def smm(kt, ps, c0, kb=kb, qb=qms[m], tb=tb):
                        return [(ps.ap[:, c0:512], kb.ap[:, kt * 128:(kt + 1) * 128], qb.ap[:, tb * 512 + c0:(tb + 1) * 512], True, True)]
                    if m == 0:
                        def cons(po, rec, o1=o1):
                            dve_tt(o1, o1.ap, po, po.ap, rec, rec.ap, ALU.mult)
                    else:
                        def cons(po, rec, o1=o1, h=h, tb=tb):
                            dve_tt(rec, rec.ap, po, po.ap, rec, rec.ap, ALU.mult)
                            dve_stt(o1, o1.ap, rec, rec.ap, neglam, o1, o1.ap, ALU.mult, ALU.add, extra=[lamt])
                            pendn.append((o1, h, tb))
                            if len(pendn) > 1:
                                post(pendn.pop(0))
                    attention(None, smm, lambda kt, vdst=vdst: vdst[:, kt, :], cons, tb, [kb, qms[m], vb])
        fin_stage2()
        while pendn:
            post(pendn.pop(0))
        P.handoff(o1b, [blk_res[6]])
        wo_accumulate(l, w_o, 1536, half)

    def cross(l, half):
        isq = 128 ** -0.5
        memT = bview(0, 2)
        ld(memT, memT.ap.rearrange("p (c t) -> p c t", c=NCH), SC["memT"].rearrange("(c p) t -> p c t", p=128), SCR["memT"])
        memv = memT.ap.rearrange("p (c t) -> p c t", c=NCH)
        qx = bview(2, 2)
        kx = bview(4, 1)
        vx = bview(5, 1)
        for u in range(2):
            wv = load_w(w_ck[l, :, u * 256:(u + 1) * 256].rearrange("(c p) j -> p c j", p=128), NCH, 256)
            for t in range(2):
                h = u * 2 + t
                ps = ps_s()
                mm(ps, [(ps.ap[:, 0:256], wv.ap[:, c, t * 128:(t + 1) * 128], memv[:, c, :], c == 0, c == NCH - 1) for c in range(NCH)], [wv, memT])
                copy_any(kx, kx.ap[:, h * 256:(h + 1) * 256], ps, ps.ap[:, 0:256])
        for u in range(2):
            wv = load_w(w_cv[l, :, u * 256:(u + 1) * 256].rearrange("(c p) j -> p c j", p=128), NCH, 256)
            for mt in range(2):
                ps = ps_s()
                mm(ps, [(ps.ap[:, 0:256], memv[:, c, mt * 128:(mt + 1) * 128], wv.ap[:, c, :], c == 0, c == NCH - 1) for c in range(NCH)], [wv, memT])
                copy_any(vx, vx.ap[:, mt * 512 + u * 256:mt * 512 + (u + 1) * 256], ps, ps.ap[:, 0:256])
        norm_to_hT("g_cross", l)
        for u in range(2):
            wv = load_w(w_cq[l, :, u * 256:(u + 1) * 256].rearrange("(c p) j -> p c j", p=128), NCH, 256)
            for t in range(2):
                h = u * 2 + t
                for tb in range(NTB):
                    ps = proj_fm(wv, lambda k, t=t, wv=wv: wv.ap[:, k, t * 128:(t + 1) * 128], NCH, hrhs(tb), tb)
                    copy_any(qx, qx.ap[:, h * Th + tb * 512:h * Th + (tb + 1) * 512], ps, ps.ap, isq)
        for h in range(4):
            for tb in range(NTB):
                po, pz = ps_o(), ps_z()
                for mt in range(2):
                    ps = ps_s()
                    mm(ps, [(ps.ap, kx.ap[:, h * 256 + mt * 128:h * 256 + (mt + 1) * 128], qx.ap[:, h * Th + tb * 512:h * Th + (tb + 1) * 512], True, True)], [kx, qx])
                    pt = bt()
                    act(pt, pt.ap, ps, ps.ap, AF.Exp)
                    mm(po, [(po.ap, vx.ap[:, mt * 512 + h * 128:mt * 512 + (h + 1) * 128], pt.ap, mt == 0, mt == 1)], [pt, vx])
                    mm(pz, [(pz.ap, ones_b, pt.ap, mt == 0, mt == 1)], [pt, cb])
                rec = tmp()
                act(rec, rec.ap, pz, pz.ap, AF.Ln)
                act(rec, rec.ap, rec, rec.ap, AF.Exp, scale=-1.0)
                dve_tt(yT[h][tb], yT[h][tb].ap, po, po.ap, rec, rec.ap, ALU.mult)
        wo_accumulate(l, w_co, 0, half)

    def ffn(l, half):
        norm_to_hT("g_ffn", l)
        abuf = [bview(0, 1), bview(1, 1)]
        wpool = wslots + [bview(2, 2), bview(4, 2), bview(6, 2), bview(8, 2), bview(10, 2)]
        nslots = 2 * NTB
        dper = NCH // nslots

        def down_groups(wd, ab, dts):
            for dt_ in dts:
                for tb in range(NTB):
                    ps = nxt("O", psb[4:8])
                    mm(ps, [(ps.ap, wd.ap[:, t, dt_ * 128:(dt_ + 1) * 128], ab.ap[:, t * Th + tb * 512:t * Th + (tb + 1) * 512], t == 0, t == 1) for t in range(2)],
                       [wd, ab])
                    dve_tt(xs[dt_][tb], xs[dt_][tb].ap, ps, ps.ap, xs[dt_][tb], xs[dt_][tb].ap, ALU.add)
        prev = None
        for j in range(FF // 256):
            wg = load_w(w_gate[l, :, j * 256:(j + 1) * 256].rearrange("(c p) j -> p c j", p=128), NCH, 256, wpool)
            wu = load_w(w_up[l, :, j * 256:(j + 1) * 256].rearrange("(c p) j -> p c j", p=128), NCH, 256, wpool)
            ab = abuf[j % 2]
            k = 0
            for t in range(2):
                for tb in range(NTB):
                    pg = proj_fm(wg, lambda k_, t=t, wg=wg: wg.ap[:, k_, t * 128:(t + 1) * 128], NCH, hrhs(tb), tb)
                    sg = tmp()
                    act(sg, sg.ap, pg, pg.ap, AF.Silu)
                    if prev is not None and "B" in KOPT:
                        down_groups(prev[0], prev[1], range(k * dper, k * dper + dper // 2))
                    pu = proj_fm(wu, lambda k_, t=t, wu=wu: wu.ap[:, k_, t * 128:(t + 1) * 128], NCH, hrhs(tb), tb)
                    dve_tt(ab, ab.ap[:, t * Th + tb * 512:t * Th + (tb + 1) * 512], pu, pu.ap, sg, sg.ap, ALU.mult)
                    if prev is not None:
                        down_groups(prev[0], prev[1], range(k * dper + dper // 2, (k + 1) * dper) if "B" in KOPT else range(k * dper, (k + 1) * dper))
                    k += 1
            wd = load_w(w_down[l, j * 256:(j + 1) * 256, :].rearrange("(k p) d -> p k d", p=128), 2, 2048, wpool)
            prev = (wd, ab)
        down_groups(prev[0], prev[1], range(NCH))

    for half in range(HALVES):
        cur["half"] = half
        for tt in range(0 if "xload" in KSKIP else Th // 128):
            tb, o_ = divmod(tt, 4)
            r0 = half * Th + tt * 128
            for c0 in range(0, NCH, 4):
                xin = tmp()
                ld(xin, xin.ap, x_d[r0:r0 + 128, c0 * 128:(c0 + 4) * 128], Res("xd"))
                if "xmm" in KSKIP:
                    continue
                ps = ps_s()
                for j in range(4):
                    P.op("pe", lambda e, ps=ps, j=j, xin=xin: e.matmul(ps.ap[:, j * 128:(j + 1) * 128], lhsT=xin.ap[:, j * 128:(j + 1) * 128], rhs=ident_f, start=True, stop=True),
                         reads=[xin, cst], writes=[ps], pe_acc=True, nmm=1)
                dst_ap = xs_ap[:, c0:c0 + 4, tb * 512 + o_ * 128:tb * 512 + (o_ + 1) * 128]
                if "xcp" in KSKIP:
                    continue
                dv_ = V(dst_ap, [xs[c0 + j][tb].res[0] for j in range(4)])
                src_ = ps.ap.rearrange("p (j t) -> p j t", j=4)
                P.op("dve", lambda e, dst_ap=dst_ap, src_=src_: e.tensor_copy(out=dst_ap, in_=src_), reads=[ps], writes=[dv_])
        import os
        stg = os.environ.get("KSTAGE", "fmsdcx")
        for l in range(L):
            P.phases.append((f"h{half}l{l}:norm", P.nmm))
            norm_to_hT("g_mix", l)
            P.phases.append((f"h{half}l{l}:projs", P.nmm))
            fox(l, half, "proj")
            mla(l, half, "proj")
            diff(l, half, "proj")
            P.phases.append((f"h{half}l{l}:sgu", P.nmm))
            sgu(l, half)
            P.phases.append((f"h{half}l{l}:fox", P.nmm))
            fox(l, half, "attn")
            P.phases.append((f"h{half}l{l}:mla", P.nmm))
            mla(l, half, "attn")
            P.phases.append((f"h{half}l{l}:diff", P.nmm))
            diff(l, half, "attn")
            P.phases.append((f"h{half}l{l}:cross", P.nmm))
            cross(l, half)
            P.phases.append((f"h{half}l{l}:ffn", P.nmm))
            ffn(l, half)
        P.phases.append((f"h{half}:final", P.nmm))
        for tb in range(0 if "final" in KSKIP else NTB):
            ps = sumsq_ps([(xs[c][tb], xs[c][tb].ap) for c in range(NCH)], 128)
            r0_ = rstd_from_ps(ps, 128, D)
            r = V(bview(11, 1, F32).ap[:, 0:512], [blk_res[11]])
            P.op("dve", lambda e, r=r, r0_=r0_: e.tensor_copy(out=r.ap, in_=r0_.ap), reads=[r0_], writes=[r])
            for o_ in range(4):
                osb = bview(2 * (o_ % 2), 2, F32)
                for c0 in range(0, NCH, 4):
                    pst = ps_s()
                    for j in range(4):
                        c = c0 + j
                        xn = tmp()
                        dve_stt(xn, xn.ap[:, 0:128], xs[c][tb], xs[c][tb].ap[:, o_ * 128:(o_ + 1) * 128], sml("g_final", 0, c),
                                r, r.ap[:, o_ * 128:(o_ + 1) * 128], ALU.mult, ALU.mult, extra=[small])
                        P.op("pe", lambda e, pst=pst, j=j, xn=xn: e.matmul(pst.ap[:, j * 128:(j + 1) * 128], lhsT=xn.ap[:, 0:128], rhs=ident_f, start=True, stop=True),
                             reads=[xn, cst], writes=[pst], pe_acc=True, nmm=1)
                    copy_any(osb, osb.ap[:, c0 * 128:(c0 + 4) * 128], pst, pst.ap)
                r0 = half * Th + tb * 512 + o_ * 128
                st(out_d[r0:r0 + 128, :], Res("outd"), osb, osb.ap)
    P.wait_all("sp", list(ALL_RES))
    for e in COMPUTE:
        if P.ecnt[e]:
            P._need("sp", (P.esem[e], P.ecnt[e], e))
    P.emit()
    return nc, P


def pack_small(inp, b, L):
    soff, NS = small_layout(L)
    sm = np.zeros((128, NS), np.float32)

    def fm(v):
        return np.ascontiguousarray(v.reshape(-1, 128).T)
    for l in range(L):
        for nm in ("g_mix", "g_cross", "g_ffn", "g_cq", "g_ckv", "g_diff"):
            a = fm(np.asarray(inp[nm][l], np.float32))
            sm[:, soff[(nm, l)]:soff[(nm, l)] + a.shape[1]] = a
        sm[0:4, soff[("b_f", l)]] = np.asarray(inp["b_f"][l], np.float32)
    for nm in ("g_mem", "g_final"):
        a = fm(np.asarray(inp[nm], np.float32))
        sm[:, soff[(nm, 0)]:soff[(nm, 0)] + 16] = a
    return sm


def run(inp, S, L, FF, ncores):
    nc, P = build_program(S, L, FF)
    consts = make_consts()
    small = pack_small(inp, 0, L)
    shared = {}
    for k in ("w_in", "w_uq", "w_ukv", "sgu_ln_g", "sgu_ln_b", "w_s", "lam_q1", "lam_k1", "lam_q2", "lam_k2",
              "w_o", "w_cq", "w_ck", "w_cv", "w_co", "w_gate", "w_up", "w_down"):
        shared[k] = np.ascontiguousarray(np.asarray(inp[k], np.float32))
    shared["b_s"] = np.ascontiguousarray(np.asarray(inp["b_s"], np.float32).reshape(L, 512))
    shared["small"] = small
    shared["consts"] = consts
    in_maps = []
    for b in range(ncores):
        m = dict(shared)
        m["x"] = np.ascontiguousarray(np.asarray(inp["x"][b], np.float32))
        m["mem"] = np.ascontiguousarray(np.asarray(inp["mem"][b], np.float32))
        m["positions"] = np.ascontiguousarray(np.asarray(inp["positions"][b], np.int32).reshape(1, S))
        in_maps.append(m)
    res = run_bass_kernel_spmd(nc, in_maps, core_ids=list(range(ncores)))
    return np.stack([r["out"] for r in res.results], axis=0)


def kernel(**inputs):
    return run(inputs, 2048, 4, 5632, 8).astype(np.float32)
```

```python
import math
import numpy as np
import concourse.bass as bass
import concourse.mybir as mybir
from concourse.bass_utils import run_bass_kernel_spmd

F32 = mybir.dt.float32
BF16 = mybir.dt.bfloat16
I32 = mybir.dt.int32
AF = mybir.ActivationFunctionType
ALU = mybir.AluOpType

D = 2048
NCH = 16
MEM = 256
EPS = 1e-6
N_IN = 4804
COMPUTE = ("pe", "act", "dve", "pool")


ALL_RES = []


class Res:
    __slots__ = ("name", "wtok", "rtoks", "ld_sem", "ld_n", "st_sem", "st_n", "psum")

    def __init__(self, name, psum=False):
        ALL_RES.append(self)
        self.name = name
        self.psum = psum
        self.wtok = None
        self.rtoks = {}
        self.ld_sem = None
        self.ld_n = 0
        self.st_sem = None
        self.st_n = 0


class V:
    __slots__ = ("ap", "res")

    def __init__(self, ap, res):
        self.ap = ap
        self.res = res if isinstance(res, list) else [res]

    def __getitem__(self, idx):
        return self.ap[idx]


def _rl(vs):
    out = []
    for v in vs:
        if isinstance(v, V):
            out.extend(v.res)
        else:
            out.append(v)
    return out


class Prog:
    def __init__(self, nc):
        self.nc = nc
        self.streams = {e: [] for e in ("pe", "act", "dve", "pool", "sp")}
        self.esem = {e: nc.alloc_semaphore(f"es_{e}") for e in COMPUTE}
        self.ecnt = {e: 0 for e in COMPUTE}
        self.waited = {e: {} for e in self.streams}
        self.nsem = 0
        self.ninstr = 0
        self.nmm = 0
        self.phases = []

    def new_sem(self, name):
        self.nsem += 1
        return self.nc.alloc_semaphore(f"s{self.nsem}_{name}")

    def _need(self, e, tok):
        if tok is None:
            return
        sem, val, _ = tok
        w = self.waited[e]
        if w.get(sem.num, 0) >= val:
            return
        w[sem.num] = val
        self.streams[e].append(("wait", sem, val))

    def _deps(self, e, reads, writes, pe_acc):
        for r in reads:
            self._need(e, r.wtok)
            if r.psum:
                for tok in r.rtoks.values():
                    if tok[2] != e:
                        self._need(e, tok)
        for r in writes:
            if not (pe_acc and r.wtok is not None and r.wtok[2] == "pe"):
                self._need(e, r.wtok)
            for tok in r.rtoks.values():
                self._need(e, tok)

    @staticmethod
    def _mark(tok, reads, writes):
        for r in reads:
            r.rtoks[tok[0].num] = tok
        for r in writes:
            r.wtok = tok
            r.rtoks = {}

    def op(self, e, fn, reads=(), writes=(), pe_acc=False, nmm=0):
        self.nmm += nmm
        reads, writes = _rl(reads), _rl(writes)
        self._deps(e, reads, writes, pe_acc)
        self.ecnt[e] += 1
        tok = (self.esem[e], self.ecnt[e], e)
        self.streams[e].append(("op", fn, self.esem[e]))
        self._mark(tok, reads, writes)
        self.ninstr += 1

    def dma(self, q, fns, reads=(), writes=(), owner=None, kind="ld"):
        reads, writes = _rl(reads), _rl(writes)
        self._deps(q, reads, writes, False)
        o = owner.res[0]
        if kind == "ld":
            if o.ld_sem is None:
                o.ld_sem = self.new_sem("ld")
            o.ld_n += len(fns)
            sem, n = o.ld_sem, o.ld_n
        else:
            if o.st_sem is None:
                o.st_sem = self.new_sem("st")
            o.st_n += len(fns)
            sem, n = o.st_sem, o.st_n
        tok = (sem, 16 * n, "dma")
        for fn in fns:
            self.streams[q].append(("dma", fn, sem))
        self._mark(tok, reads, writes)
        self.ninstr += len(fns)

    @staticmethod
    def handoff(srcs, dsts):
        toks = []
        for s_ in _rl(srcs):
            if s_.wtok is not None:
                toks.append(s_.wtok)
            toks.extend(s_.rtoks.values())
        for d in _rl(dsts):
            for t in toks:
                cur_ = d.rtoks.get(t[0].num)
                if cur_ is None or cur_[1] < t[1]:
                    d.rtoks[t[0].num] = t

    def wait_all(self, e, vs):
        for r in _rl(vs):
            self._need(e, r.wtok)
            for tok in r.rtoks.values():
                self._need(e, tok)

    def emit(self):
        nc = self.nc
        streams = self.streams

        def run(name):
            def body(engine):
                for item in streams[name]:
                    if item[0] == "wait":
                        engine.wait_ge(item[1], item[2])
                    elif item[0] == "op":
                        item[1](engine).then_inc(item[2], 1)
                    else:
                        item[1](engine).then_inc(item[2], 16)
            return body

        with nc.Block() as block:
            block.tensor(run("pe"))
            block.scalar(run("act"))
            block.vector(run("dve"))
            block.gpsimd(run("pool"))
            block.sync(run("sp"))


def small_layout(L):
    off = {}
    c = 0
    for l in range(L):
        for nm, w in (("g_mix", 16), ("g_cross", 16), ("g_ffn", 16), ("g_cq", 3), ("g_ckv", 2),
                      ("g_diff", 1), ("b_f", 1)):
            off[(nm, l)] = c
            c += w
    for nm in ("g_mem", "g_final"):
        off[(nm, 0)] = c
        c += 16
    return off, c


CONST_COLS = {"ident": 0, "tril": 128, "freq_mla": 256, "freq_diff": 257, "mask": 258}
NCST_SB = 258
NCONST = 258 + 4 * 512 + 128


def make_consts():
    c = np.zeros((128, NCONST), np.float32)
    c[:, 0:128] = np.eye(128, dtype=np.float32)
    p = np.arange(128)[:, None]
    f = np.arange(128)[None, :]
    c[:, 128:256] = (f <= p).astype(np.float32)
    ff = np.arange(512)[None, :]
    for j in range(4):
        c[:, 258 + j * 512:258 + (j + 1) * 512] = (ff - p - 128 * j >= 0).astype(np.float32)
    theta = 500000.0
    fm = theta ** (-np.arange(0, 64, 2, dtype=np.float32) / 64.0)
    c[0:64, CONST_COLS["freq_mla"]] = np.concatenate([fm, fm])
    fd = theta ** (-np.arange(0, 16, 2, dtype=np.float32) / 16.0)
    one = np.zeros(64, np.float32)
    one[0:8] = fd
    one[8:16] = fd
    c[:, CONST_COLS["freq_diff"]] = np.concatenate([one, one])
    R = np.zeros((128, 128), np.float32)
    for base in (0, 64):
        for i in range(8):
            R[base + 8 + i, base + i] = -1.0
            R[base + i, base + 8 + i] = 1.0
    c[:, 258 + 2048:258 + 2048 + 128] = R
    return c


def build_program(S, L, FF, dbg=None):
    del ALL_RES[:]
    import os as _os
    KOPT = _os.environ.get("KOPT", "FCD")
    HALVES = 2
    Th = S // HALVES
    NTB = Th // 512
    NKT = S // 128
    nc = bass.Bass("TRN2", target_bir_lowering=False)
    P = Prog(nc)

    def din(name, shape, dt=F32):
        return nc.dram_tensor(name, list(shape), dt, kind="ExternalInput").ap()

    x_d = din("x", [S, D])
    mem_d = din("mem", [MEM, D])
    pos_d = din("positions", [1, S], I32)
    w_in = din("w_in", [L, D, N_IN])
    w_uq = din("w_uq", [L, 384, 768])
    w_ukv = din("w_ukv", [L, 256, 1024])
    sgu_ln_g = din("sgu_ln_g", [L, 512])
    sgu_ln_b = din("sgu_ln_b", [L, 512])
    w_s = din("w_s", [L, 4, 128, 128])
    b_s = din("b_s", [L, 512])
    lam_d = {k: din(k, [L, 64]) for k in ("lam_q1", "lam_k1", "lam_q2", "lam_k2")}
    w_o = din("w_o", [L, D, D])
    w_cq = din("w_cq", [L, D, 512])
    w_ck = din("w_ck", [L, D, 512])
    w_cv = din("w_cv", [L, D, 512])
    w_co = din("w_co", [L, 512, D])
    w_gate = din("w_gate", [L, D, FF])
    w_up = din("w_up", [L, D, FF])
    w_down = din("w_down", [L, FF, D])
    soff, NS = small_layout(L)
    small_d = din("small", [128, NS])
    consts_d = din("consts", [128, NCONST])
    out_d = nc.dram_tensor("out", [S, D], F32, kind="ExternalOutput").ap()

    def scratch(name, shape, dt=BF16):
        return nc.dram_tensor("scr_" + name, list(shape), dt, kind="ExternalOutput").ap()

    SC = {}
    SCR = {}
    for l in range(L):
        for nm, shp, dt in (("fq", [512, S], BF16), ("fk", [512, S], BF16), ("fv", [S, 512], BF16),
                            ("fFp", [4, 3, S], BF16), ("fFn", [4, 3, S], BF16),
                            ("mqn", [512, S], BF16), ("mqr", [256, S], BF16), ("mkn", [512, S], BF16),
                            ("mkr", [64, S], BF16), ("mv", [S, 512], BF16),
                            ("dq", [512, S], BF16), ("dk", [512, S], BF16), ("dv", [S, 512], BF16)):
            SC[(nm, l)] = scratch(f"{nm}{l}", shp, dt)
            SCR[(nm, l)] = Res(f"{nm}{l}")
    for nm, shp in (("rCm", [64, S]), ("rSm", [64, S]), ("rCd", [128, S]), ("rSd", [128, S])):
        SC[nm] = scratch(nm, shp, F32)
        SCR[nm] = Res(nm)
    SC["memT"] = scratch("memT", [D, MEM], BF16)
    SCR["memT"] = Res("memT")

    def sb(name, shape, dt):
        return nc.alloc_sbuf_tensor("sb_" + name, list(shape), dt).ap()

    xs_ap = sb("xs", [128, NCH, Th], F32)
    xs = [[V(xs_ap[:, c, tb * 512:(tb + 1) * 512], Res(f"xs{c}_{tb}")) for tb in range(NTB)] for c in range(NCH)]
    hT_ap = sb("hT", [128, NCH, Th], BF16)
    hT = [[V(hT_ap[:, c, tb * 512:(tb + 1) * 512], Res(f"hT{c}_{tb}")) for tb in range(NTB)] for c in range(NCH)]
    NW = 4
    wslots = [V(sb(f"w{i}", [128, 4096], BF16), Res(f"w{i}")) for i in range(NW)]
    yT_ap = sb("yT", [128, 4, Th], BF16)
    yT = [[V(yT_ap[:, k, tb * 512:(tb + 1) * 512], Res(f"yT{k}_{tb}")) for tb in range(NTB)] for k in range(4)]
    NTMP = 6
    tmps = [V(sb(f"tmp{i}", [128, 512], F32), Res(f"tmp{i}")) for i in range(NTMP)]
    NBT = 4
    bts = [V(sb(f"bt{i}", [128, 512], BF16), Res(f"bt{i}")) for i in range(NBT)]
    small = V(sb("small", [128, NS], F32), Res("small"))
    cst = V(sb("cst", [128, NCST_SB], F32), Res("cst"))
    ident_f = cst.ap[:, 0:128]
    misc = V(sb("misc", [128, 16], F32), Res("misc"))
    cb = V(sb("cb", [128, 128 + 128 + 128 + 4 * 512 + 128], BF16), Res("cb"))
    ident_b = cb.ap[:, 0:128]
    ones_b = cb.ap[:, 128:256]
    tril_b = cb.ap[:, 256:384]
    mask_b = [cb.ap[:, 384 + j * 512:384 + (j + 1) * 512] for j in range(4)]
    Rd_b = cb.ap[:, 384 + 2048:384 + 2048 + 128]
    NBLK = 12
    blk_ap = sb("blk", [128, NBLK * 2048], BF16)
    blk_res = [Res(f"blk{i}") for i in range(NBLK)]

    def bview(b0, nb, dt=BF16, shape=None):
        ap = blk_ap[:, b0 * 2048:(b0 + nb) * 2048]
        if dt != BF16:
            ap = ap.bitcast(dt)
        return V(ap, blk_res[b0:b0 + nb])

    psb = [V(nc.alloc_psum_tensor(f"ps{i}", [128, 512], F32).ap(), Res(f"ps{i}", psum=True)) for i in range(8)]
    rot = {"w": 0, "wf": 0, "tmp": 0, "bt": 0, "S": 0, "O": 0, "Z": 0, "ev": 0}

    def nxt(kind, lst):
        i = rot[kind]
        rot[kind] = i + 1
        return lst[i % len(lst)]

    def ps_s():
        return nxt("S", psb[0:4])

    def ps_o():
        return nxt("O", psb[4:6])

    def ps_z():
        return nxt("Z", psb[6:8])

    def tmp():
        return nxt("tmp", tmps)

    def bt():
        return nxt("bt", bts)

    def evac_engine():
        rot["ev"] += 1
        return "act" if rot["ev"] % 2 else "dve"

    def sml(nm, l, j=0, n=128):
        c = soff[(nm, l)] + j
        return small.ap[0:n, c:c + 1]

    def load_w(src, a, b, pool=None):
        slot = nxt("w", wslots) if pool is None else nxt("wf", pool)
        npart = src.shape[0]
        dst = slot.ap[0:npart, 0:a * b].rearrange("p (a b) -> p a b", a=a)
        if b > 512:
            bb = max(d_ for d_ in range(1, 513) if b % d_ == 0)
            d2 = dst.rearrange("p a (b2 b) -> p a b2 b", b=bb)
            s2 = src.rearrange("p a (b2 b) -> p a b2 b", b=bb)
        else:
            d2, s2 = dst, src
        P.dma("pool", [lambda e: e.dma_start(out=d2, in_=s2)], writes=[slot], owner=slot)
        return V(dst, slot.res)

    def mm(out_v, mms, reads):
        def fn(e):
            ins = None
            for (o, lt, r, st, sp) in mms:
                ins = e.matmul(o, lhsT=lt, rhs=r, start=st, stop=sp)
            return ins
        P.op("pe", fn, reads=reads, writes=[out_v], pe_acc=True, nmm=len(mms))

    def act(out_v, out_ap, in_v, in_ap, func, scale=1.0, bias=None, extra_reads=()):
        kw = {}
        if bias is not None:
            kw["bias"] = bias
        P.op("act", lambda e: e.activation(out=out_ap, in_=in_ap, func=func, scale=scale, **kw),
             reads=[in_v] + list(extra_reads), writes=[out_v])

    def copy_any(out_v, out_ap, in_v, in_ap, scale=None):
        eng = evac_engine()
        if eng == "act":
            if scale is None:
                P.op("act", lambda e: e.activation(out=out_ap, in_=in_ap, func=AF.Identity), reads=[in_v], writes=[out_v])
            else:
                P.op("act", lambda e: e.activation(out=out_ap, in_=in_ap, func=AF.Identity, scale=scale), reads=[in_v], writes=[out_v])
        else:
            if scale is None:
                P.op("dve", lambda e: e.tensor_copy(out=out_ap, in_=in_ap), reads=[in_v], writes=[out_v])
            else:
                P.op("dve", lambda e: e.tensor_scalar(out=out_ap, in0=in_ap, scalar1=scale, scalar2=None, op0=ALU.mult),
                     reads=[in_v], writes=[out_v])

    def dve_tt(out_v, out_ap, a_v, a_ap, b_v, b_ap, op):
        P.op("dve", lambda e: e.tensor_tensor(out=out_ap, in0=a_ap, in1=b_ap, op=op), reads=[a_v, b_v], writes=[out_v])

    def dve_ts(out_v, out_ap, a_v, a_ap, s1, s2, op0, op1=None, extra=()):
        if op1 is None:
            P.op("dve", lambda e: e.tensor_scalar(out=out_ap, in0=a_ap, scalar1=s1, scalar2=None, op0=op0),
                 reads=[a_v] + list(extra), writes=[out_v])
        else:
            P.op("dve", lambda e: e.tensor_scalar(out=out_ap, in0=a_ap, scalar1=s1, scalar2=s2, op0=op0, op1=op1),
                 reads=[a_v] + list(extra), writes=[out_v])

    def dve_stt(out_v, out_ap, a_v, a_ap, sc, b_v, b_ap, op0, op1, extra=()):
        P.op("dve", lambda e: e.scalar_tensor_tensor(out=out_ap, in0=a_ap, scalar=sc, in1=b_ap, op0=op0, op1=op1),
             reads=[a_v, b_v] + list(extra), writes=[out_v])

    def ld(dst_v, dst_ap, src_ap, src_res):
        P.dma("sp", [lambda e: e.dma_start(out=dst_ap, in_=src_ap)], reads=[src_res], writes=[dst_v], owner=dst_v)

    def st(dst_ap, dst_res, src_v, src_ap):
        P.dma("sp", [lambda e: e.dma_start(out=dst_ap, in_=src_ap)], reads=[src_v], writes=[dst_res], owner=src_v, kind="st")

    def rstd_from_ps(ps, npart, dim):
        t = tmp()
        act(t, t.ap[0:npart, :], ps, ps.ap[0:npart, :], AF.Ln, scale=1.0 / dim, bias=misc.ap[0:npart, 0:1], extra_reads=[misc])
        act(t, t.ap[0:npart, :], t, t.ap[0:npart, :], AF.Exp, scale=-0.5)
        return t

    def sumsq_ps(srcs, npart):
        ps = ps_s()
        n = len(srcs)
        for i, (sv, sap) in enumerate(srcs):
            q = bt()
            act(q, q.ap[0:npart, :], sv, sap, AF.Square)
            mm(ps, [(ps.ap[:, :], ones_b[0:npart, :], q.ap[0:npart, :], i == 0, i == n - 1)], [q, cb])
        return ps

    P.dma("sp", [lambda e: e.dma_start(out=small.ap, in_=small_d)], writes=[small], owner=small)
    P.dma("sp", [lambda e: e.dma_start(out=cst.ap, in_=consts_d[:, 0:NCST_SB])], writes=[cst], owner=cst)
    P.op("dve", lambda e: e.memset(misc.ap[:, 0:1], EPS), writes=[misc])
    P.op("dve", lambda e: e.memset(misc.ap[:, 1:2], math.pi), writes=[misc])
    P.op("dve", lambda e: e.memset(misc.ap[:, 2:3], 1.0), writes=[misc])
    P.op("dve", lambda e: e.tensor_copy(out=cb.ap[:, 0:128], in_=cst.ap[:, 0:128]), reads=[cst], writes=[cb])
    P.op("dve", lambda e: e.memset(cb.ap[:, 128:256], 1.0), writes=[cb])
    P.op("dve", lambda e: e.tensor_copy(out=cb.ap[:, 256:384], in_=cst.ap[:, 128:256]), reads=[cst], writes=[cb])
    import os
    if "maskdma" not in os.environ.get("KSKIP", ""):
        P.dma("pool", [lambda e: e.dma_start(out=cb.ap[:, 384:384 + 2048].rearrange("p (j b) -> p j b", b=512),
                                             in_=consts_d[:, 258:258 + 2048].rearrange("p (j b) -> p j b", b=512))], writes=[cb], owner=cb)
        P.dma("pool", [lambda e: e.dma_start(out=cb.ap[:, 384 + 2048:384 + 2048 + 128], in_=consts_d[:, 258 + 2048:258 + 2048 + 128])], writes=[cb], owner=cb)

    lamt = V(sb("lamt", [128, 8], F32), Res("lamt"))
    for l_ in range(L):
        lam_init_ = 0.8 - 0.6 * math.exp(-0.3 * l_)
        la = tmp()
        P.dma("sp", [lambda e, la=la, nm=nm, i_=i_, l_=l_: e.dma_start(out=la.ap[:, i_ * 64:(i_ + 1) * 64], in_=lam_d[nm][l_, :].partition_broadcast(128))
                     for i_, nm in enumerate(("lam_q1", "lam_k1", "lam_q2", "lam_k2"))], writes=[la], owner=la)
        dve_tt(la, la.ap[:, 256:320], la, la.ap[:, 0:64], la, la.ap[:, 64:128], ALU.mult)
        dve_tt(la, la.ap[:, 320:384], la, la.ap[:, 128:192], la, la.ap[:, 192:256], ALU.mult)
        for i_ in range(2):
            P.op("act", lambda e, la=la, i_=i_: e.activation(out=la.ap[:, 384 + i_ * 64:448 + i_ * 64], in_=la.ap[:, 256 + i_ * 64:320 + i_ * 64], func=AF.Identity,
                                                            accum_out=misc.ap[:, 13 + i_:14 + i_]), reads=[la], writes=[la, misc])
        act(misc, misc.ap[:, 13:15], misc, misc.ap[:, 13:15], AF.Exp)
        dve_tt(lamt, lamt.ap[:, l_:l_ + 1], misc, misc.ap[:, 14:15], misc, misc.ap[:, 13:14], ALU.subtract)
        dve_ts(lamt, lamt.ap[:, l_:l_ + 1], lamt, lamt.ap[:, l_:l_ + 1], -lam_init_, None, ALU.add)

    import os
    KSKIP = os.environ.get("KSKIP", "")
    for (fc, npart, cn, sn) in [] if "rope" in KSKIP else ((CONST_COLS["freq_mla"], 64, "rCm", "rSm"), (CONST_COLS["freq_diff"], 128, "rCd", "rSd")):
        for b0 in range(0, S, 512):
            pi_ = bview(0, 1, I32)
            ld(pi_, pi_.ap[0:npart, 0:512], pos_d[0, b0:b0 + 512].partition_broadcast(npart), Res("posd"))
            ang = tmp()
            P.op("dve", lambda e, ang=ang, pi_=pi_, npart=npart: e.tensor_copy(out=ang.ap[0:npart, :], in_=pi_.ap[0:npart, 0:512]),
                 reads=[pi_], writes=[ang])
            dve_ts(ang, ang.ap[0:npart, :], ang, ang.ap[0:npart, :], cst.ap[0:npart, fc:fc + 1], None, ALU.mult, extra=[cst])
            for (shift, nm) in ((0.0, sn), (math.pi / 2, cn)):
                a2 = tmp()
                dve_ts(a2, a2.ap[0:npart, :], ang, ang.ap[0:npart, :], shift, None, ALU.add)
                kf = tmp()
                dve_ts(kf, kf.ap[0:npart, :], a2, a2.ap[0:npart, :], 1.0 / (2 * math.pi), None, ALU.mult)
                ki = bview(1, 1, I32)
                P.op("dve", lambda e, ki=ki, kf=kf, npart=npart: e.tensor_copy(out=ki.ap[0:npart, 0:512], in_=kf.ap[0:npart, :]), reads=[kf], writes=[ki])
                P.op("dve", lambda e, ki=ki, kf=kf, npart=npart: e.tensor_copy(out=kf.ap[0:npart, :], in_=ki.ap[0:npart, 0:512]), reads=[ki], writes=[kf])
                r = tmp()
                dve_stt(r, r.ap[0:npart, :], kf, kf.ap[0:npart, :], -2 * math.pi, a2, a2.ap[0:npart, :], ALU.mult, ALU.add)
                dve_ts(kf, kf.ap[0:npart, :], r, r.ap[0:npart, :], math.pi, 2 * math.pi, ALU.is_gt, ALU.mult)
                dve_tt(r, r.ap[0:npart, :], r, r.ap[0:npart, :], kf, kf.ap[0:npart, :], ALU.subtract)
                dve_ts(kf, kf.ap[0:npart, :], r, r.ap[0:npart, :], -math.pi, 2 * math.pi, ALU.is_lt, ALU.mult)
                dve_tt(r, r.ap[0:npart, :], r, r.ap[0:npart, :], kf, kf.ap[0:npart, :], ALU.add)
                dve_ts(r, r.ap[0:npart, :], r, r.ap[0:npart, :], 3.141592, -3.141592, ALU.min, ALU.max)
                act(r, r.ap[0:npart, :], r, r.ap[0:npart, :], AF.Sin)
                st(SC[nm][0:npart, b0:b0 + 512], SCR[nm], r, r.ap[0:npart, :])

    for mt in range(0 if "mem" in KSKIP else MEM // 128):
        pcs = []
        for q4 in range(4):
            mtile = tmp()
            ld(mtile, mtile.ap, mem_d[mt * 128:(mt + 1) * 128, q4 * 512:(q4 + 1) * 512], Res("memd"))
            junk = bt()
            P.op("act", lambda e, mtile=mtile, junk=junk, q4=q4: e.activation(out=junk.ap, in_=mtile.ap, func=AF.Square, accum_out=misc.ap[:, 4 + q4:5 + q4]),
                 reads=[mtile], writes=[junk, misc])
            pcs.append(mtile)
        for q4 in range(1, 4):
            dve_tt(misc, misc.ap[:, 4:5], misc, misc.ap[:, 4:5], misc, misc.ap[:, 4 + q4:5 + q4], ALU.add)
        act(misc, misc.ap[:, 5:6], misc, misc.ap[:, 4:5], AF.Sqrt, scale=1.0 / D, bias=misc.ap[:, 0:1])
        P.op("dve", lambda e: e.reciprocal(out=misc.ap[:, 5:6], in_=misc.ap[:, 5:6]), reads=[misc], writes=[misc])
        for q4 in range(4):
            mtile = pcs[q4]
            dve_ts(mtile, mtile.ap, mtile, mtile.ap, misc.ap[:, 5:6], None, ALU.mult, extra=[misc])
            ps = ps_s()
            for j in range(4):
                P.op("pe", lambda e, ps=ps, j=j, mtile=mtile: e.matmul(ps.ap[:, j * 128:(j + 1) * 128], lhsT=mtile.ap[:, j * 128:(j + 1) * 128], rhs=ident_f, start=True, stop=True),
                     reads=[mtile, cst], writes=[ps], pe_acc=True, nmm=1)
            o = bt()
            c0 = q4 * 4
            for j in range(4):
                c = c0 + j
                dve_ts(o, o.ap[:, j * 128:(j + 1) * 128], ps, ps.ap[:, j * 128:(j + 1) * 128], sml("g_mem", 0, c), None, ALU.mult, extra=[small])
            st(SC["memT"][c0 * 128:(c0 + 4) * 128, mt * 128:(mt + 1) * 128].rearrange("(j p) t -> p j t", p=128), SCR["memT"],
               o, o.ap.rearrange("p (j t) -> p j t", j=4))

    def norm_to_hT(gname, l):
        for tb in range(NTB):
            ps = sumsq_ps([(xs[c][tb], xs[c][tb].ap) for c in range(NCH)], 128)
            r = rstd_from_ps(ps, 128, D)
            for c in range(NCH):
                dve_stt(hT[c][tb], hT[c][tb].ap, xs[c][tb], xs[c][tb].ap, sml(gname, l, c), r, r.ap, ALU.mult, ALU.mult, extra=[small])

    def proj_fm(wv, wsel, nk, rhs, tb, npart_out=128):
        ps = ps_s()
        mms = []
        for k in range(nk):
            rv, rap = rhs[k]
            mms.append((ps.ap[0:npart_out, :], wsel(k), rap, k == 0, k == nk - 1))
        mm(ps, mms, [wv] + [rv for rv, _ in rhs])
        return ps

    def hrhs(tb):
        return [(hT[c][tb], hT[c][tb].ap) for c in range(NCH)]

    def wo_accumulate(l, wd, row0, half):
        for cg in range(2):
            wv = load_w(wd[l, row0:row0 + 512, cg * 1024:(cg + 1) * 1024].rearrange("(k p) j -> p k j", p=128), 4, 1024)
            for dtl in range(8):
                dt_ = cg * 8 + dtl
                for tb in range(NTB):
                    ps = nxt("O", psb[4:8])
                    mm(ps, [(ps.ap, wv.ap[:, k, dtl * 128:(dtl + 1) * 128], yT[k][tb].ap, k == 0, k == 3) for k in range(4)],
                       [wv] + [yT[k][tb] for k in range(4)])
                    dve_tt(xs[dt_][tb], xs[dt_][tb].ap, ps, ps.ap, xs[dt_][tb], xs[dt_][tb].ap, ALU.add)

    def store_fm(ps, npart, dst_ap, dst_res, scale=None):
        o = bt()
        copy_any(o, o.ap[0:npart, :], ps, ps.ap[0:npart, :], scale)
        st(dst_ap, dst_res, o, o.ap[0:npart, :])

    def proj_tm_store(l, wsrc_cols, dst_key, half, rhs_sel=None):
        for u in range(2):
            c0 = wsrc_cols + u * 256
            wv = load_w(w_in[l, :, c0:c0 + 256].rearrange("(c p) j -> p c j", p=128), NCH, 256)
            for tt in range(Th // 128):
                tb, o_ = divmod(tt, 4)
                ps = ps_s()
                mm(ps, [(ps.ap[:, 0:256], hT[c][tb].ap[:, o_ * 128:(o_ + 1) * 128], wv.ap[:, c, :], c == 0, c == NCH - 1) for c in range(NCH)],
                   [wv] + [hT[c][tb] for c in range(NCH)])
                o = bt()
                copy_any(o, o.ap[:, 0:256], ps, ps.ap[:, 0:256])
                r0 = half * Th + tt * 128
                st(SC[(dst_key, l)][r0:r0 + 128, u * 256:(u + 1) * 256], SCR[(dst_key, l)], o, o.ap[:, 0:256])

    qbuf = [bview(0, 1), bview(1, 1)]
    kbuf = [bview(2, 1), bview(3, 1)]
    vbuf = [bview(4, 1), bview(5, 1)]
    qrbuf = [bview(6, 1), bview(7, 1)]
    krbuf = bview(8, 1)
    Abuf = bview(9, 1)
    Bbuf = bview(10, 1)
    rot["hb"] = 0

    fin = {"st": None}

    def fin_stage1():
        f = fin["st"]
        if f is None or f.get("rec") is not None:
            return
        rec = tmp()
        act(rec, rec.ap, f["pz"], f["pz"].ap, AF.Ln)
        f["rec"] = rec

    def fin_stage2():
        f = fin["st"]
        if f is None:
            return
        fin_stage1()
        rec = f["rec"]
        act(rec, rec.ap, rec, rec.ap, AF.Exp, scale=-1.0)
        fin["st"] = None
        f["consume"](f["po"], rec)

    def attention(nkeys_fn, score_mms, vsel, consume, tb, extra_reads, post=None):
        t0 = tb_global(tb)
        nkt = (t0 + 512) // 128
        po, pz = ps_o(), ps_z()
        pss = {}
        tri = mask_b[0][:, 0:128]

        def c0_of(kt):
            return max(0, kt - t0 // 128) * 128

        def score(kt):
            ps = ps_s()
            mm(ps, score_mms(kt, ps, c0_of(kt)), extra_reads)
            pss[kt] = ps
        LOOK = 2
        for k in range(min(LOOK, nkt)):
            score(k)
        for kt in range(nkt):
            if kt + LOOK < nkt:
                score(kt + LOOK)
            ps = pss.pop(kt)
            pt = bt()
            j = kt - t0 // 128
            c0 = c0_of(kt)
            if j >= 0 and nkeys_fn == "premask":
                P.op("dve", lambda e, ps=ps, c0=c0: e.scalar_tensor_tensor(out=ps.ap[:, c0:c0 + 128], in0=tri, scalar=60000.0, in1=ps.ap[:, c0:c0 + 128],
                                                                    op0=ALU.mult, op1=ALU.min),
                     reads=[cb, ps], writes=[ps])
            act(pt, pt.ap[:, c0:512], ps, ps.ap[:, c0:512], AF.Exp)
            if kt == 1:
                fin_stage1()
            if j >= 0:
                dve_tt(pt, pt.ap[:, c0:c0 + 128], pt, pt.ap[:, c0:c0 + 128], cb, tri, ALU.mult)
            mm(po, [(po.ap[:, c0:512], vsel(kt), pt.ap[:, c0:512], kt == 0, kt == nkt - 1)], [pt] + extra_reads)
            mm(pz, [(pz.ap[:, c0:512], ones_b, pt.ap[:, c0:512], kt == 0, kt == nkt - 1)], [pt, cb])
            if kt == 2:
                fin_stage2()
        assert fin["st"] is None
        fin["st"] = {"po": po, "pz": pz, "consume": consume}
        if "F" not in KOPT:
            fin_stage2()

    cur = {"half": 0}

    def tb_global(tb):
        return cur["half"] * Th + tb * 512

    def load_kv(l, kkey, vkey, h, i, nkeys):
        kb, vb = kbuf[i], vbuf[i]
        ld(kb, kb.ap[:, 0:nkeys], SC[(kkey, l)][h * 128:(h + 1) * 128, 0:nkeys], SCR[(kkey, l)])
        vdst = vb.ap[:, 0:(nkeys // 128) * 128].rearrange("p (t d) -> p t d", d=128)
        ld(vb, vdst, SC[(vkey, l)][0:nkeys, h * 128:(h + 1) * 128].rearrange("(t p) d -> p t d", p=128), SCR[(vkey, l)])
        return kb, vb, vdst

    def load_q(l, qkey, h, i, half):
        qb = qbuf[i]
        ld(qb, qb.ap[:, 0:Th], SC[(qkey, l)][h * 128:(h + 1) * 128, half * Th:(half + 1) * Th], SCR[(qkey, l)])
        return qb

    def fox(l, half, part):
        isq = 128 ** -0.5
        if part == "proj":
            wv = load_w(w_in[l, :, 1536:1540].rearrange("(c p) j -> p c j", p=128), NCH, 4)
            carry = V(misc.ap[0:4, 8 + l:9 + l], misc.res)
            if half == 0:
                P.op("dve", lambda e: e.memset(misc.ap[0:4, 8 + l:9 + l], 0.0), writes=[misc])
            for tb in range(NTB):
                ps = proj_fm(wv, lambda k, wv=wv: wv.ap[:, k, 0:4], NCH, hrhs(tb), tb, npart_out=4)
                nbf = V(misc.ap[0:4, 3:4], misc.res)
                dve_ts(misc, misc.ap[0:4, 3:4], small, sml("b_f", l, 0, 4), -1.0, None, ALU.mult)
                e1 = tmp()
                act(e1, e1.ap[0:4, :], ps, ps.ap[0:4, :], AF.Exp, scale=-1.0, bias=misc.ap[0:4, 3:4], extra_reads=[misc])
                act(e1, e1.ap[0:4, :], e1, e1.ap[0:4, :], AF.Ln, scale=1.0, bias=misc.ap[0:4, 2:3], extra_reads=[misc])
                dve_ts(e1, e1.ap[0:4, :], e1, e1.ap[0:4, :], -1.0, None, ALU.mult)
                onesf = tmp()
                P.op("dve", lambda e, onesf=onesf: e.memset(onesf.ap[0:4, :], 1.0), writes=[onesf])
                Ft = tmp()
                P.op("dve", lambda e, Ft=Ft, onesf=onesf, e1=e1: e.tensor_tensor_scan(
                    out=Ft.ap[0:4, :], data0=onesf.ap[0:4, :], data1=e1.ap[0:4, :], initial=misc.ap[0:4, 8 + l:9 + l],
                    op0=ALU.mult, op1=ALU.add), reads=[onesf, e1, misc], writes=[Ft])
                P.op("dve", lambda e, Ft=Ft: e.tensor_copy(out=misc.ap[0:4, 8 + l:9 + l], in_=Ft.ap[0:4, 511:512]), reads=[Ft], writes=[misc])
                g0 = tb_global(tb)
                resid = Ft
                for part in range(3):
                    hp = bt()
                    P.op("dve", lambda e, hp=hp, resid=resid: e.tensor_copy(out=hp.ap[0:4, :], in_=resid.ap[0:4, :]), reads=[resid], writes=[hp])
                    hn = bt()
                    dve_ts(hn, hn.ap[0:4, :], hp, hp.ap[0:4, :], -1.0, None, ALU.mult)
                    st(SC[("fFp", l)][:, part, g0:g0 + 512], SCR[("fFp", l)], hp, hp.ap[0:4, :])
                    st(SC[("fFn", l)][:, part, g0:g0 + 512], SCR[("fFn", l)], hn, hn.ap[0:4, :])
                    if part < 2:
                        nr = tmp()
                        dve_tt(nr, nr.ap[0:4, :], resid, resid.ap[0:4, :], hp, hp.ap[0:4, :], ALU.subtract)
                        resid = nr
            for (c0, key, scale) in ((0, "fq", isq), (512, "fk", None)):
                for u in range(2):
                    wv = load_w(w_in[l, :, c0 + u * 256:c0 + (u + 1) * 256].rearrange("(c p) j -> p c j", p=128), NCH, 256)
                    for t in range(2):
                        for tb in range(NTB):
                            ps = proj_fm(wv, lambda k, t=t, wv=wv: wv.ap[:, k, t * 128:(t + 1) * 128], NCH, hrhs(tb), tb)
                            r0 = (u * 2 + t) * 128
                            g0 = tb_global(tb)
                            store_fm(ps, 128, SC[(key, l)][r0:r0 + 128, g0:g0 + 512], SCR[(key, l)], scale)
            proj_tm_store(l, 1024, "fv", half)

            return
        nkeys = (half + 1) * Th
        P.op("dve", lambda e: e.memset(Abuf.ap[:, :], 0.0), writes=[Abuf])
        P.op("dve", lambda e: e.memset(Bbuf.ap[:, :], 0.0), writes=[Bbuf])
        P.op("dve", lambda e: e.memset(Abuf.ap[0:6, :], 1.0), writes=[Abuf])
        P.op("dve", lambda e: e.memset(Bbuf.ap[0:6, :], 1.0), writes=[Bbuf])
        for h in range(4):
            i = h % 2
            kb, vb, vdst = load_kv(l, "fk", "fv", h, i, nkeys)
            qb = load_q(l, "fq", h, i, half)
            ld(Abuf, Abuf.ap[3:6, 0:nkeys], SC[("fFn", l)][h, :, 0:nkeys], SCR[("fFn", l)])
            ld(Bbuf, Bbuf.ap[0:3, 0:Th], SC[("fFp", l)][h, :, half * Th:(half + 1) * Th], SCR[("fFp", l)])
            for tb in range(NTB):
                def smm(kt, ps, c0, kb=kb, qb=qb, tb=tb):
                    return [(ps.ap[:, c0:512], kb.ap[:, kt * 128:(kt + 1) * 128], qb.ap[:, tb * 512 + c0:(tb + 1) * 512], True, False),
                            (ps.ap[:, c0:512], Abuf.ap[:, kt * 128:(kt + 1) * 128], Bbuf.ap[:, tb * 512 + c0:(tb + 1) * 512], False, True)]
                attention("premask", smm, lambda kt, vdst=vdst: vdst[:, kt, :],
                          lambda po, rec, h=h, tb=tb: dve_tt(yT[h][tb], yT[h][tb].ap, po, po.ap, rec, rec.ap, ALU.mult),
                          tb, [kb, qb, vb, Abuf, Bbuf])
        fin_stage2()
        wo_accumulate(l, w_o, 0, half)

    def rope_store(pa, pb, npart, Cv, Sv, tb, dst_ap, dst_res, scale):
        c_ap = Cv.ap[0:npart, tb * 512:(tb + 1) * 512]
        s_ap = Sv.ap[0:npart, tb * 512:(tb + 1) * 512]
        t1, t2 = tmp(), tmp()
        dve_tt(t1, t1.ap[0:npart, :], pa, pa.ap[0:npart, :], Cv, c_ap, ALU.mult)
        dve_tt(t2, t2.ap[0:npart, :], pb, pb.ap[0:npart, :], Sv, s_ap, ALU.mult)
        o = bt()
        if scale is None:
            dve_tt(o, o.ap[0:npart, :], t1, t1.ap[0:npart, :], t2, t2.ap[0:npart, :], ALU.add)
        else:
            dve_tt(t1, t1.ap[0:npart, :], t1, t1.ap[0:npart, :], t2, t2.ap[0:npart, :], ALU.add)
            act(o, o.ap[0:npart, :], t1, t1.ap[0:npart, :], AF.Identity, scale=scale)
        st(dst_ap, dst_res, o, o.ap[0:npart, :])

    def load_rope_tables(Cn, Sn, npart, Cv, Sv, half):
        ld(Cv, Cv.ap[0:npart, 0:Th], SC[Cn][0:npart, half * Th:(half + 1) * Th], SCR[Cn])
        ld(Sv, Sv.ap[0:npart, 0:Th], SC[Sn][0:npart, half * Th:(half + 1) * Th], SCR[Sn])

    def mla(l, half, part):
        isq = 192 ** -0.5
        if part == "proj":
            lat = [V(bview(i, 1, F32).ap[:, 0:512], [blk_res[i]]) for i in range(5)]
            cqn = bview(5, 2)
            ckvn = bview(7, 1)
            krot = V(blk_ap[:, 8 * 2048:8 * 2048 + 1024].rearrange("p (c j) -> p c j", c=NCH), [blk_res[8]])
            rt = V(blk_ap[:, 9 * 2048 + 1024:9 * 2048 + 2048].bitcast(F32), [blk_res[9]])
            uqrot_ap = blk_ap[:, 9 * 2048:9 * 2048 + 768].rearrange("p (kh d) -> p kh d", d=64)
            uqrot = V(uqrot_ap, [blk_res[9]])
            Cm, Sm = bview(10, 1, F32), bview(11, 1, F32)
            load_rope_tables("rCm", "rSm", 64, Cm, Sm, half)
            wq = load_w(w_in[l, :, 1540:1796].rearrange("(c p) j -> p c j", p=128), NCH, 256)
            wq2 = load_w(w_in[l, :, 1796:2052].rearrange("(c p) j -> p c j", p=128), NCH, 256)
            wq3 = load_w(w_in[l, :, 2052:2244].rearrange("(c p) j -> p c j", p=128), NCH, 192)

            def wcol(j):
                u, o_ = divmod(j * 128, 256)
                return (wq, wq2, wq3)[u], o_
            krw = wq3.ap[:, :, 128:192]
            P.op("dve", lambda e: e.tensor_scalar(out=krot.ap[:, :, 0:32], in0=krw[:, :, 32:64], scalar1=-1.0, scalar2=None, op0=ALU.mult),
                 reads=[wq3], writes=[krot])
            P.op("dve", lambda e: e.tensor_copy(out=krot.ap[:, :, 32:64], in_=krw[:, :, 0:32]), reads=[wq3], writes=[krot])
            for tb in range(NTB):
                g0 = tb_global(tb)
                for j in range(5):
                    wv_, o_ = wcol(j)
                    ps = proj_fm(wv_, lambda k, wv_=wv_, o_=o_: wv_.ap[:, k, o_:o_ + 128], NCH, hrhs(tb), tb)
                    copy_any(lat[j], lat[j].ap, ps, ps.ap)
                for (idx, dim, gnm, dstv) in (((0, 1, 2), 384, "g_cq", cqn), ((3, 4), 256, "g_ckv", ckvn)):
                    ps = sumsq_ps([(lat[i], lat[i].ap) for i in idx], 128)
                    act(rt, rt.ap, ps, ps.ap, AF.Ln, scale=1.0 / dim, bias=misc.ap[:, 0:1], extra_reads=[misc])
                    act(rt, rt.ap, rt, rt.ap, AF.Exp, scale=-0.5)
                    for n_, i in enumerate(idx):
                        dve_stt(dstv, dstv.ap[:, n_ * Th + tb * 512:n_ * Th + (tb + 1) * 512], lat[i], lat[i].ap, sml(gnm, l, n_), rt, rt.ap,
                                ALU.mult, ALU.mult, extra=[small])
                pa = proj_fm(wq3, lambda k: wq3.ap[:, k, 128:192], NCH, hrhs(tb), tb, npart_out=64)
                pb = proj_fm(krot, lambda k: krot.ap[:, k, :], NCH, hrhs(tb), tb, npart_out=64)
                rope_store(pa, pb, 64, Cm, Sm, tb, SC[("mkr", l)][:, g0:g0 + 512], SCR[("mkr", l)], None)
            wuq = load_w(w_uq[l].rearrange("(k p) j -> p k j", p=128), 3, 768)
            wukv = load_w(w_ukv[l].rearrange("(k p) j -> p k j", p=128), 2, 1024)
            uq4 = wuq.ap.rearrange("p k (h d) -> p (k h) d", d=192)
            P.op("dve", lambda e: e.tensor_scalar(out=uqrot_ap[:, :, 0:32], in0=uq4[:, :, 160:192], scalar1=-1.0, scalar2=None, op0=ALU.mult),
                 reads=[wuq], writes=[uqrot])
            P.op("dve", lambda e: e.tensor_copy(out=uqrot_ap[:, :, 32:64], in_=uq4[:, :, 128:160]), reads=[wuq], writes=[uqrot])
            wv4 = wukv.ap.rearrange("p k (h two d) -> p k h two d", two=2, d=128)
            for tb in range(NTB):
                g0 = tb_global(tb)
                cq_r = [(cqn, cqn.ap[:, k * Th + tb * 512:k * Th + (tb + 1) * 512]) for k in range(3)]
                ckv_r = [(ckvn, ckvn.ap[:, k * Th + tb * 512:k * Th + (tb + 1) * 512]) for k in range(2)]
                for h in range(4):
                    ps = proj_fm(wuq, lambda k, h=h: wuq.ap[:, k, h * 192:h * 192 + 128], 3, cq_r, tb)
                    store_fm(ps, 128, SC[("mqn", l)][h * 128:(h + 1) * 128, g0:g0 + 512], SCR[("mqn", l)], isq)
                    pa = proj_fm(wuq, lambda k, h=h: wuq.ap[:, k, h * 192 + 128:h * 192 + 192], 3, cq_r, tb, npart_out=64)
                    pb = proj_fm(uqrot, lambda k, h=h: uqrot_ap[:, k * 4 + h, :], 3, cq_r, tb, npart_out=64)
                    rope_store(pa, pb, 64, Cm, Sm, tb, SC[("mqr", l)][h * 64:(h + 1) * 64, g0:g0 + 512], SCR[("mqr", l)], isq)
                    ps = proj_fm(wukv, lambda k, h=h: wukv.ap[:, k, h * 256:h * 256 + 128], 2, ckv_r, tb)
                    store_fm(ps, 128, SC[("mkn", l)][h * 128:(h + 1) * 128, g0:g0 + 512], SCR[("mkn", l)])
                for o_ in range(4):
                    ps = ps_s()
                    mm(ps, [(ps.ap.rearrange("p (h d) -> p h d", d=128),
                             ckvn.ap[:, k * Th + tb * 512 + o_ * 128:k * Th + tb * 512 + (o_ + 1) * 128], wv4[:, k, :, 1, :], k == 0, k == 1)
                            for k in range(2)], [wukv, ckvn])
                    o = bt()
                    copy_any(o, o.ap, ps, ps.ap)
                    r0 = g0 + o_ * 128
                    st(SC[("mv", l)][r0:r0 + 128, :], SCR[("mv", l)], o, o.ap)

            return
        nkeys = (half + 1) * Th
        P.op("dve", lambda e: e.memset(krbuf.ap[64:128, :], 0.0), writes=[krbuf])
        for qr_ in qrbuf:
            P.op("dve", lambda e, qr_=qr_: e.memset(qr_.ap[64:128, :], 0.0), writes=[qr_])
        ld(krbuf, krbuf.ap[0:64, 0:nkeys], SC[("mkr", l)][:, 0:nkeys], SCR[("mkr", l)])
        for h in range(4):
            i = h % 2
            kb, vb, vdst = load_kv(l, "mkn", "mv", h, i, nkeys)
            qb = load_q(l, "mqn", h, i, half)
            qr = qrbuf[i]
            ld(qr, qr.ap[0:64, 0:Th], SC[("mqr", l)][h * 64:(h + 1) * 64, half * Th:(half + 1) * Th], SCR[("mqr", l)])
            for tb in range(NTB):
                def smm(kt, ps, c0, kb=kb, qb=qb, qr=qr, tb=tb):
                    return [(ps.ap[:, c0:512], kb.ap[:, kt * 128:(kt + 1) * 128], qb.ap[:, tb * 512 + c0:(tb + 1) * 512], True, False),
                            (ps.ap[:, c0:512], krbuf.ap[:, kt * 128:(kt + 1) * 128], qr.ap[:, tb * 512 + c0:(tb + 1) * 512], False, True)]
                attention(None, smm, lambda kt, vdst=vdst: vdst[:, kt, :],
                          lambda po, rec, h=h, tb=tb: dve_tt(yT[h][tb], yT[h][tb].ap, po, po.ap, rec, rec.ap, ALU.mult),
                          tb, [kb, qb, vb, qr, krbuf])
        fin_stage2()
        wo_accumulate(l, w_o, 512, half)

    sgst_ap = sb("sgst", [128, 8 * (Th // 128)], F32)
    sgst = [V(sgst_ap[:, i * 8:(i + 1) * 8], Res(f"sgst{i}")) for i in range(Th // 128)]

    def sgu(l, half):
        NT = Th // 128
        uT = bview(0, 2)
        bsb = V(bview(2, 1, F32).ap[:, 0:512], [blk_res[2]])
        lng = V(bview(3, 1, F32).ap[:, 0:512], [blk_res[3]])
        lnb = V(bview(4, 1, F32).ap[:, 0:512], [blk_res[4]])
        wsT = V(blk_ap[:, 5 * 2048:5 * 2048 + 512], [blk_res[5]])
        wsn = V(blk_ap[:, 5 * 2048 + 1024:5 * 2048 + 2048].bitcast(F32), [blk_res[5]])
        vt = [V(blk_ap[:, (6 + i // 2) * 2048 + (i % 2) * 1024:(6 + i // 2) * 2048 + (i % 2 + 1) * 1024].bitcast(F32), Res(f"vt{i}"))
              for i in range(NT)]
        vnb = [V(blk_ap[:, 10 * 2048 + i * 512:10 * 2048 + (i + 1) * 512], [Res(f"vnb{i}")]) for i in range(4)]
        P.handoff(blk_res[6:11], vt + vnb)
        ld(bsb, bsb.ap, b_s[l, :].partition_broadcast(128), Res("bsd"))
        ld(lng, lng.ap, sgu_ln_g[l, :].partition_broadcast(128), Res("lngd"))
        ld(lnb, lnb.ap, sgu_ln_b[l, :].partition_broadcast(128), Res("lnbd"))
        ld(wsn, wsn.ap.rearrange("p (g s) -> p g s", g=4), w_s[l].rearrange("g t s -> t g s"), Res("wsd"))
        built = [False]

        def build_wsT():
            if built[0]:
                return
            built[0] = True
            wsm = bt()
            for g in range(4):
                dve_tt(wsm, wsm.ap[:, g * 128:(g + 1) * 128], wsn, wsn.ap[:, g * 128:(g + 1) * 128], cst, cst.ap[:, 128:256], ALU.mult)
            pst = ps_s()
            for g in range(4):
                P.op("pe", lambda e, g=g: e.matmul(pst.ap[:, g * 128:(g + 1) * 128], lhsT=wsm.ap[:, g * 128:(g + 1) * 128], rhs=ident_b, start=True, stop=True),
                     reads=[wsm, cb], writes=[pst], pe_acc=True, nmm=1)
            P.op("dve", lambda e: e.tensor_copy(out=wsT.ap, in_=pst.ap), reads=[pst], writes=[wsT])
        wu = [load_w(w_in[l, :, 2244 + u * 256:2244 + (u + 1) * 256].rearrange("(c p) j -> p c j", p=128), NCH, 256) for u in range(2)]
        wvv = [load_w(w_in[l, :, 2756 + u * 256:2756 + (u + 1) * 256].rearrange("(c p) j -> p c j", p=128), NCH, 256) for u in range(2)]
        pend = []

        def mix(item):
            tt, vn = item
            tb, o_ = divmod(tt, 4)
            build_wsT()
            ps = ps_s()
            for g in range(4):
                mm(ps, [(ps.ap[:, g * 128:(g + 1) * 128], vn.ap[:, g * 128:(g + 1) * 128], wsT.ap[:, g * 128:(g + 1) * 128], True, True)], [vn, wsT])
            t2 = tmp()
            dve_tt(t2, t2.ap, ps, ps.ap, bsb, bsb.ap, ALU.add)
            y3 = yT_ap[:, :, tb * 512 + o_ * 128:tb * 512 + (o_ + 1) * 128]
            u3 = uT.ap[:, 0:4 * Th].rearrange("p (g t) -> p g t", g=4)[:, :, tt * 128:(tt + 1) * 128]
            P.op("dve", lambda e, y3=y3, t2=t2, u3=u3: e.tensor_tensor(out=y3, in0=t2.ap.rearrange("p (g t) -> p g t", g=4), in1=u3, op=ALU.mult),
                 reads=[t2, uT], writes=[yT[g][tb] for g in range(4)])
        for tb in range(NTB):
            for g in range(4):
                wv_ = wu[g // 2]
                ps = proj_fm(wv_, lambda k, wv_=wv_, g=g: wv_.ap[:, k, (g % 2) * 128:(g % 2) * 128 + 128], NCH, hrhs(tb), tb)
                act(uT, uT.ap[:, g * Th + tb * 512:g * Th + (tb + 1) * 512], ps, ps.ap, AF.Gelu_apprx_tanh)
            build_wsT()
            for o_ in range(4):
                tt = tb * 4 + o_
                t, stt_ = vt[tt], sgst[tt]
                ps = ps_s()
                for u in range(2):
                    mm(ps, [(ps.ap[:, u * 256:(u + 1) * 256], hT[c][tb].ap[:, o_ * 128:(o_ + 1) * 128], wvv[u].ap[:, c, :], c == 0, c == NCH - 1) for c in range(NCH)],
                       [wvv[u]] + [hT[c][tb] for c in range(NCH)])
                P.op("act", lambda e, t=t, ps=ps, stt_=stt_: e.activation(out=t.ap, in_=ps.ap, func=AF.Gelu_apprx_tanh, accum_out=stt_.ap[:, 0:1]),
                     reads=[ps], writes=[t, stt_])
                junk = bt()
                P.op("act", lambda e, t=t, junk=junk, stt_=stt_: e.activation(out=junk.ap, in_=t.ap, func=AF.Square, accum_out=stt_.ap[:, 1:2]),
                     reads=[t], writes=[junk, stt_])
                dve_ts(stt_, stt_.ap[:, 2:4], stt_, stt_.ap[:, 0:2], 1.0 / 512, None, ALU.mult)
                dve_stt(stt_, stt_.ap[:, 4:5], stt_, stt_.ap[:, 2:3], stt_.ap[:, 2:3], stt_, stt_.ap[:, 3:4], ALU.mult, ALU.subtract)
                act(stt_, stt_.ap[:, 5:6], stt_, stt_.ap[:, 4:5], AF.Sqrt, scale=-1.0, bias=misc.ap[:, 0:1], extra_reads=[misc])
                P.op("dve", lambda e, stt_=stt_: e.reciprocal(out=stt_.ap[:, 5:6], in_=stt_.ap[:, 5:6]), reads=[stt_], writes=[stt_])
                dve_ts(t, t.ap, t, t.ap, stt_.ap[:, 2:3], stt_.ap[:, 5:6], ALU.subtract, ALU.mult, extra=[stt_])
                dve_tt(t, t.ap, t, t.ap, lng, lng.ap, ALU.mult)
                vn = vnb[tt % 4]
                dve_tt(vn, vn.ap, t, t.ap, lnb, lnb.ap, ALU.add)
                pend.append((tt, vn))
                if len(pend) > 3:
                    mix(pend.pop(0))
        while pend:
            mix(pend.pop(0))
        P.handoff(vt + vnb, blk_res[6:11])
        wo_accumulate(l, w_o, 1024, half)

    def diff(l, half, part):
        lam_init = 0.8 - 0.6 * math.exp(-0.3 * l)
        isq = 64 ** -0.5
        if part == "proj":
            Cd, Sd = bview(0, 1, F32), bview(1, 1, F32)
            load_rope_tables("rCd", "rSd", 128, Cd, Sd, half)
            pend = []

            def finish(item):
                pa, q16, key, r0, tb, scale = item
                g0 = tb_global(tb)
                pb = ps_s()
                mm(pb, [(pb.ap, Rd_b, q16.ap, True, True)], [q16, cb])
                rope_store(pa, pb, 128, Cd, Sd, tb, SC[(key, l)][r0:r0 + 128, g0:g0 + 512], SCR[(key, l)], scale)
            for (c0, key, scale) in ((3268, "dq", isq), (3780, "dk", None)):
                for u in range(2):
                    wv = load_w(w_in[l, :, c0 + u * 256:c0 + (u + 1) * 256].rearrange("(c p) j -> p c j", p=128), NCH, 256)
                    for t in range(2):
                        for tb in range(NTB):
                            pa = proj_fm(wv, lambda k, t=t, wv=wv: wv.ap[:, k, t * 128:(t + 1) * 128], NCH, hrhs(tb), tb)
                            q16 = bt()
                            act(q16, q16.ap, pa, pa.ap, AF.Identity)
                            pend.append((pa, q16, key, (u * 2 + t) * 128, tb, scale))
                            if len(pend) > 1:
                                finish(pend.pop(0))
            while pend:
                finish(pend.pop(0))
            proj_tm_store(l, 4292, "dv", half)

            return
        neglam = lamt.ap[:, l:l + 1]
        nkeys = (half + 1) * Th
        o1b = [V(blk_ap[:, 6 * 2048 + i_ * 1024:6 * 2048 + (i_ + 1) * 1024].bitcast(F32), [Res(f"o1b{i_}")]) for i_ in range(2)]
        P.handoff([blk_res[6]], o1b)
        pendn = []

        def post(item):
            o1, h, tb = item
            ps = sumsq_ps([(o1, o1.ap)], 128)
            r = rstd_from_ps(ps, 128, 128)
            dve_stt(o1, o1.ap, o1, o1.ap, sml("g_diff", l, 0), r, r.ap, ALU.mult, ALU.mult, extra=[small])
            dve_ts(yT[h][tb], yT[h][tb].ap, o1, o1.ap, 1.0 - lam_init, None, ALU.mult)
        nq = 0
        qm1 = [bview(7, 1), bview(8, 1)]
        for i_ in range(2):
            P.op("dve", lambda e, i_=i_: e.memset(qbuf[i_].ap[64:128, :], 0.0), writes=[qbuf[i_]])
            P.op("dve", lambda e, i_=i_: e.memset(qm1[i_].ap[0:64, :], 0.0), writes=[qm1[i_]])
        for h in range(4):
            i = h % 2
            kb, vb, vdst = load_kv(l, "dk", "dv", h, i, nkeys)
            qms = (qbuf[i], qm1[i])
            ld(qms[0], qms[0].ap[0:64, 0:Th], SC[("dq", l)][h * 128:h * 128 + 64, half * Th:(half + 1) * Th], SCR[("dq", l)])
            ld(qms[1], qms[1].ap[64:128, 0:Th], SC[("dq", l)][h * 128 + 64:(h + 1) * 128, half * Th:(half + 1) * Th], SCR[("dq", l)])
            for tb in range(NTB):
                o1 = o1b[nq % 2]
                nq += 1
                for m in range(2):
                    def smm(kt, ps, c0, kb=kb, qb=qms[m], tb=tb):
                        return [(ps.ap[:, c0:512], kb.ap[:, kt * 128:(kt + 1) * 128], qb.ap[:, tb * 512 + c0:(tb + 1) * 512], True, True)]
                    if m == 0:
                        def cons(po, rec, o1=o1):
                            dve_tt(o1, o1.ap, po, po.ap, rec, rec.ap, ALU.mult)
                    else:
                        def cons(po, rec, o1=o1, h=h, tb=tb):
                            dve_tt(rec, rec.ap, po, po.ap, rec, rec.ap, ALU.mult)
                            dve_stt(o1, o1.ap, rec, rec.ap, neglam, o1, o1.ap, ALU.mult, ALU.add, extra=[lamt])
                            pendn.append((o1, h, tb))
                            if len(pendn) > 1:
                                post(pendn.pop(0))
                    attention(None, smm, lambda kt, vdst=vdst: vdst[:, kt, :], cons, tb, [kb, qms[m], vb])
        fin_stage2()
        while pendn:
            post(pendn.pop(0))
        P.handoff(o1b, [blk_res[6]])
        wo_accumulate(l, w_o, 1536, half)

    def cross(l, half):
        isq = 128 ** -0.5
        memT = bview(0, 2)
        ld(memT, memT.ap.rearrange("p (c t) -> p c t", c=NCH), SC["memT"].rearrange("(c p) t -> p c t", p=128), SCR["memT"])
        memv = memT.ap.rearrange("p (c t) -> p c t", c=NCH)
        qx = bview(2, 2)
        kx = bview(4, 1)
        vx = bview(5, 1)
        for u in range(2):
            wv = load_w(w_ck[l, :, u * 256:(u + 1) * 256].rearrange("(c p) j -> p c j", p=128), NCH, 256)
            for t in range(2):
                h = u * 2 + t
                ps = ps_s()
                mm(ps, [(ps.ap[:, 0:256], wv.ap[:, c, t * 128:(t + 1) * 128], memv[:, c, :], c == 0, c == NCH - 1) for c in range(NCH)], [wv, memT])
                copy_any(kx, kx.ap[:, h * 256:(h + 1) * 256], ps, ps.ap[:, 0:256])
        for u in range(2):
            wv = load_w(w_cv[l, :, u * 256:(u + 1) * 256].rearrange("(c p) j -> p c j", p=128), NCH, 256)
            for mt in range(2):
                ps = ps_s()
                mm(ps, [(ps.ap[:, 0:256], memv[:, c, mt * 128:(mt + 1) * 128], wv.ap[:, c, :], c == 0, c == NCH - 1) for c in range(NCH)], [wv, memT])
                copy_any(vx, vx.ap[:, mt * 512 + u * 256:mt * 512 + (u + 1) * 256], ps, ps.ap[:, 0:256])
        norm_to_hT("g_cross", l)
        for u in range(2):
            wv = load_w(w_cq[l, :, u * 256:(u + 1) * 256].rearrange("(c p) j -> p c j", p=128), NCH, 256)
            for t in range(2):
                h = u * 2 + t
                for tb in range(NTB):
                    ps = proj_fm(wv, lambda k, t=t, wv=wv: wv.ap[:, k, t * 128:(t + 1) * 128], NCH, hrhs(tb), tb)
                    copy_any(qx, qx.ap[:, h * Th + tb * 512:h * Th + (tb + 1) * 512], ps, ps.ap, isq)
        pendc = [None]

        def flushc():
            f = pendc[0]
            if f is None:
                return
            pendc[0] = None
            po_, pz_, h_, tb_ = f
            rec = tmp()
            act(rec, rec.ap, pz_, pz_.ap, AF.Ln)
            act(rec, rec.ap, rec, rec.ap, AF.Exp, scale=-1.0)
            dve_tt(yT[h_][tb_], yT[h_][tb_].ap, po_, po_.ap, rec, rec.ap, ALU.mult)
        for h in range(4):
            for tb in range(NTB):
                po, pz = ps_o(), ps_z()
                for mt in range(2):
                    ps = ps_s()
                    mm(ps, [(ps.ap, kx.ap[:, h * 256 + mt * 128:h * 256 + (mt + 1) * 128], qx.ap[:, h * Th + tb * 512:h * Th + (tb + 1) * 512], True, True)], [kx, qx])
                    pt = bt()
                    act(pt, pt.ap, ps, ps.ap, AF.Exp)
                    if mt == 0:
                        flushc()
                    mm(po, [(po.ap, vx.ap[:, mt * 512 + h * 128:mt * 512 + (h + 1) * 128], pt.ap, mt == 0, mt == 1)], [pt, vx])
                    mm(pz, [(pz.ap, ones_b, pt.ap, mt == 0, mt == 1)], [pt, cb])
                pendc[0] = (po, pz, h, tb)
        flushc()
        wo_accumulate(l, w_co, 0, half)

    def ffn(l, half):
        norm_to_hT("g_ffn", l)
        abuf = [bview(0, 1), bview(1, 1)]
        wpool = wslots + [bview(2, 2), bview(4, 2), bview(6, 2), bview(8, 2), bview(10, 2)]
        nslots = 2 * NTB
        dper = NCH // nslots

        def down_groups(wd, ab, dts):
            for dt_ in dts:
                for tb in range(NTB):
                    ps = nxt("O", psb[4:8])
                    mm(ps, [(ps.ap, wd.ap[:, t, dt_ * 128:(dt_ + 1) * 128], ab.ap[:, t * Th + tb * 512:t * Th + (tb + 1) * 512], t == 0, t == 1) for t in range(2)],
                       [wd, ab])
                    dve_tt(xs[dt_][tb], xs[dt_][tb].ap, ps, ps.ap, xs[dt_][tb], xs[dt_][tb].ap, ALU.add)
        prev = None
        for j in range(FF // 256):
            wg = load_w(w_gate[l, :, j * 256:(j + 1) * 256].rearrange("(c p) j -> p c j", p=128), NCH, 256, wpool)
            wu = load_w(w_up[l, :, j * 256:(j + 1) * 256].rearrange("(c p) j -> p c j", p=128), NCH, 256, wpool)
            ab = abuf[j % 2]
            k = 0
            for t in range(2):
                for tb in range(NTB):
                    pg = proj_fm(wg, lambda k_, t=t, wg=wg: wg.ap[:, k_, t * 128:(t + 1) * 128], NCH, hrhs(tb), tb)
                    sg = tmp()
                    act(sg, sg.ap, pg, pg.ap, AF.Silu)
                    if prev is not None and "B" in KOPT:
                        down_groups(prev[0], prev[1], range(k * dper, k * dper + dper // 2))
                    pu = proj_fm(wu, lambda k_, t=t, wu=wu: wu.ap[:, k_, t * 128:(t + 1) * 128], NCH, hrhs(tb), tb)
                    dve_tt(ab, ab.ap[:, t * Th + tb * 512:t * Th + (tb + 1) * 512], pu, pu.ap, sg, sg.ap, ALU.mult)
                    if prev is not None:
                        down_groups(prev[0], prev[1], range(k * dper + dper // 2, (k + 1) * dper) if "B" in KOPT else range(k * dper, (k + 1) * dper))
                    k += 1
            wd = load_w(w_down[l, j * 256:(j + 1) * 256, :].rearrange("(k p) d -> p k d", p=128), 2, 2048, wpool)
            prev = (wd, ab)
        down_groups(prev[0], prev[1], range(NCH))

    for half in range(HALVES):
        cur["half"] = half
        for tt in range(0 if "xload" in KSKIP else Th // 128):
            tb, o_ = divmod(tt, 4)
            r0 = half * Th + tt * 128
            for c0 in range(0, NCH, 4):
                xin = tmp()
                ld(xin, xin.ap, x_d[r0:r0 + 128, c0 * 128:(c0 + 4) * 128], Res("xd"))
                if "xmm" in KSKIP:
                    continue
                ps = ps_s()
                for j in range(4):
                    P.op("pe", lambda e, ps=ps, j=j, xin=xin: e.matmul(ps.ap[:, j * 128:(j + 1) * 128], lhsT=xin.ap[:, j * 128:(j + 1) * 128], rhs=ident_f, start=True, stop=True),
                         reads=[xin, cst], writes=[ps], pe_acc=True, nmm=1)
                dst_ap = xs_ap[:, c0:c0 + 4, tb * 512 + o_ * 128:tb * 512 + (o_ + 1) * 128]
                if "xcp" in KSKIP:
                    continue
                dv_ = V(dst_ap, [xs[c0 + j][tb].res[0] for j in range(4)])
                src_ = ps.ap.rearrange("p (j t) -> p j t", j=4)
                P.op("dve", lambda e, dst_ap=dst_ap, src_=src_: e.tensor_copy(out=dst_ap, in_=src_), reads=[ps], writes=[dv_])
        import os
        stg = os.environ.get("KSTAGE", "fmsdcx")
        for l in range(L):
            P.phases.append((f"h{half}l{l}:norm", P.nmm))
            norm_to_hT("g_mix", l)
            P.phases.append((f"h{half}l{l}:projs", P.nmm))
            fox(l, half, "proj")
            mla(l, half, "proj")
            diff(l, half, "proj")
            P.phases.append((f"h{half}l{l}:sgu", P.nmm))
            sgu(l, half)
            P.phases.append((f"h{half}l{l}:fox", P.nmm))
            fox(l, half, "attn")
            P.phases.append((f"h{half}l{l}:mla", P.nmm))
            mla(l, half, "attn")
            P.phases.append((f"h{half}l{l}:diff", P.nmm))
            diff(l, half, "attn")
            P.phases.append((f"h{half}l{l}:cross", P.nmm))
            cross(l, half)
            P.phases.append((f"h{half}l{l}:ffn", P.nmm))
            ffn(l, half)
        P.phases.append((f"h{half}:final", P.nmm))
        for tb in range(0 if "final" in KSKIP else NTB):
            ps = sumsq_ps([(xs[c][tb], xs[c][tb].ap) for c in range(NCH)], 128)
            r0_ = rstd_from_ps(ps, 128, D)
            r = V(bview(11, 1, F32).ap[:, 0:512], [blk_res[11]])
            P.op("dve", lambda e, r=r, r0_=r0_: e.tensor_copy(out=r.ap, in_=r0_.ap), reads=[r0_], writes=[r])
            for o_ in range(4):
                osb = bview(2 * (o_ % 2), 2, F32)
                for c0 in range(0, NCH, 4):
                    pst = ps_s()
                    for j in range(4):
                        c = c0 + j
                        xn = tmp()
                        dve_stt(xn, xn.ap[:, 0:128], xs[c][tb], xs[c][tb].ap[:, o_ * 128:(o_ + 1) * 128], sml("g_final", 0, c),
                                r, r.ap[:, o_ * 128:(o_ + 1) * 128], ALU.mult, ALU.mult, extra=[small])
                        P.op("pe", lambda e, pst=pst, j=j, xn=xn: e.matmul(pst.ap[:, j * 128:(j + 1) * 128], lhsT=xn.ap[:, 0:128], rhs=ident_f, start=True, stop=True),
                             reads=[xn, cst], writes=[pst], pe_acc=True, nmm=1)
                    copy_any(osb, osb.ap[:, c0 * 128:(c0 + 4) * 128], pst, pst.ap)
                r0 = half * Th + tb * 512 + o_ * 128
                st(out_d[r0:r0 + 128, :], Res("outd"), osb, osb.ap)
    P.wait_all("sp", list(ALL_RES))
    for e in COMPUTE:
        if P.ecnt[e]:
            P._need("sp", (P.esem[e], P.ecnt[e], e))
    P.emit()
    return nc, P


def pack_small(inp, b, L):
    soff, NS = small_layout(L)
    sm = np.zeros((128, NS), np.float32)

    def fm(v):
        return np.ascontiguousarray(v.reshape(-1, 128).T)
    for l in range(L):
        for nm in ("g_mix", "g_cross", "g_ffn", "g_cq", "g_ckv", "g_diff"):
            a = fm(np.asarray(inp[nm][l], np.float32))
            sm[:, soff[(nm, l)]:soff[(nm, l)] + a.shape[1]] = a
        sm[0:4, soff[("b_f", l)]] = np.asarray(inp["b_f"][l], np.float32)
    for nm in ("g_mem", "g_final"):
        a = fm(np.asarray(inp[nm], np.float32))
        sm[:, soff[(nm, 0)]:soff[(nm, 0)] + 16] = a
    return sm


def run(inp, S, L, FF, ncores):
    nc, P = build_program(S, L, FF)
    consts = make_consts()
    small = pack_small(inp, 0, L)
    shared = {}
    for k in ("w_in", "w_uq", "w_ukv", "sgu_ln_g", "sgu_ln_b", "w_s", "lam_q1", "lam_k1", "lam_q2", "lam_k2",
              "w_o", "w_cq", "w_ck", "w_cv", "w_co", "w_gate", "w_up", "w_down"):
        shared[k] = np.ascontiguousarray(np.asarray(inp[k], np.float32))
    shared["b_s"] = np.ascontiguousarray(np.asarray(inp["b_s"], np.float32).reshape(L, 512))
    shared["small"] = small
    shared["consts"] = consts
    in_maps = []
    for b in range(ncores):
        m = dict(shared)
        m["x"] = np.ascontiguousarray(np.asarray(inp["x"][b], np.float32))
        m["mem"] = np.ascontiguousarray(np.asarray(inp["mem"][b], np.float32))
        m["positions"] = np.ascontiguousarray(np.asarray(inp["positions"][b], np.int32).reshape(1, S))
        in_maps.append(m)
    res = run_bass_kernel_spmd(nc, in_maps, core_ids=list(range(ncores)))
    return np.stack([r["out"] for r in res.results], axis=0)


def kernel(**inputs):
    return run(inputs, 2048, 4, 5632, 8).astype(np.float32)
```
